# Optimizing a Trainium2 kernel written in Bass

```python
import math
import jax, jax.numpy as jnp
from jax import lax
import numpy as np

D_MODEL = 1024
BATCH = 2
SEQ = 8192
DEPTH = 2

HEAD_DIM = 64
A_Q_HEADS = 8
A_KV_HEADS = 2
B_HEADS = 8
B_BRANCHES = ((128, 1), (512, 4), (2048, 16))
C_HEADS = 8
C_V_DIM = 2 * HEAD_DIM
D_FF = ((-(-8 * D_MODEL // 3) + 255) // 256) * 256
GRID_W = 64
ROPE_THETA = 10000.0
Q_BLOCK = 128
EPS = 1e-6

N_EVEN = (DEPTH + 1) // 2
N_ODD = DEPTH // 2

A_Q_W = A_Q_HEADS * HEAD_DIM
A_KV_W = A_KV_HEADS * HEAD_DIM
B_W = B_HEADS * HEAD_DIM
EVEN_SPLITS = tuple(int(v) for v in np.cumsum([A_Q_W, A_KV_W, A_KV_W, B_W, B_W]))
EVEN_IN_W = A_Q_W + 2 * A_KV_W + 3 * B_W
EVEN_OUT_W = A_Q_W + B_W
C_QK_W = C_HEADS * 2 * HEAD_DIM
C_V_W = C_HEADS * C_V_DIM
ODD_SPLITS = (C_QK_W, 2 * C_QK_W)
ODD_IN_W = 2 * C_QK_W + C_V_W
ODD_OUT_W = C_V_W

kernel_name = "hybrid_gqa_dilated_diffattn_encoder"


def rms_norm(x, g):
    xf = x.astype(jnp.float32)
    y = xf * lax.rsqrt(jnp.mean(xf * xf, axis=-1, keepdims=True) + EPS)
    return (y * g.astype(jnp.float32)).astype(x.dtype)


def rope_angles(pos, dim):
    inv = ROPE_THETA ** (-jnp.arange(0, dim, 2, dtype=jnp.float32) / dim)
    return pos.astype(jnp.float32)[:, None] * inv[None, :]


def apply_rope(x, ang):
    cos = jnp.cos(ang)[None, :, None, :]
    sin = jnp.sin(ang)[None, :, None, :]
    x1, x2 = jnp.split(x.astype(jnp.float32), 2, axis=-1)
    out = jnp.concatenate([x1 * cos - x2 * sin, x1 * sin + x2 * cos], axis=-1)
    return out.astype(x.dtype)


def gqa_attention(q, k, v):
    bn, s, hq, d = q.shape
    hkv = k.shape[2]
    g = hq // hkv
    nb = s // Q_BLOCK
    scale = d ** -0.5
    qb = q.reshape(bn, nb, Q_BLOCK, hkv, g, d).transpose(1, 0, 2, 3, 4, 5)

    def block(qblk):
        sc = jnp.einsum('bqkgd,bskd->bkgqs', qblk, k).astype(jnp.float32) * scale
        p = jax.nn.softmax(sc, axis=-1).astype(v.dtype)
        return jnp.einsum('bkgqs,bskd->bqkgd', p, v)

    o = lax.map(block, qb)
    return o.transpose(1, 0, 2, 3, 4, 5).reshape(bn, s, hq * d)


def dilated_attention(q, k, v):
    bn, s, h, d = q.shape
    nb = s // Q_BLOCK
    scale = d ** -0.5
    neg = jnp.finfo(jnp.float32).min

    def block(b_idx):
        start = b_idx * Q_BLOCK
        t = start + jnp.arange(Q_BLOCK, dtype=jnp.int32)
        qblk = lax.dynamic_slice_in_dim(q, start, Q_BLOCK, axis=1)
        outs, lses = [], []
        for window, dil in B_BRANCHES:
            half = window // (2 * dil)
            offs = dil * jnp.arange(-half, half + 1, dtype=jnp.int32)
            idx = t[:, None] + offs[None, :]
            valid = (idx >= 0) & (idx < s)
            idx_c = jnp.clip(idx, 0, s - 1)
            kg = k[:, idx_c]
            vg = v[:, idx_c]
            sc = jnp.einsum('bqhd,bqkhd->bhqk', qblk, kg).astype(jnp.float32) * scale
            sc = jnp.where(valid[None, None], sc, neg)
            lse = jax.nn.logsumexp(sc, axis=-1)
            p = jnp.exp(sc - lse[..., None]).astype(v.dtype)
            outs.append(jnp.einsum('bhqk,bqkhd->bqhd', p, vg))
            lses.append(lse)
        w = jax.nn.softmax(jnp.stack(lses, axis=0), axis=0)
        w = w.transpose(0, 1, 3, 2)[..., None].astype(v.dtype)
        return jnp.sum(w * jnp.stack(outs, axis=0), axis=0)

    o = lax.map(block, jnp.arange(nb, dtype=jnp.int32))
    return o.transpose(1, 0, 2, 3, 4).reshape(bn, s, h * d)


def diff_attention(q1, q2, k1, k2, v, lam):
    bn, s, h, d = q1.shape
    nb = s // Q_BLOCK
    scale = d ** -0.5
    q1b = q1.reshape(bn, nb, Q_BLOCK, h, d).transpose(1, 0, 2, 3, 4)
    q2b = q2.reshape(bn, nb, Q_BLOCK, h, d).transpose(1, 0, 2, 3, 4)

    def block(qs):
        qa, qb = qs
        s1 = jnp.einsum('bqhd,bshd->bhqs', qa, k1).astype(jnp.float32) * scale
        s2 = jnp.einsum('bqhd,bshd->bhqs', qb, k2).astype(jnp.float32) * scale
        p = jax.nn.softmax(s1, axis=-1) - lam * jax.nn.softmax(s2, axis=-1)
        return jnp.einsum('bhqs,bshe->bqhe', p.astype(v.dtype), v)

    o = lax.map(block, (q1b, q2b))
    return o.transpose(1, 0, 2, 3, 4).reshape(bn, s, h, v.shape[-1])


def swiglu(h, w_gate, w_up, w_down):
    return (jax.nn.silu(h @ w_gate) * (h @ w_up)) @ w_down


def setup_inputs(seed: int = 0) -> dict:
    key = jax.random.key(seed)
    ks = jax.random.split(key, 20)

    def dense(k, shape):
        return jax.random.normal(k, shape, jnp.float32) * shape[-2] ** -0.5

    def gain(k, shape):
        return 1.0 + 0.02 * jax.random.normal(k, shape, jnp.float32)

    return {
        "x": jax.random.normal(ks[0], (BATCH, SEQ, D_MODEL), jnp.float32),
        "attn_norm": gain(ks[1], (DEPTH, D_MODEL)),
        "ffn_norm": gain(ks[2], (DEPTH, D_MODEL)),
        "final_norm": gain(ks[3], (D_MODEL,)),
        "w_in_even": dense(ks[4], (N_EVEN, D_MODEL, EVEN_IN_W)),
        "a_q_norm": gain(ks[5], (N_EVEN, HEAD_DIM)),
        "a_k_norm": gain(ks[6], (N_EVEN, HEAD_DIM)),
        "w_out_even": dense(ks[7], (N_EVEN, EVEN_OUT_W, D_MODEL)),
        "w_in_odd": dense(ks[8], (N_ODD, D_MODEL, ODD_IN_W)),
        "lambda_q1": 0.1 * jax.random.normal(ks[9], (N_ODD, HEAD_DIM), jnp.float32),
        "lambda_k1": 0.1 * jax.random.normal(ks[10], (N_ODD, HEAD_DIM), jnp.float32),
        "lambda_q2": 0.1 * jax.random.normal(ks[11], (N_ODD, HEAD_DIM), jnp.float32),
        "lambda_k2": 0.1 * jax.random.normal(ks[12], (N_ODD, HEAD_DIM), jnp.float32),
        "c_sub_norm": gain(ks[13], (N_ODD, C_V_DIM)),
        "w_out_odd": dense(ks[14], (N_ODD, ODD_OUT_W, D_MODEL)),
        "w_gate": dense(ks[15], (DEPTH, D_MODEL, D_FF)),
        "w_up": dense(ks[16], (DEPTH, D_MODEL, D_FF)),
        "w_down": dense(ks[17], (DEPTH, D_FF, D_MODEL)),
    }


def reference(x, attn_norm, ffn_norm, final_norm, w_in_even, a_q_norm, a_k_norm, w_out_even,
              w_in_odd, lambda_q1, lambda_k1, lambda_q2, lambda_k2, c_sub_norm, w_out_odd,
              w_gate, w_up, w_down):
    bn, s, _ = x.shape
    rows = s // GRID_W
    row_ids = jnp.repeat(jnp.arange(rows, dtype=jnp.int32), GRID_W)
    col_ids = jnp.tile(jnp.arange(GRID_W, dtype=jnp.int32), rows)
    pos = jnp.arange(rows * GRID_W, dtype=jnp.int32)
    ang_1d = rope_angles(pos, HEAD_DIM)
    ang_2d = jnp.concatenate([rope_angles(row_ids, HEAD_DIM // 2),
                              rope_angles(col_ids, HEAD_DIM // 2)], axis=-1)

    for i in range(DEPTH):
        j = i // 2
        h = rms_norm(x, attn_norm[i])
        if i % 2 == 0:
            proj = h @ w_in_even[j]
            aq, ak, av, bq, bk, bv = jnp.split(proj, EVEN_SPLITS, axis=-1)
            aq = aq.reshape(bn, s, A_Q_HEADS, HEAD_DIM)
            ak = ak.reshape(bn, s, A_KV_HEADS, HEAD_DIM)
            av = av.reshape(bn, s, A_KV_HEADS, HEAD_DIM)
            aq = apply_rope(rms_norm(aq, a_q_norm[j]), ang_2d)
            ak = apply_rope(rms_norm(ak, a_k_norm[j]), ang_2d)
            bq = apply_rope(bq.reshape(bn, s, B_HEADS, HEAD_DIM), ang_1d)
            bk = apply_rope(bk.reshape(bn, s, B_HEADS, HEAD_DIM), ang_1d)
            bv = bv.reshape(bn, s, B_HEADS, HEAD_DIM)
            mix = jnp.concatenate([gqa_attention(aq, ak, av),
                                   dilated_attention(bq, bk, bv)], axis=-1)
            x = x + mix @ w_out_even[j]
        else:
            proj = h @ w_in_odd[j]
            q, k, v = jnp.split(proj, ODD_SPLITS, axis=-1)
            q = q.reshape(bn, s, C_HEADS, 2, HEAD_DIM)
            k = k.reshape(bn, s, C_HEADS, 2, HEAD_DIM)
            v = v.reshape(bn, s, C_HEADS, C_V_DIM)
            q1 = apply_rope(q[..., 0, :], ang_1d)
            q2 = apply_rope(q[..., 1, :], ang_1d)
            k1 = apply_rope(k[..., 0, :], ang_1d)
            k2 = apply_rope(k[..., 1, :], ang_1d)
            lam_init = 0.8 - 0.6 * math.exp(-0.3 * i)
            lam = (jnp.exp(jnp.sum(lambda_q1[j].astype(jnp.float32) * lambda_k1[j].astype(jnp.float32)))
                   - jnp.exp(jnp.sum(lambda_q2[j].astype(jnp.float32) * lambda_k2[j].astype(jnp.float32)))
                   + lam_init)
            o = diff_attention(q1, q2, k1, k2, v, lam)
            o = rms_norm(o, c_sub_norm[j]) * (1.0 - lam_init)
            x = x + o.reshape(bn, s, ODD_OUT_W) @ w_out_odd[j]
        h = rms_norm(x, ffn_norm[i])
        x = x + swiglu(h, w_gate[i], w_up[i], w_down[i])
    return rms_norm(x, final_norm)
```

```python
import math
from contextlib import ExitStack

import numpy as np
import ml_dtypes

import concourse.bass as bass
import concourse.mybir as mybir
from concourse.bass_utils import run_bass_kernel_spmd

F32 = mybir.dt.float32
BF16 = mybir.dt.bfloat16
ALU = mybir.AluOpType
AF = mybir.ActivationFunctionType
AX = mybir.AxisListType

NCORE = 8
S = 8192
D = 1024
T = 2048
NT = 16
DFF = 2816
NF = 22
EPS = 1e-6
LAM_INIT = 0.8 - 0.6 * math.exp(-0.3 * 1)
KPAD = 1024 + S + 1536
VBLK = 8 + 64 + 12
NM = 20


class TL:
    def __init__(self, nc, es, eng, name):
        self.e = eng
        self.sem = es.enter_context(nc.semaphore(name))
        self.n = 0
        self.seen = {}

    def wait(self, *ms):
        for m in ms:
            if m is None:
                continue
            if isinstance(m, list):
                self.wait(*m)
                continue
            sem, v = m
            if sem is self.sem and False:
                continue
            k = id(sem)
            if self.seen.get(k, 0) >= v:
                continue
            self.e.wait_ge(sem, v)
            self.seen[k] = v

    def mark(self, ins):
        self.n += 1
        ins.then_inc(self.sem, 1)
        return (self.sem, self.n)


class DS:
    def __init__(self, nc, es, name):
        self.sem = es.enter_context(nc.semaphore(name))
        self.n = 0

    def add(self, ins):
        self.n += 16
        ins.then_inc(self.sem, 16)
        return (self.sem, self.n)

    def all(self):
        return (self.sem, self.n) if self.n else None


def build(debug=False):
    nc = bass.Bass("TRN2", target_bir_lowering=False)

    def din(name, shape, dt=F32):
        return nc.dram_tensor(name, list(shape), dt, kind="ExternalInput").ap()

    def dint(name, shape, dt=BF16):
        return nc.dram_tensor(name, list(shape), dt, kind="Internal").ap()

    x_in = din("x", [T, D])
    attn_norm = din("attn_norm", [2, D])
    ffn_norm = din("ffn_norm", [2, D])
    final_norm = din("final_norm", [1, D])
    w_in_even = din("w_in_even", [D, 2304])
    a_q_norm = din("a_q_norm", [1, 64])
    a_k_norm = din("a_k_norm", [1, 64])
    w_out_even = din("w_out_even", [D, D])
    w_in_odd = din("w_in_odd", [D, 3072])
    lam_in = [din(n, [1, 64]) for n in ("lambda_q1", "lambda_k1", "lambda_q2", "lambda_k2")]
    c_sub_norm = din("c_sub_norm", [1, 128])
    w_out_odd = din("w_out_odd", [D, D])
    w_gate = din("w_gate", [2, D, DFF])
    w_up = din("w_up", [2, D, DFF])
    w_down = din("w_down", [2, DFF, D])
    rope_in = din("rope", [4, T, 32])
    masks_in = din("masks", [NM, 128, 512], BF16)
    ident_in = din("ident", [128, 128], BF16)
    out = nc.dram_tensor("out", [T, D], F32, kind="ExternalOutput").ap()

    wb_in = [dint("wb_in0", [D, 2304]), dint("wb_in1", [D, 3072])]
    wb_out = [dint("wb_out0", [D, D]), dint("wb_out1", [D, D])]
    wb_g = [dint(f"wb_g{l}", [D, DFF]) for l in range(2)]
    wb_u = [dint(f"wb_u{l}", [D, DFF]) for l in range(2)]
    wb_d = [dint(f"wb_d{l}", [DFF, D]) for l in range(2)]

    qt = [dint("qt0", [8 * 128, T]), dint("qt1", [8 * 128, T])]
    NKU = [5, 8]
    VGH = [2, 1]
    NVG = [5, 8]
    VF = [16 * 65, 16 * 129]
    kt_loc = [[dint("kt%d_loc%d" % (l, u), [128, T]) for u in range(NKU[l])] for l in range(2)]
    kt_all = [[dint("kt%d_all%d" % (l, u), [4 * 128, T]) for u in range(NKU[l])] for l in range(2)]
    v_loc = [[dint("v%d_loc%d" % (l, g), [VGH[l] * 128, VF[l]]) for g in range(NVG[l])] for l in range(2)]
    v_all = [[dint("v%d_all%d" % (l, g), [4 * VGH[l] * 128, VF[l]]) for g in range(NVG[l])] for l in range(2)]
    ktB_pad = dint("ktB_pad", [4 * 128, KPAD])
    vB_pad = dint("vB_pad", [8 * 128, VBLK * 65])
    ktB_win = dint("ktB_win", [4 * 128, 4096])
    vB_win = dint("vB_win", [8 * 128, 32 * 65])

    dbg = {}
    if debug:
        dbg["X"] = nc.dram_tensor("dbg_X", [T, D], F32, kind="ExternalOutput").ap()
        dbg["AT"] = nc.dram_tensor("dbg_AT", [128, 8 * T], BF16, kind="ExternalOutput").ap()
        dbg["qt"] = nc.dram_tensor("dbg_qt", [8 * 128, T], BF16, kind="ExternalOutput").ap()
        dbg["kt"] = nc.dram_tensor("dbg_kt", [8 * 512, T], BF16, kind="ExternalOutput").ap()
        dbg["v"] = nc.dram_tensor("dbg_v", [8 * 1024, 16 * 129], BF16, kind="ExternalOutput").ap()

    es = ExitStack()
    with es:
        pe = TL(nc, es, nc.tensor, "t_pe")
        act = TL(nc, es, nc.scalar, "t_act")
        dve = TL(nc, es, nc.vector, "t_dve")
        pool = TL(nc, es, nc.gpsimd, "t_pool")
        sp = TL(nc, es, nc.sync, "t_sp")
        engines = [pe, act, dve, pool, sp]

        def sb(name, shape, dt, stack=None):
            return (stack or es).enter_context(nc.sbuf_tensor(name, list(shape), dt))

        def psum(name, shape, dt, stack):
            return stack.enter_context(nc.psum_tensor(name, list(shape), dt))

        def dma(q, ds, out_ap, in_ap):
            return ds.add(q.e.dma_start(out=out_ap, in_=in_ap))

        X = sb("X", [128, NT, D], F32)
        AT = sb("AT", [128, 8, T], BF16)
        ROPE = sb("ROPE", [128, 4, NT, 32], F32)
        IDENT = sb("IDENT", [128, 128], BF16)
        EPST = sb("EPST", [128, 1], F32)
        FENCE = sb("FENCE", [128, 4], F32)
        NEGLAM = sb("NEGLAM", [128, 1], F32)
        GAINC = sb("GAINC", [128, 128], F32)
        GQK = sb("GQK", [128, 2, 64], F32)
        ZERO = sb("ZERO", [128, 2048], BF16)

        fence_ps_stack = ExitStack()

        def barrier():
            ms = []
            ms.append(dve.mark(nc.vector.memset(FENCE[:, 0:1], 0.0)))
            ms.append(pool.mark(nc.gpsimd.memset(FENCE[:, 1:2], 0.0)))
            ms.append(act.mark(nc.scalar.activation(out=FENCE[:, 2:3], in_=EPST[:, 0:1], func=AF.Copy)))
            ms.append((pe.sem, pe.n) if pe.n else None)
            ms += [d.all() for d in all_ds]
            for e in engines:
                e.wait(*ms)

        all_ds = []

        def newds(name, in_barrier=True):
            d = DS(nc, es, name)
            if in_barrier:
                all_ds.append(d)
            return d

        ds_c = newds("ds_c")
        ds_x = newds("ds_x")
        nc.vector.memset(EPST[:], EPS)
        m_zero = pool.mark(nc.gpsimd.memset(ZERO[:], 0.0))
        dma(sp, ds_c, IDENT[:], ident_in)
        dma(sp, ds_c, ROPE[:], rope_in.rearrange("c (i p) d -> p c i d", p=128))
        dma(sp, ds_c, GQK[:, 0, :], a_q_norm.partition_broadcast(128))
        dma(sp, ds_c, GQK[:, 1, :], a_k_norm.partition_broadcast(128))
        dma(sp, ds_c, GAINC[:], c_sub_norm.partition_broadcast(128))
        for i in range(NT):
            dma(sp, ds_x, X[:, i, :], x_in[i * 128:(i + 1) * 128, :])

        ds_w = {}

        def cast_w(key, dst, src, rows):
            d = newds("ds_w_" + key, in_barrier=False)
            ds_w[key] = d
            for r0 in range(0, rows, 128):
                dma(pool, d, dst[r0:r0 + 128, :], src[r0:r0 + 128, :])

        cast_w("in0", wb_in[0], w_in_even, D)
        ds_pad = newds("ds_pad")
        sp.wait(m_zero)
        for u in range(4):
            dma(sp, ds_pad, ktB_pad[u * 128:(u + 1) * 128, 0:1024], ZERO[:, 0:1024])
            dma(sp, ds_pad, ktB_pad[u * 128:(u + 1) * 128, 1024 + S:KPAD], ZERO[:, 0:1536])
        for h in range(8):
            dma(sp, ds_pad, vB_pad[h * 128:(h + 1) * 128, 0:8 * 65], ZERO[:, 0:8 * 65])
            dma(sp, ds_pad, vB_pad[h * 128:(h + 1) * 128, 72 * 65:VBLK * 65], ZERO[:, 0:12 * 65])
        cast_w("out0", wb_out[0], w_out_even, D)
        cast_w("g0", wb_g[0], w_gate[0], D)
        cast_w("u0", wb_u[0], w_up[0], D)
        cast_w("d0", wb_d[0], w_down[0], DFF)
        cast_w("in1", wb_in[1], w_in_odd, D)
        cast_w("out1", wb_out[1], w_out_odd, D)
        cast_w("g1", wb_g[1], w_gate[1], D)
        cast_w("u1", wb_u[1], w_up[1], D)
        cast_w("d1", wb_d[1], w_down[1], DFF)

        with ExitStack() as st:
            LQ = sb("LQ", [128, 4, 64], F32, st)
            LS = sb("LS", [128, 4], F32, st)
            LJ = sb("LJ", [128, 64], F32, st)
            ds_l = newds("ds_l")
            for j in range(4):
                dma(sp, ds_l, LQ[:, j, :], lam_in[j].partition_broadcast(128))
            dve.wait(ds_l.all(), ds_c.all())
            nc.vector.memset(LS[:], 0.0)
            m = None
            for j in range(2):
                m = dve.mark(nc.vector.tensor_tensor(out=LJ[:], in0=LQ[:, 2 * j, :], in1=LQ[:, 2 * j + 1, :], op=ALU.mult))
                dve.wait(m)
                m = dve.mark(nc.vector.tensor_reduce(out=LS[:, j:j + 1], in_=LJ[:], axis=AX.X, op=ALU.add))
                dve.wait(m)
            act.wait(m)
            m = act.mark(nc.scalar.activation(out=LS[:, 2:4], in_=LS[:, 0:2], func=AF.Exp))
            dve.wait(m)
            m = dve.mark(nc.vector.tensor_tensor(out=NEGLAM[:], in0=LS[:, 3:4], in1=LS[:, 2:3], op=ALU.subtract))
            dve.wait(m)
            m = dve.mark(nc.vector.tensor_scalar(out=NEGLAM[:], in0=NEGLAM[:], scalar1=-LAM_INIT, scalar2=None, op0=ALU.add))
            dve.wait(m)
            dve.mark(nc.vector.tensor_scalar(out=GAINC[:], in0=GAINC[:], scalar1=1.0 - LAM_INIT, scalar2=None, op0=ALU.mult))
            barrier()

        def rms_rstd(st, src_of_tile, nt, gdram_row, tag):
            G = sb("G" + tag, [128, D], F32, st)
            JUNK = sb("JUNK" + tag, [128, D], BF16, st)
            SS = sb("SS" + tag, [128, nt], F32, st)
            RSTD = sb("RSTD" + tag, [128, nt], F32, st)
            dsg = newds("ds_g" + tag)
            dma(sp, dsg, G[:], gdram_row.partition_broadcast(128))
            m0 = dve.mark(nc.vector.memset(SS[:], 0.0))
            act.wait(m0)
            m = None
            for i in range(nt):
                m = act.mark(nc.scalar.activation(out=JUNK[:], in_=src_of_tile(i), func=AF.Square,
                                                  accum_out=SS[:, i:i + 1]))
            act.wait(m)
            m = act.mark(nc.scalar.activation(out=RSTD[:], in_=SS[:], func=AF.Sqrt, bias=EPST[:, 0:1], scale=1.0 / D))
            dve.wait(m, dsg.all())
            m = dve.mark(nc.vector.reciprocal(out=RSTD[:], in_=RSTD[:]))
            dve.wait(m)
            return G, RSTD

        def norm_AT(gdram_row, tag):
            with ExitStack() as st:
                G, RSTD = rms_rstd(st, lambda i: X[:, i, :], NT, gdram_row, tag)
                HB = sb("HB" + tag, [128, 2, D], BF16, st)
                tp = psum("tp" + tag, [128, 2, D], BF16, st)
                m_tr = [None, None]
                m_ev = [None, None]
                for i in range(NT):
                    s = i % 2
                    dve.wait(m_tr[s])
                    mh = dve.mark(nc.vector.scalar_tensor_tensor(out=HB[:, s, :], in0=X[:, i, :], scalar=RSTD[:, i:i + 1],
                                                                 in1=G[:], op0=ALU.mult, op1=ALU.mult))
                    pe.wait(mh, m_ev[s])
                    for k in range(8):
                        ins = nc.tensor.transpose(tp[:, s, k * 128:(k + 1) * 128], HB[:, s, k * 128:(k + 1) * 128], IDENT[:])
                    m_tr[s] = pe.mark(ins)
                    act.wait(m_tr[s])
                    m_ev[s] = act.mark(nc.scalar.copy(out=AT[:, :, i * 128:(i + 1) * 128],
                                                      in_=tp[:, s, :].rearrange("p (k c) -> p k c", c=128)))
                barrier()

        def inproj(layer):
            wsrc = wb_in[layer]
            dsw = ds_w["in%d" % layer]
            if layer == 0:
                groups = [("qn", 0, 512, 0), ("kn", 512, 128, 0), ("v", 640, 128, 0),
                          ("q", 768, 512, 4), ("k", 1280, 512, 1), ("v", 1792, 512, 2)]
                dv = 64
            else:
                groups = [("q", 0, 512, 0), ("q", 512, 512, 4), ("k", 1024, 512, 0), ("k", 1536, 512, 4),
                          ("v", 2048, 512, 0), ("v", 2560, 512, 4)]
                dv = 128
            ds_kt = newds("ds_kt%d" % layer)
            ds_v = newds("ds_v%d" % layer)
            ds_q = newds("ds_q%d" % layer)
            cc_k = {}
            cc_v = {}
            with ExitStack() as st:
                WG = [sb("WG%d_%d" % (layer, s), [128, 8, 512], BF16, st) for s in range(2)]
                dsW = [newds("ds_WG%d_%d" % (layer, s)) for s in range(2)]
                PR = [sb("PR%d_%d" % (layer, s), [128, 512], F32, st) for s in range(2)]
                TMP = [sb("TMP%d_%d" % (layer, s), [128, 256], F32, st) for s in range(4)]
                SQ = sb("SQ%d" % layer, [128, 512], F32, st)
                SSQ = sb("SSQ%d" % layer, [128, 8], F32, st)
                QK = [sb("QK%d_%d" % (layer, s), [128, 512], BF16, st) for s in range(2)]
                STG = [sb("STG%d_%d" % (layer, s), [128, 8320], BF16, st) for s in range(2)]
                pj = psum("pj%d" % layer, [128, 2, 512], F32, st)
                tq = psum("tq%d" % layer, [128, 2, 1024], BF16, st)

                m_w_free = [None, None]
                m_stg_free = [None, None]
                m_pj_free = [None, None]
                m_pr_free = [None, None]
                m_qk_free = [None, None]
                m_tq_free = [None, None]
                m_tmp_free = [None, None]
                pending = []
                cnt = 0

                def load_w(gi):
                    kind, c0, wd, _ = groups[gi]
                    s = gi % 2
                    sp.wait(m_w_free[s], dsw.all())
                    dma(sp, dsW[s], WG[s][:, :, 0:wd], wsrc[:, c0:c0 + wd].rearrange("(k p) c -> p k c", p=128))

                load_w(0)
                for gi, (kind, c0, wd, dbase) in enumerate(groups):
                    ws = gi % 2
                    ss = gi % 2
                    if gi + 1 < len(groups):
                        load_w(gi + 1)
                    isv = kind == "v"
                    nh = wd // 64
                    nu = wd // 128
                    stg = STG[ss]
                    if isv:
                        nhv = wd // dv
                        stg_v = stg[:, 0:nhv * 16 * (dv + 1)].rearrange("p (h i d) -> p h i d", h=nhv, i=16)
                        pool.wait(m_stg_free[ss])
                        m_ones = pool.mark(nc.gpsimd.memset(stg[:, 0:nhv * 16 * (dv + 1)], 1.0))
                    else:
                        stg_q = stg[:, 0:nu * T].rearrange("p (u t) -> p u t", u=nu)
                        m_ones = None
                    last_ev = None
                    for i in range(NT):
                        b = cnt % 2
                        cnt += 1
                        pe.wait(dsW[ws].all(), m_pj_free[b])
                        for k in range(8):
                            ins = nc.tensor.matmul(pj[:, b, 0:wd], lhsT=AT[:, k, i * 128:(i + 1) * 128], rhs=WG[ws][:, k, 0:wd],
                                                   start=(k == 0), stop=(k == 7))
                        m_mm = pe.mark(ins)
                        m_w_free[ws] = m_mm
                        while pending:
                            last_ev = pending.pop(0)()
                        if isv:
                            act.wait(m_mm, m_ones, m_stg_free[ss])
                            m_c = act.mark(nc.scalar.copy(out=stg_v[:, :, i, 0:dv],
                                                          in_=pj[:, b, 0:wd].rearrange("p (h d) -> p h d", d=dv)))
                            m_pj_free[b] = m_c
                            last_ev = m_c
                            continue
                        act.wait(m_mm, m_pr_free[b])
                        m_c = act.mark(nc.scalar.copy(out=PR[b][:, 0:wd], in_=pj[:, b, 0:wd]))
                        m_pj_free[b] = m_c
                        pr3 = PR[b][:, 0:wd].rearrange("p (h d) -> p h d", d=64)
                        m_src = m_c
                        if kind in ("qn", "kn"):
                            gsel = 0 if kind == "qn" else 1
                            dve.wait(m_src)
                            m1 = dve.mark(nc.vector.tensor_tensor(out=SQ[:, 0:wd], in0=PR[b][:, 0:wd], in1=PR[b][:, 0:wd], op=ALU.mult))
                            dve.wait(m1)
                            m2 = dve.mark(nc.vector.tensor_reduce(out=SSQ[:, 0:nh], in_=SQ[:, 0:wd].rearrange("p (h d) -> p h d", d=64),
                                                                  axis=AX.X, op=ALU.add))
                            act.wait(m2)
                            m3 = act.mark(nc.scalar.activation(out=SSQ[:, 0:nh], in_=SSQ[:, 0:nh], func=AF.Sqrt,
                                                               bias=EPST[:, 0:1], scale=1.0 / 64))
                            dve.wait(m3)
                            m4 = dve.mark(nc.vector.reciprocal(out=SSQ[:, 0:nh], in_=SSQ[:, 0:nh]))
                            dve.wait(m4)
                            m5 = dve.mark(nc.vector.tensor_tensor(out=pr3, in0=pr3,
                                                                  in1=SSQ[:, 0:nh].unsqueeze(2).to_broadcast([128, nh, 64]), op=ALU.mult))
                            dve.wait(m5)
                            m_src = dve.mark(nc.vector.tensor_tensor(out=pr3, in0=pr3,
                                                                     in1=GQK[:, gsel, :].unsqueeze(1).to_broadcast([128, nh, 64]), op=ALU.mult))
                        ctab, stab = (2, 3) if kind in ("qn", "kn") else (0, 1)
                        cosb = ROPE[:, ctab, i, :].unsqueeze(1).to_broadcast([128, nh, 32])
                        sinb = ROPE[:, stab, i, :].unsqueeze(1).to_broadcast([128, nh, 32])
                        x1 = pr3[:, :, 0:32]
                        x2 = pr3[:, :, 32:64]
                        qk3 = QK[b][:, 0:wd].rearrange("p (h d) -> p h d", d=64)
                        t0 = TMP[0][:, 0:nh * 32].rearrange("p (h d) -> p h d", d=32)
                        t1 = TMP[1][:, 0:nh * 32].rearrange("p (h d) -> p h d", d=32)
                        t2 = TMP[2][:, 0:nh * 32].rearrange("p (h d) -> p h d", d=32)
                        t3 = TMP[3][:, 0:nh * 32].rearrange("p (h d) -> p h d", d=32)
                        dve.wait(m_src, m_qk_free[b], m_tmp_free[0])
                        ma = dve.mark(nc.vector.tensor_tensor(out=t0, in0=x1, in1=cosb, op=ALU.mult))
                        mb = dve.mark(nc.vector.tensor_tensor(out=t1, in0=x2, in1=sinb, op=ALU.mult))
                        dve.wait(ma, mb)
                        mq1 = dve.mark(nc.vector.tensor_tensor(out=qk3[:, :, 0:32], in0=t0, in1=t1, op=ALU.subtract))
                        pool.wait(m_src, m_qk_free[b], m_tmp_free[1])
                        mc = pool.mark(nc.gpsimd.tensor_tensor(out=t2, in0=x1, in1=sinb, op=ALU.mult))
                        md = pool.mark(nc.gpsimd.tensor_tensor(out=t3, in0=x2, in1=cosb, op=ALU.mult))
                        pool.wait(mc, md)
                        mq2 = pool.mark(nc.gpsimd.tensor_tensor(out=qk3[:, :, 32:64], in0=t2, in1=t3, op=ALU.add))
                        m_pr_free[b] = [mq1, mq2]
                        m_tmp_free[0] = mq1
                        m_tmp_free[1] = mq2

                        def do_tr(b=b, i=i, mq1=mq1, mq2=mq2):
                            pe.wait(mq1, mq2, m_tq_free[b])
                            for u in range(nu):
                                ins = nc.tensor.transpose(tq[:, b, u * 128:(u + 1) * 128], QK[b][:, u * 128:(u + 1) * 128], IDENT[:])
                            m_t = pe.mark(ins)
                            m_qk_free[b] = m_t
                            act.wait(m_t, m_stg_free[ss])
                            m_e = act.mark(nc.scalar.copy(out=stg_q[:, :, i * 128:(i + 1) * 128],
                                                          in_=tq[:, b, 0:wd].rearrange("p (u c) -> p u c", c=128)))
                            m_tq_free[b] = m_e
                            return m_e
                        pending.append(do_tr)
                    while pending:
                        last_ev = pending.pop(0)()
                    sp.wait(last_ev)
                    if isv:
                        ng = nhv // VGH[layer]
                        g0 = dbase // VGH[layer]
                        src = stg[:, 0:nhv * 16 * (dv + 1)].rearrange("p (h f) -> p h f", h=nhv)
                        for gg in range(ng):
                            m_st = dma(sp, ds_v, v_loc[layer][g0 + gg].rearrange("(h p) f -> p h f", p=128),
                                       src[:, gg * VGH[layer]:(gg + 1) * VGH[layer], :])
                        m_stg_free[ss] = m_st
                        pool.wait(m_st)
                        for gg in range(ng):
                            d = DS(nc, es, "cc_v%d_%d" % (layer, g0 + gg))
                            ins = nc.gpsimd.collective_compute("AllGather", ALU.bypass, replica_groups=[[0, 1, 2, 3], [4, 5, 6, 7]],
                                                               ins=[v_loc[layer][g0 + gg].opt()], outs=[v_all[layer][g0 + gg].opt()])
                            ins.then_inc(d.sem, 1)
                            d.n = 1
                            cc_v[g0 + gg] = d
                            all_ds.append(d)
                    elif kind in ("q", "qn"):
                        m_stg_free[ss] = dma(sp, ds_q, qt[layer][dbase * 128:(dbase + nu) * 128, :].rearrange("(u p) t -> p u t", p=128), stg_q)
                    else:
                        for uu in range(nu):
                            m_st = dma(sp, ds_kt, kt_loc[layer][dbase + uu], stg_q[:, uu, :])
                        m_stg_free[ss] = m_st
                        pool.wait(m_st)
                        for uu in range(nu):
                            d = DS(nc, es, "cc_k%d_%d" % (layer, dbase + uu))
                            ins = nc.gpsimd.collective_compute("AllGather", ALU.bypass, replica_groups=[[0, 1, 2, 3], [4, 5, 6, 7]],
                                                               ins=[kt_loc[layer][dbase + uu].opt()], outs=[kt_all[layer][dbase + uu].opt()])
                            ins.then_inc(d.sem, 1)
                            d.n = 1
                            cc_k[dbase + uu] = d
                            all_ds.append(d)
                barrier()
            return cc_k, cc_v

        def attention(layer):
            dvp = 65 if layer == 0 else 129
            dv = dvp - 1
            nbank = 2 if layer == 0 else 3
            per_bank = 4 if layer == 0 else 3
            with ExitStack() as st:
                psS = psum("psS%d" % layer, [128, 4, 512], F32, st)
                psA = psum("psA%d" % layer, [128, nbank, 512], F32, st)
                psT = psum("psT%d" % layer, [128, 1024], BF16, st)
                QT = [sb("QT%d_%d" % (layer, s), [128, T], BF16, st) for s in range(2)]
                dsQ = [newds("dsQ%d_%d" % (layer, s)) for s in range(2)]
                P = [[sb("P%d_%d_%d" % (layer, mp, s), [128, 512], BF16, st) for s in range(3)] for mp in range(2)]
                ACCS = sb("ACCS%d" % layer, [128, 8 * dvp], F32, st)
                RDEN = sb("RDEN%d" % layer, [128, 8], F32, st)
                MT = sb("MT%d" % layer, [128, 512], BF16, st)
                if layer == 1:
                    O1 = sb("O1", [128, 512], F32, st)
                    O2 = sb("O2", [128, 512], F32, st)
                    SS4 = sb("SS4", [128, 4], F32, st)
                    RS4 = sb("RS4", [128, 4], F32, st)
                if layer == 0:
                    KTA = sb("KTA", [128, 4, T], BF16, st)
                    VA = sb("VA", [128, 4, 16, 65], BF16, st)
                    dsKA = newds("dsKA")
                    KTB = [sb("KTB%d" % s, [128, NM * 128], BF16, st) for s in range(2)]
                    VB = [sb("VB%d" % s, [128, 2, NM, 65], BF16, st) for s in range(2)]
                    dsKB = [newds("dsKB%d" % s) for s in range(2)]
                    MASK = sb("MASK", [128, NM, 512], BF16, st)
                    dsM = newds("dsM")
                    dma(sp, dsM, MASK[:], masks_in.rearrange("m p q -> p m q"))
                else:
                    KT = [sb("KT1_%d" % s, [128, 4, T], BF16, st) for s in range(2)]
                    V1 = [sb("V1_%d" % s, [128, 4, 16, 129], BF16, st) for s in range(2)]
                    dsKV = [newds("dsKV%d" % s) for s in range(2)]

                state = dict(m_s_free=[[None, None], [None, None]], m_p_free=[[None] * 3, [None] * 3],
                             m_acc_free=None, m_accs_free=None, m_mt_free=None, m_psT_free=None, kbc=0)
                m_q_free = [None, None]

                def acc_ap(n):
                    bk, o = n // per_bank, (n % per_bank) * dvp
                    return psA[:, bk, o:o + dvp], bk

                def qgroup(qtile, g, chunk, blocks, m_load, m_buf_free_cb):
                    nb = len(blocks)
                    started = set()
                    m_exp = {}
                    m_s = {}
                    last_pv = None

                    def emit_s(kb):
                        la, lb, _, _, _, _ = blocks[kb]
                        sl = state["kbc_base"] + kb
                        s = sl % 2
                        pe.wait(m_load, state["m_s_free"][0][s], state["m_s_free"][1][s])
                        nc.tensor.matmul(psS[:, 0 + s, :], lhsT=la, rhs=qtile[0:64, g * 512:(g + 1) * 512], start=True, stop=True)
                        ins = nc.tensor.matmul(psS[:, 2 + s, :], lhsT=lb, rhs=qtile[64:128, g * 512:(g + 1) * 512], start=True, stop=True)
                        m_s[kb] = pe.mark(ins)

                    state["kbc_base"] = state["kbc"]
                    emit_s(0)
                    for kb in range(nb):
                        if kb + 1 < nb:
                            emit_s(kb + 1)
                        _, _, va, vb, mi, jl = blocks[kb]
                        sl = state["kbc_base"] + kb
                        s = sl % 2
                        ps3 = sl % 3
                        ms_e = []
                        for mp in range(2):
                            act.wait(m_s[kb], state["m_p_free"][mp][ps3])
                            me = act.mark(nc.scalar.activation(out=P[mp][ps3][:], in_=psS[:, 2 * mp + s, :], func=AF.Exp, scale=0.125))
                            state["m_s_free"][mp][s] = me
                            if mi is not None:
                                pool.wait(me)
                                me = pool.mark(nc.gpsimd.tensor_tensor(out=P[mp][ps3][:], in0=P[mp][ps3][:], in1=MASK[:, mi, :], op=ALU.mult))
                            ms_e.append(me)
                        pe.wait(ms_e[0], ms_e[1], state["m_acc_free"])
                        ins = None
                        for mp in range(2):
                            vv = va if mp == 0 else vb
                            for j in jl:
                                o_ap, bk = acc_ap(mp * 4 + j)
                                first = bk not in started
                                started.add(bk)
                                ins = nc.tensor.matmul(o_ap, lhsT=P[mp][ps3][:, j * 128:(j + 1) * 128], rhs=vv,
                                                       start=first, stop=(kb == nb - 1), skip_group_check=True)
                        m_pv = pe.mark(ins)
                        state["m_p_free"][0][ps3] = m_pv
                        state["m_p_free"][1][ps3] = m_pv
                        last_pv = m_pv
                    state["kbc"] = state["kbc_base"] + nb
                    m_buf_free_cb(last_pv)
                    dve.wait(last_pv, state["m_accs_free"])
                    m = None
                    for bk in range(nbank):
                        ncols = min(per_bank, 8 - bk * per_bank) * dvp
                        m = dve.mark(nc.vector.tensor_copy(out=ACCS[:, bk * per_bank * dvp: bk * per_bank * dvp + ncols], in_=psA[:, bk, 0:ncols]))
                    state["m_acc_free"] = m
                    dve.wait(m)
                    a3 = ACCS[:].rearrange("p (n d) -> p n d", d=dvp)
                    m = dve.mark(nc.vector.reciprocal(out=RDEN[:], in_=a3[:, :, dv]))
                    dve.wait(m, state["m_mt_free"])
                    if layer == 0:
                        mt3 = MT[:].rearrange("p (j f) -> p j f", f=128)
                        for mp in range(2):
                            m = dve.mark(nc.vector.tensor_tensor(out=mt3[:, :, mp * 64:(mp + 1) * 64], in0=a3[:, mp * 4:(mp + 1) * 4, 0:64],
                                                                 in1=RDEN[:, mp * 4:(mp + 1) * 4].unsqueeze(2).to_broadcast([128, 4, 64]), op=ALU.mult))
                        state["m_accs_free"] = m
                        m_mt = m
                    else:
                        o13 = O1[:].rearrange("p (j f) -> p j f", f=128)
                        o23 = O2[:].rearrange("p (j f) -> p j f", f=128)
                        m = dve.mark(nc.vector.tensor_scalar(out=RDEN[:, 4:8], in0=RDEN[:, 4:8], scalar1=NEGLAM[:, 0:1], scalar2=None, op0=ALU.mult))
                        dve.wait(m)
                        nc.vector.tensor_tensor(out=o13, in0=a3[:, 0:4, 0:128], in1=RDEN[:, 0:4].unsqueeze(2).to_broadcast([128, 4, 128]), op=ALU.mult)
                        m = dve.mark(nc.vector.tensor_tensor(out=o23, in0=a3[:, 4:8, 0:128], in1=RDEN[:, 4:8].unsqueeze(2).to_broadcast([128, 4, 128]), op=ALU.mult))
                        state["m_accs_free"] = m
                        dve.wait(m)
                        m = dve.mark(nc.vector.tensor_tensor(out=O1[:], in0=O1[:], in1=O2[:], op=ALU.add))
                        dve.wait(m)
                        m = dve.mark(nc.vector.tensor_tensor(out=O2[:], in0=O1[:], in1=O1[:], op=ALU.mult))
                        dve.wait(m)
                        m = dve.mark(nc.vector.tensor_reduce(out=SS4[:], in_=o23, axis=AX.X, op=ALU.add))
                        act.wait(m)
                        m = act.mark(nc.scalar.activation(out=RS4[:], in_=SS4[:], func=AF.Ln, bias=EPST[:, 0:1], scale=1.0 / 128))
                        act.wait(m)
                        m = act.mark(nc.scalar.activation(out=RS4[:], in_=RS4[:], func=AF.Exp, scale=-0.5))
                        dve.wait(m)
                        m = dve.mark(nc.vector.tensor_tensor(out=o13, in0=o13, in1=RS4[:].unsqueeze(2).to_broadcast([128, 4, 128]), op=ALU.mult))
                        dve.wait(m)
                        m_mt = dve.mark(nc.vector.tensor_tensor(out=MT[:].rearrange("p (j f) -> p j f", f=128), in0=o13,
                                                                in1=GAINC[:].unsqueeze(1).to_broadcast([128, 4, 128]), op=ALU.mult))
                    pe.wait(m_mt, state["m_psT_free"])
                    for j in range(4):
                        ins = nc.tensor.transpose(psT[:, j * 128:(j + 1) * 128], MT[:, j * 128:(j + 1) * 128], IDENT[:])
                    m_t = pe.mark(ins)
                    state["m_mt_free"] = m_t
                    dve.wait(m_t)
                    state["m_psT_free"] = dve.mark(nc.vector.tensor_copy(out=AT[:, chunk, g * 512:(g + 1) * 512], in_=psT[:, 0:512]))

                def load_q(u, slot):
                    sp.wait(m_q_free[slot])
                    return dma(sp, dsQ[slot], QT[slot][:], qt[layer][u * 128:(u + 1) * 128, :])

                if layer == 0:
                    cc_k, cc_v = cc[0]
                    kA = kt_all[0][0].rearrange("(r p) t -> p r t", p=128)
                    vA = v_all[0][0].rearrange("(r h p) f -> h p r f", r=4, h=2)
                    ds_rl = newds("ds_rl")
                    sp.wait(ds_pad.all())
                    for u in range(4):
                        sp.wait(cc_k[1 + u].all())
                        dma(sp, ds_rl, ktB_pad[u * 128:(u + 1) * 128, 1024:1024 + 4 * T].rearrange("p (r t) -> p r t", r=4),
                            kt_all[0][1 + u].rearrange("(r p) t -> p r t", p=128))
                    for gi in range(4):
                        sp.wait(cc_v[1 + gi].all())
                        srcv = v_all[0][1 + gi].rearrange("(r h p) f -> h p r f", r=4, h=2)
                        for hh in range(2):
                            hB = 2 * gi + hh
                            dma(sp, ds_rl, vB_pad[hB * 128:(hB + 1) * 128, 8 * 65:72 * 65].rearrange("p (r f) -> p r f", r=4), srcv[hh])
                    m_ka_free = None
                    qslot = 0
                    mq = load_q(0, 0)
                    for u in range(4):
                        kvh = u // 2
                        if u % 2 == 0:
                            sp.wait(m_ka_free, cc_k[0].all(), cc_v[0].all())
                            for half in range(2):
                                dma(sp, dsKA, KTA[half * 64:(half + 1) * 64, :, :], kA[kvh * 64:(kvh + 1) * 64, :, :])
                            dma(sp, dsKA, VA[:].rearrange("p r i d -> p r (i d)"), vA[kvh])
                        mq_next = load_q(u + 1, (qslot + 1) % 2) if u < 3 else None
                        lastholder = []
                        for g in range(4):
                            blocks = []
                            for r in range(4):
                                for i in range(16):
                                    blocks.append((KTA[0:64, r, i * 128:(i + 1) * 128], KTA[64:128, r, i * 128:(i + 1) * 128],
                                                   VA[:, r, i, :], VA[:, r, i, :], None, [0, 1, 2, 3]))
                            qgroup(QT[qslot], g, u, blocks, [mq, dsKA.all()], lambda m: lastholder.append(m))
                        m_q_free[qslot] = lastholder[-1]
                        m_ka_free = lastholder[-1]
                        qslot = (qslot + 1) % 2
                        mq = mq_next
                    pid = nc.sync.partition_id()
                    rank = pid % 4
                    ds_win = newds("ds_win")
                    sp.wait(ds_rl.all())
                    dma(sp, ds_win, ktB_win, ktB_pad[:, bass.ds(rank * T, 4096)])
                    dma(sp, ds_win, vB_win, vB_pad[:, bass.ds(rank * (16 * 65), 32 * 65)])
                    m_kb_free = [None, None]
                    widx = 0
                    mq = load_q(4, qslot)
                    for u in range(4):
                        mq_next = None
                        lastholder = []
                        for g in range(4):
                            ws = widx % 2
                            widx += 1
                            sp.wait(m_kb_free[ws], ds_win.all())
                            dma(sp, dsKB[ws], KTB[ws][:], ktB_win[u * 128:(u + 1) * 128, g * 512:g * 512 + NM * 128])
                            for hh in range(2):
                                h = 2 * u + hh
                                dma(sp, dsKB[ws], VB[ws][:, hh, :, :].rearrange("p m d -> p (m d)"),
                                    vB_win[h * 128:(h + 1) * 128, g * 4 * 65:(g * 4 + NM) * 65])
                            if g == 0 and u < 3:
                                mq_next = load_q(4 + u + 1, (qslot + 1) % 2)
                            blocks = []
                            for mi in range(NM):
                                jl = [j for j in range(4) if j <= mi <= j + 16]
                                blocks.append((KTB[ws][0:64, mi * 128:(mi + 1) * 128], KTB[ws][64:128, mi * 128:(mi + 1) * 128],
                                               VB[ws][:, 0, mi, :], VB[ws][:, 1, mi, :], mi, jl))

                            def cb(m, ws=ws):
                                m_kb_free[ws] = m
                                lastholder.append(m)
                            qgroup(QT[qslot], g, 4 + u, blocks, [mq, dsKB[ws].all(), dsM.all()], cb)
                        m_q_free[qslot] = lastholder[-1]
                        qslot = (qslot + 1) % 2
                        mq = mq_next
                else:
                    cc_k, cc_v = cc[1]
                    m_kv_free = [None, None]

                    def load_kv(h, slot):
                        sp.wait(m_kv_free[slot], cc_k[h].all(), cc_v[h].all())
                        dma(sp, dsKV[slot], KT[slot][:], kt_all[1][h].rearrange("(r p) t -> p r t", p=128))
                        dma(sp, dsKV[slot], V1[slot][:].rearrange("p r i d -> p r (i d)"), v_all[1][h].rearrange("(r p) f -> p r f", p=128))

                    mq = load_q(0, 0)
                    load_kv(0, 0)
                    for h in range(8):
                        slot = h % 2
                        mq_next = None
                        if h < 7:
                            mq_next = load_q(h + 1, 1 - slot)
                            load_kv(h + 1, 1 - slot)
                        lastholder = []
                        for g in range(4):
                            blocks = []
                            for r in range(4):
                                for i in range(16):
                                    blocks.append((KT[slot][0:64, r, i * 128:(i + 1) * 128], KT[slot][64:128, r, i * 128:(i + 1) * 128],
                                                   V1[slot][:, r, i, :], V1[slot][:, r, i, :], None, [0, 1, 2, 3]))
                            qgroup(QT[slot], g, h, blocks, [mq, dsKV[slot].all()], lambda m: lastholder.append(m))
                        m_q_free[slot] = lastholder[-1]
                        m_kv_free[slot] = lastholder[-1]
                        mq = mq_next
                barrier()

        def outproj(layer):
            with ExitStack() as st:
                WO = sb("WO%d" % layer, [128, 8, D], BF16, st)
                dsO = newds("dsO%d" % layer)
                psO = psum("psO%d" % layer, [128, 2, 512], F32, st)
                sp.wait(ds_w["out%d" % layer].all())
                dma(sp, dsO, WO[:], wb_out[layer].rearrange("(k p) c -> p k c", p=128))
                m_free = [None, None]
                cnt = 0
                for i in range(NT):
                    for c in range(2):
                        b = cnt % 2
                        cnt += 1
                        pe.wait(dsO.all(), m_free[b])
                        for k in range(8):
                            ins = nc.tensor.matmul(psO[:, b, :], lhsT=AT[:, k, i * 128:(i + 1) * 128], rhs=WO[:, k, c * 512:(c + 1) * 512],
                                                   start=(k == 0), stop=(k == 7))
                        mm = pe.mark(ins)
                        dve.wait(mm)
                        m_free[b] = dve.mark(nc.vector.tensor_tensor(out=X[:, i, c * 512:(c + 1) * 512], in0=X[:, i, c * 512:(c + 1) * 512],
                                                                     in1=psO[:, b, :], op=ALU.add))
                barrier()

        def ffn(layer):
            with ExitStack() as st:
                WD = sb("WD%d" % layer, [128, NF, D], BF16, st)
                dsD = newds("dsD%d" % layer)
                WGU = [sb("WGU%d_%d" % (layer, s), [128, 2, 8, 256], BF16, st) for s in range(2)]
                dsGU = [newds("dsGU%d_%d" % (layer, s)) for s in range(2)]
                AFT = sb("AFT%d" % layer, [128, NF, 512], BF16, st)
                SG = [sb("SG%d_%d" % (layer, s), [128, 512], F32, st) for s in range(2)]
                psG = psum("psG%d" % layer, [128, 2, 512], F32, st)
                psU = psum("psU%d" % layer, [128, 2, 512], F32, st)
                psD = psum("psD%d" % layer, [128, 2, 512], F32, st)
                sp.wait(ds_w["d%d" % layer].all(), ds_w["g%d" % layer].all(), ds_w["u%d" % layer].all())
                for half in range(2):
                    dma(sp, dsD, WD[:, half * 11:(half + 1) * 11, :],
                        wb_d[layer][half * 11 * 128:(half + 1) * 11 * 128, :].rearrange("(f p) c -> p f c", p=128))
                m_gu_free = [None, None]
                m_g_free = [None, None]
                m_u_free = [None, None]
                m_sg_free = [None, None]
                m_d_free = [None, None]
                m_down_last = None
                lc = 0
                fc_cnt = 0
                dcnt = 0

                def load_gu(fg, slot):
                    sp.wait(m_gu_free[slot])
                    dma(sp, dsGU[slot], WGU[slot][:, 0, :, :], wb_g[layer][:, fg * 256:(fg + 1) * 256].rearrange("(k p) c -> p k c", p=128))
                    dma(sp, dsGU[slot], WGU[slot][:, 1, :, :], wb_u[layer][:, fg * 256:(fg + 1) * 256].rearrange("(k p) c -> p k c", p=128))

                seq = [(tg, fg) for tg in range(4) for fg in range(11)]
                load_gu(seq[0][1], 0)
                for si, (tg, fg) in enumerate(seq):
                    slot = si % 2
                    if si + 1 < len(seq):
                        load_gu(seq[si + 1][1], 1 - slot)
                    for fc in range(2):
                        f = fg * 2 + fc
                        b = fc_cnt % 2
                        fc_cnt += 1
                        pe.wait(dsGU[slot].all(), m_g_free[b], m_u_free[b])
                        for k in range(8):
                            nc.tensor.matmul(psG[:, b, :], lhsT=WGU[slot][:, 0, k, fc * 128:(fc + 1) * 128], rhs=AT[:, k, tg * 512:(tg + 1) * 512],
                                             start=(k == 0), stop=(k == 7))
                        for k in range(8):
                            ins = nc.tensor.matmul(psU[:, b, :], lhsT=WGU[slot][:, 1, k, fc * 128:(fc + 1) * 128], rhs=AT[:, k, tg * 512:(tg + 1) * 512],
                                                   start=(k == 0), stop=(k == 7))
                        mm = pe.mark(ins)
                        m_gu_free[slot] = mm
                        act.wait(mm, m_sg_free[b])
                        ms = act.mark(nc.scalar.activation(out=SG[b][:], in_=psG[:, b, :], func=AF.Silu))
                        m_g_free[b] = ms
                        dve.wait(ms, m_down_last)
                        ma = dve.mark(nc.vector.tensor_tensor(out=AFT[:, f, :], in0=SG[b][:], in1=psU[:, b, :], op=ALU.mult))
                        m_u_free[b] = ma
                        m_sg_free[b] = ma
                        last_a = ma
                    if fg == 10:
                        for j in range(4):
                            for c in range(2):
                                b = dcnt % 2
                                dcnt += 1
                                pe.wait(last_a, dsD.all(), m_d_free[b])
                                for f in range(NF):
                                    ins = nc.tensor.matmul(psD[:, b, :], lhsT=AFT[:, f, j * 128:(j + 1) * 128], rhs=WD[:, f, c * 512:(c + 1) * 512],
                                                           start=(f == 0), stop=(f == NF - 1))
                                mm = pe.mark(ins)
                                m_down_last = mm
                                i = tg * 4 + j
                                dve.wait(mm)
                                m_d_free[b] = dve.mark(nc.vector.tensor_tensor(out=X[:, i, c * 512:(c + 1) * 512], in0=X[:, i, c * 512:(c + 1) * 512],
                                                                               in1=psD[:, b, :], op=ALU.add))
                barrier()

        def final():
            with ExitStack() as st:
                G, RSTD = rms_rstd(st, lambda i: X[:, i, :], NT, final_norm[0:1, :], "fin")
                OB = [sb("OB%d" % s, [128, D], F32, st) for s in range(2)]
                ds_o = newds("ds_out")
                m_free = [None, None]
                for i in range(NT):
                    s = i % 2
                    dve.wait(m_free[s])
                    m = dve.mark(nc.vector.scalar_tensor_tensor(out=OB[s][:], in0=X[:, i, :], scalar=RSTD[:, i:i + 1], in1=G[:],
                                                                op0=ALU.mult, op1=ALU.mult))
                    sp.wait(m)
                    m_free[s] = dma(sp, ds_o, out[i * 128:(i + 1) * 128, :], OB[s][:])
                sp.wait(ds_o.all())
                pool.wait(ds_o.all())

        cc = {}
        dve.wait(ds_x.all(), ds_c.all())
        act.wait(ds_x.all(), ds_c.all())
        pe.wait(ds_c.all())
        pool.wait(ds_c.all())

        def dump(layer):
            barrier()
            dsd = newds("ds_dbg")
            for i in range(NT):
                dma(sp, dsd, dbg["X"][i * 128:(i + 1) * 128, :], X[:, i, :])
            dma(sp, dsd, dbg["AT"], AT[:].rearrange("p k t -> p (k t)"))
            dma(sp, dsd, dbg["qt"], qt[layer])
            for u in range(NKU[layer]):
                dma(sp, dsd, dbg["kt"][u * 512:(u + 1) * 512, :], kt_all[layer][u])
            for g in range(NVG[layer]):
                nr = 4 * VGH[layer] * 128
                dma(sp, dsd, dbg["v"][g * nr:(g + 1) * nr, 0:VF[layer]], v_all[layer][g])
            sp.wait(dsd.all())
            pool.wait(dsd.all())

        steps = [
            ("norm_a0", lambda: norm_AT(attn_norm[0:1, :], "a0")),
            ("inproj0", lambda: cc.__setitem__(0, inproj(0))),
            ("attn0", lambda: attention(0)),
            ("outproj0", lambda: outproj(0)),
            ("norm_f0", lambda: norm_AT(ffn_norm[0:1, :], "f0")),
            ("ffn0", lambda: ffn(0)),
            ("norm_a1", lambda: norm_AT(attn_norm[1:2, :], "a1")),
            ("inproj1", lambda: cc.__setitem__(1, inproj(1))),
            ("attn1", lambda: attention(1)),
            ("outproj1", lambda: outproj(1)),
            ("norm_f1", lambda: norm_AT(ffn_norm[1:2, :], "f1")),
            ("ffn1", lambda: ffn(1)),
            ("final", final),
        ]
        for name, fn in steps:
            fn()
            if debug and debug == name:
                dump(0 if name in ("norm_a0", "inproj0", "attn0", "outproj0", "norm_f0", "ffn0", "norm_a1") else 1)
                break
    return nc


def _const_tables():
    theta = 10000.0
    pos = np.arange(S, dtype=np.float32)
    inv64 = (theta ** (-np.arange(0, 64, 2, dtype=np.float32) / 64)).astype(np.float32)
    inv32 = (theta ** (-np.arange(0, 32, 2, dtype=np.float32) / 32)).astype(np.float32)
    ang1 = (pos[:, None] * inv64[None, :]).astype(np.float32)
    rows = (np.arange(S) // 64).astype(np.float32)
    cols = (np.arange(S) % 64).astype(np.float32)
    ang2 = np.concatenate([rows[:, None] * inv32[None, :], cols[:, None] * inv32[None, :]], axis=-1).astype(np.float32)
    rope = np.stack([np.cos(ang1.astype(np.float64)), np.sin(ang1.astype(np.float64)),
                     np.cos(ang2.astype(np.float64)), np.sin(ang2.astype(np.float64))], 0).astype(np.float32)
    k = np.arange(128)[:, None]
    q = np.arange(512)[None, :]
    masks = np.zeros((NM, 128, 512), np.float32)
    for m in range(NM):
        d = 128 * m + k - q - 1024
        ad = np.abs(d)
        masks[m] = (ad <= 64).astype(np.float32) + ((d % 4 == 0) & (ad <= 256)) + ((d % 16 == 0) & (ad <= 1024))
    ident = np.eye(128, dtype=np.float32)
    return rope, masks.astype(ml_dtypes.bfloat16), ident.astype(ml_dtypes.bfloat16)


_NC_CACHE = {}


def kernel(x, attn_norm, ffn_norm, final_norm, w_in_even, a_q_norm, a_k_norm, w_out_even,
           w_in_odd, lambda_q1, lambda_k1, lambda_q2, lambda_k2, c_sub_norm, w_out_odd,
           w_gate, w_up, w_down, _debug=False):
    f = lambda a: np.ascontiguousarray(np.asarray(a, dtype=np.float32))
    x = f(x)
    rope, masks, ident = _const_tables()
    shared = {
        "attn_norm": f(attn_norm), "ffn_norm": f(ffn_norm), "final_norm": f(final_norm).reshape(1, D),
        "w_in_even": f(w_in_even)[0], "a_q_norm": f(a_q_norm), "a_k_norm": f(a_k_norm),
        "w_out_even": f(w_out_even)[0], "w_in_odd": f(w_in_odd)[0],
        "lambda_q1": f(lambda_q1), "lambda_k1": f(lambda_k1), "lambda_q2": f(lambda_q2), "lambda_k2": f(lambda_k2),
        "c_sub_norm": f(c_sub_norm), "w_out_odd": f(w_out_odd)[0],
        "w_gate": f(w_gate), "w_up": f(w_up), "w_down": f(w_down),
        "masks": masks, "ident": ident,
    }
    in_maps = []
    for c in range(NCORE):
        b, r = c // 4, c % 4
        m = dict(shared)
        m["x"] = np.ascontiguousarray(x[b, r * T:(r + 1) * T, :])
        m["rope"] = np.ascontiguousarray(rope[:, r * T:(r + 1) * T, :])
        in_maps.append(m)
    key = _debug
    if key not in _NC_CACHE:
        _NC_CACHE[key] = build(debug=key)
    nc = _NC_CACHE[key]
    res = run_bass_kernel_spmd(nc, in_maps, core_ids=list(range(NCORE)))
    outp = np.zeros((2, S, D), np.float32)
    for c in range(NCORE):
        b, r = c // 4, c % 4
        outp[b, r * T:(r + 1) * T, :] = np.asarray(res.results[c]["out"])
    if _debug:
        return outp, res
    return outp
```

```python
import math
from contextlib import ExitStack

import numpy as np
import ml_dtypes

import concourse.bass as bass
import concourse.mybir as mybir
from concourse.bass_utils import run_bass_kernel_spmd

F32 = mybir.dt.float32
BF16 = mybir.dt.bfloat16
ALU = mybir.AluOpType
AF = mybir.ActivationFunctionType
AX = mybir.AxisListType

NCORE = 8
S = 8192
D = 1024
T = 2048
NT = 16
DFF = 2816
NF = 22
EPS = 1e-6
LAM_INIT = 0.8 - 0.6 * math.exp(-0.3 * 1)
KPAD = 1024 + S + 1536
VBLK = 8 + 64 + 12
NM = 20


class TL:
    def __init__(self, nc, es, eng, name):
        self.e = eng
        self.sem = es.enter_context(nc.semaphore(name))
        self.n = 0
        self.seen = {}

    def wait(self, *ms):
        for m in ms:
            if m is None:
                continue
            if isinstance(m, list):
                self.wait(*m)
                continue
            sem, v = m
            if sem is self.sem and False:
                continue
            k = id(sem)
            if self.seen.get(k, 0) >= v:
                continue
            self.e.wait_ge(sem, v)
            self.seen[k] = v

    def mark(self, ins):
        self.n += 1
        ins.then_inc(self.sem, 1)
        return (self.sem, self.n)


class DS:
    def __init__(self, nc, es, name):
        self.sem = es.enter_context(nc.semaphore(name))
        self.n = 0

    def add(self, ins):
        self.n += 16
        ins.then_inc(self.sem, 16)
        return (self.sem, self.n)

    def all(self):
        return (self.sem, self.n) if self.n else None


def build(debug=False):
    nc = bass.Bass("TRN2", target_bir_lowering=False)

    def din(name, shape, dt=F32):
        return nc.dram_tensor(name, list(shape), dt, kind="ExternalInput").ap()

    def dint(name, shape, dt=BF16):
        return nc.dram_tensor(name, list(shape), dt, kind="Internal").ap()

    x_in = din("x", [T, D])
    attn_norm = din("attn_norm", [2, D])
    ffn_norm = din("ffn_norm", [2, D])
    final_norm = din("final_norm", [1, D])
    w_in_even = din("w_in_even", [D, 2304])
    a_q_norm = din("a_q_norm", [1, 64])
    a_k_norm = din("a_k_norm", [1, 64])
    w_out_even = din("w_out_even", [D, D])
    w_in_odd = din("w_in_odd", [D, 3072])
    lam_in = [din(n, [1, 64]) for n in ("lambda_q1", "lambda_k1", "lambda_q2", "lambda_k2")]
    c_sub_norm = din("c_sub_norm", [1, 128])
    w_out_odd = din("w_out_odd", [D, D])
    w_gate = din("w_gate", [2, D, DFF])
    w_up = din("w_up", [2, D, DFF])
    w_down = din("w_down", [2, DFF, D])
    rope_in = din("rope", [4, T, 32])
    masks_in = din("masks", [NM, 128, 512], BF16)
    ident_in = din("ident", [128, 128], BF16)
    out = nc.dram_tensor("out", [T, D], F32, kind="ExternalOutput").ap()

    wb_in = [dint("wb_in0", [D, 2304]), dint("wb_in1", [D, 3072])]
    wb_out = [dint("wb_out0", [D, D]), dint("wb_out1", [D, D])]
    wb_g = [dint(f"wb_g{l}", [D, DFF]) for l in range(2)]
    wb_u = [dint(f"wb_u{l}", [D, DFF]) for l in range(2)]
    wb_d = [dint(f"wb_d{l}", [DFF, D]) for l in range(2)]

    qt = [dint("qt0", [8 * 128, T]), dint("qt1", [8 * 128, T])]
    NKU = [5, 8]
    VGH = [2, 1]
    NVG = [5, 8]
    VF = [16 * 65, 16 * 129]
    kt_loc = [[dint("kt%d_loc%d" % (l, u), [128, T]) for u in range(NKU[l])] for l in range(2)]
    kt_all = [[dint("kt%d_all%d" % (l, u), [4 * 128, T]) for u in range(NKU[l])] for l in range(2)]
    v_loc = [[dint("v%d_loc%d" % (l, g), [VGH[l] * 128, VF[l]]) for g in range(NVG[l])] for l in range(2)]
    v_all = [[dint("v%d_all%d" % (l, g), [4 * VGH[l] * 128, VF[l]]) for g in range(NVG[l])] for l in range(2)]
    ktB_pad = dint("ktB_pad", [4 * 128, KPAD])
    vB_pad = dint("vB_pad", [8 * 128, VBLK * 65])
    ktB_win = dint("ktB_win", [4 * 128, 4096])
    vB_win = dint("vB_win", [8 * 128, 32 * 65])

    dbg = {}
    if debug:
        dbg["X"] = nc.dram_tensor("dbg_X", [T, D], F32, kind="ExternalOutput").ap()
        dbg["AT"] = nc.dram_tensor("dbg_AT", [128, 8 * T], BF16, kind="ExternalOutput").ap()
        dbg["qt"] = nc.dram_tensor("dbg_qt", [8 * 128, T], BF16, kind="ExternalOutput").ap()
        dbg["kt"] = nc.dram_tensor("dbg_kt", [8 * 512, T], BF16, kind="ExternalOutput").ap()
        dbg["v"] = nc.dram_tensor("dbg_v", [8 * 1024, 16 * 129], BF16, kind="ExternalOutput").ap()

    es = ExitStack()
    with es:
        pe = TL(nc, es, nc.tensor, "t_pe")
        act = TL(nc, es, nc.scalar, "t_act")
        dve = TL(nc, es, nc.vector, "t_dve")
        pool = TL(nc, es, nc.gpsimd, "t_pool")
        sp = TL(nc, es, nc.sync, "t_sp")
        engines = [pe, act, dve, pool, sp]

        def sb(name, shape, dt, stack=None):
            return (stack or es).enter_context(nc.sbuf_tensor(name, list(shape), dt))

        def psum(name, shape, dt, stack):
            return stack.enter_context(nc.psum_tensor(name, list(shape), dt))

        def dma(q, ds, out_ap, in_ap):
            return ds.add(q.e.dma_start(out=out_ap, in_=in_ap))

        X = sb("X", [128, NT, D], F32)
        AT = sb("AT", [128, 8, T], BF16)
        ROPE = sb("ROPE", [128, 4, NT, 32], F32)
        IDENT = sb("IDENT", [128, 128], BF16)
        EPST = sb("EPST", [128, 1], F32)
        FENCE = sb("FENCE", [128, 4], F32)
        NEGLAM = sb("NEGLAM", [128, 1], F32)
        GAINC = sb("GAINC", [128, 128], F32)
        GQK = sb("GQK", [128, 2, 64], F32)
        ZERO = sb("ZERO", [128, 2048], BF16)

        fence_ps_stack = ExitStack()

        def barrier():
            ms = []
            ms.append(dve.mark(nc.vector.memset(FENCE[:, 0:1], 0.0)))
            ms.append(pool.mark(nc.gpsimd.memset(FENCE[:, 1:2], 0.0)))
            ms.append(act.mark(nc.scalar.activation(out=FENCE[:, 2:3], in_=EPST[:, 0:1], func=AF.Copy)))
            ms.append((pe.sem, pe.n) if pe.n else None)
            ms += [d.all() for d in all_ds]
            for e in engines:
                e.wait(*ms)

        all_ds = []

        def newds(name, in_barrier=True):
            d = DS(nc, es, name)
            if in_barrier:
                all_ds.append(d)
            return d

        ds_c = newds("ds_c")
        ds_x = newds("ds_x")
        nc.vector.memset(EPST[:], EPS)
        m_zero = pool.mark(nc.gpsimd.memset(ZERO[:], 0.0))
        dma(sp, ds_c, IDENT[:], ident_in)
        dma(sp, ds_c, ROPE[:], rope_in.rearrange("c (i p) d -> p c i d", p=128))
        dma(sp, ds_c, GQK[:, 0, :], a_q_norm.partition_broadcast(128))
        dma(sp, ds_c, GQK[:, 1, :], a_k_norm.partition_broadcast(128))
        dma(sp, ds_c, GAINC[:], c_sub_norm.partition_broadcast(128))
        for i in range(NT):
            dma(sp, ds_x, X[:, i, :], x_in[i * 128:(i + 1) * 128, :])

        ds_w = {}

        def cast_w(key, dst, src, rows):
            d = newds("ds_w_" + key, in_barrier=False)
            ds_w[key] = d
            for r0 in range(0, rows, 128):
                dma(pool, d, dst[r0:r0 + 128, :], src[r0:r0 + 128, :])

        cast_w("in0", wb_in[0], w_in_even, D)
        ds_pad = newds("ds_pad")
        sp.wait(m_zero)
        for u in range(4):
            dma(sp, ds_pad, ktB_pad[u * 128:(u + 1) * 128, 0:1024], ZERO[:, 0:1024])
            dma(sp, ds_pad, ktB_pad[u * 128:(u + 1) * 128, 1024 + S:KPAD], ZERO[:, 0:1536])
        for h in range(8):
            dma(sp, ds_pad, vB_pad[h * 128:(h + 1) * 128, 0:8 * 65], ZERO[:, 0:8 * 65])
            dma(sp, ds_pad, vB_pad[h * 128:(h + 1) * 128, 72 * 65:VBLK * 65], ZERO[:, 0:12 * 65])
        def casts_layer0_rest():
            cast_w("out0", wb_out[0], w_out_even, D)
            cast_w("g0", wb_g[0], w_gate[0], D)
            cast_w("u0", wb_u[0], w_up[0], D)
            cast_w("d0", wb_d[0], w_down[0], DFF)
            cast_w("in1", wb_in[1], w_in_odd, D)
            cast_w("out1", wb_out[1], w_out_odd, D)

        def casts_layer1_rest():
            cast_w("g1", wb_g[1], w_gate[1], D)
            cast_w("u1", wb_u[1], w_up[1], D)
            cast_w("d1", wb_d[1], w_down[1], DFF)

        with ExitStack() as st:
            LQ = sb("LQ", [128, 4, 64], F32, st)
            LS = sb("LS", [128, 4], F32, st)
            LJ = sb("LJ", [128, 64], F32, st)
            ds_l = newds("ds_l")
            for j in range(4):
                dma(sp, ds_l, LQ[:, j, :], lam_in[j].partition_broadcast(128))
            dve.wait(ds_l.all(), ds_c.all())
            nc.vector.memset(LS[:], 0.0)
            m = None
            for j in range(2):
                m = dve.mark(nc.vector.tensor_tensor(out=LJ[:], in0=LQ[:, 2 * j, :], in1=LQ[:, 2 * j + 1, :], op=ALU.mult))
                dve.wait(m)
                m = dve.mark(nc.vector.tensor_reduce(out=LS[:, j:j + 1], in_=LJ[:], axis=AX.X, op=ALU.add))
                dve.wait(m)
            act.wait(m)
            m = act.mark(nc.scalar.activation(out=LS[:, 2:4], in_=LS[:, 0:2], func=AF.Exp))
            dve.wait(m)
            m = dve.mark(nc.vector.tensor_tensor(out=NEGLAM[:], in0=LS[:, 3:4], in1=LS[:, 2:3], op=ALU.subtract))
            dve.wait(m)
            m = dve.mark(nc.vector.tensor_scalar(out=NEGLAM[:], in0=NEGLAM[:], scalar1=-LAM_INIT, scalar2=None, op0=ALU.add))
            dve.wait(m)
            dve.mark(nc.vector.tensor_scalar(out=GAINC[:], in0=GAINC[:], scalar1=1.0 - LAM_INIT, scalar2=None, op0=ALU.mult))
            barrier()

        def rms_rstd(st, src_of_tile, nt, gdram_row, tag):
            G = sb("G" + tag, [128, D], F32, st)
            JUNK = sb("JUNK" + tag, [128, D], BF16, st)
            SS = sb("SS" + tag, [128, nt], F32, st)
            RSTD = sb("RSTD" + tag, [128, nt], F32, st)
            dsg = newds("ds_g" + tag)
            dma(sp, dsg, G[:], gdram_row.partition_broadcast(128))
            m0 = dve.mark(nc.vector.memset(SS[:], 0.0))
            act.wait(m0)
            m = None
            for i in range(nt):
                m = act.mark(nc.scalar.activation(out=JUNK[:], in_=src_of_tile(i), func=AF.Square,
                                                  accum_out=SS[:, i:i + 1]))
            act.wait(m)
            m = act.mark(nc.scalar.activation(out=RSTD[:], in_=SS[:], func=AF.Sqrt, bias=EPST[:, 0:1], scale=1.0 / D))
            dve.wait(m, dsg.all())
            m = dve.mark(nc.vector.reciprocal(out=RSTD[:], in_=RSTD[:]))
            dve.wait(m)
            return G, RSTD

        def norm_AT(gdram_row, tag):
            with ExitStack() as st:
                G, RSTD = rms_rstd(st, lambda i: X[:, i, :], NT, gdram_row, tag)
                HB = sb("HB" + tag, [128, 2, D], BF16, st)
                tp = psum("tp" + tag, [128, 2, D], BF16, st)
                m_tr = [None, None]
                m_ev = [None, None]
                for i in range(NT):
                    s = i % 2
                    dve.wait(m_tr[s])
                    mh = dve.mark(nc.vector.scalar_tensor_tensor(out=HB[:, s, :], in0=X[:, i, :], scalar=RSTD[:, i:i + 1],
                                                                 in1=G[:], op0=ALU.mult, op1=ALU.mult))
                    pe.wait(mh, m_ev[s])
                    for k in range(8):
                        ins = nc.tensor.transpose(tp[:, s, k * 128:(k + 1) * 128], HB[:, s, k * 128:(k + 1) * 128], IDENT[:])
                    m_tr[s] = pe.mark(ins)
                    act.wait(m_tr[s])
                    m_ev[s] = act.mark(nc.scalar.copy(out=AT[:, :, i * 128:(i + 1) * 128],
                                                      in_=tp[:, s, :].rearrange("p (k c) -> p k c", c=128)))
                barrier()

        def inproj(layer):
            wsrc = wb_in[layer]
            dsw = ds_w["in%d" % layer]
            if layer == 0:
                groups = [("kn", 512, 128, 0), ("v", 640, 128, 0), ("k", 1280, 512, 1), ("v", 1792, 512, 2),
                          ("qn", 0, 512, 0), ("q", 768, 512, 4)]
                dv = 64
            else:
                groups = [("k", 1024, 512, 0), ("k", 1536, 512, 4), ("v", 2048, 512, 0), ("v", 2560, 512, 4),
                          ("q", 0, 512, 0), ("q", 512, 512, 4)]
                dv = 128
            ds_kt = newds("ds_kt%d" % layer)
            ds_v = newds("ds_v%d" % layer)
            ds_q = newds("ds_q%d" % layer)
            cc_k = {}
            cc_v = {}
            with ExitStack() as st:
                WG = [sb("WG%d_%d" % (layer, s), [128, 8, 512], BF16, st) for s in range(2)]
                dsW = [newds("ds_WG%d_%d" % (layer, s)) for s in range(2)]
                PR = [sb("PR%d_%d" % (layer, s), [128, 512], F32, st) for s in range(4)]
                TMP = [sb("TMP%d_%d" % (layer, s), [128, 256], F32, st) for s in range(4)]
                SQ = sb("SQ%d" % layer, [128, 512], F32, st)
                SSQ = sb("SSQ%d" % layer, [128, 8], F32, st)
                QK = [sb("QK%d_%d" % (layer, s), [128, 512], BF16, st) for s in range(4)]
                STG = [sb("STG%d_%d" % (layer, s), [128, 8320], BF16, st) for s in range(2)]
                pj = psum("pj%d" % layer, [128, 4, 512], F32, st)
                tq = psum("tq%d" % layer, [128, 4, 1024], BF16, st)

                m_w_free = [None, None]
                m_stg_free = [None, None]
                m_pj_free = [None] * 4
                m_pr_free = [None] * 4
                m_qk_free = [None] * 4
                m_tq_free = [None] * 4
                m_tmp_free = [None, None]
                pending = []
                cnt = 0

                def load_w(gi):
                    kind, c0, wd, _ = groups[gi]
                    s = gi % 2
                    sp.wait(m_w_free[s], dsw.all())
                    dma(sp, dsW[s], WG[s][:, :, 0:wd], wsrc[:, c0:c0 + wd].rearrange("(k p) c -> p k c", p=128))

                load_w(0)
                for gi, (kind, c0, wd, dbase) in enumerate(groups):
                    ws = gi % 2
                    ss = gi % 2
                    if gi + 1 < len(groups):
                        load_w(gi + 1)
                    isv = kind == "v"
                    nh = wd // 64
                    nu = wd // 128
                    stg = STG[ss]
                    if isv:
                        nhv = wd // dv
                        stg_v = stg[:, 0:nhv * 16 * (dv + 1)].rearrange("p (h i d) -> p h i d", h=nhv, i=16)
                        pool.wait(m_stg_free[ss])
                        m_ones = pool.mark(nc.gpsimd.memset(stg[:, 0:nhv * 16 * (dv + 1)], 1.0))
                    else:
                        stg_q = stg[:, 0:nu * T].rearrange("p (u t) -> p u t", u=nu)
                        m_ones = None
                    last_ev = None
                    for i in range(NT):
                        b = cnt % 4
                        cnt += 1
                        pe.wait(dsW[ws].all(), m_pj_free[b])
                        for k in range(8):
                            ins = nc.tensor.matmul(pj[:, b, 0:wd], lhsT=AT[:, k, i * 128:(i + 1) * 128], rhs=WG[ws][:, k, 0:wd],
                                                   start=(k == 0), stop=(k == 7))
                        m_mm = pe.mark(ins)
                        m_w_free[ws] = m_mm
                        while len(pending) > 2:
                            last_ev = pending.pop(0)()
                        if isv:
                            act.wait(m_mm, m_ones, m_stg_free[ss])
                            m_c = act.mark(nc.scalar.copy(out=stg_v[:, :, i, 0:dv],
                                                          in_=pj[:, b, 0:wd].rearrange("p (h d) -> p h d", d=dv)))
                            m_pj_free[b] = m_c
                            last_ev = m_c
                            continue
                        act.wait(m_mm, m_pr_free[b])
                        m_c = act.mark(nc.scalar.copy(out=PR[b][:, 0:wd], in_=pj[:, b, 0:wd]))
                        m_pj_free[b] = m_c
                        pr3 = PR[b][:, 0:wd].rearrange("p (h d) -> p h d", d=64)
                        m_src = m_c
                        if kind in ("qn", "kn"):
                            gsel = 0 if kind == "qn" else 1
                            dve.wait(m_src)
                            m1 = dve.mark(nc.vector.tensor_tensor(out=SQ[:, 0:wd], in0=PR[b][:, 0:wd], in1=PR[b][:, 0:wd], op=ALU.mult))
                            dve.wait(m1)
                            m2 = dve.mark(nc.vector.tensor_reduce(out=SSQ[:, 0:nh], in_=SQ[:, 0:wd].rearrange("p (h d) -> p h d", d=64),
                                                                  axis=AX.X, op=ALU.add))
                            act.wait(m2)
                            m3 = act.mark(nc.scalar.activation(out=SSQ[:, 0:nh], in_=SSQ[:, 0:nh], func=AF.Sqrt,
                                                               bias=EPST[:, 0:1], scale=1.0 / 64))
                            dve.wait(m3)
                            m4 = dve.mark(nc.vector.reciprocal(out=SSQ[:, 0:nh], in_=SSQ[:, 0:nh]))
                            dve.wait(m4)
                            m5 = dve.mark(nc.vector.tensor_tensor(out=pr3, in0=pr3,
                                                                  in1=SSQ[:, 0:nh].unsqueeze(2).to_broadcast([128, nh, 64]), op=ALU.mult))
                            dve.wait(m5)
                            m_src = dve.mark(nc.vector.tensor_tensor(out=pr3, in0=pr3,
                                                                     in1=GQK[:, gsel, :].unsqueeze(1).to_broadcast([128, nh, 64]), op=ALU.mult))
                        ctab, stab = (2, 3) if kind in ("qn", "kn") else (0, 1)
                        cosb = ROPE[:, ctab, i, :].unsqueeze(1).to_broadcast([128, nh, 32])
                        sinb = ROPE[:, stab, i, :].unsqueeze(1).to_broadcast([128, nh, 32])
                        x1 = pr3[:, :, 0:32]
                        x2 = pr3[:, :, 32:64]
                        qk3 = QK[b][:, 0:wd].rearrange("p (h d) -> p h d", d=64)
                        t0 = TMP[0][:, 0:nh * 32].rearrange("p (h d) -> p h d", d=32)
                        t1 = TMP[1][:, 0:nh * 32].rearrange("p (h d) -> p h d", d=32)
                        t2 = TMP[2][:, 0:nh * 32].rearrange("p (h d) -> p h d", d=32)
                        t3 = TMP[3][:, 0:nh * 32].rearrange("p (h d) -> p h d", d=32)
                        dve.wait(m_src, m_qk_free[b], m_tmp_free[0])
                        ma = dve.mark(nc.vector.tensor_tensor(out=t0, in0=x1, in1=cosb, op=ALU.mult))
                        mb = dve.mark(nc.vector.tensor_tensor(out=t1, in0=x2, in1=sinb, op=ALU.mult))
                        dve.wait(ma, mb)
                        mq1 = dve.mark(nc.vector.tensor_tensor(out=qk3[:, :, 0:32], in0=t0, in1=t1, op=ALU.subtract))
                        pool.wait(m_src, m_qk_free[b], m_tmp_free[1])
                        mc = pool.mark(nc.gpsimd.tensor_tensor(out=t2, in0=x1, in1=sinb, op=ALU.mult))
                        md = pool.mark(nc.gpsimd.tensor_tensor(out=t3, in0=x2, in1=cosb, op=ALU.mult))
                        pool.wait(mc, md)
                        mq2 = pool.mark(nc.gpsimd.tensor_tensor(out=qk3[:, :, 32:64], in0=t2, in1=t3, op=ALU.add))
                        m_pr_free[b] = [mq1, mq2]
                        m_tmp_free[0] = mq1
                        m_tmp_free[1] = mq2

                        def do_tr(b=b, i=i, mq1=mq1, mq2=mq2):
                            pe.wait(mq1, mq2, m_tq_free[b])
                            for u in range(nu):
                                ins = nc.tensor.transpose(tq[:, b, u * 128:(u + 1) * 128], QK[b][:, u * 128:(u + 1) * 128], IDENT[:])
                            m_t = pe.mark(ins)
                            m_qk_free[b] = m_t
                            act.wait(m_t, m_stg_free[ss])
                            m_e = act.mark(nc.scalar.copy(out=stg_q[:, :, i * 128:(i + 1) * 128],
                                                          in_=tq[:, b, 0:wd].rearrange("p (u c) -> p u c", c=128)))
                            m_tq_free[b] = m_e
                            return m_e
                        pending.append(do_tr)
                    while pending:
                        last_ev = pending.pop(0)()
                    sp.wait(last_ev)
                    if isv:
                        ng = nhv // VGH[layer]
                        g0 = dbase // VGH[layer]
                        src = stg[:, 0:nhv * 16 * (dv + 1)].rearrange("p (h f) -> p h f", h=nhv)
                        for gg in range(ng):
                            m_st = dma(sp, ds_v, v_loc[layer][g0 + gg].rearrange("(h p) f -> p h f", p=128),
                                       src[:, gg * VGH[layer]:(gg + 1) * VGH[layer], :])
                        m_stg_free[ss] = m_st
                        pool.wait(m_st)
                        for gg in range(ng):
                            d = DS(nc, es, "cc_v%d_%d" % (layer, g0 + gg))
                            ins = nc.gpsimd.collective_compute("AllGather", ALU.bypass, replica_groups=[[0, 1, 2, 3], [4, 5, 6, 7]],
                                                               ins=[v_loc[layer][g0 + gg].opt()], outs=[v_all[layer][g0 + gg].opt()])
                            ins.then_inc(d.sem, 1)
                            d.n = 1
                            cc_v[g0 + gg] = d
                    elif kind in ("q", "qn"):
                        m_stg_free[ss] = dma(sp, ds_q, qt[layer][dbase * 128:(dbase + nu) * 128, :].rearrange("(u p) t -> p u t", p=128), stg_q)
                    else:
                        for uu in range(nu):
                            m_st = dma(sp, ds_kt, kt_loc[layer][dbase + uu], stg_q[:, uu, :])
                        m_stg_free[ss] = m_st
                        pool.wait(m_st)
                        for uu in range(nu):
                            d = DS(nc, es, "cc_k%d_%d" % (layer, dbase + uu))
                            ins = nc.gpsimd.collective_compute("AllGather", ALU.bypass, replica_groups=[[0, 1, 2, 3], [4, 5, 6, 7]],
                                                               ins=[kt_loc[layer][dbase + uu].opt()], outs=[kt_all[layer][dbase + uu].opt()])
                            ins.then_inc(d.sem, 1)
                            d.n = 1
                            cc_k[dbase + uu] = d
                barrier()
            return cc_k, cc_v

        def attention(layer):
            dvp = 65 if layer == 0 else 129
            dv = dvp - 1
            nbank = 2 if layer == 0 else 3
            per_bank = 4 if layer == 0 else 3
            with ExitStack() as st:
                psS = psum("psS%d" % layer, [128, 2, 2, 512], F32, st)
                psA = psum("psA%d" % layer, [128, nbank, 512], F32, st)
                psT = psum("psT%d" % layer, [128, 1024], BF16, st)
                QT = [sb("QT%d_%d" % (layer, s), [128, T], BF16, st) for s in range(2)]
                dsQ = [newds("dsQ%d_%d" % (layer, s)) for s in range(2)]
                PP = sb("PP%d" % layer, [128, 3, 2, 512], BF16, st)
                ACCS = sb("ACCS%d" % layer, [128, 8 * dvp], F32, st)
                RDEN = sb("RDEN%d" % layer, [128, 8], F32, st)
                MT = sb("MT%d" % layer, [128, 512], BF16, st)
                if layer == 1:
                    O1 = sb("O1", [128, 512], F32, st)
                    O2 = sb("O2", [128, 512], F32, st)
                    SS4 = sb("SS4", [128, 4], F32, st)
                    RS4 = sb("RS4", [128, 4], F32, st)
                if layer == 0:
                    KTA = sb("KTA", [128, 4, T], BF16, st)
                    VA = sb("VA", [128, 4, 16, 65], BF16, st)
                    dsKA = newds("dsKA")
                    KTB = [sb("KTB%d" % s, [128, NM * 128], BF16, st) for s in range(2)]
                    VB = [sb("VB%d" % s, [128, 2, NM, 65], BF16, st) for s in range(2)]
                    dsKB = [newds("dsKB%d" % s) for s in range(2)]
                    MASK = sb("MASK", [128, NM, 512], BF16, st)
                    dsM = newds("dsM")
                    dma(sp, dsM, MASK[:], masks_in.rearrange("m p q -> p m q"))
                else:
                    KT = [sb("KT1_%d" % s, [128, 4, T], BF16, st) for s in range(2)]
                    V1 = [sb("V1_%d" % s, [128, 4, 16, 129], BF16, st) for s in range(2)]
                    dsKV = [newds("dsKV%d" % s) for s in range(2)]

                state = dict(m_s_free=[None, None], m_p_free=[None] * 3,
                             m_acc_free=None, m_accs_free=None, m_mt_free=None, m_psT_free=None, kbc=0)
                m_q_free = [None, None]

                def acc_ap(n):
                    bk, o = n // per_bank, (n % per_bank) * dvp
                    return psA[:, bk, o:o + dvp], bk

                def qgroup(qtile, g, chunk, blocks, m_load, m_buf_free_cb):
                    nb = len(blocks)
                    started = set()
                    m_exp = {}
                    m_s = {}
                    last_pv = None

                    def emit_s(kb):
                        la, lb, _, _, _, _ = blocks[kb]
                        sl = state["kbc_base"] + kb
                        s = sl % 2
                        pe.wait(m_load, state["m_s_free"][s])
                        nc.tensor.matmul(psS[:, s, 0, :], lhsT=la, rhs=qtile[0:64, g * 512:(g + 1) * 512], start=True, stop=True)
                        ins = nc.tensor.matmul(psS[:, s, 1, :], lhsT=lb, rhs=qtile[64:128, g * 512:(g + 1) * 512], start=True, stop=True)
                        m_s[kb] = pe.mark(ins)

                    state["kbc_base"] = state["kbc"]
                    emit_s(0)
                    if nb > 1:
                        emit_s(1)
                    for kb in range(nb):
                        _, _, va, vb, mi, jl = blocks[kb]
                        sl = state["kbc_base"] + kb
                        s = sl % 2
                        ps3 = sl % 3
                        act.wait(m_s[kb], state["m_p_free"][ps3])
                        me = act.mark(nc.scalar.activation(out=PP[:, ps3, :, :], in_=psS[:, s, :, :], func=AF.Exp, scale=0.125))
                        state["m_s_free"][s] = me
                        if mi is not None:
                            dve.wait(me)
                            me = dve.mark(nc.vector.tensor_tensor(out=PP[:, ps3, :, :], in0=PP[:, ps3, :, :],
                                                                  in1=MASK[:, mi, :].unsqueeze(1).to_broadcast([128, 2, 512]), op=ALU.mult))
                        if kb + 2 < nb:
                            emit_s(kb + 2)
                        pe.wait(me, state["m_acc_free"])
                        ins = None
                        for mp in range(2):
                            vv = va if mp == 0 else vb
                            for j in jl:
                                o_ap, bk = acc_ap(mp * 4 + j)
                                first = bk not in started
                                started.add(bk)
                                ins = nc.tensor.matmul(o_ap, lhsT=PP[:, ps3, mp, j * 128:(j + 1) * 128], rhs=vv,
                                                       start=first, stop=(kb == nb - 1), skip_group_check=True)
                        m_pv = pe.mark(ins)
                        state["m_p_free"][ps3] = m_pv
                        last_pv = m_pv
                    state["kbc"] = state["kbc_base"] + nb
                    m_buf_free_cb(last_pv)
                    dve.wait(last_pv, state["m_accs_free"])
                    m = None
                    for bk in range(nbank):
                        ncols = min(per_bank, 8 - bk * per_bank) * dvp
                        m = dve.mark(nc.vector.tensor_copy(out=ACCS[:, bk * per_bank * dvp: bk * per_bank * dvp + ncols], in_=psA[:, bk, 0:ncols]))
                    state["m_acc_free"] = m
                    dve.wait(m)
                    a3 = ACCS[:].rearrange("p (n d) -> p n d", d=dvp)
                    m = dve.mark(nc.vector.reciprocal(out=RDEN[:], in_=a3[:, :, dv]))
                    dve.wait(m, state["m_mt_free"])
                    if layer == 0:
                        mt3 = MT[:].rearrange("p (j f) -> p j f", f=128)
                        for mp in range(2):
                            m = dve.mark(nc.vector.tensor_tensor(out=mt3[:, :, mp * 64:(mp + 1) * 64], in0=a3[:, mp * 4:(mp + 1) * 4, 0:64],
                                                                 in1=RDEN[:, mp * 4:(mp + 1) * 4].unsqueeze(2).to_broadcast([128, 4, 64]), op=ALU.mult))
                        state["m_accs_free"] = m
                        m_mt = m
                    else:
                        o13 = O1[:].rearrange("p (j f) -> p j f", f=128)
                        o23 = O2[:].rearrange("p (j f) -> p j f", f=128)
                        m = dve.mark(nc.vector.tensor_scalar(out=RDEN[:, 4:8], in0=RDEN[:, 4:8], scalar1=NEGLAM[:, 0:1], scalar2=None, op0=ALU.mult))
                        dve.wait(m)
                        nc.vector.tensor_tensor(out=o13, in0=a3[:, 0:4, 0:128], in1=RDEN[:, 0:4].unsqueeze(2).to_broadcast([128, 4, 128]), op=ALU.mult)
                        m = dve.mark(nc.vector.tensor_tensor(out=o23, in0=a3[:, 4:8, 0:128], in1=RDEN[:, 4:8].unsqueeze(2).to_broadcast([128, 4, 128]), op=ALU.mult))
                        state["m_accs_free"] = m
                        dve.wait(m)
                        m = dve.mark(nc.vector.tensor_tensor(out=O1[:], in0=O1[:], in1=O2[:], op=ALU.add))
                        dve.wait(m)
                        m = dve.mark(nc.vector.tensor_tensor(out=O2[:], in0=O1[:], in1=O1[:], op=ALU.mult))
                        dve.wait(m)
                        m = dve.mark(nc.vector.tensor_reduce(out=SS4[:], in_=o23, axis=AX.X, op=ALU.add))
                        act.wait(m)
                        m = act.mark(nc.scalar.activation(out=RS4[:], in_=SS4[:], func=AF.Ln, bias=EPST[:, 0:1], scale=1.0 / 128))
                        act.wait(m)
                        m = act.mark(nc.scalar.activation(out=RS4[:], in_=RS4[:], func=AF.Exp, scale=-0.5))
                        dve.wait(m)
                        m = dve.mark(nc.vector.tensor_tensor(out=o13, in0=o13, in1=RS4[:].unsqueeze(2).to_broadcast([128, 4, 128]), op=ALU.mult))
                        dve.wait(m)
                        m_mt = dve.mark(nc.vector.tensor_tensor(out=MT[:].rearrange("p (j f) -> p j f", f=128), in0=o13,
                                                                in1=GAINC[:].unsqueeze(1).to_broadcast([128, 4, 128]), op=ALU.mult))
                    pe.wait(m_mt, state["m_psT_free"])
                    for j in range(4):
                        ins = nc.tensor.transpose(psT[:, j * 128:(j + 1) * 128], MT[:, j * 128:(j + 1) * 128], IDENT[:])
                    m_t = pe.mark(ins)
                    state["m_mt_free"] = m_t
                    dve.wait(m_t)
                    state["m_psT_free"] = dve.mark(nc.vector.tensor_copy(out=AT[:, chunk, g * 512:(g + 1) * 512], in_=psT[:, 0:512]))

                def load_q(u, slot):
                    sp.wait(m_q_free[slot])
                    return dma(sp, dsQ[slot], QT[slot][:], qt[layer][u * 128:(u + 1) * 128, :])

                if layer == 0:
                    cc_k, cc_v = cc[0]
                    kA = kt_all[0][0].rearrange("(r p) t -> p r t", p=128)
                    vA = v_all[0][0].rearrange("(r h p) f -> h p r f", r=4, h=2)
                    ds_rl = newds("ds_rl")
                    sp.wait(ds_pad.all())
                    for u in range(4):
                        sp.wait(cc_k[1 + u].all())
                        dma(sp, ds_rl, ktB_pad[u * 128:(u + 1) * 128, 1024:1024 + 4 * T].rearrange("p (r t) -> p r t", r=4),
                            kt_all[0][1 + u].rearrange("(r p) t -> p r t", p=128))
                    for gi in range(4):
                        sp.wait(cc_v[1 + gi].all())
                        srcv = v_all[0][1 + gi].rearrange("(r h p) f -> h p r f", r=4, h=2)
                        for hh in range(2):
                            hB = 2 * gi + hh
                            dma(sp, ds_rl, vB_pad[hB * 128:(hB + 1) * 128, 8 * 65:72 * 65].rearrange("p (r f) -> p r f", r=4), srcv[hh])
                    m_ka_free = None
                    qslot = 0
                    mq = load_q(0, 0)
                    for u in range(4):
                        if u == 1:
                            casts_layer0_rest()
                        kvh = u // 2
                        if u % 2 == 0:
                            sp.wait(m_ka_free, cc_k[0].all(), cc_v[0].all())
                            for half in range(2):
                                dma(sp, dsKA, KTA[half * 64:(half + 1) * 64, :, :], kA[kvh * 64:(kvh + 1) * 64, :, :])
                            dma(sp, dsKA, VA[:].rearrange("p r i d -> p r (i d)"), vA[kvh])
                        mq_next = load_q(u + 1, (qslot + 1) % 2) if u < 3 else None
                        lastholder = []
                        for g in range(4):
                            blocks = []
                            for r in range(4):
                                for i in range(16):
                                    blocks.append((KTA[0:64, r, i * 128:(i + 1) * 128], KTA[64:128, r, i * 128:(i + 1) * 128],
                                                   VA[:, r, i, :], VA[:, r, i, :], None, [0, 1, 2, 3]))
                            qgroup(QT[qslot], g, u, blocks, [mq, dsKA.all()], lambda m: lastholder.append(m))
                        m_q_free[qslot] = lastholder[-1]
                        m_ka_free = lastholder[-1]
                        qslot = (qslot + 1) % 2
                        mq = mq_next
                    pid = nc.sync.partition_id()
                    rank = pid % 4
                    ds_win = newds("ds_win")
                    sp.wait(ds_rl.all())
                    dma(sp, ds_win, ktB_win, ktB_pad[:, bass.ds(rank * T, 4096)])
                    dma(sp, ds_win, vB_win, vB_pad[:, bass.ds(rank * (16 * 65), 32 * 65)])
                    m_kb_free = [None, None]
                    widx = 0
                    mq = load_q(4, qslot)
                    for u in range(4):
                        mq_next = None
                        lastholder = []
                        for g in range(4):
                            ws = widx % 2
                            widx += 1
                            sp.wait(m_kb_free[ws], ds_win.all())
                            dma(sp, dsKB[ws], KTB[ws][:], ktB_win[u * 128:(u + 1) * 128, g * 512:g * 512 + NM * 128])
                            for hh in range(2):
                                h = 2 * u + hh
                                dma(sp, dsKB[ws], VB[ws][:, hh, :, :].rearrange("p m d -> p (m d)"),
                                    vB_win[h * 128:(h + 1) * 128, g * 4 * 65:(g * 4 + NM) * 65])
                            if g == 0 and u < 3:
                                mq_next = load_q(4 + u + 1, (qslot + 1) % 2)
                            blocks = []
                            for mi in range(NM):
                                jl = [j for j in range(4) if j <= mi <= j + 16]
                                blocks.append((KTB[ws][0:64, mi * 128:(mi + 1) * 128], KTB[ws][64:128, mi * 128:(mi + 1) * 128],
                                               VB[ws][:, 0, mi, :], VB[ws][:, 1, mi, :], mi, jl))

                            def cb(m, ws=ws):
                                m_kb_free[ws] = m
                                lastholder.append(m)
                            qgroup(QT[qslot], g, 4 + u, blocks, [mq, dsKB[ws].all(), dsM.all()], cb)
                        m_q_free[qslot] = lastholder[-1]
                        qslot = (qslot + 1) % 2
                        mq = mq_next
                else:
                    cc_k, cc_v = cc[1]
                    m_kv_free = [None, None]

                    def load_kv(h, slot):
                        sp.wait(m_kv_free[slot], cc_k[h].all(), cc_v[h].all())
                        dma(sp, dsKV[slot], KT[slot][:], kt_all[1][h].rearrange("(r p) t -> p r t", p=128))
                        dma(sp, dsKV[slot], V1[slot][:].rearrange("p r i d -> p r (i d)"), v_all[1][h].rearrange("(r p) f -> p r f", p=128))

                    mq = load_q(0, 0)
                    load_kv(0, 0)
                    for h in range(8):
                        slot = h % 2
                        mq_next = None
                        if h == 1:
                            casts_layer1_rest()
                        if h < 7:
                            mq_next = load_q(h + 1, 1 - slot)
                            load_kv(h + 1, 1 - slot)
                        lastholder = []
                        for g in range(4):
                            blocks = []
                            for r in range(4):
                                for i in range(16):
                                    blocks.append((KT[slot][0:64, r, i * 128:(i + 1) * 128], KT[slot][64:128, r, i * 128:(i + 1) * 128],
                                                   V1[slot][:, r, i, :], V1[slot][:, r, i, :], None, [0, 1, 2, 3]))
                            qgroup(QT[slot], g, h, blocks, [mq, dsKV[slot].all()], lambda m: lastholder.append(m))
                        m_q_free[slot] = lastholder[-1]
                        m_kv_free[slot] = lastholder[-1]
                        mq = mq_next
                barrier()

        def outproj(layer):
            with ExitStack() as st:
                WO = sb("WO%d" % layer, [128, 8, D], BF16, st)
                dsO = newds("dsO%d" % layer)
                psO = psum("psO%d" % layer, [128, 2, 512], F32, st)
                sp.wait(ds_w["out%d" % layer].all())
                dma(sp, dsO, WO[:], wb_out[layer].rearrange("(k p) c -> p k c", p=128))
                m_free = [None, None]
                cnt = 0
                for i in range(NT):
                    for c in range(2):
                        b = cnt % 2
                        cnt += 1
                        pe.wait(dsO.all(), m_free[b])
                        for k in range(8):
                            ins = nc.tensor.matmul(psO[:, b, :], lhsT=AT[:, k, i * 128:(i + 1) * 128], rhs=WO[:, k, c * 512:(c + 1) * 512],
                                                   start=(k == 0), stop=(k == 7))
                        mm = pe.mark(ins)
                        dve.wait(mm)
                        m_free[b] = dve.mark(nc.vector.tensor_tensor(out=X[:, i, c * 512:(c + 1) * 512], in0=X[:, i, c * 512:(c + 1) * 512],
                                                                     in1=psO[:, b, :], op=ALU.add))
                barrier()

        def ffn(layer):
            with ExitStack() as st:
                WD = sb("WD%d" % layer, [128, NF, D], BF16, st)
                dsD = newds("dsD%d" % layer)
                WGU = [sb("WGU%d_%d" % (layer, s), [128, 2, 8, 256], BF16, st) for s in range(2)]
                dsGU = [newds("dsGU%d_%d" % (layer, s)) for s in range(2)]
                AFT = sb("AFT%d" % layer, [128, NF, 512], BF16, st)
                SG = [sb("SG%d_%d" % (layer, s), [128, 512], F32, st) for s in range(2)]
                psG = psum("psG%d" % layer, [128, 2, 512], F32, st)
                psU = psum("psU%d" % layer, [128, 2, 512], F32, st)
                psD = psum("psD%d" % layer, [128, 2, 512], F32, st)
                sp.wait(ds_w["d%d" % layer].all(), ds_w["g%d" % layer].all(), ds_w["u%d" % layer].all())
                for half in range(2):
                    dma(sp, dsD, WD[:, half * 11:(half + 1) * 11, :],
                        wb_d[layer][half * 11 * 128:(half + 1) * 11 * 128, :].rearrange("(f p) c -> p f c", p=128))
                m_gu_free = [None, None]
                m_g_free = [None, None]
                m_u_free = [None, None]
                m_sg_free = [None, None]
                m_d_free = [None, None]
                m_down_last = None
                lc = 0
                fc_cnt = 0
                dcnt = 0

                def load_gu(fg, slot):
                    sp.wait(m_gu_free[slot])
                    dma(sp, dsGU[slot], WGU[slot][:, 0, :, :], wb_g[layer][:, fg * 256:(fg + 1) * 256].rearrange("(k p) c -> p k c", p=128))
                    dma(sp, dsGU[slot], WGU[slot][:, 1, :, :], wb_u[layer][:, fg * 256:(fg + 1) * 256].rearrange("(k p) c -> p k c", p=128))

                seq = [(tg, fg) for tg in range(4) for fg in range(11)]
                load_gu(seq[0][1], 0)
                for si, (tg, fg) in enumerate(seq):
                    slot = si % 2
                    if si + 1 < len(seq):
                        load_gu(seq[si + 1][1], 1 - slot)
                    for fc in range(2):
                        f = fg * 2 + fc
                        b = fc_cnt % 2
                        fc_cnt += 1
                        pe.wait(dsGU[slot].all(), m_g_free[b], m_u_free[b])
                        for k in range(8):
                            nc.tensor.matmul(psG[:, b, :], lhsT=WGU[slot][:, 0, k, fc * 128:(fc + 1) * 128], rhs=AT[:, k, tg * 512:(tg + 1) * 512],
                                             start=(k == 0), stop=(k == 7))
                        for k in range(8):
                            ins = nc.tensor.matmul(psU[:, b, :], lhsT=WGU[slot][:, 1, k, fc * 128:(fc + 1) * 128], rhs=AT[:, k, tg * 512:(tg + 1) * 512],
                                                   start=(k == 0), stop=(k == 7))
                        mm = pe.mark(ins)
                        m_gu_free[slot] = mm
                        act.wait(mm, m_sg_free[b])
                        ms = act.mark(nc.scalar.activation(out=SG[b][:], in_=psG[:, b, :], func=AF.Silu))
                        m_g_free[b] = ms
                        dve.wait(ms, m_down_last)
                        ma = dve.mark(nc.vector.tensor_tensor(out=AFT[:, f, :], in0=SG[b][:], in1=psU[:, b, :], op=ALU.mult))
                        m_u_free[b] = ma
                        m_sg_free[b] = ma
                        last_a = ma
                    if fg == 10:
                        for j in range(4):
                            for c in range(2):
                                b = dcnt % 2
                                dcnt += 1
                                pe.wait(last_a, dsD.all(), m_d_free[b])
                                for f in range(NF):
                                    ins = nc.tensor.matmul(psD[:, b, :], lhsT=AFT[:, f, j * 128:(j + 1) * 128], rhs=WD[:, f, c * 512:(c + 1) * 512],
                                                           start=(f == 0), stop=(f == NF - 1))
                                mm = pe.mark(ins)
                                m_down_last = mm
                                i = tg * 4 + j
                                dve.wait(mm)
                                m_d_free[b] = dve.mark(nc.vector.tensor_tensor(out=X[:, i, c * 512:(c + 1) * 512], in0=X[:, i, c * 512:(c + 1) * 512],
                                                                               in1=psD[:, b, :], op=ALU.add))
                barrier()

        def final():
            with ExitStack() as st:
                G, RSTD = rms_rstd(st, lambda i: X[:, i, :], NT, final_norm[0:1, :], "fin")
                OB = [sb("OB%d" % s, [128, D], F32, st) for s in range(2)]
                ds_o = newds("ds_out")
                m_free = [None, None]
                for i in range(NT):
                    s = i % 2
                    dve.wait(m_free[s])
                    m = dve.mark(nc.vector.scalar_tensor_tensor(out=OB[s][:], in0=X[:, i, :], scalar=RSTD[:, i:i + 1], in1=G[:],
                                                                op0=ALU.mult, op1=ALU.mult))
                    sp.wait(m)
                    m_free[s] = dma(sp, ds_o, out[i * 128:(i + 1) * 128, :], OB[s][:])
                sp.wait(ds_o.all())
                pool.wait(ds_o.all())

        cc = {}
        dve.wait(ds_x.all(), ds_c.all())
        act.wait(ds_x.all(), ds_c.all())
        pe.wait(ds_c.all())
        pool.wait(ds_c.all())

        def dump(layer):
            barrier()
            dsd = newds("ds_dbg")
            for i in range(NT):
                dma(sp, dsd, dbg["X"][i * 128:(i + 1) * 128, :], X[:, i, :])
            dma(sp, dsd, dbg["AT"], AT[:].rearrange("p k t -> p (k t)"))
            dma(sp, dsd, dbg["qt"], qt[layer])
            for u in range(NKU[layer]):
                dma(sp, dsd, dbg["kt"][u * 512:(u + 1) * 512, :], kt_all[layer][u])
            for g in range(NVG[layer]):
                nr = 4 * VGH[layer] * 128
                dma(sp, dsd, dbg["v"][g * nr:(g + 1) * nr, 0:VF[layer]], v_all[layer][g])
            sp.wait(dsd.all())
            pool.wait(dsd.all())

        steps = [
            ("norm_a0", lambda: norm_AT(attn_norm[0:1, :], "a0")),
            ("inproj0", lambda: cc.__setitem__(0, inproj(0))),
            ("attn0", lambda: attention(0)),
            ("outproj0", lambda: outproj(0)),
            ("norm_f0", lambda: norm_AT(ffn_norm[0:1, :], "f0")),
            ("ffn0", lambda: ffn(0)),
            ("norm_a1", lambda: norm_AT(attn_norm[1:2, :], "a1")),
            ("inproj1", lambda: cc.__setitem__(1, inproj(1))),
            ("attn1", lambda: attention(1)),
            ("outproj1", lambda: outproj(1)),
            ("norm_f1", lambda: norm_AT(ffn_norm[1:2, :], "f1")),
            ("ffn1", lambda: ffn(1)),
            ("final", final),
        ]
        for name, fn in steps:
            fn()
            if debug and debug == name:
                dump(0 if name in ("norm_a0", "inproj0", "attn0", "outproj0", "norm_f0", "ffn0", "norm_a1") else 1)
                break
    return nc


def _const_tables():
    theta = 10000.0
    pos = np.arange(S, dtype=np.float32)
    inv64 = (theta ** (-np.arange(0, 64, 2, dtype=np.float32) / 64)).astype(np.float32)
    inv32 = (theta ** (-np.arange(0, 32, 2, dtype=np.float32) / 32)).astype(np.float32)
    ang1 = (pos[:, None] * inv64[None, :]).astype(np.float32)
    rows = (np.arange(S) // 64).astype(np.float32)
    cols = (np.arange(S) % 64).astype(np.float32)
    ang2 = np.concatenate([rows[:, None] * inv32[None, :], cols[:, None] * inv32[None, :]], axis=-1).astype(np.float32)
    rope = np.stack([np.cos(ang1.astype(np.float64)), np.sin(ang1.astype(np.float64)),
                     np.cos(ang2.astype(np.float64)), np.sin(ang2.astype(np.float64))], 0).astype(np.float32)
    k = np.arange(128)[:, None]
    q = np.arange(512)[None, :]
    masks = np.zeros((NM, 128, 512), np.float32)
    for m in range(NM):
        d = 128 * m + k - q - 1024
        ad = np.abs(d)
        masks[m] = (ad <= 64).astype(np.float32) + ((d % 4 == 0) & (ad <= 256)) + ((d % 16 == 0) & (ad <= 1024))
    ident = np.eye(128, dtype=np.float32)
    return rope, masks.astype(ml_dtypes.bfloat16), ident.astype(ml_dtypes.bfloat16)


_NC_CACHE = {}


def kernel(x, attn_norm, ffn_norm, final_norm, w_in_even, a_q_norm, a_k_norm, w_out_even,
           w_in_odd, lambda_q1, lambda_k1, lambda_q2, lambda_k2, c_sub_norm, w_out_odd,
           w_gate, w_up, w_down, _debug=False):
    f = lambda a: np.ascontiguousarray(np.asarray(a, dtype=np.float32))
    x = f(x)
    rope, masks, ident = _const_tables()
    shared = {
        "attn_norm": f(attn_norm), "ffn_norm": f(ffn_norm), "final_norm": f(final_norm).reshape(1, D),
        "w_in_even": f(w_in_even)[0], "a_q_norm": f(a_q_norm), "a_k_norm": f(a_k_norm),
        "w_out_even": f(w_out_even)[0], "w_in_odd": f(w_in_odd)[0],
        "lambda_q1": f(lambda_q1), "lambda_k1": f(lambda_k1), "lambda_q2": f(lambda_q2), "lambda_k2": f(lambda_k2),
        "c_sub_norm": f(c_sub_norm), "w_out_odd": f(w_out_odd)[0],
        "w_gate": f(w_gate), "w_up": f(w_up), "w_down": f(w_down),
        "masks": masks, "ident": ident,
    }
    in_maps = []
    for c in range(NCORE):
        b, r = c // 4, c % 4
        m = dict(shared)
        m["x"] = np.ascontiguousarray(x[b, r * T:(r + 1) * T, :])
        m["rope"] = np.ascontiguousarray(rope[:, r * T:(r + 1) * T, :])
        in_maps.append(m)
    key = _debug
    if key not in _NC_CACHE:
        _NC_CACHE[key] = build(debug=key)
    nc = _NC_CACHE[key]
    res = run_bass_kernel_spmd(nc, in_maps, core_ids=list(range(NCORE)))
    outp = np.zeros((2, S, D), np.float32)
    for c in range(NCORE):
        b, r = c // 4, c % 4
        outp[b, r * T:(r + 1) * T, :] = np.asarray(res.results[c]["out"])
    if _debug:
        return outp, res
    return outp
```

```python
import math
from contextlib import ExitStack

import numpy as np
import ml_dtypes

import concourse.bass as bass
import concourse.mybir as mybir
from concourse.bass_utils import run_bass_kernel_spmd

F32 = mybir.dt.float32
BF16 = mybir.dt.bfloat16
ALU = mybir.AluOpType
AF = mybir.ActivationFunctionType
AX = mybir.AxisListType

NCORE = 8
S = 8192
D = 1024
T = 2048
NT = 16
DFF = 2816
NF = 22
EPS = 1e-6
LAM_INIT = 0.8 - 0.6 * math.exp(-0.3 * 1)
KPAD = 1024 + S + 1536
VBLK = 8 + 64 + 12
NM = 20


class TL:
    def __init__(self, nc, es, eng, name):
        self.e = eng
        self.sem = es.enter_context(nc.semaphore(name))
        self.n = 0
        self.seen = {}

    def wait(self, *ms):
        for m in ms:
            if m is None:
                continue
            if isinstance(m, list):
                self.wait(*m)
                continue
            sem, v = m
            if sem is self.sem and False:
                continue
            k = id(sem)
            if self.seen.get(k, 0) >= v:
                continue
            self.e.wait_ge(sem, v)
            self.seen[k] = v

    def mark(self, ins):
        self.n += 1
        ins.then_inc(self.sem, 1)
        return (self.sem, self.n)


class DS:
    def __init__(self, nc, es, name):
        self.sem = es.enter_context(nc.semaphore(name))
        self.n = 0

    def add(self, ins):
        self.n += 16
        ins.then_inc(self.sem, 16)
        return (self.sem, self.n)

    def all(self):
        return (self.sem, self.n) if self.n else None


def build(debug=False):
    nc = bass.Bass("TRN2", target_bir_lowering=False)

    def din(name, shape, dt=F32):
        return nc.dram_tensor(name, list(shape), dt, kind="ExternalInput").ap()

    def dint(name, shape, dt=BF16):
        return nc.dram_tensor(name, list(shape), dt, kind="Internal").ap()

    x_in = din("x", [T, D])
    attn_norm = din("attn_norm", [2, D])
    ffn_norm = din("ffn_norm", [2, D])
    final_norm = din("final_norm", [1, D])
    w_in_even = din("w_in_even", [D, 2304])
    a_q_norm = din("a_q_norm", [1, 64])
    a_k_norm = din("a_k_norm", [1, 64])
    w_out_even = din("w_out_even", [D, D])
    w_in_odd = din("w_in_odd", [D, 3072])
    lam_in = [din(n, [1, 64]) for n in ("lambda_q1", "lambda_k1", "lambda_q2", "lambda_k2")]
    c_sub_norm = din("c_sub_norm", [1, 128])
    w_out_odd = din("w_out_odd", [D, D])
    w_gate = din("w_gate", [2, D, DFF])
    w_up = din("w_up", [2, D, DFF])
    w_down = din("w_down", [2, DFF, D])
    rope_in = din("rope", [4, T, 32])
    masks_in = din("masks", [NM, 128, 512], BF16)
    ident_in = din("ident", [128, 128], BF16)
    out = nc.dram_tensor("out", [T, D], F32, kind="ExternalOutput").ap()

    wb_in = [dint("wb_in0", [D, 2304]), dint("wb_in1", [D, 3072])]
    wb_out = [dint("wb_out0", [D, D]), dint("wb_out1", [D, D])]
    wb_g = [dint(f"wb_g{l}", [D, DFF]) for l in range(2)]
    wb_u = [dint(f"wb_u{l}", [D, DFF]) for l in range(2)]
    wb_d = [dint(f"wb_d{l}", [DFF, D]) for l in range(2)]

    qt = [dint("qt0", [8 * 128, T]), dint("qt1", [8 * 128, T])]
    NKU = [5, 8]
    VGH = [2, 1]
    NVG = [5, 8]
    VF = [16 * 65, 16 * 129]
    kt_loc = [[dint("kt%d_loc%d" % (l, u), [128, T]) for u in range(NKU[l])] for l in range(2)]
    kt_all = [[dint("kt%d_all%d" % (l, u), [4 * 128, T]) for u in range(NKU[l])] for l in range(2)]
    v_loc = [[dint("v%d_loc%d" % (l, g), [VGH[l] * 128, VF[l]]) for g in range(NVG[l])] for l in range(2)]
    v_all = [[dint("v%d_all%d" % (l, g), [4 * VGH[l] * 128, VF[l]]) for g in range(NVG[l])] for l in range(2)]
    ktB_pad = dint("ktB_pad", [4 * 128, KPAD])
    vB_pad = dint("vB_pad", [8 * 128, VBLK * 65])
    ktB_win = dint("ktB_win", [4 * 128, 4096])
    vB_win = dint("vB_win", [8 * 128, 32 * 65])

    dbg = {}
    if debug:
        dbg["X"] = nc.dram_tensor("dbg_X", [T, D], F32, kind="ExternalOutput").ap()
        dbg["AT"] = nc.dram_tensor("dbg_AT", [128, 8 * T], BF16, kind="ExternalOutput").ap()
        dbg["qt"] = nc.dram_tensor("dbg_qt", [8 * 128, T], BF16, kind="ExternalOutput").ap()
        dbg["kt"] = nc.dram_tensor("dbg_kt", [8 * 512, T], BF16, kind="ExternalOutput").ap()
        dbg["v"] = nc.dram_tensor("dbg_v", [8 * 1024, 16 * 129], BF16, kind="ExternalOutput").ap()

    es = ExitStack()
    with es:
        pe = TL(nc, es, nc.tensor, "t_pe")
        act = TL(nc, es, nc.scalar, "t_act")
        dve = TL(nc, es, nc.vector, "t_dve")
        pool = TL(nc, es, nc.gpsimd, "t_pool")
        sp = TL(nc, es, nc.sync, "t_sp")
        engines = [pe, act, dve, pool, sp]

        def sb(name, shape, dt, stack=None):
            return (stack or es).enter_context(nc.sbuf_tensor(name, list(shape), dt))

        def psum(name, shape, dt, stack):
            return stack.enter_context(nc.psum_tensor(name, list(shape), dt))

        def dma(q, ds, out_ap, in_ap):
            return ds.add(q.e.dma_start(out=out_ap, in_=in_ap))

        X = sb("X", [128, NT, D], F32)
        AT = sb("AT", [128, 8, T], BF16)
        ROPE = sb("ROPE", [128, 4, NT, 32], F32)
        IDENT = sb("IDENT", [128, 128], BF16)
        EPST = sb("EPST", [128, 1], F32)
        FENCE = sb("FENCE", [128, 4], F32)
        NEGLAM = sb("NEGLAM", [128, 1], F32)
        GAINC = sb("GAINC", [128, 128], F32)
        GQK = sb("GQK", [128, 2, 64], F32)
        ZERO = sb("ZERO", [128, 2048], BF16)

        fence_ps_stack = ExitStack()

        def barrier():
            ms = []
            ms.append(dve.mark(nc.vector.memset(FENCE[:, 0:1], 0.0)))
            ms.append(pool.mark(nc.gpsimd.memset(FENCE[:, 1:2], 0.0)))
            ms.append(act.mark(nc.scalar.activation(out=FENCE[:, 2:3], in_=EPST[:, 0:1], func=AF.Copy)))
            ms.append((pe.sem, pe.n) if pe.n else None)
            ms += [d.all() for d in all_ds]
            for e in engines:
                e.wait(*ms)

        all_ds = []

        def newds(name, in_barrier=True):
            d = DS(nc, es, name)
            if in_barrier:
                all_ds.append(d)
            return d

        ds_c = newds("ds_c")
        ds_x = newds("ds_x")
        nc.vector.memset(EPST[:], EPS)
        m_zero = pool.mark(nc.gpsimd.memset(ZERO[:], 0.0))
        dma(sp, ds_c, IDENT[:], ident_in)
        dma(sp, ds_c, ROPE[:], rope_in.rearrange("c (i p) d -> p c i d", p=128))
        dma(sp, ds_c, GQK[:, 0, :], a_q_norm.partition_broadcast(128))
        dma(sp, ds_c, GQK[:, 1, :], a_k_norm.partition_broadcast(128))
        dma(sp, ds_c, GAINC[:], c_sub_norm.partition_broadcast(128))
        for i in range(NT):
            dma(sp, ds_x, X[:, i, :], x_in[i * 128:(i + 1) * 128, :])

        ds_w = {}

        def cast_w(key, dst, src, rows):
            d = newds("ds_w_" + key, in_barrier=False)
            ds_w[key] = d
            for r0 in range(0, rows, 128):
                dma(pool, d, dst[r0:r0 + 128, :], src[r0:r0 + 128, :])

        cast_w("in0", wb_in[0], w_in_even, D)
        ds_pad = newds("ds_pad")
        sp.wait(m_zero)
        for u in range(4):
            dma(sp, ds_pad, ktB_pad[u * 128:(u + 1) * 128, 0:1024], ZERO[:, 0:1024])
            dma(sp, ds_pad, ktB_pad[u * 128:(u + 1) * 128, 1024 + S:KPAD], ZERO[:, 0:1536])
        for h in range(8):
            dma(sp, ds_pad, vB_pad[h * 128:(h + 1) * 128, 0:8 * 65], ZERO[:, 0:8 * 65])
            dma(sp, ds_pad, vB_pad[h * 128:(h + 1) * 128, 72 * 65:VBLK * 65], ZERO[:, 0:12 * 65])
        def casts_layer0_rest():
            cast_w("out0", wb_out[0], w_out_even, D)
            cast_w("g0", wb_g[0], w_gate[0], D)
            cast_w("u0", wb_u[0], w_up[0], D)
            cast_w("d0", wb_d[0], w_down[0], DFF)
            cast_w("in1", wb_in[1], w_in_odd, D)
            cast_w("out1", wb_out[1], w_out_odd, D)

        def casts_layer1_rest():
            cast_w("g1", wb_g[1], w_gate[1], D)
            cast_w("u1", wb_u[1], w_up[1], D)
            cast_w("d1", wb_d[1], w_down[1], DFF)

        with ExitStack() as st:
            LQ = sb("LQ", [128, 4, 64], F32, st)
            LS = sb("LS", [128, 4], F32, st)
            LJ = sb("LJ", [128, 64], F32, st)
            ds_l = newds("ds_l")
            for j in range(4):
                dma(sp, ds_l, LQ[:, j, :], lam_in[j].partition_broadcast(128))
            dve.wait(ds_l.all(), ds_c.all())
            nc.vector.memset(LS[:], 0.0)
            m = None
            for j in range(2):
                m = dve.mark(nc.vector.tensor_tensor(out=LJ[:], in0=LQ[:, 2 * j, :], in1=LQ[:, 2 * j + 1, :], op=ALU.mult))
                dve.wait(m)
                m = dve.mark(nc.vector.tensor_reduce(out=LS[:, j:j + 1], in_=LJ[:], axis=AX.X, op=ALU.add))
                dve.wait(m)
            act.wait(m)
            m = act.mark(nc.scalar.activation(out=LS[:, 2:4], in_=LS[:, 0:2], func=AF.Exp))
            dve.wait(m)
            m = dve.mark(nc.vector.tensor_tensor(out=NEGLAM[:], in0=LS[:, 3:4], in1=LS[:, 2:3], op=ALU.subtract))
            dve.wait(m)
            m = dve.mark(nc.vector.tensor_scalar(out=NEGLAM[:], in0=NEGLAM[:], scalar1=-LAM_INIT, scalar2=None, op0=ALU.add))
            dve.wait(m)
            dve.mark(nc.vector.tensor_scalar(out=GAINC[:], in0=GAINC[:], scalar1=1.0 - LAM_INIT, scalar2=None, op0=ALU.mult))
            barrier()

        def rms_rstd(st, src_of_tile, nt, gdram_row, tag):
            G = sb("G" + tag, [128, D], F32, st)
            JUNK = sb("JUNK" + tag, [128, D], BF16, st)
            SS = sb("SS" + tag, [128, nt], F32, st)
            RSTD = sb("RSTD" + tag, [128, nt], F32, st)
            dsg = newds("ds_g" + tag)
            dma(sp, dsg, G[:], gdram_row.partition_broadcast(128))
            m0 = dve.mark(nc.vector.memset(SS[:], 0.0))
            act.wait(m0)
            m = None
            for i in range(nt):
                m = act.mark(nc.scalar.activation(out=JUNK[:], in_=src_of_tile(i), func=AF.Square,
                                                  accum_out=SS[:, i:i + 1]))
            act.wait(m)
            m = act.mark(nc.scalar.activation(out=RSTD[:], in_=SS[:], func=AF.Sqrt, bias=EPST[:, 0:1], scale=1.0 / D))
            dve.wait(m, dsg.all())
            m = dve.mark(nc.vector.reciprocal(out=RSTD[:], in_=RSTD[:]))
            dve.wait(m)
            return G, RSTD

        def norm_AT(gdram_row, tag):
            with ExitStack() as st:
                G, RSTD = rms_rstd(st, lambda i: X[:, i, :], NT, gdram_row, tag)
                HB = sb("HB" + tag, [128, 2, D], BF16, st)
                tp = psum("tp" + tag, [128, 2, D], BF16, st)
                m_tr = [None, None]
                m_ev = [None, None]
                for i in range(NT):
                    s = i % 2
                    dve.wait(m_tr[s])
                    mh = dve.mark(nc.vector.scalar_tensor_tensor(out=HB[:, s, :], in0=X[:, i, :], scalar=RSTD[:, i:i + 1],
                                                                 in1=G[:], op0=ALU.mult, op1=ALU.mult))
                    pe.wait(mh, m_ev[s])
                    for k in range(8):
                        ins = nc.tensor.transpose(tp[:, s, k * 128:(k + 1) * 128], HB[:, s, k * 128:(k + 1) * 128], IDENT[:])
                    m_tr[s] = pe.mark(ins)
                    act.wait(m_tr[s])
                    m_ev[s] = act.mark(nc.scalar.copy(out=AT[:, :, i * 128:(i + 1) * 128],
                                                      in_=tp[:, s, :].rearrange("p (k c) -> p k c", c=128)))
                barrier()

        def inproj(layer):
            wsrc = wb_in[layer]
            dsw = ds_w["in%d" % layer]
            if layer == 0:
                groups = [("kn", 512, 128, 0), ("v", 640, 128, 0), ("k", 1280, 512, 1), ("v", 1792, 512, 2),
                          ("qn", 0, 512, 0), ("q", 768, 512, 4)]
                dv = 64
            else:
                groups = [("k", 1024, 512, 0), ("v", 2048, 512, 0), ("k", 1536, 512, 4), ("v", 2560, 512, 4),
                          ("q", 0, 512, 0), ("q", 512, 512, 4)]
                dv = 128
            ds_kt = newds("ds_kt%d" % layer)
            ds_v = newds("ds_v%d" % layer)
            ds_q = newds("ds_q%d" % layer)
            cc_k = {}
            cc_v = {}
            with ExitStack() as st:
                WG = [sb("WG%d_%d" % (layer, s), [128, 8, 512], BF16, st) for s in range(2)]
                dsW = [newds("ds_WG%d_%d" % (layer, s)) for s in range(2)]
                PR = [sb("PR%d_%d" % (layer, s), [128, 512], F32, st) for s in range(4)]
                TMP = [sb("TMP%d_%d" % (layer, s), [128, 256], F32, st) for s in range(4)]
                SQ = sb("SQ%d" % layer, [128, 512], F32, st)
                SSQ = sb("SSQ%d" % layer, [128, 8], F32, st)
                QK = [sb("QK%d_%d" % (layer, s), [128, 512], BF16, st) for s in range(4)]
                STG = [sb("STG%d_%d" % (layer, s), [128, 8320], BF16, st) for s in range(2)]
                pj = psum("pj%d" % layer, [128, 4, 512], F32, st)
                tq = psum("tq%d" % layer, [128, 4, 1024], BF16, st)

                m_w_free = [None, None]
                m_stg_free = [None, None]
                m_pj_free = [None] * 4
                m_pr_free = [None] * 4
                m_qk_free = [None] * 4
                m_tq_free = [None] * 4
                m_tmp_free = [None, None]
                pending = []
                cnt = 0

                def load_w(gi):
                    kind, c0, wd, _ = groups[gi]
                    s = gi % 2
                    sp.wait(m_w_free[s], dsw.all())
                    dma(sp, dsW[s], WG[s][:, :, 0:wd], wsrc[:, c0:c0 + wd].rearrange("(k p) c -> p k c", p=128))

                load_w(0)
                for gi, (kind, c0, wd, dbase) in enumerate(groups):
                    ws = gi % 2
                    ss = gi % 2
                    if gi + 1 < len(groups):
                        load_w(gi + 1)
                    isv = kind == "v"
                    nh = wd // 64
                    nu = wd // 128
                    stg = STG[ss]
                    if isv:
                        nhv = wd // dv
                        stg_v = stg[:, 0:nhv * 16 * (dv + 1)].rearrange("p (h i d) -> p h i d", h=nhv, i=16)
                        pool.wait(m_stg_free[ss])
                        m_ones = pool.mark(nc.gpsimd.memset(stg[:, 0:nhv * 16 * (dv + 1)], 1.0))
                    else:
                        stg_q = stg[:, 0:nu * T].rearrange("p (u t) -> p u t", u=nu)
                        m_ones = None
                    last_ev = None
                    for i in range(NT):
                        b = cnt % 4
                        cnt += 1
                        pe.wait(dsW[ws].all(), m_pj_free[b])
                        for k in range(8):
                            ins = nc.tensor.matmul(pj[:, b, 0:wd], lhsT=AT[:, k, i * 128:(i + 1) * 128], rhs=WG[ws][:, k, 0:wd],
                                                   start=(k == 0), stop=(k == 7))
                        m_mm = pe.mark(ins)
                        m_w_free[ws] = m_mm
                        while len(pending) > 2:
                            last_ev = pending.pop(0)()
                        if isv:
                            act.wait(m_mm, m_ones, m_stg_free[ss])
                            m_c = act.mark(nc.scalar.copy(out=stg_v[:, :, i, 0:dv],
                                                          in_=pj[:, b, 0:wd].rearrange("p (h d) -> p h d", d=dv)))
                            m_pj_free[b] = m_c
                            last_ev = m_c
                            continue
                        act.wait(m_mm, m_pr_free[b])
                        m_c = act.mark(nc.scalar.copy(out=PR[b][:, 0:wd], in_=pj[:, b, 0:wd]))
                        m_pj_free[b] = m_c
                        pr3 = PR[b][:, 0:wd].rearrange("p (h d) -> p h d", d=64)
                        m_src = m_c
                        if kind in ("qn", "kn"):
                            gsel = 0 if kind == "qn" else 1
                            dve.wait(m_src)
                            m1 = dve.mark(nc.vector.tensor_tensor(out=SQ[:, 0:wd], in0=PR[b][:, 0:wd], in1=PR[b][:, 0:wd], op=ALU.mult))
                            dve.wait(m1)
                            m2 = dve.mark(nc.vector.tensor_reduce(out=SSQ[:, 0:nh], in_=SQ[:, 0:wd].rearrange("p (h d) -> p h d", d=64),
                                                                  axis=AX.X, op=ALU.add))
                            act.wait(m2)
                            m3 = act.mark(nc.scalar.activation(out=SSQ[:, 0:nh], in_=SSQ[:, 0:nh], func=AF.Sqrt,
                                                               bias=EPST[:, 0:1], scale=1.0 / 64))
                            dve.wait(m3)
                            m4 = dve.mark(nc.vector.reciprocal(out=SSQ[:, 0:nh], in_=SSQ[:, 0:nh]))
                            dve.wait(m4)
                            m5 = dve.mark(nc.vector.tensor_tensor(out=pr3, in0=pr3,
                                                                  in1=SSQ[:, 0:nh].unsqueeze(2).to_broadcast([128, nh, 64]), op=ALU.mult))
                            dve.wait(m5)
                            m_src = dve.mark(nc.vector.tensor_tensor(out=pr3, in0=pr3,
                                                                     in1=GQK[:, gsel, :].unsqueeze(1).to_broadcast([128, nh, 64]), op=ALU.mult))
                        ctab, stab = (2, 3) if kind in ("qn", "kn") else (0, 1)
                        cosb = ROPE[:, ctab, i, :].unsqueeze(1).to_broadcast([128, nh, 32])
                        sinb = ROPE[:, stab, i, :].unsqueeze(1).to_broadcast([128, nh, 32])
                        x1 = pr3[:, :, 0:32]
                        x2 = pr3[:, :, 32:64]
                        qk3 = QK[b][:, 0:wd].rearrange("p (h d) -> p h d", d=64)
                        t0 = TMP[0][:, 0:nh * 32].rearrange("p (h d) -> p h d", d=32)
                        t1 = TMP[1][:, 0:nh * 32].rearrange("p (h d) -> p h d", d=32)
                        t2 = TMP[2][:, 0:nh * 32].rearrange("p (h d) -> p h d", d=32)
                        t3 = TMP[3][:, 0:nh * 32].rearrange("p (h d) -> p h d", d=32)
                        dve.wait(m_src, m_qk_free[b], m_tmp_free[0])
                        ma = dve.mark(nc.vector.tensor_tensor(out=t0, in0=x1, in1=cosb, op=ALU.mult))
                        mb = dve.mark(nc.vector.tensor_tensor(out=t1, in0=x2, in1=sinb, op=ALU.mult))
                        dve.wait(ma, mb)
                        mq1 = dve.mark(nc.vector.tensor_tensor(out=qk3[:, :, 0:32], in0=t0, in1=t1, op=ALU.subtract))
                        pool.wait(m_src, m_qk_free[b], m_tmp_free[1])
                        mc = pool.mark(nc.gpsimd.tensor_tensor(out=t2, in0=x1, in1=sinb, op=ALU.mult))
                        md = pool.mark(nc.gpsimd.tensor_tensor(out=t3, in0=x2, in1=cosb, op=ALU.mult))
                        pool.wait(mc, md)
                        mq2 = pool.mark(nc.gpsimd.tensor_tensor(out=qk3[:, :, 32:64], in0=t2, in1=t3, op=ALU.add))
                        m_pr_free[b] = [mq1, mq2]
                        m_tmp_free[0] = mq1
                        m_tmp_free[1] = mq2

                        def do_tr(b=b, i=i, mq1=mq1, mq2=mq2):
                            pe.wait(mq1, mq2, m_tq_free[b])
                            for u in range(nu):
                                ins = nc.tensor.transpose(tq[:, b, u * 128:(u + 1) * 128], QK[b][:, u * 128:(u + 1) * 128], IDENT[:])
                            m_t = pe.mark(ins)
                            m_qk_free[b] = m_t
                            act.wait(m_t, m_stg_free[ss])
                            m_e = act.mark(nc.scalar.copy(out=stg_q[:, :, i * 128:(i + 1) * 128],
                                                          in_=tq[:, b, 0:wd].rearrange("p (u c) -> p u c", c=128)))
                            m_tq_free[b] = m_e
                            return m_e
                        pending.append(do_tr)
                    while pending:
                        last_ev = pending.pop(0)()
                    sp.wait(last_ev)
                    if isv:
                        ng = nhv // VGH[layer]
                        g0 = dbase // VGH[layer]
                        src = stg[:, 0:nhv * 16 * (dv + 1)].rearrange("p (h f) -> p h f", h=nhv)
                        for gg in range(ng):
                            m_st = dma(sp, ds_v, v_loc[layer][g0 + gg].rearrange("(h p) f -> p h f", p=128),
                                       src[:, gg * VGH[layer]:(gg + 1) * VGH[layer], :])
                        m_stg_free[ss] = m_st
                        pool.wait(m_st)
                        for gg in range(ng):
                            d = DS(nc, es, "cc_v%d_%d" % (layer, g0 + gg))
                            ins = nc.gpsimd.collective_compute("AllGather", ALU.bypass, replica_groups=[[0, 1, 2, 3], [4, 5, 6, 7]],
                                                               ins=[v_loc[layer][g0 + gg].opt()], outs=[v_all[layer][g0 + gg].opt()])
                            ins.then_inc(d.sem, 1)
                            d.n = 1
                            cc_v[g0 + gg] = d
                    elif kind in ("q", "qn"):
                        m_stg_free[ss] = dma(sp, ds_q, qt[layer][dbase * 128:(dbase + nu) * 128, :].rearrange("(u p) t -> p u t", p=128), stg_q)
                    else:
                        for uu in range(nu):
                            m_st = dma(sp, ds_kt, kt_loc[layer][dbase + uu], stg_q[:, uu, :])
                        m_stg_free[ss] = m_st
                        pool.wait(m_st)
                        for uu in range(nu):
                            d = DS(nc, es, "cc_k%d_%d" % (layer, dbase + uu))
                            ins = nc.gpsimd.collective_compute("AllGather", ALU.bypass, replica_groups=[[0, 1, 2, 3], [4, 5, 6, 7]],
                                                               ins=[kt_loc[layer][dbase + uu].opt()], outs=[kt_all[layer][dbase + uu].opt()])
                            ins.then_inc(d.sem, 1)
                            d.n = 1
                            cc_k[dbase + uu] = d
                barrier()
            return cc_k, cc_v

        def attention(layer):
            dvp = 65 if layer == 0 else 129
            dv = dvp - 1
            nbank = 2 if layer == 0 else 3
            per_bank = 4 if layer == 0 else 3
            with ExitStack() as st:
                psS = psum("psS%d" % layer, [128, 2, 2, 512], F32, st)
                psA = psum("psA%d" % layer, [128, nbank, 512], F32, st)
                psT = psum("psT%d" % layer, [128, 1024], BF16, st)
                QT = [sb("QT%d_%d" % (layer, s), [128, T], BF16, st) for s in range(2)]
                dsQ = [newds("dsQ%d_%d" % (layer, s)) for s in range(2)]
                PP = sb("PP%d" % layer, [128, 3, 2, 512], BF16, st)
                ACCS = sb("ACCS%d" % layer, [128, 8 * dvp], F32, st)
                RDEN = sb("RDEN%d" % layer, [128, 8], F32, st)
                MT = sb("MT%d" % layer, [128, 512], BF16, st)
                if layer == 1:
                    O1 = sb("O1", [128, 512], F32, st)
                    O2 = sb("O2", [128, 512], F32, st)
                    SS4 = sb("SS4", [128, 4], F32, st)
                    RS4 = sb("RS4", [128, 4], F32, st)
                if layer == 0:
                    KTA = sb("KTA", [128, 4, T], BF16, st)
                    VA = sb("VA", [128, 4, 16, 65], BF16, st)
                    dsKA = newds("dsKA")
                    KTB = [sb("KTB%d" % s, [128, NM * 128], BF16, st) for s in range(2)]
                    VB = [sb("VB%d" % s, [128, 2, NM, 65], BF16, st) for s in range(2)]
                    dsKB = [newds("dsKB%d" % s) for s in range(2)]
                    MASK = sb("MASK", [128, NM, 512], BF16, st)
                    dsM = newds("dsM")
                else:
                    KT = [sb("KT1_%d" % s, [128, 4, T], BF16, st) for s in range(2)]
                    V1 = [sb("V1_%d" % s, [128, 4, 16, 129], BF16, st) for s in range(2)]
                    dsKV = [newds("dsKV%d" % s) for s in range(2)]

                state = dict(m_s_free=[None, None], m_p_free=[None] * 3,
                             m_acc_free=None, m_accs_free=None, m_mt_free=None, m_psT_free=None, kbc=0)
                m_q_free = [None, None]

                def acc_ap(n):
                    bk, o = n // per_bank, (n % per_bank) * dvp
                    return psA[:, bk, o:o + dvp], bk

                def qgroup(qtile, g, chunk, blocks, m_load, m_buf_free_cb):
                    nb = len(blocks)
                    started = set()
                    m_exp = {}
                    m_s = {}
                    last_pv = None

                    def emit_s(kb):
                        la, lb, _, _, _, _ = blocks[kb]
                        sl = state["kbc_base"] + kb
                        s = sl % 2
                        pe.wait(m_load, state["m_s_free"][s])
                        nc.tensor.matmul(psS[:, s, 0, :], lhsT=la, rhs=qtile[0:64, g * 512:(g + 1) * 512], start=True, stop=True)
                        ins = nc.tensor.matmul(psS[:, s, 1, :], lhsT=lb, rhs=qtile[64:128, g * 512:(g + 1) * 512], start=True, stop=True)
                        m_s[kb] = pe.mark(ins)

                    state["kbc_base"] = state["kbc"]
                    emit_s(0)
                    if nb > 1:
                        emit_s(1)
                    for kb in range(nb):
                        _, _, va, vb, mi, jl = blocks[kb]
                        sl = state["kbc_base"] + kb
                        s = sl % 2
                        ps3 = sl % 3
                        act.wait(m_s[kb], state["m_p_free"][ps3])
                        me = act.mark(nc.scalar.activation(out=PP[:, ps3, :, :], in_=psS[:, s, :, :], func=AF.Exp, scale=0.125))
                        state["m_s_free"][s] = me
                        if mi is not None:
                            dve.wait(me)
                            me = dve.mark(nc.vector.tensor_tensor(out=PP[:, ps3, :, :], in0=PP[:, ps3, :, :],
                                                                  in1=MASK[:, mi, :].unsqueeze(1).to_broadcast([128, 2, 512]), op=ALU.mult))
                        if kb + 2 < nb:
                            emit_s(kb + 2)
                        pe.wait(me, state["m_acc_free"])
                        ins = None
                        for mp in range(2):
                            vv = va if mp == 0 else vb
                            for j in jl:
                                o_ap, bk = acc_ap(mp * 4 + j)
                                first = bk not in started
                                started.add(bk)
                                ins = nc.tensor.matmul(o_ap, lhsT=PP[:, ps3, mp, j * 128:(j + 1) * 128], rhs=vv,
                                                       start=first, stop=(kb == nb - 1), skip_group_check=True)
                        m_pv = pe.mark(ins)
                        state["m_p_free"][ps3] = m_pv
                        last_pv = m_pv
                    state["kbc"] = state["kbc_base"] + nb
                    m_buf_free_cb(last_pv)
                    dve.wait(last_pv, state["m_accs_free"])
                    m = None
                    for bk in range(nbank):
                        ncols = min(per_bank, 8 - bk * per_bank) * dvp
                        m = dve.mark(nc.vector.tensor_copy(out=ACCS[:, bk * per_bank * dvp: bk * per_bank * dvp + ncols], in_=psA[:, bk, 0:ncols]))
                    state["m_acc_free"] = m
                    dve.wait(m)
                    a3 = ACCS[:].rearrange("p (n d) -> p n d", d=dvp)
                    m = dve.mark(nc.vector.reciprocal(out=RDEN[:], in_=a3[:, :, dv]))
                    dve.wait(m, state["m_mt_free"])
                    if layer == 0:
                        mt3 = MT[:].rearrange("p (j f) -> p j f", f=128)
                        for mp in range(2):
                            m = dve.mark(nc.vector.tensor_tensor(out=mt3[:, :, mp * 64:(mp + 1) * 64], in0=a3[:, mp * 4:(mp + 1) * 4, 0:64],
                                                                 in1=RDEN[:, mp * 4:(mp + 1) * 4].unsqueeze(2).to_broadcast([128, 4, 64]), op=ALU.mult))
                        state["m_accs_free"] = m
                        m_mt = m
                    else:
                        o13 = O1[:].rearrange("p (j f) -> p j f", f=128)
                        o23 = O2[:].rearrange("p (j f) -> p j f", f=128)
                        m = dve.mark(nc.vector.tensor_scalar(out=RDEN[:, 4:8], in0=RDEN[:, 4:8], scalar1=NEGLAM[:, 0:1], scalar2=None, op0=ALU.mult))
                        dve.wait(m)
                        nc.vector.tensor_tensor(out=o13, in0=a3[:, 0:4, 0:128], in1=RDEN[:, 0:4].unsqueeze(2).to_broadcast([128, 4, 128]), op=ALU.mult)
                        m = dve.mark(nc.vector.tensor_tensor(out=o23, in0=a3[:, 4:8, 0:128], in1=RDEN[:, 4:8].unsqueeze(2).to_broadcast([128, 4, 128]), op=ALU.mult))
                        state["m_accs_free"] = m
                        dve.wait(m)
                        m = dve.mark(nc.vector.tensor_tensor(out=O1[:], in0=O1[:], in1=O2[:], op=ALU.add))
                        dve.wait(m)
                        m = dve.mark(nc.vector.tensor_tensor(out=O2[:], in0=O1[:], in1=O1[:], op=ALU.mult))
                        dve.wait(m)
                        m = dve.mark(nc.vector.tensor_reduce(out=SS4[:], in_=o23, axis=AX.X, op=ALU.add))
                        act.wait(m)
                        m = act.mark(nc.scalar.activation(out=RS4[:], in_=SS4[:], func=AF.Ln, bias=EPST[:, 0:1], scale=1.0 / 128))
                        act.wait(m)
                        m = act.mark(nc.scalar.activation(out=RS4[:], in_=RS4[:], func=AF.Exp, scale=-0.5))
                        dve.wait(m)
                        m = dve.mark(nc.vector.tensor_tensor(out=o13, in0=o13, in1=RS4[:].unsqueeze(2).to_broadcast([128, 4, 128]), op=ALU.mult))
                        dve.wait(m)
                        m_mt = dve.mark(nc.vector.tensor_tensor(out=MT[:].rearrange("p (j f) -> p j f", f=128), in0=o13,
                                                                in1=GAINC[:].unsqueeze(1).to_broadcast([128, 4, 128]), op=ALU.mult))
                    pe.wait(m_mt, state["m_psT_free"])
                    for j in range(4):
                        ins = nc.tensor.transpose(psT[:, j * 128:(j + 1) * 128], MT[:, j * 128:(j + 1) * 128], IDENT[:])
                    m_t = pe.mark(ins)
                    state["m_mt_free"] = m_t
                    dve.wait(m_t)
                    state["m_psT_free"] = dve.mark(nc.vector.tensor_copy(out=AT[:, chunk, g * 512:(g + 1) * 512], in_=psT[:, 0:512]))

                def load_q(u, slot):
                    sp.wait(m_q_free[slot])
                    return dma(sp, dsQ[slot], QT[slot][:], qt[layer][u * 128:(u + 1) * 128, :])

                if layer == 0:
                    cc_k, cc_v = cc[0]
                    kA = kt_all[0][0].rearrange("(r p) t -> p r t", p=128)
                    vA = v_all[0][0].rearrange("(r h p) f -> h p r f", r=4, h=2)
                    ds_rl = newds("ds_rl")
                    ds_win = newds("ds_win")
                    pool.wait(ds_pad.all())
                    for u in range(4):
                        pool.wait(cc_k[1 + u].all())
                        dma(pool, ds_rl, ktB_pad[u * 128:(u + 1) * 128, 1024:1024 + 4 * T].rearrange("p (r t) -> p r t", r=4),
                            kt_all[0][1 + u].rearrange("(r p) t -> p r t", p=128))
                    for gi in range(4):
                        pool.wait(cc_v[1 + gi].all())
                        srcv = v_all[0][1 + gi].rearrange("(r h p) f -> h p r f", r=4, h=2)
                        for hh in range(2):
                            hB = 2 * gi + hh
                            dma(pool, ds_rl, vB_pad[hB * 128:(hB + 1) * 128, 8 * 65:72 * 65].rearrange("p (r f) -> p r f", r=4), srcv[hh])
                    pid = nc.gpsimd.partition_id()
                    rank = pid % 4
                    pool.wait(ds_rl.all())
                    dma(pool, ds_win, ktB_win, ktB_pad[:, bass.ds(rank * T, 4096)])
                    dma(pool, ds_win, vB_win, vB_pad[:, bass.ds(rank * (16 * 65), 32 * 65)])
                    m_ka_free = None
                    qslot = 0
                    mq = load_q(0, 0)
                    for u in range(4):
                        if u == 1:
                            pool.wait(m_ka_free)
                            casts_layer0_rest()
                            dma(sp, dsM, MASK[:], masks_in.rearrange("m p q -> p m q"))
                        kvh = u // 2
                        if u % 2 == 0:
                            sp.wait(m_ka_free, cc_k[0].all(), cc_v[0].all())
                            for half in range(2):
                                dma(sp, dsKA, KTA[half * 64:(half + 1) * 64, :, :], kA[kvh * 64:(kvh + 1) * 64, :, :])
                            dma(sp, dsKA, VA[:].rearrange("p r i d -> p r (i d)"), vA[kvh])
                        mq_next = load_q(u + 1, (qslot + 1) % 2) if u < 3 else None
                        lastholder = []
                        for g in range(4):
                            blocks = []
                            for r in range(4):
                                for i in range(16):
                                    blocks.append((KTA[0:64, r, i * 128:(i + 1) * 128], KTA[64:128, r, i * 128:(i + 1) * 128],
                                                   VA[:, r, i, :], VA[:, r, i, :], None, [0, 1, 2, 3]))
                            qgroup(QT[qslot], g, u, blocks, [mq, dsKA.all()], lambda m: lastholder.append(m))
                        m_q_free[qslot] = lastholder[-1]
                        m_ka_free = lastholder[-1]
                        qslot = (qslot + 1) % 2
                        mq = mq_next
                    m_kb_free = [None, None]
                    widx = 0
                    mq = load_q(4, qslot)
                    for u in range(4):
                        mq_next = None
                        lastholder = []
                        for g in range(4):
                            ws = widx % 2
                            widx += 1
                            sp.wait(m_kb_free[ws], ds_win.all())
                            dma(sp, dsKB[ws], KTB[ws][:], ktB_win[u * 128:(u + 1) * 128, g * 512:g * 512 + NM * 128])
                            for hh in range(2):
                                h = 2 * u + hh
                                dma(sp, dsKB[ws], VB[ws][:, hh, :, :].rearrange("p m d -> p (m d)"),
                                    vB_win[h * 128:(h + 1) * 128, g * 4 * 65:(g * 4 + NM) * 65])
                            if g == 0 and u < 3:
                                mq_next = load_q(4 + u + 1, (qslot + 1) % 2)
                            blocks = []
                            for mi in range(NM):
                                jl = [j for j in range(4) if j <= mi <= j + 16]
                                blocks.append((KTB[ws][0:64, mi * 128:(mi + 1) * 128], KTB[ws][64:128, mi * 128:(mi + 1) * 128],
                                               VB[ws][:, 0, mi, :], VB[ws][:, 1, mi, :], mi, jl))

                            def cb(m, ws=ws):
                                m_kb_free[ws] = m
                                lastholder.append(m)
                            qgroup(QT[qslot], g, 4 + u, blocks, [mq, dsKB[ws].all(), dsM.all()], cb)
                        m_q_free[qslot] = lastholder[-1]
                        qslot = (qslot + 1) % 2
                        mq = mq_next
                else:
                    cc_k, cc_v = cc[1]
                    m_kv_free = [None, None]

                    def load_kv(h, slot):
                        sp.wait(m_kv_free[slot], cc_k[h].all(), cc_v[h].all())
                        dma(sp, dsKV[slot], KT[slot][:], kt_all[1][h].rearrange("(r p) t -> p r t", p=128))
                        dma(sp, dsKV[slot], V1[slot][:].rearrange("p r i d -> p r (i d)"), v_all[1][h].rearrange("(r p) f -> p r f", p=128))

                    mq = load_q(0, 0)
                    load_kv(0, 0)
                    for h in range(8):
                        slot = h % 2
                        mq_next = None
                        if h == 1:
                            pool.wait(m_kv_free[0])
                            casts_layer1_rest()
                        if h < 7:
                            mq_next = load_q(h + 1, 1 - slot)
                            load_kv(h + 1, 1 - slot)
                        lastholder = []
                        for g in range(4):
                            blocks = []
                            for r in range(4):
                                for i in range(16):
                                    blocks.append((KT[slot][0:64, r, i * 128:(i + 1) * 128], KT[slot][64:128, r, i * 128:(i + 1) * 128],
                                                   V1[slot][:, r, i, :], V1[slot][:, r, i, :], None, [0, 1, 2, 3]))
                            qgroup(QT[slot], g, h, blocks, [mq, dsKV[slot].all()], lambda m: lastholder.append(m))
                        m_q_free[slot] = lastholder[-1]
                        m_kv_free[slot] = lastholder[-1]
                        mq = mq_next
                barrier()

        def outproj(layer):
            with ExitStack() as st:
                WO = sb("WO%d" % layer, [128, 8, D], BF16, st)
                dsO = newds("dsO%d" % layer)
                psO = psum("psO%d" % layer, [128, 2, 512], F32, st)
                sp.wait(ds_w["out%d" % layer].all())
                dma(sp, dsO, WO[:], wb_out[layer].rearrange("(k p) c -> p k c", p=128))
                m_free = [None, None]
                cnt = 0
                for i in range(NT):
                    for c in range(2):
                        b = cnt % 2
                        cnt += 1
                        pe.wait(dsO.all(), m_free[b])
                        for k in range(8):
                            ins = nc.tensor.matmul(psO[:, b, :], lhsT=AT[:, k, i * 128:(i + 1) * 128], rhs=WO[:, k, c * 512:(c + 1) * 512],
                                                   start=(k == 0), stop=(k == 7))
                        mm = pe.mark(ins)
                        dve.wait(mm)
                        m_free[b] = dve.mark(nc.vector.tensor_tensor(out=X[:, i, c * 512:(c + 1) * 512], in0=X[:, i, c * 512:(c + 1) * 512],
                                                                     in1=psO[:, b, :], op=ALU.add))
                barrier()

        def ffn(layer):
            with ExitStack() as st:
                WD = sb("WD%d" % layer, [128, NF, D], BF16, st)
                dsD = newds("dsD%d" % layer)
                WGU = [sb("WGU%d_%d" % (layer, s), [128, 2, 8, 256], BF16, st) for s in range(2)]
                dsGU = [newds("dsGU%d_%d" % (layer, s)) for s in range(2)]
                AFT = sb("AFT%d" % layer, [128, NF, 512], BF16, st)
                SG = [sb("SG%d_%d" % (layer, s), [128, 512], F32, st) for s in range(2)]
                psG = psum("psG%d" % layer, [128, 2, 512], F32, st)
                psU = psum("psU%d" % layer, [128, 2, 512], F32, st)
                psD = psum("psD%d" % layer, [128, 2, 512], F32, st)
                sp.wait(ds_w["d%d" % layer].all(), ds_w["g%d" % layer].all(), ds_w["u%d" % layer].all())
                for half in range(2):
                    dma(sp, dsD, WD[:, half * 11:(half + 1) * 11, :],
                        wb_d[layer][half * 11 * 128:(half + 1) * 11 * 128, :].rearrange("(f p) c -> p f c", p=128))
                m_gu_free = [None, None]
                m_g_free = [None, None]
                m_u_free = [None, None]
                m_sg_free = [None, None]
                m_d_free = [None, None]
                m_down_last = None
                lc = 0
                fc_cnt = 0
                dcnt = 0

                def load_gu(fg, slot):
                    sp.wait(m_gu_free[slot])
                    dma(sp, dsGU[slot], WGU[slot][:, 0, :, :], wb_g[layer][:, fg * 256:(fg + 1) * 256].rearrange("(k p) c -> p k c", p=128))
                    dma(sp, dsGU[slot], WGU[slot][:, 1, :, :], wb_u[layer][:, fg * 256:(fg + 1) * 256].rearrange("(k p) c -> p k c", p=128))

                seq = [(tg, fg) for tg in range(4) for fg in range(11)]
                load_gu(seq[0][1], 0)
                for si, (tg, fg) in enumerate(seq):
                    slot = si % 2
                    if si + 1 < len(seq):
                        load_gu(seq[si + 1][1], 1 - slot)
                    for fc in range(2):
                        f = fg * 2 + fc
                        b = fc_cnt % 2
                        fc_cnt += 1
                        pe.wait(dsGU[slot].all(), m_g_free[b], m_u_free[b])
                        for k in range(8):
                            nc.tensor.matmul(psG[:, b, :], lhsT=WGU[slot][:, 0, k, fc * 128:(fc + 1) * 128], rhs=AT[:, k, tg * 512:(tg + 1) * 512],
                                             start=(k == 0), stop=(k == 7))
                        for k in range(8):
                            ins = nc.tensor.matmul(psU[:, b, :], lhsT=WGU[slot][:, 1, k, fc * 128:(fc + 1) * 128], rhs=AT[:, k, tg * 512:(tg + 1) * 512],
                                                   start=(k == 0), stop=(k == 7))
                        mm = pe.mark(ins)
                        m_gu_free[slot] = mm
                        act.wait(mm, m_sg_free[b])
                        ms = act.mark(nc.scalar.activation(out=SG[b][:], in_=psG[:, b, :], func=AF.Silu))
                        m_g_free[b] = ms
                        dve.wait(ms, m_down_last)
                        ma = dve.mark(nc.vector.tensor_tensor(out=AFT[:, f, :], in0=SG[b][:], in1=psU[:, b, :], op=ALU.mult))
                        m_u_free[b] = ma
                        m_sg_free[b] = ma
                        last_a = ma
                    if fg == 10:
                        for j in range(4):
                            for c in range(2):
                                b = dcnt % 2
                                dcnt += 1
                                pe.wait(last_a, dsD.all(), m_d_free[b])
                                for f in range(NF):
                                    ins = nc.tensor.matmul(psD[:, b, :], lhsT=AFT[:, f, j * 128:(j + 1) * 128], rhs=WD[:, f, c * 512:(c + 1) * 512],
                                                           start=(f == 0), stop=(f == NF - 1))
                                mm = pe.mark(ins)
                                m_down_last = mm
                                i = tg * 4 + j
                                dve.wait(mm)
                                m_d_free[b] = dve.mark(nc.vector.tensor_tensor(out=X[:, i, c * 512:(c + 1) * 512], in0=X[:, i, c * 512:(c + 1) * 512],
                                                                               in1=psD[:, b, :], op=ALU.add))
                barrier()

        def final():
            with ExitStack() as st:
                G, RSTD = rms_rstd(st, lambda i: X[:, i, :], NT, final_norm[0:1, :], "fin")
                OB = [sb("OB%d" % s, [128, D], F32, st) for s in range(2)]
                ds_o = newds("ds_out")
                m_free = [None, None]
                for i in range(NT):
                    s = i % 2
                    dve.wait(m_free[s])
                    m = dve.mark(nc.vector.scalar_tensor_tensor(out=OB[s][:], in0=X[:, i, :], scalar=RSTD[:, i:i + 1], in1=G[:],
                                                                op0=ALU.mult, op1=ALU.mult))
                    sp.wait(m)
                    m_free[s] = dma(sp, ds_o, out[i * 128:(i + 1) * 128, :], OB[s][:])
                sp.wait(ds_o.all())
                pool.wait(ds_o.all())

        cc = {}
        dve.wait(ds_x.all(), ds_c.all())
        act.wait(ds_x.all(), ds_c.all())
        pe.wait(ds_c.all())
        pool.wait(ds_c.all())

        def dump(layer):
            barrier()
            dsd = newds("ds_dbg")
            for i in range(NT):
                dma(sp, dsd, dbg["X"][i * 128:(i + 1) * 128, :], X[:, i, :])
            dma(sp, dsd, dbg["AT"], AT[:].rearrange("p k t -> p (k t)"))
            dma(sp, dsd, dbg["qt"], qt[layer])
            for u in range(NKU[layer]):
                dma(sp, dsd, dbg["kt"][u * 512:(u + 1) * 512, :], kt_all[layer][u])
            for g in range(NVG[layer]):
                nr = 4 * VGH[layer] * 128
                dma(sp, dsd, dbg["v"][g * nr:(g + 1) * nr, 0:VF[layer]], v_all[layer][g])
            sp.wait(dsd.all())
            pool.wait(dsd.all())

        steps = [
            ("norm_a0", lambda: norm_AT(attn_norm[0:1, :], "a0")),
            ("inproj0", lambda: cc.__setitem__(0, inproj(0))),
            ("attn0", lambda: attention(0)),
            ("outproj0", lambda: outproj(0)),
            ("norm_f0", lambda: norm_AT(ffn_norm[0:1, :], "f0")),
            ("ffn0", lambda: ffn(0)),
            ("norm_a1", lambda: norm_AT(attn_norm[1:2, :], "a1")),
            ("inproj1", lambda: cc.__setitem__(1, inproj(1))),
            ("attn1", lambda: attention(1)),
            ("outproj1", lambda: outproj(1)),
            ("norm_f1", lambda: norm_AT(ffn_norm[1:2, :], "f1")),
            ("ffn1", lambda: ffn(1)),
            ("final", final),
        ]
        for name, fn in steps:
            fn()
            if debug and debug == name:
                dump(0 if name in ("norm_a0", "inproj0", "attn0", "outproj0", "norm_f0", "ffn0", "norm_a1") else 1)
                break
    return nc


def _const_tables():
    theta = 10000.0
    pos = np.arange(S, dtype=np.float32)
    inv64 = (theta ** (-np.arange(0, 64, 2, dtype=np.float32) / 64)).astype(np.float32)
    inv32 = (theta ** (-np.arange(0, 32, 2, dtype=np.float32) / 32)).astype(np.float32)
    ang1 = (pos[:, None] * inv64[None, :]).astype(np.float32)
    rows = (np.arange(S) // 64).astype(np.float32)
    cols = (np.arange(S) % 64).astype(np.float32)
    ang2 = np.concatenate([rows[:, None] * inv32[None, :], cols[:, None] * inv32[None, :]], axis=-1).astype(np.float32)
    rope = np.stack([np.cos(ang1.astype(np.float64)), np.sin(ang1.astype(np.float64)),
                     np.cos(ang2.astype(np.float64)), np.sin(ang2.astype(np.float64))], 0).astype(np.float32)
    k = np.arange(128)[:, None]
    q = np.arange(512)[None, :]
    masks = np.zeros((NM, 128, 512), np.float32)
    for m in range(NM):
        d = 128 * m + k - q - 1024
        ad = np.abs(d)
        masks[m] = (ad <= 64).astype(np.float32) + ((d % 4 == 0) & (ad <= 256)) + ((d % 16 == 0) & (ad <= 1024))
    ident = np.eye(128, dtype=np.float32)
    return rope, masks.astype(ml_dtypes.bfloat16), ident.astype(ml_dtypes.bfloat16)


_NC_CACHE = {}


def kernel(x, attn_norm, ffn_norm, final_norm, w_in_even, a_q_norm, a_k_norm, w_out_even,
           w_in_odd, lambda_q1, lambda_k1, lambda_q2, lambda_k2, c_sub_norm, w_out_odd,
           w_gate, w_up, w_down, _debug=False):
    f = lambda a: np.ascontiguousarray(np.asarray(a, dtype=np.float32))
    x = f(x)
    rope, masks, ident = _const_tables()
    shared = {
        "attn_norm": f(attn_norm), "ffn_norm": f(ffn_norm), "final_norm": f(final_norm).reshape(1, D),
        "w_in_even": f(w_in_even)[0], "a_q_norm": f(a_q_norm), "a_k_norm": f(a_k_norm),
        "w_out_even": f(w_out_even)[0], "w_in_odd": f(w_in_odd)[0],
        "lambda_q1": f(lambda_q1), "lambda_k1": f(lambda_k1), "lambda_q2": f(lambda_q2), "lambda_k2": f(lambda_k2),
        "c_sub_norm": f(c_sub_norm), "w_out_odd": f(w_out_odd)[0],
        "w_gate": f(w_gate), "w_up": f(w_up), "w_down": f(w_down),
        "masks": masks, "ident": ident,
    }
    in_maps = []
    for c in range(NCORE):
        b, r = c // 4, c % 4
        m = dict(shared)
        m["x"] = np.ascontiguousarray(x[b, r * T:(r + 1) * T, :])
        m["rope"] = np.ascontiguousarray(rope[:, r * T:(r + 1) * T, :])
        in_maps.append(m)
    key = _debug
    if key not in _NC_CACHE:
        _NC_CACHE[key] = build(debug=key)
    nc = _NC_CACHE[key]
    res = run_bass_kernel_spmd(nc, in_maps, core_ids=list(range(NCORE)))
    outp = np.zeros((2, S, D), np.float32)
    for c in range(NCORE):
        b, r = c // 4, c % 4
        outp[b, r * T:(r + 1) * T, :] = np.asarray(res.results[c]["out"])
    if _debug:
        return outp, res
    return outp
```

```python
import math
from contextlib import ExitStack

import numpy as np
import ml_dtypes

import concourse.bass as bass
import concourse.mybir as mybir
from concourse.bass_utils import run_bass_kernel_spmd

F32 = mybir.dt.float32
BF16 = mybir.dt.bfloat16
ALU = mybir.AluOpType
AF = mybir.ActivationFunctionType
AX = mybir.AxisListType

NCORE = 8
S = 8192
D = 1024
T = 2048
NT = 16
DFF = 2816
NF = 22
EPS = 1e-6
LAM_INIT = 0.8 - 0.6 * math.exp(-0.3 * 1)
KPAD = 1024 + S + 1536
VBLK = 8 + 64 + 12
NM = 20


class TL:
    def __init__(self, nc, es, eng, name):
        self.e = eng
        self.sem = es.enter_context(nc.semaphore(name))
        self.n = 0
        self.seen = {}

    def wait(self, *ms):
        for m in ms:
            if m is None:
                continue
            if isinstance(m, list):
                self.wait(*m)
                continue
            sem, v = m
            if sem is self.sem and False:
                continue
            k = id(sem)
            if self.seen.get(k, 0) >= v:
                continue
            self.e.wait_ge(sem, v)
            self.seen[k] = v

    def mark(self, ins):
        self.n += 1
        ins.then_inc(self.sem, 1)
        return (self.sem, self.n)


class DS:
    def __init__(self, nc, es, name):
        self.sem = es.enter_context(nc.semaphore(name))
        self.n = 0

    def add(self, ins):
        self.n += 16
        ins.then_inc(self.sem, 16)
        return (self.sem, self.n)

    def all(self):
        return (self.sem, self.n) if self.n else None


def build(debug=False):
    nc = bass.Bass("TRN2", target_bir_lowering=False)

    def din(name, shape, dt=F32):
        return nc.dram_tensor(name, list(shape), dt, kind="ExternalInput").ap()

    def dint(name, shape, dt=BF16):
        return nc.dram_tensor(name, list(shape), dt, kind="Internal").ap()

    x_in = din("x", [T, D])
    attn_norm = din("attn_norm", [2, D])
    ffn_norm = din("ffn_norm", [2, D])
    final_norm = din("final_norm", [1, D])
    w_in_even = din("w_in_even", [D, 2304])
    a_q_norm = din("a_q_norm", [1, 64])
    a_k_norm = din("a_k_norm", [1, 64])
    w_out_even = din("w_out_even", [D, D])
    w_in_odd = din("w_in_odd", [D, 3072])
    lam_in = [din(n, [1, 64]) for n in ("lambda_q1", "lambda_k1", "lambda_q2", "lambda_k2")]
    c_sub_norm = din("c_sub_norm", [1, 128])
    w_out_odd = din("w_out_odd", [D, D])
    w_gate = din("w_gate", [2, D, DFF])
    w_up = din("w_up", [2, D, DFF])
    w_down = din("w_down", [2, DFF, D])
    rope_in = din("rope", [4, T, 32])
    masks_in = din("masks", [NM, 128, 512], BF16)
    ident_in = din("ident", [128, 128], BF16)
    out = nc.dram_tensor("out", [T, D], F32, kind="ExternalOutput").ap()

    wb_in = [dint("wb_in0", [D, 2304]), dint("wb_in1", [D, 3072])]
    wb_out = [dint("wb_out0", [D, D]), dint("wb_out1", [D, D])]
    wb_g = [dint(f"wb_g{l}", [D, DFF]) for l in range(2)]
    wb_u = [dint(f"wb_u{l}", [D, DFF]) for l in range(2)]
    wb_d = [dint(f"wb_d{l}", [DFF, D]) for l in range(2)]

    qt = [dint("qt0", [8 * 128, T]), dint("qt1", [8 * 128, T])]
    NKU = [5, 8]
    VGH = [2, 1]
    NVG = [5, 8]
    VF = [16 * 65, 16 * 129]
    kt_loc = [[dint("kt%d_loc%d" % (l, u), [128, T]) for u in range(NKU[l])] for l in range(2)]
    kt_all = [[dint("kt%d_all%d" % (l, u), [4 * 128, T]) for u in range(NKU[l])] for l in range(2)]
    v_loc = [[dint("v%d_loc%d" % (l, g), [VGH[l] * 128, VF[l]]) for g in range(NVG[l])] for l in range(2)]
    v_all = [[dint("v%d_all%d" % (l, g), [4 * VGH[l] * 128, VF[l]]) for g in range(NVG[l])] for l in range(2)]
    ktB_pad = dint("ktB_pad", [4 * 128, KPAD])
    vB_pad = dint("vB_pad", [8 * 128, VBLK * 65])
    ktB_win = dint("ktB_win", [4 * 128, 4096])
    vB_win = dint("vB_win", [8 * 128, 32 * 65])

    dbg = {}
    if debug:
        dbg["X"] = nc.dram_tensor("dbg_X", [T, D], F32, kind="ExternalOutput").ap()
        dbg["AT"] = nc.dram_tensor("dbg_AT", [128, 8 * T], BF16, kind="ExternalOutput").ap()
        dbg["qt"] = nc.dram_tensor("dbg_qt", [8 * 128, T], BF16, kind="ExternalOutput").ap()
        dbg["kt"] = nc.dram_tensor("dbg_kt", [8 * 512, T], BF16, kind="ExternalOutput").ap()
        dbg["v"] = nc.dram_tensor("dbg_v", [8 * 1024, 16 * 129], BF16, kind="ExternalOutput").ap()

    es = ExitStack()
    with es:
        pe = TL(nc, es, nc.tensor, "t_pe")
        act = TL(nc, es, nc.scalar, "t_act")
        dve = TL(nc, es, nc.vector, "t_dve")
        pool = TL(nc, es, nc.gpsimd, "t_pool")
        sp = TL(nc, es, nc.sync, "t_sp")
        engines = [pe, act, dve, pool, sp]

        def sb(name, shape, dt, stack=None):
            return (stack or es).enter_context(nc.sbuf_tensor(name, list(shape), dt))

        def psum(name, shape, dt, stack):
            return stack.enter_context(nc.psum_tensor(name, list(shape), dt))

        def dma(q, ds, out_ap, in_ap):
            return ds.add(q.e.dma_start(out=out_ap, in_=in_ap))

        X = sb("X", [128, NT, D], F32)
        AT = sb("AT", [128, 8, T], BF16)
        ROPE = sb("ROPE", [128, 4, NT, 32], F32)
        IDENT = sb("IDENT", [128, 128], BF16)
        EPST = sb("EPST", [128, 1], F32)
        FENCE = sb("FENCE", [128, 4], F32)
        NEGLAM = sb("NEGLAM", [128, 1], F32)
        GAINC = sb("GAINC", [128, 128], F32)
        GQK = sb("GQK", [128, 2, 64], F32)
        ZERO = sb("ZERO", [128, 2048], BF16)

        fence_ps_stack = ExitStack()

        def barrier():
            ms = []
            ms.append(dve.mark(nc.vector.memset(FENCE[:, 0:1], 0.0)))
            ms.append(pool.mark(nc.gpsimd.memset(FENCE[:, 1:2], 0.0)))
            ms.append(act.mark(nc.scalar.activation(out=FENCE[:, 2:3], in_=EPST[:, 0:1], func=AF.Copy)))
            ms.append((pe.sem, pe.n) if pe.n else None)
            ms += [d.all() for d in all_ds]
            for e in engines:
                e.wait(*ms)

        all_ds = []

        def newds(name, in_barrier=True):
            d = DS(nc, es, name)
            if in_barrier:
                all_ds.append(d)
            return d

        ds_c = newds("ds_c")
        ds_x = newds("ds_x")
        nc.vector.memset(EPST[:], EPS)
        m_zero = pool.mark(nc.gpsimd.memset(ZERO[:], 0.0))
        dma(sp, ds_c, IDENT[:], ident_in)
        dma(sp, ds_c, ROPE[:], rope_in.rearrange("c (i p) d -> p c i d", p=128))
        dma(sp, ds_c, GQK[:, 0, :], a_q_norm.partition_broadcast(128))
        dma(sp, ds_c, GQK[:, 1, :], a_k_norm.partition_broadcast(128))
        dma(sp, ds_c, GAINC[:], c_sub_norm.partition_broadcast(128))
        for i in range(NT):
            dma(sp, ds_x, X[:, i, :], x_in[i * 128:(i + 1) * 128, :])

        ds_w = {}

        def cast_w(key, dst, src, rows):
            d = newds("ds_w_" + key, in_barrier=False)
            ds_w[key] = d
            for r0 in range(0, rows, 128):
                dma(pool, d, dst[r0:r0 + 128, :], src[r0:r0 + 128, :])

        cast_w("in0", wb_in[0], w_in_even, D)
        ds_pad = newds("ds_pad")
        sp.wait(m_zero)
        for u in range(4):
            dma(sp, ds_pad, ktB_pad[u * 128:(u + 1) * 128, 0:1024], ZERO[:, 0:1024])
            dma(sp, ds_pad, ktB_pad[u * 128:(u + 1) * 128, 1024 + S:KPAD], ZERO[:, 0:1536])
        for h in range(8):
            dma(sp, ds_pad, vB_pad[h * 128:(h + 1) * 128, 0:8 * 65], ZERO[:, 0:8 * 65])
            dma(sp, ds_pad, vB_pad[h * 128:(h + 1) * 128, 72 * 65:VBLK * 65], ZERO[:, 0:12 * 65])
        def casts_layer0_rest():
            cast_w("out0", wb_out[0], w_out_even, D)
            cast_w("g0", wb_g[0], w_gate[0], D)
            cast_w("u0", wb_u[0], w_up[0], D)
            cast_w("d0", wb_d[0], w_down[0], DFF)
            cast_w("in1", wb_in[1], w_in_odd, D)
            cast_w("out1", wb_out[1], w_out_odd, D)

        def casts_layer1_rest():
            cast_w("g1", wb_g[1], w_gate[1], D)
            cast_w("u1", wb_u[1], w_up[1], D)
            cast_w("d1", wb_d[1], w_down[1], DFF)

        with ExitStack() as st:
            LQ = sb("LQ", [128, 4, 64], F32, st)
            LS = sb("LS", [128, 4], F32, st)
            LJ = sb("LJ", [128, 64], F32, st)
            ds_l = newds("ds_l")
            for j in range(4):
                dma(sp, ds_l, LQ[:, j, :], lam_in[j].partition_broadcast(128))
            dve.wait(ds_l.all(), ds_c.all())
            nc.vector.memset(LS[:], 0.0)
            m = None
            for j in range(2):
                m = dve.mark(nc.vector.tensor_tensor(out=LJ[:], in0=LQ[:, 2 * j, :], in1=LQ[:, 2 * j + 1, :], op=ALU.mult))
                dve.wait(m)
                m = dve.mark(nc.vector.tensor_reduce(out=LS[:, j:j + 1], in_=LJ[:], axis=AX.X, op=ALU.add))
                dve.wait(m)
            act.wait(m)
            m = act.mark(nc.scalar.activation(out=LS[:, 2:4], in_=LS[:, 0:2], func=AF.Exp))
            dve.wait(m)
            m = dve.mark(nc.vector.tensor_tensor(out=NEGLAM[:], in0=LS[:, 3:4], in1=LS[:, 2:3], op=ALU.subtract))
            dve.wait(m)
            m = dve.mark(nc.vector.tensor_scalar(out=NEGLAM[:], in0=NEGLAM[:], scalar1=-LAM_INIT, scalar2=None, op0=ALU.add))
            dve.wait(m)
            dve.mark(nc.vector.tensor_scalar(out=GAINC[:], in0=GAINC[:], scalar1=1.0 - LAM_INIT, scalar2=None, op0=ALU.mult))
            barrier()

        def rms_rstd(st, src_of_tile, nt, gdram_row, tag):
            G = sb("G" + tag, [128, D], F32, st)
            JUNK = sb("JUNK" + tag, [128, D], BF16, st)
            SS = sb("SS" + tag, [128, nt], F32, st)
            RSTD = sb("RSTD" + tag, [128, nt], F32, st)
            dsg = newds("ds_g" + tag)
            dma(sp, dsg, G[:], gdram_row.partition_broadcast(128))
            m0 = dve.mark(nc.vector.memset(SS[:], 0.0))
            act.wait(m0)
            m = None
            for i in range(nt):
                m = act.mark(nc.scalar.activation(out=JUNK[:], in_=src_of_tile(i), func=AF.Square,
                                                  accum_out=SS[:, i:i + 1]))
            act.wait(m)
            m = act.mark(nc.scalar.activation(out=RSTD[:], in_=SS[:], func=AF.Sqrt, bias=EPST[:, 0:1], scale=1.0 / D))
            dve.wait(m, dsg.all())
            m = dve.mark(nc.vector.reciprocal(out=RSTD[:], in_=RSTD[:]))
            dve.wait(m)
            return G, RSTD

        def norm_AT(gdram_row, tag):
            with ExitStack() as st:
                G, RSTD = rms_rstd(st, lambda i: X[:, i, :], NT, gdram_row, tag)
                HB = sb("HB" + tag, [128, 2, D], BF16, st)
                tp = psum("tp" + tag, [128, 2, D], BF16, st)
                m_tr = [None, None]
                m_ev = [None, None]
                for i in range(NT):
                    s = i % 2
                    dve.wait(m_tr[s])
                    mh = dve.mark(nc.vector.scalar_tensor_tensor(out=HB[:, s, :], in0=X[:, i, :], scalar=RSTD[:, i:i + 1],
                                                                 in1=G[:], op0=ALU.mult, op1=ALU.mult))
                    pe.wait(mh, m_ev[s])
                    for k in range(8):
                        ins = nc.tensor.transpose(tp[:, s, k * 128:(k + 1) * 128], HB[:, s, k * 128:(k + 1) * 128], IDENT[:])
                    m_tr[s] = pe.mark(ins)
                    act.wait(m_tr[s])
                    m_ev[s] = act.mark(nc.scalar.copy(out=AT[:, :, i * 128:(i + 1) * 128],
                                                      in_=tp[:, s, :].rearrange("p (k c) -> p k c", c=128)))
                barrier()

        def inproj(layer):
            wsrc = wb_in[layer]
            dsw = ds_w["in%d" % layer]
            if layer == 0:
                groups = [("kn", 512, 128, 0), ("v", 640, 128, 0), ("k", 1280, 512, 1), ("v", 1792, 512, 2),
                          ("qn", 0, 512, 0), ("q", 768, 512, 4)]
                dv = 64
            else:
                groups = [("k", 1024, 512, 0), ("v", 2048, 512, 0), ("k", 1536, 512, 4), ("v", 2560, 512, 4),
                          ("q", 0, 512, 0), ("q", 512, 512, 4)]
                dv = 128
            ds_kt = newds("ds_kt%d" % layer)
            ds_v = newds("ds_v%d" % layer)
            ds_q = newds("ds_q%d" % layer)
            cc_k = {}
            cc_v = {}
            with ExitStack() as st:
                WG = [sb("WG%d_%d" % (layer, s), [128, 8, 512], BF16, st) for s in range(2)]
                dsW = [newds("ds_WG%d_%d" % (layer, s)) for s in range(2)]
                PR = [sb("PR%d_%d" % (layer, s), [128, 512], F32, st) for s in range(4)]
                TMP = [sb("TMP%d_%d" % (layer, s), [128, 256], F32, st) for s in range(4)]
                SQ = sb("SQ%d" % layer, [128, 512], F32, st)
                SSQ = sb("SSQ%d" % layer, [128, 8], F32, st)
                QK = [sb("QK%d_%d" % (layer, s), [128, 512], BF16, st) for s in range(4)]
                STG = [sb("STG%d_%d" % (layer, s), [128, 8320], BF16, st) for s in range(2)]
                pj = psum("pj%d" % layer, [128, 4, 512], F32, st)
                tq = psum("tq%d" % layer, [128, 4, 1024], BF16, st)

                m_w_free = [None, None]
                m_stg_free = [None, None]
                m_pj_free = [None] * 4
                m_pr_free = [None] * 4
                m_qk_free = [None] * 4
                m_tq_free = [None] * 4
                m_tmp_free = [None, None]
                pending = []
                cnt = 0

                def load_w(gi):
                    kind, c0, wd, _ = groups[gi]
                    s = gi % 2
                    sp.wait(m_w_free[s], dsw.all())
                    dma(sp, dsW[s], WG[s][:, :, 0:wd], wsrc[:, c0:c0 + wd].rearrange("(k p) c -> p k c", p=128))

                load_w(0)
                for gi, (kind, c0, wd, dbase) in enumerate(groups):
                    ws = gi % 2
                    ss = gi % 2
                    if gi + 1 < len(groups):
                        load_w(gi + 1)
                    isv = kind == "v"
                    nh = wd // 64
                    nu = wd // 128
                    stg = STG[ss]
                    if isv:
                        nhv = wd // dv
                        stg_v = stg[:, 0:nhv * 16 * (dv + 1)].rearrange("p (h i d) -> p h i d", h=nhv, i=16)
                        pool.wait(m_stg_free[ss])
                        m_ones = pool.mark(nc.gpsimd.memset(stg[:, 0:nhv * 16 * (dv + 1)], 1.0))
                    else:
                        stg_q = stg[:, 0:nu * T].rearrange("p (u t) -> p u t", u=nu)
                        m_ones = None
                    last_ev = None
                    for i in range(NT):
                        b = cnt % 4
                        cnt += 1
                        pe.wait(dsW[ws].all(), m_pj_free[b])
                        for k in range(8):
                            ins = nc.tensor.matmul(pj[:, b, 0:wd], lhsT=AT[:, k, i * 128:(i + 1) * 128], rhs=WG[ws][:, k, 0:wd],
                                                   start=(k == 0), stop=(k == 7))
                        m_mm = pe.mark(ins)
                        m_w_free[ws] = m_mm
                        while len(pending) > 2:
                            last_ev = pending.pop(0)()
                        if isv:
                            act.wait(m_mm, m_ones, m_stg_free[ss])
                            m_c = act.mark(nc.scalar.copy(out=stg_v[:, :, i, 0:dv],
                                                          in_=pj[:, b, 0:wd].rearrange("p (h d) -> p h d", d=dv)))
                            m_pj_free[b] = m_c
                            last_ev = m_c
                            continue
                        act.wait(m_mm, m_pr_free[b])
                        m_c = act.mark(nc.scalar.copy(out=PR[b][:, 0:wd], in_=pj[:, b, 0:wd]))
                        m_pj_free[b] = m_c
                        pr3 = PR[b][:, 0:wd].rearrange("p (h d) -> p h d", d=64)
                        m_src = m_c
                        if kind in ("qn", "kn"):
                            gsel = 0 if kind == "qn" else 1
                            dve.wait(m_src)
                            m1 = dve.mark(nc.vector.tensor_tensor(out=SQ[:, 0:wd], in0=PR[b][:, 0:wd], in1=PR[b][:, 0:wd], op=ALU.mult))
                            dve.wait(m1)
                            m2 = dve.mark(nc.vector.tensor_reduce(out=SSQ[:, 0:nh], in_=SQ[:, 0:wd].rearrange("p (h d) -> p h d", d=64),
                                                                  axis=AX.X, op=ALU.add))
                            act.wait(m2)
                            m3 = act.mark(nc.scalar.activation(out=SSQ[:, 0:nh], in_=SSQ[:, 0:nh], func=AF.Sqrt,
                                                               bias=EPST[:, 0:1], scale=1.0 / 64))
                            dve.wait(m3)
                            m4 = dve.mark(nc.vector.reciprocal(out=SSQ[:, 0:nh], in_=SSQ[:, 0:nh]))
                            dve.wait(m4)
                            m5 = dve.mark(nc.vector.tensor_tensor(out=pr3, in0=pr3,
                                                                  in1=SSQ[:, 0:nh].unsqueeze(2).to_broadcast([128, nh, 64]), op=ALU.mult))
                            dve.wait(m5)
                            m_src = dve.mark(nc.vector.tensor_tensor(out=pr3, in0=pr3,
                                                                     in1=GQK[:, gsel, :].unsqueeze(1).to_broadcast([128, nh, 64]), op=ALU.mult))
                        ctab, stab = (2, 3) if kind in ("qn", "kn") else (0, 1)
                        cosb = ROPE[:, ctab, i, :].unsqueeze(1).to_broadcast([128, nh, 32])
                        sinb = ROPE[:, stab, i, :].unsqueeze(1).to_broadcast([128, nh, 32])
                        x1 = pr3[:, :, 0:32]
                        x2 = pr3[:, :, 32:64]
                        qk3 = QK[b][:, 0:wd].rearrange("p (h d) -> p h d", d=64)
                        t0 = TMP[0][:, 0:nh * 32].rearrange("p (h d) -> p h d", d=32)
                        t1 = TMP[1][:, 0:nh * 32].rearrange("p (h d) -> p h d", d=32)
                        t2 = TMP[2][:, 0:nh * 32].rearrange("p (h d) -> p h d", d=32)
                        t3 = TMP[3][:, 0:nh * 32].rearrange("p (h d) -> p h d", d=32)
                        dve.wait(m_src, m_qk_free[b], m_tmp_free[0])
                        ma = dve.mark(nc.vector.tensor_tensor(out=t0, in0=x1, in1=cosb, op=ALU.mult))
                        mb = dve.mark(nc.vector.tensor_tensor(out=t1, in0=x2, in1=sinb, op=ALU.mult))
                        dve.wait(ma, mb)
                        mq1 = dve.mark(nc.vector.tensor_tensor(out=qk3[:, :, 0:32], in0=t0, in1=t1, op=ALU.subtract))
                        pool.wait(m_src, m_qk_free[b], m_tmp_free[1])
                        mc = pool.mark(nc.gpsimd.tensor_tensor(out=t2, in0=x1, in1=sinb, op=ALU.mult))
                        md = pool.mark(nc.gpsimd.tensor_tensor(out=t3, in0=x2, in1=cosb, op=ALU.mult))
                        pool.wait(mc, md)
                        mq2 = pool.mark(nc.gpsimd.tensor_tensor(out=qk3[:, :, 32:64], in0=t2, in1=t3, op=ALU.add))
                        m_pr_free[b] = [mq1, mq2]
                        m_tmp_free[0] = mq1
                        m_tmp_free[1] = mq2

                        def do_tr(b=b, i=i, mq1=mq1, mq2=mq2):
                            pe.wait(mq1, mq2, m_tq_free[b])
                            for u in range(nu):
                                ins = nc.tensor.transpose(tq[:, b, u * 128:(u + 1) * 128], QK[b][:, u * 128:(u + 1) * 128], IDENT[:])
                            m_t = pe.mark(ins)
                            m_qk_free[b] = m_t
                            act.wait(m_t, m_stg_free[ss])
                            m_e = act.mark(nc.scalar.copy(out=stg_q[:, :, i * 128:(i + 1) * 128],
                                                          in_=tq[:, b, 0:wd].rearrange("p (u c) -> p u c", c=128)))
                            m_tq_free[b] = m_e
                            return m_e
                        pending.append(do_tr)
                    while pending:
                        last_ev = pending.pop(0)()
                    sp.wait(last_ev)
                    if isv:
                        ng = nhv // VGH[layer]
                        g0 = dbase // VGH[layer]
                        src = stg[:, 0:nhv * 16 * (dv + 1)].rearrange("p (h f) -> p h f", h=nhv)
                        for gg in range(ng):
                            m_st = dma(sp, ds_v, v_loc[layer][g0 + gg].rearrange("(h p) f -> p h f", p=128),
                                       src[:, gg * VGH[layer]:(gg + 1) * VGH[layer], :])
                        m_stg_free[ss] = m_st
                        pool.wait(m_st)
                        for gg in range(ng):
                            d = DS(nc, es, "cc_v%d_%d" % (layer, g0 + gg))
                            ins = nc.gpsimd.collective_compute("AllGather", ALU.bypass, replica_groups=[[0, 1, 2, 3], [4, 5, 6, 7]],
                                                               ins=[v_loc[layer][g0 + gg].opt()], outs=[v_all[layer][g0 + gg].opt()])
                            ins.then_inc(d.sem, 1)
                            d.n = 1
                            cc_v[g0 + gg] = d
                    elif kind in ("q", "qn"):
                        m_stg_free[ss] = dma(sp, ds_q, qt[layer][dbase * 128:(dbase + nu) * 128, :].rearrange("(u p) t -> p u t", p=128), stg_q)
                    else:
                        for uu in range(nu):
                            m_st = dma(sp, ds_kt, kt_loc[layer][dbase + uu], stg_q[:, uu, :])
                        m_stg_free[ss] = m_st
                        pool.wait(m_st)
                        for uu in range(nu):
                            d = DS(nc, es, "cc_k%d_%d" % (layer, dbase + uu))
                            ins = nc.gpsimd.collective_compute("AllGather", ALU.bypass, replica_groups=[[0, 1, 2, 3], [4, 5, 6, 7]],
                                                               ins=[kt_loc[layer][dbase + uu].opt()], outs=[kt_all[layer][dbase + uu].opt()])
                            ins.then_inc(d.sem, 1)
                            d.n = 1
                            cc_k[dbase + uu] = d
                barrier()
            return cc_k, cc_v

        def attention(layer):
            dvp = 65 if layer == 0 else 129
            dv = dvp - 1
            nbank = 2 if layer == 0 else 3
            per_bank = 4 if layer == 0 else 3
            with ExitStack() as st:
                psS = psum("psS%d" % layer, [128, 2, 2, 512], F32, st)
                psA = psum("psA%d" % layer, [128, nbank, 512], F32, st)
                psT = psum("psT%d" % layer, [128, 1024], BF16, st)
                QT = [sb("QT%d_%d" % (layer, s), [128, T], BF16, st) for s in range(2)]
                dsQ = [newds("dsQ%d_%d" % (layer, s)) for s in range(2)]
                PP = sb("PP%d" % layer, [128, 3, 2, 512], BF16, st)
                ACCS = sb("ACCS%d" % layer, [128, 8 * dvp], F32, st)
                RDEN = sb("RDEN%d" % layer, [128, 8], F32, st)
                MT = sb("MT%d" % layer, [128, 512], BF16, st)
                if layer == 1:
                    O1 = sb("O1", [128, 512], F32, st)
                    O2 = sb("O2", [128, 512], F32, st)
                    SS4 = sb("SS4", [128, 4], F32, st)
                    RS4 = sb("RS4", [128, 4], F32, st)
                if layer == 0:
                    KTA = sb("KTA", [128, 4, T], BF16, st)
                    VA = sb("VA", [128, 4, 16, 65], BF16, st)
                    dsKA = newds("dsKA")
                    KTB = [sb("KTB%d" % s, [128, NM * 128], BF16, st) for s in range(2)]
                    VB = [sb("VB%d" % s, [128, 2, NM, 65], BF16, st) for s in range(2)]
                    dsKB = [newds("dsKB%d" % s) for s in range(2)]
                    MASK = sb("MASK", [128, NM, 512], BF16, st)
                    dsM = newds("dsM")
                else:
                    KT = [sb("KT1_%d" % s, [128, 4, T], BF16, st) for s in range(2)]
                    V1 = [sb("V1_%d" % s, [128, 4, 16, 129], BF16, st) for s in range(2)]
                    dsKV = [newds("dsKV%d" % s) for s in range(2)]

                state = dict(m_s_free=[None, None], m_p_free=[None] * 3,
                             m_acc_free=None, m_accs_free=None, m_mt_free=None, m_psT_free=None, kbc=0)
                m_q_free = [None, None]

                def acc_ap(n):
                    bk, o = n // per_bank, (n % per_bank) * dvp
                    return psA[:, bk, o:o + dvp], bk

                def qgroup(qtile, g, chunk, blocks, m_load, m_buf_free_cb):
                    nb = len(blocks)
                    started = set()
                    m_exp = {}
                    m_s = {}
                    last_pv = None

                    def emit_s(kb):
                        la, lb, _, _, _, _ = blocks[kb]
                        sl = state["kbc_base"] + kb
                        s = sl % 2
                        pe.wait(m_load, state["m_s_free"][s])
                        nc.tensor.matmul(psS[:, s, 0, :], lhsT=la, rhs=qtile[0:64, g * 512:(g + 1) * 512], start=True, stop=True)
                        ins = nc.tensor.matmul(psS[:, s, 1, :], lhsT=lb, rhs=qtile[64:128, g * 512:(g + 1) * 512], start=True, stop=True)
                        m_s[kb] = pe.mark(ins)

                    state["kbc_base"] = state["kbc"]
                    emit_s(0)
                    if nb > 1:
                        emit_s(1)
                    for kb in range(nb):
                        _, _, va, vb, mi, jl = blocks[kb]
                        sl = state["kbc_base"] + kb
                        s = sl % 2
                        ps3 = sl % 3
                        act.wait(m_s[kb], state["m_p_free"][ps3])
                        me = act.mark(nc.scalar.activation(out=PP[:, ps3, :, :], in_=psS[:, s, :, :], func=AF.Exp, scale=0.125))
                        state["m_s_free"][s] = me
                        if mi is not None:
                            dve.wait(me)
                            me = dve.mark(nc.vector.tensor_tensor(out=PP[:, ps3, :, :], in0=PP[:, ps3, :, :],
                                                                  in1=MASK[:, mi, :].unsqueeze(1).to_broadcast([128, 2, 512]), op=ALU.mult))
                        if kb + 2 < nb:
                            emit_s(kb + 2)
                        if kb == 10 and state.get("deferred"):
                            state.pop("deferred")()
                        pe.wait(me, state["m_acc_free"])
                        ins = None
                        for mp in range(2):
                            vv = va if mp == 0 else vb
                            for j in jl:
                                o_ap, bk = acc_ap(mp * 4 + j)
                                first = bk not in started
                                started.add(bk)
                                ins = nc.tensor.matmul(o_ap, lhsT=PP[:, ps3, mp, j * 128:(j + 1) * 128], rhs=vv,
                                                       start=first, stop=(kb == nb - 1), skip_group_check=True)
                        m_pv = pe.mark(ins)
                        state["m_p_free"][ps3] = m_pv
                        last_pv = m_pv
                    state["kbc"] = state["kbc_base"] + nb
                    m_buf_free_cb(last_pv)
                    dve.wait(last_pv, state["m_accs_free"])
                    m = None
                    for bk in range(nbank):
                        ncols = min(per_bank, 8 - bk * per_bank) * dvp
                        m = dve.mark(nc.vector.tensor_copy(out=ACCS[:, bk * per_bank * dvp: bk * per_bank * dvp + ncols], in_=psA[:, bk, 0:ncols]))
                    state["m_acc_free"] = m
                    dve.wait(m)
                    a3 = ACCS[:].rearrange("p (n d) -> p n d", d=dvp)
                    m = dve.mark(nc.vector.reciprocal(out=RDEN[:], in_=a3[:, :, dv]))
                    dve.wait(m, state["m_mt_free"])
                    if layer == 0:
                        mt3 = MT[:].rearrange("p (j f) -> p j f", f=128)
                        for mp in range(2):
                            m = dve.mark(nc.vector.tensor_tensor(out=mt3[:, :, mp * 64:(mp + 1) * 64], in0=a3[:, mp * 4:(mp + 1) * 4, 0:64],
                                                                 in1=RDEN[:, mp * 4:(mp + 1) * 4].unsqueeze(2).to_broadcast([128, 4, 64]), op=ALU.mult))
                        state["m_accs_free"] = m
                        m_mt = m
                    else:
                        o13 = O1[:].rearrange("p (j f) -> p j f", f=128)
                        o23 = O2[:].rearrange("p (j f) -> p j f", f=128)
                        m = dve.mark(nc.vector.tensor_scalar(out=RDEN[:, 4:8], in0=RDEN[:, 4:8], scalar1=NEGLAM[:, 0:1], scalar2=None, op0=ALU.mult))
                        dve.wait(m)
                        nc.vector.tensor_tensor(out=o13, in0=a3[:, 0:4, 0:128], in1=RDEN[:, 0:4].unsqueeze(2).to_broadcast([128, 4, 128]), op=ALU.mult)
                        m = dve.mark(nc.vector.tensor_tensor(out=o23, in0=a3[:, 4:8, 0:128], in1=RDEN[:, 4:8].unsqueeze(2).to_broadcast([128, 4, 128]), op=ALU.mult))
                        state["m_accs_free"] = m
                        dve.wait(m)
                        m = dve.mark(nc.vector.tensor_tensor(out=O1[:], in0=O1[:], in1=O2[:], op=ALU.add))
                        dve.wait(m)
                        m = dve.mark(nc.vector.tensor_tensor(out=O2[:], in0=O1[:], in1=O1[:], op=ALU.mult))
                        dve.wait(m)
                        m = dve.mark(nc.vector.tensor_reduce(out=SS4[:], in_=o23, axis=AX.X, op=ALU.add))
                        act.wait(m)
                        m = act.mark(nc.scalar.activation(out=RS4[:], in_=SS4[:], func=AF.Ln, bias=EPST[:, 0:1], scale=1.0 / 128))
                        act.wait(m)
                        m = act.mark(nc.scalar.activation(out=RS4[:], in_=RS4[:], func=AF.Exp, scale=-0.5))
                        dve.wait(m)
                        m = dve.mark(nc.vector.tensor_tensor(out=o13, in0=o13, in1=RS4[:].unsqueeze(2).to_broadcast([128, 4, 128]), op=ALU.mult))
                        dve.wait(m)
                        m_mt = dve.mark(nc.vector.tensor_tensor(out=MT[:].rearrange("p (j f) -> p j f", f=128), in0=o13,
                                                                in1=GAINC[:].unsqueeze(1).to_broadcast([128, 4, 128]), op=ALU.mult))

                    def do_transposes(m_mt=m_mt, chunk=chunk, g=g):
                        pe.wait(m_mt, state["m_psT_free"])
                        for j in range(4):
                            ins = nc.tensor.transpose(psT[:, j * 128:(j + 1) * 128], MT[:, j * 128:(j + 1) * 128], IDENT[:])
                        m_t = pe.mark(ins)
                        state["m_mt_free"] = m_t
                        dve.wait(m_t)
                        state["m_psT_free"] = dve.mark(nc.vector.tensor_copy(out=AT[:, chunk, g * 512:(g + 1) * 512], in_=psT[:, 0:512]))
                    state["deferred"] = do_transposes

                def load_q(u, slot):
                    sp.wait(m_q_free[slot])
                    return dma(sp, dsQ[slot], QT[slot][:], qt[layer][u * 128:(u + 1) * 128, :])

                if layer == 0:
                    cc_k, cc_v = cc[0]
                    kA = kt_all[0][0].rearrange("(r p) t -> p r t", p=128)
                    vA = v_all[0][0].rearrange("(r h p) f -> h p r f", r=4, h=2)
                    ds_rl = newds("ds_rl")
                    ds_win = newds("ds_win")
                    pool.wait(ds_pad.all())
                    for u in range(4):
                        pool.wait(cc_k[1 + u].all())
                        dma(pool, ds_rl, ktB_pad[u * 128:(u + 1) * 128, 1024:1024 + 4 * T].rearrange("p (r t) -> p r t", r=4),
                            kt_all[0][1 + u].rearrange("(r p) t -> p r t", p=128))
                    for gi in range(4):
                        pool.wait(cc_v[1 + gi].all())
                        srcv = v_all[0][1 + gi].rearrange("(r h p) f -> h p r f", r=4, h=2)
                        for hh in range(2):
                            hB = 2 * gi + hh
                            dma(pool, ds_rl, vB_pad[hB * 128:(hB + 1) * 128, 8 * 65:72 * 65].rearrange("p (r f) -> p r f", r=4), srcv[hh])
                    pid = nc.gpsimd.partition_id()
                    rank = pid % 4
                    pool.wait(ds_rl.all())
                    dma(pool, ds_win, ktB_win, ktB_pad[:, bass.ds(rank * T, 4096)])
                    dma(pool, ds_win, vB_win, vB_pad[:, bass.ds(rank * (16 * 65), 32 * 65)])
                    m_ka_free = None
                    qslot = 0
                    mq = load_q(0, 0)
                    for u in range(4):
                        if u == 1:
                            pool.wait(m_ka_free)
                            casts_layer0_rest()
                            dma(sp, dsM, MASK[:], masks_in.rearrange("m p q -> p m q"))
                        kvh = u // 2
                        if u % 2 == 0:
                            sp.wait(m_ka_free, cc_k[0].all(), cc_v[0].all())
                            for half in range(2):
                                dma(sp, dsKA, KTA[half * 64:(half + 1) * 64, :, :], kA[kvh * 64:(kvh + 1) * 64, :, :])
                            dma(sp, dsKA, VA[:].rearrange("p r i d -> p r (i d)"), vA[kvh])
                        mq_next = load_q(u + 1, (qslot + 1) % 2) if u < 3 else None
                        lastholder = []
                        for g in range(4):
                            blocks = []
                            for r in range(4):
                                for i in range(16):
                                    blocks.append((KTA[0:64, r, i * 128:(i + 1) * 128], KTA[64:128, r, i * 128:(i + 1) * 128],
                                                   VA[:, r, i, :], VA[:, r, i, :], None, [0, 1, 2, 3]))
                            qgroup(QT[qslot], g, u, blocks, [mq, dsKA.all()], lambda m: lastholder.append(m))
                        m_q_free[qslot] = lastholder[-1]
                        m_ka_free = lastholder[-1]
                        qslot = (qslot + 1) % 2
                        mq = mq_next
                    m_kb_free = [None, None]
                    widx = 0
                    mq = load_q(4, qslot)
                    for u in range(4):
                        mq_next = None
                        lastholder = []
                        for g in range(4):
                            ws = widx % 2
                            widx += 1
                            sp.wait(m_kb_free[ws], ds_win.all())
                            dma(sp, dsKB[ws], KTB[ws][:], ktB_win[u * 128:(u + 1) * 128, g * 512:g * 512 + NM * 128])
                            for hh in range(2):
                                h = 2 * u + hh
                                dma(sp, dsKB[ws], VB[ws][:, hh, :, :].rearrange("p m d -> p (m d)"),
                                    vB_win[h * 128:(h + 1) * 128, g * 4 * 65:(g * 4 + NM) * 65])
                            if g == 0 and u < 3:
                                mq_next = load_q(4 + u + 1, (qslot + 1) % 2)
                            blocks = []
                            for mi in range(NM):
                                jl = [j for j in range(4) if j <= mi <= j + 16]
                                blocks.append((KTB[ws][0:64, mi * 128:(mi + 1) * 128], KTB[ws][64:128, mi * 128:(mi + 1) * 128],
                                               VB[ws][:, 0, mi, :], VB[ws][:, 1, mi, :], mi, jl))

                            def cb(m, ws=ws):
                                m_kb_free[ws] = m
                                lastholder.append(m)
                            qgroup(QT[qslot], g, 4 + u, blocks, [mq, dsKB[ws].all(), dsM.all()], cb)
                        m_q_free[qslot] = lastholder[-1]
                        qslot = (qslot + 1) % 2
                        mq = mq_next
                else:
                    cc_k, cc_v = cc[1]
                    m_kv_free = [None, None]

                    def load_kv(h, slot):
                        sp.wait(m_kv_free[slot], cc_k[h].all(), cc_v[h].all())
                        dma(sp, dsKV[slot], KT[slot][:], kt_all[1][h].rearrange("(r p) t -> p r t", p=128))
                        dma(sp, dsKV[slot], V1[slot][:].rearrange("p r i d -> p r (i d)"), v_all[1][h].rearrange("(r p) f -> p r f", p=128))

                    mq = load_q(0, 0)
                    load_kv(0, 0)
                    for h in range(8):
                        slot = h % 2
                        mq_next = None
                        if h == 1:
                            pool.wait(m_kv_free[0])
                            casts_layer1_rest()
                        if h < 7:
                            mq_next = load_q(h + 1, 1 - slot)
                            load_kv(h + 1, 1 - slot)
                        lastholder = []
                        for g in range(4):
                            blocks = []
                            for r in range(4):
                                for i in range(16):
                                    blocks.append((KT[slot][0:64, r, i * 128:(i + 1) * 128], KT[slot][64:128, r, i * 128:(i + 1) * 128],
                                                   V1[slot][:, r, i, :], V1[slot][:, r, i, :], None, [0, 1, 2, 3]))
                            qgroup(QT[slot], g, h, blocks, [mq, dsKV[slot].all()], lambda m: lastholder.append(m))
                        m_q_free[slot] = lastholder[-1]
                        m_kv_free[slot] = lastholder[-1]
                        mq = mq_next
                if state.get("deferred"):
                    state.pop("deferred")()
                barrier()

        def outproj(layer):
            with ExitStack() as st:
                WO = sb("WO%d" % layer, [128, 8, D], BF16, st)
                dsO = newds("dsO%d" % layer)
                psO = psum("psO%d" % layer, [128, 2, 512], F32, st)
                sp.wait(ds_w["out%d" % layer].all())
                dma(sp, dsO, WO[:], wb_out[layer].rearrange("(k p) c -> p k c", p=128))
                m_free = [None, None]
                cnt = 0
                for i in range(NT):
                    for c in range(2):
                        b = cnt % 2
                        cnt += 1
                        pe.wait(dsO.all(), m_free[b])
                        for k in range(8):
                            ins = nc.tensor.matmul(psO[:, b, :], lhsT=AT[:, k, i * 128:(i + 1) * 128], rhs=WO[:, k, c * 512:(c + 1) * 512],
                                                   start=(k == 0), stop=(k == 7))
                        mm = pe.mark(ins)
                        dve.wait(mm)
                        m_free[b] = dve.mark(nc.vector.tensor_tensor(out=X[:, i, c * 512:(c + 1) * 512], in0=X[:, i, c * 512:(c + 1) * 512],
                                                                     in1=psO[:, b, :], op=ALU.add))
                barrier()

        def ffn(layer):
            with ExitStack() as st:
                WD = sb("WD%d" % layer, [128, NF, D], BF16, st)
                dsD = newds("dsD%d" % layer)
                WGU = [sb("WGU%d_%d" % (layer, s), [128, 2, 8, 256], BF16, st) for s in range(2)]
                dsGU = [newds("dsGU%d_%d" % (layer, s)) for s in range(2)]
                AFT = sb("AFT%d" % layer, [128, NF, 512], BF16, st)
                SG = [sb("SG%d_%d" % (layer, s), [128, 512], F32, st) for s in range(2)]
                psG = psum("psG%d" % layer, [128, 2, 512], F32, st)
                psU = psum("psU%d" % layer, [128, 2, 512], F32, st)
                psD = psum("psD%d" % layer, [128, 2, 512], F32, st)
                sp.wait(ds_w["d%d" % layer].all(), ds_w["g%d" % layer].all(), ds_w["u%d" % layer].all())
                for half in range(2):
                    dma(sp, dsD, WD[:, half * 11:(half + 1) * 11, :],
                        wb_d[layer][half * 11 * 128:(half + 1) * 11 * 128, :].rearrange("(f p) c -> p f c", p=128))
                m_gu_free = [None, None]
                m_g_free = [None, None]
                m_u_free = [None, None]
                m_sg_free = [None, None]
                m_d_free = [None, None]
                m_down_last = None
                lc = 0
                fc_cnt = 0
                dcnt = 0

                def load_gu(fg, slot):
                    sp.wait(m_gu_free[slot])
                    dma(sp, dsGU[slot], WGU[slot][:, 0, :, :], wb_g[layer][:, fg * 256:(fg + 1) * 256].rearrange("(k p) c -> p k c", p=128))
                    dma(sp, dsGU[slot], WGU[slot][:, 1, :, :], wb_u[layer][:, fg * 256:(fg + 1) * 256].rearrange("(k p) c -> p k c", p=128))

                seq = [(tg, fg) for tg in range(4) for fg in range(11)]
                load_gu(seq[0][1], 0)
                for si, (tg, fg) in enumerate(seq):
                    slot = si % 2
                    if si + 1 < len(seq):
                        load_gu(seq[si + 1][1], 1 - slot)
                    for fc in range(2):
                        f = fg * 2 + fc
                        b = fc_cnt % 2
                        fc_cnt += 1
                        pe.wait(dsGU[slot].all(), m_g_free[b], m_u_free[b])
                        for k in range(8):
                            nc.tensor.matmul(psG[:, b, :], lhsT=WGU[slot][:, 0, k, fc * 128:(fc + 1) * 128], rhs=AT[:, k, tg * 512:(tg + 1) * 512],
                                             start=(k == 0), stop=(k == 7))
                        for k in range(8):
                            ins = nc.tensor.matmul(psU[:, b, :], lhsT=WGU[slot][:, 1, k, fc * 128:(fc + 1) * 128], rhs=AT[:, k, tg * 512:(tg + 1) * 512],
                                                   start=(k == 0), stop=(k == 7))
                        mm = pe.mark(ins)
                        m_gu_free[slot] = mm
                        act.wait(mm, m_sg_free[b])
                        ms = act.mark(nc.scalar.activation(out=SG[b][:], in_=psG[:, b, :], func=AF.Silu))
                        m_g_free[b] = ms
                        dve.wait(ms, m_down_last)
                        ma = dve.mark(nc.vector.tensor_tensor(out=AFT[:, f, :], in0=SG[b][:], in1=psU[:, b, :], op=ALU.mult))
                        m_u_free[b] = ma
                        m_sg_free[b] = ma
                        last_a = ma
                    if fg == 10:
                        for j in range(4):
                            for c in range(2):
                                b = dcnt % 2
                                dcnt += 1
                                pe.wait(last_a, dsD.all(), m_d_free[b])
                                for f in range(NF):
                                    ins = nc.tensor.matmul(psD[:, b, :], lhsT=AFT[:, f, j * 128:(j + 1) * 128], rhs=WD[:, f, c * 512:(c + 1) * 512],
                                                           start=(f == 0), stop=(f == NF - 1))
                                mm = pe.mark(ins)
                                m_down_last = mm
                                i = tg * 4 + j
                                dve.wait(mm)
                                m_d_free[b] = dve.mark(nc.vector.tensor_tensor(out=X[:, i, c * 512:(c + 1) * 512], in0=X[:, i, c * 512:(c + 1) * 512],
                                                                               in1=psD[:, b, :], op=ALU.add))
                barrier()

        def final():
            with ExitStack() as st:
                G, RSTD = rms_rstd(st, lambda i: X[:, i, :], NT, final_norm[0:1, :], "fin")
                OB = [sb("OB%d" % s, [128, D], F32, st) for s in range(2)]
                ds_o = newds("ds_out")
                m_free = [None, None]
                for i in range(NT):
                    s = i % 2
                    dve.wait(m_free[s])
                    m = dve.mark(nc.vector.scalar_tensor_tensor(out=OB[s][:], in0=X[:, i, :], scalar=RSTD[:, i:i + 1], in1=G[:],
                                                                op0=ALU.mult, op1=ALU.mult))
                    sp.wait(m)
                    m_free[s] = dma(sp, ds_o, out[i * 128:(i + 1) * 128, :], OB[s][:])
                sp.wait(ds_o.all())
                pool.wait(ds_o.all())

        cc = {}
        dve.wait(ds_x.all(), ds_c.all())
        act.wait(ds_x.all(), ds_c.all())
        pe.wait(ds_c.all())
        pool.wait(ds_c.all())

        def dump(layer):
            barrier()
            dsd = newds("ds_dbg")
            for i in range(NT):
                dma(sp, dsd, dbg["X"][i * 128:(i + 1) * 128, :], X[:, i, :])
            dma(sp, dsd, dbg["AT"], AT[:].rearrange("p k t -> p (k t)"))
            dma(sp, dsd, dbg["qt"], qt[layer])
            for u in range(NKU[layer]):
                dma(sp, dsd, dbg["kt"][u * 512:(u + 1) * 512, :], kt_all[layer][u])
            for g in range(NVG[layer]):
                nr = 4 * VGH[layer] * 128
                dma(sp, dsd, dbg["v"][g * nr:(g + 1) * nr, 0:VF[layer]], v_all[layer][g])
            sp.wait(dsd.all())
            pool.wait(dsd.all())

        steps = [
            ("norm_a0", lambda: norm_AT(attn_norm[0:1, :], "a0")),
            ("inproj0", lambda: cc.__setitem__(0, inproj(0))),
            ("attn0", lambda: attention(0)),
            ("outproj0", lambda: outproj(0)),
            ("norm_f0", lambda: norm_AT(ffn_norm[0:1, :], "f0")),
            ("ffn0", lambda: ffn(0)),
            ("norm_a1", lambda: norm_AT(attn_norm[1:2, :], "a1")),
            ("inproj1", lambda: cc.__setitem__(1, inproj(1))),
            ("attn1", lambda: attention(1)),
            ("outproj1", lambda: outproj(1)),
            ("norm_f1", lambda: norm_AT(ffn_norm[1:2, :], "f1")),
            ("ffn1", lambda: ffn(1)),
            ("final", final),
        ]
        for name, fn in steps:
            fn()
            if debug and debug == name:
                dump(0 if name in ("norm_a0", "inproj0", "attn0", "outproj0", "norm_f0", "ffn0", "norm_a1") else 1)
                break
    return nc


def _const_tables():
    theta = 10000.0
    pos = np.arange(S, dtype=np.float32)
    inv64 = (theta ** (-np.arange(0, 64, 2, dtype=np.float32) / 64)).astype(np.float32)
    inv32 = (theta ** (-np.arange(0, 32, 2, dtype=np.float32) / 32)).astype(np.float32)
    ang1 = (pos[:, None] * inv64[None, :]).astype(np.float32)
    rows = (np.arange(S) // 64).astype(np.float32)
    cols = (np.arange(S) % 64).astype(np.float32)
    ang2 = np.concatenate([rows[:, None] * inv32[None, :], cols[:, None] * inv32[None, :]], axis=-1).astype(np.float32)
    rope = np.stack([np.cos(ang1.astype(np.float64)), np.sin(ang1.astype(np.float64)),
                     np.cos(ang2.astype(np.float64)), np.sin(ang2.astype(np.float64))], 0).astype(np.float32)
    k = np.arange(128)[:, None]
    q = np.arange(512)[None, :]
    masks = np.zeros((NM, 128, 512), np.float32)
    for m in range(NM):
        d = 128 * m + k - q - 1024
        ad = np.abs(d)
        masks[m] = (ad <= 64).astype(np.float32) + ((d % 4 == 0) & (ad <= 256)) + ((d % 16 == 0) & (ad <= 1024))
    ident = np.eye(128, dtype=np.float32)
    return rope, masks.astype(ml_dtypes.bfloat16), ident.astype(ml_dtypes.bfloat16)


_NC_CACHE = {}


def kernel(x, attn_norm, ffn_norm, final_norm, w_in_even, a_q_norm, a_k_norm, w_out_even,
           w_in_odd, lambda_q1, lambda_k1, lambda_q2, lambda_k2, c_sub_norm, w_out_odd,
           w_gate, w_up, w_down, _debug=False):
    f = lambda a: np.ascontiguousarray(np.asarray(a, dtype=np.float32))
    x = f(x)
    rope, masks, ident = _const_tables()
    shared = {
        "attn_norm": f(attn_norm), "ffn_norm": f(ffn_norm), "final_norm": f(final_norm).reshape(1, D),
        "w_in_even": f(w_in_even)[0], "a_q_norm": f(a_q_norm), "a_k_norm": f(a_k_norm),
        "w_out_even": f(w_out_even)[0], "w_in_odd": f(w_in_odd)[0],
        "lambda_q1": f(lambda_q1), "lambda_k1": f(lambda_k1), "lambda_q2": f(lambda_q2), "lambda_k2": f(lambda_k2),
        "c_sub_norm": f(c_sub_norm), "w_out_odd": f(w_out_odd)[0],
        "w_gate": f(w_gate), "w_up": f(w_up), "w_down": f(w_down),
        "masks": masks, "ident": ident,
    }
    in_maps = []
    for c in range(NCORE):
        b, r = c // 4, c % 4
        m = dict(shared)
        m["x"] = np.ascontiguousarray(x[b, r * T:(r + 1) * T, :])
        m["rope"] = np.ascontiguousarray(rope[:, r * T:(r + 1) * T, :])
        in_maps.append(m)
    key = _debug
    if key not in _NC_CACHE:
        _NC_CACHE[key] = build(debug=key)
    nc = _NC_CACHE[key]
    res = run_bass_kernel_spmd(nc, in_maps, core_ids=list(range(NCORE)))
    outp = np.zeros((2, S, D), np.float32)
    for c in range(NCORE):
        b, r = c // 4, c % 4
        outp[b, r * T:(r + 1) * T, :] = np.asarray(res.results[c]["out"])
    if _debug:
        return outp, res
    return outp
```

```python
import math
from contextlib import ExitStack

import numpy as np
import ml_dtypes

import concourse.bass as bass
import concourse.mybir as mybir
from concourse.bass_utils import run_bass_kernel_spmd

F32 = mybir.dt.float32
BF16 = mybir.dt.bfloat16
ALU = mybir.AluOpType
AF = mybir.ActivationFunctionType
AX = mybir.AxisListType

NCORE = 8
S = 8192
D = 1024
T = 2048
NT = 16
DFF = 2816
NF = 22
EPS = 1e-6
LAM_INIT = 0.8 - 0.6 * math.exp(-0.3 * 1)
KPAD = 1024 + S + 1536
VBLK = 8 + 64 + 12
NM = 20


class TL:
    def __init__(self, nc, es, eng, name):
        self.e = eng
        self.sem = es.enter_context(nc.semaphore(name))
        self.n = 0
        self.seen = {}

    def wait(self, *ms):
        for m in ms:
            if m is None:
                continue
            if isinstance(m, list):
                self.wait(*m)
                continue
            sem, v = m
            if sem is self.sem and False:
                continue
            k = id(sem)
            if self.seen.get(k, 0) >= v:
                continue
            self.e.wait_ge(sem, v)
            self.seen[k] = v

    def mark(self, ins):
        self.n += 1
        ins.then_inc(self.sem, 1)
        return (self.sem, self.n)


class DS:
    def __init__(self, nc, es, name):
        self.sem = es.enter_context(nc.semaphore(name))
        self.n = 0

    def add(self, ins):
        self.n += 16
        ins.then_inc(self.sem, 16)
        return (self.sem, self.n)

    def all(self):
        return (self.sem, self.n) if self.n else None


def build(debug=False):
    nc = bass.Bass("TRN2", target_bir_lowering=False)

    def din(name, shape, dt=F32):
        return nc.dram_tensor(name, list(shape), dt, kind="ExternalInput").ap()

    def dint(name, shape, dt=BF16):
        return nc.dram_tensor(name, list(shape), dt, kind="Internal").ap()

    x_in = din("x", [T, D])
    attn_norm = din("attn_norm", [2, D])
    ffn_norm = din("ffn_norm", [2, D])
    final_norm = din("final_norm", [1, D])
    w_in_even = din("w_in_even", [D, 2304])
    a_q_norm = din("a_q_norm", [1, 64])
    a_k_norm = din("a_k_norm", [1, 64])
    w_out_even = din("w_out_even", [D, D])
    w_in_odd = din("w_in_odd", [D, 3072])
    lam_in = [din(n, [1, 64]) for n in ("lambda_q1", "lambda_k1", "lambda_q2", "lambda_k2")]
    c_sub_norm = din("c_sub_norm", [1, 128])
    w_out_odd = din("w_out_odd", [D, D])
    w_gate = din("w_gate", [2, D, DFF])
    w_up = din("w_up", [2, D, DFF])
    w_down = din("w_down", [2, DFF, D])
    rope_in = din("rope", [4, T, 32])
    masks_in = din("masks", [NM, 128, 512], BF16)
    ident_in = din("ident", [128, 128], BF16)
    out = nc.dram_tensor("out", [T, D], F32, kind="ExternalOutput").ap()

    wb_in = [dint("wb_in0", [D, 2304]), dint("wb_in1", [D, 3072])]
    wb_out = [dint("wb_out0", [D, D]), dint("wb_out1", [D, D])]
    wb_g = [dint(f"wb_g{l}", [D, DFF]) for l in range(2)]
    wb_u = [dint(f"wb_u{l}", [D, DFF]) for l in range(2)]
    wb_d = [dint(f"wb_d{l}", [DFF, D]) for l in range(2)]

    qt = [dint("qt0", [8 * 128, T]), dint("qt1", [8 * 128, T])]
    NKU = [5, 8]
    VGH = [2, 1]
    NVG = [5, 8]
    VF = [16 * 65, 16 * 129]
    kt_loc = [[dint("kt%d_loc%d" % (l, u), [128, T]) for u in range(NKU[l])] for l in range(2)]
    kt_all = [[dint("kt%d_all%d" % (l, u), [4 * 128, T]) for u in range(NKU[l])] for l in range(2)]
    v_loc = [[dint("v%d_loc%d" % (l, g), [VGH[l] * 128, VF[l]]) for g in range(NVG[l])] for l in range(2)]
    v_all = [[dint("v%d_all%d" % (l, g), [4 * VGH[l] * 128, VF[l]]) for g in range(NVG[l])] for l in range(2)]
    ktB_pad = dint("ktB_pad", [4 * 128, KPAD])
    vB_pad = dint("vB_pad", [8 * 128, VBLK * 65])
    ktB_win = dint("ktB_win", [4 * 128, 4096])
    vB_win = dint("vB_win", [8 * 128, 32 * 65])

    dbg = {}
    if debug:
        dbg["X"] = nc.dram_tensor("dbg_X", [T, D], F32, kind="ExternalOutput").ap()
        dbg["AT"] = nc.dram_tensor("dbg_AT", [128, 8 * T], BF16, kind="ExternalOutput").ap()
        dbg["qt"] = nc.dram_tensor("dbg_qt", [8 * 128, T], BF16, kind="ExternalOutput").ap()
        dbg["kt"] = nc.dram_tensor("dbg_kt", [8 * 512, T], BF16, kind="ExternalOutput").ap()
        dbg["v"] = nc.dram_tensor("dbg_v", [8 * 1024, 16 * 129], BF16, kind="ExternalOutput").ap()

    es = ExitStack()
    with es:
        pe = TL(nc, es, nc.tensor, "t_pe")
        act = TL(nc, es, nc.scalar, "t_act")
        dve = TL(nc, es, nc.vector, "t_dve")
        pool = TL(nc, es, nc.gpsimd, "t_pool")
        sp = TL(nc, es, nc.sync, "t_sp")
        engines = [pe, act, dve, pool, sp]

        def sb(name, shape, dt, stack=None):
            return (stack or es).enter_context(nc.sbuf_tensor(name, list(shape), dt))

        def psum(name, shape, dt, stack):
            return stack.enter_context(nc.psum_tensor(name, list(shape), dt))

        def dma(q, ds, out_ap, in_ap):
            return ds.add(q.e.dma_start(out=out_ap, in_=in_ap))

        X = sb("X", [128, NT, D], F32)
        AT = sb("AT", [128, 8, T], BF16)
        ROPE = sb("ROPE", [128, 4, NT, 32], F32)
        IDENT = sb("IDENT", [128, 128], BF16)
        EPST = sb("EPST", [128, 1], F32)
        FENCE = sb("FENCE", [128, 4], F32)
        NEGLAM = sb("NEGLAM", [128, 1], F32)
        GAINC = sb("GAINC", [128, 128], F32)
        GQK = sb("GQK", [128, 2, 64], F32)
        ZERO = sb("ZERO", [128, 2048], BF16)

        fence_ps_stack = ExitStack()

        def barrier():
            ms = []
            ms.append(dve.mark(nc.vector.memset(FENCE[:, 0:1], 0.0)))
            ms.append(pool.mark(nc.gpsimd.memset(FENCE[:, 1:2], 0.0)))
            ms.append(act.mark(nc.scalar.activation(out=FENCE[:, 2:3], in_=EPST[:, 0:1], func=AF.Copy)))
            ms.append((pe.sem, pe.n) if pe.n else None)
            ms += [d.all() for d in all_ds]
            for e in engines:
                e.wait(*ms)

        all_ds = []

        def newds(name, in_barrier=True):
            d = DS(nc, es, name)
            if in_barrier:
                all_ds.append(d)
            return d

        ds_c = newds("ds_c")
        ds_x = newds("ds_x")
        nc.vector.memset(EPST[:], EPS)
        m_zero = pool.mark(nc.gpsimd.memset(ZERO[:], 0.0))
        dma(sp, ds_c, IDENT[:], ident_in)
        dma(sp, ds_c, ROPE[:], rope_in.rearrange("c (i p) d -> p c i d", p=128))
        dma(sp, ds_c, GQK[:, 0, :], a_q_norm.partition_broadcast(128))
        dma(sp, ds_c, GQK[:, 1, :], a_k_norm.partition_broadcast(128))
        dma(sp, ds_c, GAINC[:], c_sub_norm.partition_broadcast(128))
        for i in range(NT):
            dma(sp, ds_x, X[:, i, :], x_in[i * 128:(i + 1) * 128, :])

        ds_w = {}

        def cast_w(key, dst, src, rows):
            d = newds("ds_w_" + key, in_barrier=False)
            ds_w[key] = d
            for r0 in range(0, rows, 128):
                dma(pool, d, dst[r0:r0 + 128, :], src[r0:r0 + 128, :])

        cast_w("in0", wb_in[0], w_in_even, D)
        ds_pad = newds("ds_pad")
        sp.wait(m_zero)
        for u in range(4):
            dma(sp, ds_pad, ktB_pad[u * 128:(u + 1) * 128, 0:1024], ZERO[:, 0:1024])
            dma(sp, ds_pad, ktB_pad[u * 128:(u + 1) * 128, 1024 + S:KPAD], ZERO[:, 0:1536])
        for h in range(8):
            dma(sp, ds_pad, vB_pad[h * 128:(h + 1) * 128, 0:8 * 65], ZERO[:, 0:8 * 65])
            dma(sp, ds_pad, vB_pad[h * 128:(h + 1) * 128, 72 * 65:VBLK * 65], ZERO[:, 0:12 * 65])
        def casts_layer0_rest():
            cast_w("out0", wb_out[0], w_out_even, D)
            cast_w("g0", wb_g[0], w_gate[0], D)
            cast_w("u0", wb_u[0], w_up[0], D)
            cast_w("d0", wb_d[0], w_down[0], DFF)
            cast_w("in1", wb_in[1], w_in_odd, D)
            cast_w("out1", wb_out[1], w_out_odd, D)

        def casts_layer1_rest():
            cast_w("g1", wb_g[1], w_gate[1], D)
            cast_w("u1", wb_u[1], w_up[1], D)
            cast_w("d1", wb_d[1], w_down[1], DFF)

        with ExitStack() as st:
            LQ = sb("LQ", [128, 4, 64], F32, st)
            LS = sb("LS", [128, 4], F32, st)
            LJ = sb("LJ", [128, 64], F32, st)
            ds_l = newds("ds_l")
            for j in range(4):
                dma(sp, ds_l, LQ[:, j, :], lam_in[j].partition_broadcast(128))
            dve.wait(ds_l.all(), ds_c.all())
            nc.vector.memset(LS[:], 0.0)
            m = None
            for j in range(2):
                m = dve.mark(nc.vector.tensor_tensor(out=LJ[:], in0=LQ[:, 2 * j, :], in1=LQ[:, 2 * j + 1, :], op=ALU.mult))
                dve.wait(m)
                m = dve.mark(nc.vector.tensor_reduce(out=LS[:, j:j + 1], in_=LJ[:], axis=AX.X, op=ALU.add))
                dve.wait(m)
            act.wait(m)
            m = act.mark(nc.scalar.activation(out=LS[:, 2:4], in_=LS[:, 0:2], func=AF.Exp))
            dve.wait(m)
            m = dve.mark(nc.vector.tensor_tensor(out=NEGLAM[:], in0=LS[:, 3:4], in1=LS[:, 2:3], op=ALU.subtract))
            dve.wait(m)
            m = dve.mark(nc.vector.tensor_scalar(out=NEGLAM[:], in0=NEGLAM[:], scalar1=-LAM_INIT, scalar2=None, op0=ALU.add))
            dve.wait(m)
            dve.mark(nc.vector.tensor_scalar(out=GAINC[:], in0=GAINC[:], scalar1=1.0 - LAM_INIT, scalar2=None, op0=ALU.mult))
            barrier()

        def rms_rstd(st, src_of_tile, nt, gdram_row, tag):
            G = sb("G" + tag, [128, D], F32, st)
            JUNK = sb("JUNK" + tag, [128, D], BF16, st)
            SS = sb("SS" + tag, [128, nt], F32, st)
            RSTD = sb("RSTD" + tag, [128, nt], F32, st)
            dsg = newds("ds_g" + tag)
            dma(sp, dsg, G[:], gdram_row.partition_broadcast(128))
            m0 = dve.mark(nc.vector.memset(SS[:], 0.0))
            act.wait(m0)
            m = None
            for i in range(nt):
                m = act.mark(nc.scalar.activation(out=JUNK[:], in_=src_of_tile(i), func=AF.Square,
                                                  accum_out=SS[:, i:i + 1]))
            act.wait(m)
            m = act.mark(nc.scalar.activation(out=RSTD[:], in_=SS[:], func=AF.Sqrt, bias=EPST[:, 0:1], scale=1.0 / D))
            dve.wait(m, dsg.all())
            m = dve.mark(nc.vector.reciprocal(out=RSTD[:], in_=RSTD[:]))
            dve.wait(m)
            return G, RSTD

        def norm_AT(gdram_row, tag):
            with ExitStack() as st:
                G, RSTD = rms_rstd(st, lambda i: X[:, i, :], NT, gdram_row, tag)
                HB = sb("HB" + tag, [128, 2, D], BF16, st)
                tp = psum("tp" + tag, [128, 2, D], BF16, st)
                m_tr = [None, None]
                m_ev = [None, None]
                for i in range(NT):
                    s = i % 2
                    dve.wait(m_tr[s])
                    mh = dve.mark(nc.vector.scalar_tensor_tensor(out=HB[:, s, :], in0=X[:, i, :], scalar=RSTD[:, i:i + 1],
                                                                 in1=G[:], op0=ALU.mult, op1=ALU.mult))
                    pe.wait(mh, m_ev[s])
                    for k in range(8):
                        ins = nc.tensor.transpose(tp[:, s, k * 128:(k + 1) * 128], HB[:, s, k * 128:(k + 1) * 128], IDENT[:])
                    m_tr[s] = pe.mark(ins)
                    act.wait(m_tr[s])
                    m_ev[s] = act.mark(nc.scalar.copy(out=AT[:, :, i * 128:(i + 1) * 128],
                                                      in_=tp[:, s, :].rearrange("p (k c) -> p k c", c=128)))
                barrier()

        def inproj(layer):
            wsrc = wb_in[layer]
            dsw = ds_w["in%d" % layer]
            if layer == 0:
                groups = [("kn", 512, 128, 0), ("v", 640, 128, 0), ("k", 1280, 512, 1), ("v", 1792, 512, 2),
                          ("qn", 0, 512, 0), ("q", 768, 512, 4)]
                dv = 64
            else:
                groups = [("k", 1024, 512, 0), ("v", 2048, 512, 0), ("k", 1536, 512, 4), ("v", 2560, 512, 4),
                          ("q", 0, 512, 0), ("q", 512, 512, 4)]
                dv = 128
            ds_kt = newds("ds_kt%d" % layer)
            ds_v = newds("ds_v%d" % layer)
            ds_q = newds("ds_q%d" % layer)
            cc_k = {}
            cc_v = {}
            with ExitStack() as st:
                WG = [sb("WG%d_%d" % (layer, s), [128, 8, 512], BF16, st) for s in range(2)]
                dsW = [newds("ds_WG%d_%d" % (layer, s)) for s in range(2)]
                PR = [sb("PR%d_%d" % (layer, s), [128, 512], F32, st) for s in range(4)]
                TMP = [sb("TMP%d_%d" % (layer, s), [128, 256], F32, st) for s in range(4)]
                SQ = sb("SQ%d" % layer, [128, 512], F32, st)
                SSQ = sb("SSQ%d" % layer, [128, 8], F32, st)
                QK = [sb("QK%d_%d" % (layer, s), [128, 512], BF16, st) for s in range(4)]
                STG = [sb("STG%d_%d" % (layer, s), [128, 8320], BF16, st) for s in range(2)]
                pj = psum("pj%d" % layer, [128, 4, 512], F32, st)
                tq = psum("tq%d" % layer, [128, 4, 1024], BF16, st)

                m_w_free = [None, None]
                m_stg_free = [None, None]
                m_pj_free = [None] * 4
                m_pr_free = [None] * 4
                m_qk_free = [None] * 4
                m_tq_free = [None] * 4
                m_tmp_free = [None, None]
                pending = []
                cnt = 0

                def load_w(gi):
                    kind, c0, wd, _ = groups[gi]
                    s = gi % 2
                    sp.wait(m_w_free[s], dsw.all())
                    dma(sp, dsW[s], WG[s][:, :, 0:wd], wsrc[:, c0:c0 + wd].rearrange("(k p) c -> p k c", p=128))

                load_w(0)
                for gi, (kind, c0, wd, dbase) in enumerate(groups):
                    ws = gi % 2
                    ss = gi % 2
                    if gi + 1 < len(groups):
                        load_w(gi + 1)
                    isv = kind == "v"
                    nh = wd // 64
                    nu = wd // 128
                    stg = STG[ss]
                    if isv:
                        nhv = wd // dv
                        stg_v = stg[:, 0:nhv * 16 * (dv + 1)].rearrange("p (h i d) -> p h i d", h=nhv, i=16)
                        pool.wait(m_stg_free[ss])
                        m_ones = pool.mark(nc.gpsimd.memset(stg[:, 0:nhv * 16 * (dv + 1)], 1.0))
                    else:
                        stg_q = stg[:, 0:nu * T].rearrange("p (u t) -> p u t", u=nu)
                        m_ones = None
                    last_ev = None
                    for i in range(NT):
                        b = cnt % 4
                        cnt += 1
                        pe.wait(dsW[ws].all(), m_pj_free[b])
                        for k in range(8):
                            ins = nc.tensor.matmul(pj[:, b, 0:wd], lhsT=AT[:, k, i * 128:(i + 1) * 128], rhs=WG[ws][:, k, 0:wd],
                                                   start=(k == 0), stop=(k == 7))
                        m_mm = pe.mark(ins)
                        m_w_free[ws] = m_mm
                        while len(pending) > 2:
                            last_ev = pending.pop(0)()
                        if isv:
                            act.wait(m_mm, m_ones, m_stg_free[ss])
                            m_c = act.mark(nc.scalar.copy(out=stg_v[:, :, i, 0:dv],
                                                          in_=pj[:, b, 0:wd].rearrange("p (h d) -> p h d", d=dv)))
                            m_pj_free[b] = m_c
                            last_ev = m_c
                            continue
                        act.wait(m_mm, m_pr_free[b])
                        m_c = act.mark(nc.scalar.copy(out=PR[b][:, 0:wd], in_=pj[:, b, 0:wd]))
                        m_pj_free[b] = m_c
                        pr3 = PR[b][:, 0:wd].rearrange("p (h d) -> p h d", d=64)
                        m_src = m_c
                        if kind in ("qn", "kn"):
                            gsel = 0 if kind == "qn" else 1
                            dve.wait(m_src)
                            m1 = dve.mark(nc.vector.tensor_tensor(out=SQ[:, 0:wd], in0=PR[b][:, 0:wd], in1=PR[b][:, 0:wd], op=ALU.mult))
                            dve.wait(m1)
                            m2 = dve.mark(nc.vector.tensor_reduce(out=SSQ[:, 0:nh], in_=SQ[:, 0:wd].rearrange("p (h d) -> p h d", d=64),
                                                                  axis=AX.X, op=ALU.add))
                            act.wait(m2)
                            m3 = act.mark(nc.scalar.activation(out=SSQ[:, 0:nh], in_=SSQ[:, 0:nh], func=AF.Sqrt,
                                                               bias=EPST[:, 0:1], scale=1.0 / 64))
                            dve.wait(m3)
                            m4 = dve.mark(nc.vector.reciprocal(out=SSQ[:, 0:nh], in_=SSQ[:, 0:nh]))
                            dve.wait(m4)
                            m5 = dve.mark(nc.vector.tensor_tensor(out=pr3, in0=pr3,
                                                                  in1=SSQ[:, 0:nh].unsqueeze(2).to_broadcast([128, nh, 64]), op=ALU.mult))
                            dve.wait(m5)
                            m_src = dve.mark(nc.vector.tensor_tensor(out=pr3, in0=pr3,
                                                                     in1=GQK[:, gsel, :].unsqueeze(1).to_broadcast([128, nh, 64]), op=ALU.mult))
                        ctab, stab = (2, 3) if kind in ("qn", "kn") else (0, 1)
                        cosb = ROPE[:, ctab, i, :].unsqueeze(1).to_broadcast([128, nh, 32])
                        sinb = ROPE[:, stab, i, :].unsqueeze(1).to_broadcast([128, nh, 32])
                        x1 = pr3[:, :, 0:32]
                        x2 = pr3[:, :, 32:64]
                        qk3 = QK[b][:, 0:wd].rearrange("p (h d) -> p h d", d=64)
                        t0 = TMP[0][:, 0:nh * 32].rearrange("p (h d) -> p h d", d=32)
                        t1 = TMP[1][:, 0:nh * 32].rearrange("p (h d) -> p h d", d=32)
                        t2 = TMP[2][:, 0:nh * 32].rearrange("p (h d) -> p h d", d=32)
                        t3 = TMP[3][:, 0:nh * 32].rearrange("p (h d) -> p h d", d=32)
                        dve.wait(m_src, m_qk_free[b], m_tmp_free[0])
                        ma = dve.mark(nc.vector.tensor_tensor(out=t0, in0=x1, in1=cosb, op=ALU.mult))
                        mb = dve.mark(nc.vector.tensor_tensor(out=t1, in0=x2, in1=sinb, op=ALU.mult))
                        dve.wait(ma, mb)
                        mq1 = dve.mark(nc.vector.tensor_tensor(out=qk3[:, :, 0:32], in0=t0, in1=t1, op=ALU.subtract))
                        pool.wait(m_src, m_qk_free[b], m_tmp_free[1])
                        mc = pool.mark(nc.gpsimd.tensor_tensor(out=t2, in0=x1, in1=sinb, op=ALU.mult))
                        md = pool.mark(nc.gpsimd.tensor_tensor(out=t3, in0=x2, in1=cosb, op=ALU.mult))
                        pool.wait(mc, md)
                        mq2 = pool.mark(nc.gpsimd.tensor_tensor(out=qk3[:, :, 32:64], in0=t2, in1=t3, op=ALU.add))
                        m_pr_free[b] = [mq1, mq2]
                        m_tmp_free[0] = mq1
                        m_tmp_free[1] = mq2

                        def do_tr(b=b, i=i, mq1=mq1, mq2=mq2):
                            pe.wait(mq1, mq2, m_tq_free[b])
                            for u in range(nu):
                                ins = nc.tensor.transpose(tq[:, b, u * 128:(u + 1) * 128], QK[b][:, u * 128:(u + 1) * 128], IDENT[:])
                            m_t = pe.mark(ins)
                            m_qk_free[b] = m_t
                            act.wait(m_t, m_stg_free[ss])
                            m_e = act.mark(nc.scalar.copy(out=stg_q[:, :, i * 128:(i + 1) * 128],
                                                          in_=tq[:, b, 0:wd].rearrange("p (u c) -> p u c", c=128)))
                            m_tq_free[b] = m_e
                            return m_e
                        pending.append(do_tr)
                    while pending:
                        last_ev = pending.pop(0)()
                    sp.wait(last_ev)
                    if isv:
                        ng = nhv // VGH[layer]
                        g0 = dbase // VGH[layer]
                        src = stg[:, 0:nhv * 16 * (dv + 1)].rearrange("p (h f) -> p h f", h=nhv)
                        for gg in range(ng):
                            m_st = dma(sp, ds_v, v_loc[layer][g0 + gg].rearrange("(h p) f -> p h f", p=128),
                                       src[:, gg * VGH[layer]:(gg + 1) * VGH[layer], :])
                        m_stg_free[ss] = m_st
                        pool.wait(m_st)
                        for gg in range(ng):
                            d = DS(nc, es, "cc_v%d_%d" % (layer, g0 + gg))
                            ins = nc.gpsimd.collective_compute("AllGather", ALU.bypass, replica_groups=[[0, 1, 2, 3], [4, 5, 6, 7]],
                                                               ins=[v_loc[layer][g0 + gg].opt()], outs=[v_all[layer][g0 + gg].opt()])
                            ins.then_inc(d.sem, 1)
                            d.n = 1
                            cc_v[g0 + gg] = d
                    elif kind in ("q", "qn"):
                        m_stg_free[ss] = dma(sp, ds_q, qt[layer][dbase * 128:(dbase + nu) * 128, :].rearrange("(u p) t -> p u t", p=128), stg_q)
                    else:
                        for uu in range(nu):
                            m_st = dma(sp, ds_kt, kt_loc[layer][dbase + uu], stg_q[:, uu, :])
                        m_stg_free[ss] = m_st
                        pool.wait(m_st)
                        for uu in range(nu):
                            d = DS(nc, es, "cc_k%d_%d" % (layer, dbase + uu))
                            ins = nc.gpsimd.collective_compute("AllGather", ALU.bypass, replica_groups=[[0, 1, 2, 3], [4, 5, 6, 7]],
                                                               ins=[kt_loc[layer][dbase + uu].opt()], outs=[kt_all[layer][dbase + uu].opt()])
                            ins.then_inc(d.sem, 1)
                            d.n = 1
                            cc_k[dbase + uu] = d
                barrier()
            return cc_k, cc_v

        def attention(layer):
            dvp = 65 if layer == 0 else 129
            dv = dvp - 1
            nbank = 2 if layer == 0 else 3
            per_bank = 4 if layer == 0 else 3
            with ExitStack() as st:
                psS = psum("psS%d" % layer, [128, 2, 2, 512], F32, st)
                psA = psum("psA%d" % layer, [128, nbank, 512], F32, st)
                psT = psum("psT%d" % layer, [128, 1024], BF16, st)
                QT = [sb("QT%d_%d" % (layer, s), [128, T], BF16, st) for s in range(2)]
                dsQ = [newds("dsQ%d_%d" % (layer, s)) for s in range(2)]
                PP = sb("PP%d" % layer, [128, 3, 2, 512], BF16, st)
                ACCS = sb("ACCS%d" % layer, [128, 8 * dvp], F32, st)
                RDEN = sb("RDEN%d" % layer, [128, 8], F32, st)
                MT = sb("MT%d" % layer, [128, 512], BF16, st)
                if layer == 1:
                    O1 = sb("O1", [128, 512], F32, st)
                    O2 = sb("O2", [128, 512], F32, st)
                    SS4 = sb("SS4", [128, 4], F32, st)
                    RS4 = sb("RS4", [128, 4], F32, st)
                if layer == 0:
                    KTA = sb("KTA", [128, 4, T], BF16, st)
                    VA = sb("VA", [128, 4, 16, 65], BF16, st)
                    dsKA = newds("dsKA")
                    KTB = [sb("KTB%d" % s, [128, NM * 128], BF16, st) for s in range(2)]
                    VB = [sb("VB%d" % s, [128, 2, NM, 65], BF16, st) for s in range(2)]
                    dsKB = [newds("dsKB%d" % s) for s in range(2)]
                    MASK = sb("MASK", [128, NM, 512], BF16, st)
                    dsM = newds("dsM")
                else:
                    KT = [sb("KT1_%d" % s, [128, 4, T], BF16, st) for s in range(2)]
                    V1 = [sb("V1_%d" % s, [128, 4, 16, 129], BF16, st) for s in range(2)]
                    dsKV = [newds("dsKV%d" % s) for s in range(2)]

                state = dict(m_s_free=[None, None], m_p_free=[None] * 3,
                             m_acc_free=None, m_accs_free=None, m_mt_free=None, m_psT_free=None, kbc=0)
                m_q_free = [None, None]

                def acc_ap(n):
                    bk, o = n // per_bank, (n % per_bank) * dvp
                    return psA[:, bk, o:o + dvp], bk

                def qgroup(qtile, g, chunk, blocks, m_load, m_buf_free_cb):
                    nb = len(blocks)
                    started = set()
                    m_exp = {}
                    m_s = {}
                    last_pv = None

                    def emit_s(kb):
                        la, lb, _, _, _, _ = blocks[kb]
                        sl = state["kbc_base"] + kb
                        s = sl % 2
                        pe.wait(m_load, state["m_s_free"][s])
                        nc.tensor.matmul(psS[:, s, 0, :], lhsT=la, rhs=qtile[0:64, g * 512:(g + 1) * 512], start=True, stop=True)
                        ins = nc.tensor.matmul(psS[:, s, 1, :], lhsT=lb, rhs=qtile[64:128, g * 512:(g + 1) * 512], start=True, stop=True)
                        m_s[kb] = pe.mark(ins)

                    state["kbc_base"] = state["kbc"]
                    emit_s(0)
                    if nb > 1:
                        emit_s(1)
                    for kb in range(nb):
                        _, _, va, vb, mi, jl = blocks[kb]
                        sl = state["kbc_base"] + kb
                        s = sl % 2
                        ps3 = sl % 3
                        act.wait(m_s[kb], state["m_p_free"][ps3])
                        me = act.mark(nc.scalar.activation(out=PP[:, ps3, :, :], in_=psS[:, s, :, :], func=AF.Exp, scale=0.125))
                        state["m_s_free"][s] = me
                        if mi is not None:
                            dve.wait(me)
                            me = dve.mark(nc.vector.tensor_tensor(out=PP[:, ps3, :, :], in0=PP[:, ps3, :, :],
                                                                  in1=MASK[:, mi, :].unsqueeze(1).to_broadcast([128, 2, 512]), op=ALU.mult))
                        if kb + 2 < nb:
                            emit_s(kb + 2)
                        while state.get("deferred") and state["deferred"][0][0] <= kb:
                            state["deferred"].pop(0)[1]()
                        pe.wait(me, state["m_acc_free"])
                        ins = None
                        for mp in range(2):
                            vv = va if mp == 0 else vb
                            for j in jl:
                                o_ap, bk = acc_ap(mp * 4 + j)
                                first = bk not in started
                                started.add(bk)
                                ins = nc.tensor.matmul(o_ap, lhsT=PP[:, ps3, mp, j * 128:(j + 1) * 128], rhs=vv,
                                                       start=first, stop=(kb == nb - 1), skip_group_check=True)
                        m_pv = pe.mark(ins)
                        state["m_p_free"][ps3] = m_pv
                        last_pv = m_pv
                    state["kbc"] = state["kbc_base"] + nb
                    m_buf_free_cb(last_pv)
                    dve.wait(last_pv, state["m_accs_free"])
                    m = None
                    for bk in range(nbank):
                        ncols = min(per_bank, 8 - bk * per_bank) * dvp
                        m = dve.mark(nc.vector.tensor_copy(out=ACCS[:, bk * per_bank * dvp: bk * per_bank * dvp + ncols], in_=psA[:, bk, 0:ncols]))
                    state["m_acc_free"] = m
                    m_cp = m

                    a3 = ACCS[:].rearrange("p (n d) -> p n d", d=dvp)
                    box = {}

                    def ep_dve1(m=m_cp):
                        dve.wait(m)
                        m = dve.mark(nc.vector.reciprocal(out=RDEN[:], in_=a3[:, :, dv]))
                        dve.wait(m, state["m_mt_free"])
                        if layer == 0:
                            mt3 = MT[:].rearrange("p (j f) -> p j f", f=128)
                            for mp in range(2):
                                m = dve.mark(nc.vector.tensor_tensor(out=mt3[:, :, mp * 64:(mp + 1) * 64], in0=a3[:, mp * 4:(mp + 1) * 4, 0:64],
                                                                     in1=RDEN[:, mp * 4:(mp + 1) * 4].unsqueeze(2).to_broadcast([128, 4, 64]), op=ALU.mult))
                            state["m_accs_free"] = m
                            box["m_mt"] = m
                        else:
                            o13 = O1[:].rearrange("p (j f) -> p j f", f=128)
                            o23 = O2[:].rearrange("p (j f) -> p j f", f=128)
                            m = dve.mark(nc.vector.tensor_scalar(out=RDEN[:, 4:8], in0=RDEN[:, 4:8], scalar1=NEGLAM[:, 0:1], scalar2=None, op0=ALU.mult))
                            dve.wait(m)
                            nc.vector.tensor_tensor(out=o13, in0=a3[:, 0:4, 0:128], in1=RDEN[:, 0:4].unsqueeze(2).to_broadcast([128, 4, 128]), op=ALU.mult)
                            m = dve.mark(nc.vector.tensor_tensor(out=o23, in0=a3[:, 4:8, 0:128], in1=RDEN[:, 4:8].unsqueeze(2).to_broadcast([128, 4, 128]), op=ALU.mult))
                            state["m_accs_free"] = m
                            dve.wait(m)
                            m = dve.mark(nc.vector.tensor_tensor(out=O1[:], in0=O1[:], in1=O2[:], op=ALU.add))
                            dve.wait(m)
                            m = dve.mark(nc.vector.tensor_tensor(out=O2[:], in0=O1[:], in1=O1[:], op=ALU.mult))
                            dve.wait(m)
                            box["m_ss"] = dve.mark(nc.vector.tensor_reduce(out=SS4[:], in_=o23, axis=AX.X, op=ALU.add))

                    def ep_act():
                        act.wait(box["m_ss"])
                        m = act.mark(nc.scalar.activation(out=RS4[:], in_=SS4[:], func=AF.Ln, bias=EPST[:, 0:1], scale=1.0 / 128))
                        act.wait(m)
                        box["m_rs"] = act.mark(nc.scalar.activation(out=RS4[:], in_=RS4[:], func=AF.Exp, scale=-0.5))

                    def ep_dve2():
                        o13 = O1[:].rearrange("p (j f) -> p j f", f=128)
                        dve.wait(box["m_rs"])
                        m = dve.mark(nc.vector.tensor_tensor(out=o13, in0=o13, in1=RS4[:].unsqueeze(2).to_broadcast([128, 4, 128]), op=ALU.mult))
                        dve.wait(m)
                        box["m_mt"] = dve.mark(nc.vector.tensor_tensor(out=MT[:].rearrange("p (j f) -> p j f", f=128), in0=o13,
                                                                       in1=GAINC[:].unsqueeze(1).to_broadcast([128, 4, 128]), op=ALU.mult))

                    def ep_tr(chunk=chunk, g=g):
                        pe.wait(box["m_mt"], state["m_psT_free"])
                        for j in range(4):
                            ins = nc.tensor.transpose(psT[:, j * 128:(j + 1) * 128], MT[:, j * 128:(j + 1) * 128], IDENT[:])
                        m_t = pe.mark(ins)
                        state["m_mt_free"] = m_t
                        dve.wait(m_t)
                        state["m_psT_free"] = dve.mark(nc.vector.tensor_copy(out=AT[:, chunk, g * 512:(g + 1) * 512], in_=psT[:, 0:512]))

                    if layer == 0:
                        state["deferred"] = [(4, ep_dve1), (10, ep_tr)]
                    else:
                        state["deferred"] = [(4, ep_dve1), (16, ep_act), (18, ep_dve2), (24, ep_tr)]

                def load_q(u, slot):
                    sp.wait(m_q_free[slot])
                    return dma(sp, dsQ[slot], QT[slot][:], qt[layer][u * 128:(u + 1) * 128, :])

                if layer == 0:
                    cc_k, cc_v = cc[0]
                    kA = kt_all[0][0].rearrange("(r p) t -> p r t", p=128)
                    vA = v_all[0][0].rearrange("(r h p) f -> h p r f", r=4, h=2)
                    ds_rl = newds("ds_rl")
                    ds_win = newds("ds_win")
                    pool.wait(ds_pad.all())
                    for u in range(4):
                        pool.wait(cc_k[1 + u].all())
                        dma(pool, ds_rl, ktB_pad[u * 128:(u + 1) * 128, 1024:1024 + 4 * T].rearrange("p (r t) -> p r t", r=4),
                            kt_all[0][1 + u].rearrange("(r p) t -> p r t", p=128))
                    for gi in range(4):
                        pool.wait(cc_v[1 + gi].all())
                        srcv = v_all[0][1 + gi].rearrange("(r h p) f -> h p r f", r=4, h=2)
                        for hh in range(2):
                            hB = 2 * gi + hh
                            dma(pool, ds_rl, vB_pad[hB * 128:(hB + 1) * 128, 8 * 65:72 * 65].rearrange("p (r f) -> p r f", r=4), srcv[hh])
                    pid = nc.gpsimd.partition_id()
                    rank = pid % 4
                    pool.wait(ds_rl.all())
                    dma(pool, ds_win, ktB_win, ktB_pad[:, bass.ds(rank * T, 4096)])
                    dma(pool, ds_win, vB_win, vB_pad[:, bass.ds(rank * (16 * 65), 32 * 65)])
                    m_ka_free = None
                    qslot = 0
                    mq = load_q(0, 0)
                    for u in range(4):
                        if u == 1:
                            pool.wait(m_ka_free)
                            casts_layer0_rest()
                            dma(sp, dsM, MASK[:], masks_in.rearrange("m p q -> p m q"))
                        kvh = u // 2
                        if u % 2 == 0:
                            sp.wait(m_ka_free, cc_k[0].all(), cc_v[0].all())
                            for half in range(2):
                                dma(sp, dsKA, KTA[half * 64:(half + 1) * 64, :, :], kA[kvh * 64:(kvh + 1) * 64, :, :])
                            dma(sp, dsKA, VA[:].rearrange("p r i d -> p r (i d)"), vA[kvh])
                        mq_next = load_q(u + 1, (qslot + 1) % 2) if u < 3 else None
                        lastholder = []
                        for g in range(4):
                            blocks = []
                            for r in range(4):
                                for i in range(16):
                                    blocks.append((KTA[0:64, r, i * 128:(i + 1) * 128], KTA[64:128, r, i * 128:(i + 1) * 128],
                                                   VA[:, r, i, :], VA[:, r, i, :], None, [0, 1, 2, 3]))
                            qgroup(QT[qslot], g, u, blocks, [mq, dsKA.all()], lambda m: lastholder.append(m))
                        m_q_free[qslot] = lastholder[-1]
                        m_ka_free = lastholder[-1]
                        qslot = (qslot + 1) % 2
                        mq = mq_next
                    m_kb_free = [None, None]
                    widx = 0
                    mq = load_q(4, qslot)
                    for u in range(4):
                        mq_next = None
                        lastholder = []
                        for g in range(4):
                            ws = widx % 2
                            widx += 1
                            sp.wait(m_kb_free[ws], ds_win.all())
                            dma(sp, dsKB[ws], KTB[ws][:], ktB_win[u * 128:(u + 1) * 128, g * 512:g * 512 + NM * 128])
                            for hh in range(2):
                                h = 2 * u + hh
                                dma(sp, dsKB[ws], VB[ws][:, hh, :, :].rearrange("p m d -> p (m d)"),
                                    vB_win[h * 128:(h + 1) * 128, g * 4 * 65:(g * 4 + NM) * 65])
                            if g == 0 and u < 3:
                                mq_next = load_q(4 + u + 1, (qslot + 1) % 2)
                            blocks = []
                            for mi in range(NM):
                                jl = [j for j in range(4) if j <= mi <= j + 16]
                                blocks.append((KTB[ws][0:64, mi * 128:(mi + 1) * 128], KTB[ws][64:128, mi * 128:(mi + 1) * 128],
                                               VB[ws][:, 0, mi, :], VB[ws][:, 1, mi, :], mi, jl))

                            def cb(m, ws=ws):
                                m_kb_free[ws] = m
                                lastholder.append(m)
                            qgroup(QT[qslot], g, 4 + u, blocks, [mq, dsKB[ws].all(), dsM.all()], cb)
                        m_q_free[qslot] = lastholder[-1]
                        qslot = (qslot + 1) % 2
                        mq = mq_next
                else:
                    cc_k, cc_v = cc[1]
                    m_kv_free = [None, None]

                    def load_kv(h, slot):
                        sp.wait(m_kv_free[slot], cc_k[h].all(), cc_v[h].all())
                        dma(sp, dsKV[slot], KT[slot][:], kt_all[1][h].rearrange("(r p) t -> p r t", p=128))
                        dma(sp, dsKV[slot], V1[slot][:].rearrange("p r i d -> p r (i d)"), v_all[1][h].rearrange("(r p) f -> p r f", p=128))

                    mq = load_q(0, 0)
                    load_kv(0, 0)
                    for h in range(8):
                        slot = h % 2
                        mq_next = None
                        if h == 1:
                            pool.wait(m_kv_free[0])
                            casts_layer1_rest()
                        if h < 7:
                            mq_next = load_q(h + 1, 1 - slot)
                            load_kv(h + 1, 1 - slot)
                        lastholder = []
                        for g in range(4):
                            blocks = []
                            for r in range(4):
                                for i in range(16):
                                    blocks.append((KT[slot][0:64, r, i * 128:(i + 1) * 128], KT[slot][64:128, r, i * 128:(i + 1) * 128],
                                                   V1[slot][:, r, i, :], V1[slot][:, r, i, :], None, [0, 1, 2, 3]))
                            qgroup(QT[slot], g, h, blocks, [mq, dsKV[slot].all()], lambda m: lastholder.append(m))
                        m_q_free[slot] = lastholder[-1]
                        m_kv_free[slot] = lastholder[-1]
                        mq = mq_next
                while state.get("deferred"):
                    state["deferred"].pop(0)[1]()
                barrier()

        def outproj(layer):
            with ExitStack() as st:
                WO = sb("WO%d" % layer, [128, 8, D], BF16, st)
                dsO = newds("dsO%d" % layer)
                psO = psum("psO%d" % layer, [128, 2, 512], F32, st)
                sp.wait(ds_w["out%d" % layer].all())
                dma(sp, dsO, WO[:], wb_out[layer].rearrange("(k p) c -> p k c", p=128))
                m_free = [None, None]
                cnt = 0
                for i in range(NT):
                    for c in range(2):
                        b = cnt % 2
                        cnt += 1
                        pe.wait(dsO.all(), m_free[b])
                        for k in range(8):
                            ins = nc.tensor.matmul(psO[:, b, :], lhsT=AT[:, k, i * 128:(i + 1) * 128], rhs=WO[:, k, c * 512:(c + 1) * 512],
                                                   start=(k == 0), stop=(k == 7))
                        mm = pe.mark(ins)
                        dve.wait(mm)
                        m_free[b] = dve.mark(nc.vector.tensor_tensor(out=X[:, i, c * 512:(c + 1) * 512], in0=X[:, i, c * 512:(c + 1) * 512],
                                                                     in1=psO[:, b, :], op=ALU.add))
                barrier()

        def ffn(layer):
            with ExitStack() as st:
                WD = sb("WD%d" % layer, [128, NF, D], BF16, st)
                dsD = newds("dsD%d" % layer)
                WGU = [sb("WGU%d_%d" % (layer, s), [128, 2, 8, 256], BF16, st) for s in range(2)]
                dsGU = [newds("dsGU%d_%d" % (layer, s)) for s in range(2)]
                AFT = sb("AFT%d" % layer, [128, NF, 512], BF16, st)
                SG = [sb("SG%d_%d" % (layer, s), [128, 512], F32, st) for s in range(2)]
                psG = psum("psG%d" % layer, [128, 2, 512], F32, st)
                psU = psum("psU%d" % layer, [128, 2, 512], F32, st)
                psD = psum("psD%d" % layer, [128, 2, 512], F32, st)
                sp.wait(ds_w["d%d" % layer].all(), ds_w["g%d" % layer].all(), ds_w["u%d" % layer].all())
                for half in range(2):
                    dma(sp, dsD, WD[:, half * 11:(half + 1) * 11, :],
                        wb_d[layer][half * 11 * 128:(half + 1) * 11 * 128, :].rearrange("(f p) c -> p f c", p=128))
                m_gu_free = [None, None]
                m_g_free = [None, None]
                m_u_free = [None, None]
                m_sg_free = [None, None]
                m_d_free = [None, None]
                m_down_last = None
                lc = 0
                fc_cnt = 0
                dcnt = 0

                def load_gu(fg, slot):
                    sp.wait(m_gu_free[slot])
                    dma(sp, dsGU[slot], WGU[slot][:, 0, :, :], wb_g[layer][:, fg * 256:(fg + 1) * 256].rearrange("(k p) c -> p k c", p=128))
                    dma(sp, dsGU[slot], WGU[slot][:, 1, :, :], wb_u[layer][:, fg * 256:(fg + 1) * 256].rearrange("(k p) c -> p k c", p=128))

                seq = [(tg, fg) for tg in range(4) for fg in range(11)]
                load_gu(seq[0][1], 0)
                for si, (tg, fg) in enumerate(seq):
                    slot = si % 2
                    if si + 1 < len(seq):
                        load_gu(seq[si + 1][1], 1 - slot)
                    for fc in range(2):
                        f = fg * 2 + fc
                        b = fc_cnt % 2
                        fc_cnt += 1
                        pe.wait(dsGU[slot].all(), m_g_free[b], m_u_free[b])
                        for k in range(8):
                            nc.tensor.matmul(psG[:, b, :], lhsT=WGU[slot][:, 0, k, fc * 128:(fc + 1) * 128], rhs=AT[:, k, tg * 512:(tg + 1) * 512],
                                             start=(k == 0), stop=(k == 7))
                        for k in range(8):
                            ins = nc.tensor.matmul(psU[:, b, :], lhsT=WGU[slot][:, 1, k, fc * 128:(fc + 1) * 128], rhs=AT[:, k, tg * 512:(tg + 1) * 512],
                                                   start=(k == 0), stop=(k == 7))
                        mm = pe.mark(ins)
                        m_gu_free[slot] = mm
                        act.wait(mm, m_sg_free[b])
                        ms = act.mark(nc.scalar.activation(out=SG[b][:], in_=psG[:, b, :], func=AF.Silu))
                        m_g_free[b] = ms
                        dve.wait(ms, m_down_last)
                        ma = dve.mark(nc.vector.tensor_tensor(out=AFT[:, f, :], in0=SG[b][:], in1=psU[:, b, :], op=ALU.mult))
                        m_u_free[b] = ma
                        m_sg_free[b] = ma
                        last_a = ma
                    if fg == 10:
                        for j in range(4):
                            for c in range(2):
                                b = dcnt % 2
                                dcnt += 1
                                pe.wait(last_a, dsD.all(), m_d_free[b])
                                for f in range(NF):
                                    ins = nc.tensor.matmul(psD[:, b, :], lhsT=AFT[:, f, j * 128:(j + 1) * 128], rhs=WD[:, f, c * 512:(c + 1) * 512],
                                                           start=(f == 0), stop=(f == NF - 1))
                                mm = pe.mark(ins)
                                m_down_last = mm
                                i = tg * 4 + j
                                dve.wait(mm)
                                m_d_free[b] = dve.mark(nc.vector.tensor_tensor(out=X[:, i, c * 512:(c + 1) * 512], in0=X[:, i, c * 512:(c + 1) * 512],
                                                                               in1=psD[:, b, :], op=ALU.add))
                barrier()

        def final():
            with ExitStack() as st:
                G, RSTD = rms_rstd(st, lambda i: X[:, i, :], NT, final_norm[0:1, :], "fin")
                OB = [sb("OB%d" % s, [128, D], F32, st) for s in range(2)]
                ds_o = newds("ds_out")
                m_free = [None, None]
                for i in range(NT):
                    s = i % 2
                    dve.wait(m_free[s])
                    m = dve.mark(nc.vector.scalar_tensor_tensor(out=OB[s][:], in0=X[:, i, :], scalar=RSTD[:, i:i + 1], in1=G[:],
                                                                op0=ALU.mult, op1=ALU.mult))
                    sp.wait(m)
                    m_free[s] = dma(sp, ds_o, out[i * 128:(i + 1) * 128, :], OB[s][:])
                sp.wait(ds_o.all())
                pool.wait(ds_o.all())

        cc = {}
        dve.wait(ds_x.all(), ds_c.all())
        act.wait(ds_x.all(), ds_c.all())
        pe.wait(ds_c.all())
        pool.wait(ds_c.all())

        def dump(layer):
            barrier()
            dsd = newds("ds_dbg")
            for i in range(NT):
                dma(sp, dsd, dbg["X"][i * 128:(i + 1) * 128, :], X[:, i, :])
            dma(sp, dsd, dbg["AT"], AT[:].rearrange("p k t -> p (k t)"))
            dma(sp, dsd, dbg["qt"], qt[layer])
            for u in range(NKU[layer]):
                dma(sp, dsd, dbg["kt"][u * 512:(u + 1) * 512, :], kt_all[layer][u])
            for g in range(NVG[layer]):
                nr = 4 * VGH[layer] * 128
                dma(sp, dsd, dbg["v"][g * nr:(g + 1) * nr, 0:VF[layer]], v_all[layer][g])
            sp.wait(dsd.all())
            pool.wait(dsd.all())

        steps = [
            ("norm_a0", lambda: norm_AT(attn_norm[0:1, :], "a0")),
            ("inproj0", lambda: cc.__setitem__(0, inproj(0))),
            ("attn0", lambda: attention(0)),
            ("outproj0", lambda: outproj(0)),
            ("norm_f0", lambda: norm_AT(ffn_norm[0:1, :], "f0")),
            ("ffn0", lambda: ffn(0)),
            ("norm_a1", lambda: norm_AT(attn_norm[1:2, :], "a1")),
            ("inproj1", lambda: cc.__setitem__(1, inproj(1))),
            ("attn1", lambda: attention(1)),
            ("outproj1", lambda: outproj(1)),
            ("norm_f1", lambda: norm_AT(ffn_norm[1:2, :], "f1")),
            ("ffn1", lambda: ffn(1)),
            ("final", final),
        ]
        for name, fn in steps:
            fn()
            if debug and debug == name:
                dump(0 if name in ("norm_a0", "inproj0", "attn0", "outproj0", "norm_f0", "ffn0", "norm_a1") else 1)
                break
    return nc


def _const_tables():
    theta = 10000.0
    pos = np.arange(S, dtype=np.float32)
    inv64 = (theta ** (-np.arange(0, 64, 2, dtype=np.float32) / 64)).astype(np.float32)
    inv32 = (theta ** (-np.arange(0, 32, 2, dtype=np.float32) / 32)).astype(np.float32)
    ang1 = (pos[:, None] * inv64[None, :]).astype(np.float32)
    rows = (np.arange(S) // 64).astype(np.float32)
    cols = (np.arange(S) % 64).astype(np.float32)
    ang2 = np.concatenate([rows[:, None] * inv32[None, :], cols[:, None] * inv32[None, :]], axis=-1).astype(np.float32)
    rope = np.stack([np.cos(ang1.astype(np.float64)), np.sin(ang1.astype(np.float64)),
                     np.cos(ang2.astype(np.float64)), np.sin(ang2.astype(np.float64))], 0).astype(np.float32)
    k = np.arange(128)[:, None]
    q = np.arange(512)[None, :]
    masks = np.zeros((NM, 128, 512), np.float32)
    for m in range(NM):
        d = 128 * m + k - q - 1024
        ad = np.abs(d)
        masks[m] = (ad <= 64).astype(np.float32) + ((d % 4 == 0) & (ad <= 256)) + ((d % 16 == 0) & (ad <= 1024))
    ident = np.eye(128, dtype=np.float32)
    return rope, masks.astype(ml_dtypes.bfloat16), ident.astype(ml_dtypes.bfloat16)


_NC_CACHE = {}


def kernel(x, attn_norm, ffn_norm, final_norm, w_in_even, a_q_norm, a_k_norm, w_out_even,
           w_in_odd, lambda_q1, lambda_k1, lambda_q2, lambda_k2, c_sub_norm, w_out_odd,
           w_gate, w_up, w_down, _debug=False):
    f = lambda a: np.ascontiguousarray(np.asarray(a, dtype=np.float32))
    x = f(x)
    rope, masks, ident = _const_tables()
    shared = {
        "attn_norm": f(attn_norm), "ffn_norm": f(ffn_norm), "final_norm": f(final_norm).reshape(1, D),
        "w_in_even": f(w_in_even)[0], "a_q_norm": f(a_q_norm), "a_k_norm": f(a_k_norm),
        "w_out_even": f(w_out_even)[0], "w_in_odd": f(w_in_odd)[0],
        "lambda_q1": f(lambda_q1), "lambda_k1": f(lambda_k1), "lambda_q2": f(lambda_q2), "lambda_k2": f(lambda_k2),
        "c_sub_norm": f(c_sub_norm), "w_out_odd": f(w_out_odd)[0],
        "w_gate": f(w_gate), "w_up": f(w_up), "w_down": f(w_down),
        "masks": masks, "ident": ident,
    }
    in_maps = []
    for c in range(NCORE):
        b, r = c // 4, c % 4
        m = dict(shared)
        m["x"] = np.ascontiguousarray(x[b, r * T:(r + 1) * T, :])
        m["rope"] = np.ascontiguousarray(rope[:, r * T:(r + 1) * T, :])
        in_maps.append(m)
    key = _debug
    if key not in _NC_CACHE:
        _NC_CACHE[key] = build(debug=key)
    nc = _NC_CACHE[key]
    res = run_bass_kernel_spmd(nc, in_maps, core_ids=list(range(NCORE)))
    outp = np.zeros((2, S, D), np.float32)
    for c in range(NCORE):
        b, r = c // 4, c % 4
        outp[b, r * T:(r + 1) * T, :] = np.asarray(res.results[c]["out"])
    if _debug:
        return outp, res
    return outp
```

```python
import math
from contextlib import ExitStack

import numpy as np
import ml_dtypes

import concourse.bass as bass
import concourse.mybir as mybir
from concourse.bass_utils import run_bass_kernel_spmd

F32 = mybir.dt.float32
BF16 = mybir.dt.bfloat16
ALU = mybir.AluOpType
AF = mybir.ActivationFunctionType
AX = mybir.AxisListType

NCORE = 8
S = 8192
D = 1024
T = 2048
NT = 16
DFF = 2816
NF = 22
EPS = 1e-6
LAM_INIT = 0.8 - 0.6 * math.exp(-0.3 * 1)
KPAD = 1024 + S + 1536
VBLK = 8 + 64 + 12
NM = 20


class TL:
    def __init__(self, nc, es, eng, name):
        self.e = eng
        self.sem = es.enter_context(nc.semaphore(name))
        self.n = 0
        self.seen = {}

    def wait(self, *ms):
        for m in ms:
            if m is None:
                continue
            if isinstance(m, list):
                self.wait(*m)
                continue
            sem, v = m
            if sem is self.sem and False:
                continue
            k = id(sem)
            if self.seen.get(k, 0) >= v:
                continue
            self.e.wait_ge(sem, v)
            self.seen[k] = v

    def mark(self, ins):
        self.n += 1
        ins.then_inc(self.sem, 1)
        return (self.sem, self.n)


class DS:
    def __init__(self, nc, es, name):
        self.sem = es.enter_context(nc.semaphore(name))
        self.n = 0

    def add(self, ins):
        self.n += 16
        ins.then_inc(self.sem, 16)
        return (self.sem, self.n)

    def all(self):
        return (self.sem, self.n) if self.n else None


def build(debug=False):
    nc = bass.Bass("TRN2", target_bir_lowering=False)

    def din(name, shape, dt=F32):
        return nc.dram_tensor(name, list(shape), dt, kind="ExternalInput").ap()

    def dint(name, shape, dt=BF16):
        return nc.dram_tensor(name, list(shape), dt, kind="Internal").ap()

    x_in = din("x", [T, D])
    attn_norm = din("attn_norm", [2, D])
    ffn_norm = din("ffn_norm", [2, D])
    final_norm = din("final_norm", [1, D])
    w_in_even = din("w_in_even", [D, 2304])
    a_q_norm = din("a_q_norm", [1, 64])
    a_k_norm = din("a_k_norm", [1, 64])
    w_out_even = din("w_out_even", [D, D])
    w_in_odd = din("w_in_odd", [D, 3072])
    lam_in = [din(n, [1, 64]) for n in ("lambda_q1", "lambda_k1", "lambda_q2", "lambda_k2")]
    c_sub_norm = din("c_sub_norm", [1, 128])
    w_out_odd = din("w_out_odd", [D, D])
    w_gate = din("w_gate", [2, D, DFF])
    w_up = din("w_up", [2, D, DFF])
    w_down = din("w_down", [2, DFF, D])
    rope_in = din("rope", [4, T, 32])
    masks_in = din("masks", [NM, 128, 512], BF16)
    ident_in = din("ident", [128, 128], BF16)
    out = nc.dram_tensor("out", [T, D], F32, kind="ExternalOutput").ap()

    wb_in = [dint("wb_in0", [D, 2304]), dint("wb_in1", [D, 3072])]
    wb_out = [dint("wb_out0", [D, D]), dint("wb_out1", [D, D])]
    wb_g = [dint(f"wb_g{l}", [D, DFF]) for l in range(2)]
    wb_u = [dint(f"wb_u{l}", [D, DFF]) for l in range(2)]
    wb_d = [dint(f"wb_d{l}", [DFF, D]) for l in range(2)]

    qt = [dint("qt0", [8 * 128, T]), dint("qt1", [8 * 128, T])]
    NKU = [5, 8]
    VGH = [2, 1]
    NVG = [5, 8]
    VF = [16 * 65, 16 * 129]
    kt_loc = [[dint("kt%d_loc%d" % (l, u), [128, T]) for u in range(NKU[l])] for l in range(2)]
    kt_all = [[dint("kt%d_all%d" % (l, u), [4 * 128, T]) for u in range(NKU[l])] for l in range(2)]
    v_loc = [[dint("v%d_loc%d" % (l, g), [VGH[l] * 128, VF[l]]) for g in range(NVG[l])] for l in range(2)]
    v_all = [[dint("v%d_all%d" % (l, g), [4 * VGH[l] * 128, VF[l]]) for g in range(NVG[l])] for l in range(2)]
    ktB_pad = dint("ktB_pad", [4 * 128, KPAD])
    vB_pad = dint("vB_pad", [8 * 128, VBLK * 65])
    ktB_win = dint("ktB_win", [4 * 128, 4096])
    vB_win = dint("vB_win", [8 * 128, 32 * 65])

    dbg = {}
    if debug:
        dbg["X"] = nc.dram_tensor("dbg_X", [T, D], F32, kind="ExternalOutput").ap()
        dbg["AT"] = nc.dram_tensor("dbg_AT", [128, 8 * T], BF16, kind="ExternalOutput").ap()
        dbg["qt"] = nc.dram_tensor("dbg_qt", [8 * 128, T], BF16, kind="ExternalOutput").ap()
        dbg["kt"] = nc.dram_tensor("dbg_kt", [8 * 512, T], BF16, kind="ExternalOutput").ap()
        dbg["v"] = nc.dram_tensor("dbg_v", [8 * 1024, 16 * 129], BF16, kind="ExternalOutput").ap()

    es = ExitStack()
    with es:
        pe = TL(nc, es, nc.tensor, "t_pe")
        act = TL(nc, es, nc.scalar, "t_act")
        dve = TL(nc, es, nc.vector, "t_dve")
        pool = TL(nc, es, nc.gpsimd, "t_pool")
        sp = TL(nc, es, nc.sync, "t_sp")
        engines = [pe, act, dve, pool, sp]

        def sb(name, shape, dt, stack=None):
            return (stack or es).enter_context(nc.sbuf_tensor(name, list(shape), dt))

        def psum(name, shape, dt, stack):
            return stack.enter_context(nc.psum_tensor(name, list(shape), dt))

        def dma(q, ds, out_ap, in_ap):
            return ds.add(q.e.dma_start(out=out_ap, in_=in_ap))

        X = sb("X", [128, NT, D], F32)
        AT = sb("AT", [128, 8, T], BF16)
        ROPE = sb("ROPE", [128, 4, NT, 32], F32)
        IDENT = sb("IDENT", [128, 128], BF16)
        EPST = sb("EPST", [128, 1], F32)
        FENCE = sb("FENCE", [128, 4], F32)
        NEGLAM = sb("NEGLAM", [128, 1], F32)
        GAINC = sb("GAINC", [128, 128], F32)
        GQK = sb("GQK", [128, 2, 64], F32)
        ZERO = sb("ZERO", [128, 2048], BF16)

        fence_ps_stack = ExitStack()

        def barrier():
            ms = []
            ms.append(dve.mark(nc.vector.memset(FENCE[:, 0:1], 0.0)))
            ms.append(pool.mark(nc.gpsimd.memset(FENCE[:, 1:2], 0.0)))
            ms.append(act.mark(nc.scalar.activation(out=FENCE[:, 2:3], in_=EPST[:, 0:1], func=AF.Copy)))
            ms.append((pe.sem, pe.n) if pe.n else None)
            ms += [d.all() for d in all_ds]
            for e in engines:
                e.wait(*ms)

        all_ds = []

        def newds(name, in_barrier=True):
            d = DS(nc, es, name)
            if in_barrier:
                all_ds.append(d)
            return d

        ds_c = newds("ds_c")
        ds_x = newds("ds_x")
        nc.vector.memset(EPST[:], EPS)
        m_zero = pool.mark(nc.gpsimd.memset(ZERO[:], 0.0))
        dma(sp, ds_c, IDENT[:], ident_in)
        dma(sp, ds_c, ROPE[:], rope_in.rearrange("c (i p) d -> p c i d", p=128))
        dma(sp, ds_c, GQK[:, 0, :], a_q_norm.partition_broadcast(128))
        dma(sp, ds_c, GQK[:, 1, :], a_k_norm.partition_broadcast(128))
        dma(sp, ds_c, GAINC[:], c_sub_norm.partition_broadcast(128))
        for i in range(NT):
            dma(sp, ds_x, X[:, i, :], x_in[i * 128:(i + 1) * 128, :])

        ds_w = {}

        def cast_w(key, dst, src, rows):
            d = newds("ds_w_" + key, in_barrier=False)
            ds_w[key] = d
            for r0 in range(0, rows, 128):
                dma(pool, d, dst[r0:r0 + 128, :], src[r0:r0 + 128, :])

        cast_w("in0", wb_in[0], w_in_even, D)
        ds_pad = newds("ds_pad")
        sp.wait(m_zero)
        for u in range(4):
            dma(sp, ds_pad, ktB_pad[u * 128:(u + 1) * 128, 0:1024], ZERO[:, 0:1024])
            dma(sp, ds_pad, ktB_pad[u * 128:(u + 1) * 128, 1024 + S:KPAD], ZERO[:, 0:1536])
        for h in range(8):
            dma(sp, ds_pad, vB_pad[h * 128:(h + 1) * 128, 0:8 * 65], ZERO[:, 0:8 * 65])
            dma(sp, ds_pad, vB_pad[h * 128:(h + 1) * 128, 72 * 65:VBLK * 65], ZERO[:, 0:12 * 65])
        def casts_layer0_rest():
            cast_w("out0", wb_out[0], w_out_even, D)
            cast_w("g0", wb_g[0], w_gate[0], D)
            cast_w("u0", wb_u[0], w_up[0], D)
            cast_w("d0", wb_d[0], w_down[0], DFF)
            cast_w("in1", wb_in[1], w_in_odd, D)
            cast_w("out1", wb_out[1], w_out_odd, D)

        def casts_layer1_rest():
            cast_w("g1", wb_g[1], w_gate[1], D)
            cast_w("u1", wb_u[1], w_up[1], D)
            cast_w("d1", wb_d[1], w_down[1], DFF)

        with ExitStack() as st:
            LQ = sb("LQ", [128, 4, 64], F32, st)
            LS = sb("LS", [128, 4], F32, st)
            LJ = sb("LJ", [128, 64], F32, st)
            ds_l = newds("ds_l")
            for j in range(4):
                dma(sp, ds_l, LQ[:, j, :], lam_in[j].partition_broadcast(128))
            dve.wait(ds_l.all(), ds_c.all())
            nc.vector.memset(LS[:], 0.0)
            m = None
            for j in range(2):
                m = dve.mark(nc.vector.tensor_tensor(out=LJ[:], in0=LQ[:, 2 * j, :], in1=LQ[:, 2 * j + 1, :], op=ALU.mult))
                dve.wait(m)
                m = dve.mark(nc.vector.tensor_reduce(out=LS[:, j:j + 1], in_=LJ[:], axis=AX.X, op=ALU.add))
                dve.wait(m)
            act.wait(m)
            m = act.mark(nc.scalar.activation(out=LS[:, 2:4], in_=LS[:, 0:2], func=AF.Exp))
            dve.wait(m)
            m = dve.mark(nc.vector.tensor_tensor(out=NEGLAM[:], in0=LS[:, 3:4], in1=LS[:, 2:3], op=ALU.subtract))
            dve.wait(m)
            m = dve.mark(nc.vector.tensor_scalar(out=NEGLAM[:], in0=NEGLAM[:], scalar1=-LAM_INIT, scalar2=None, op0=ALU.add))
            dve.wait(m)
            dve.mark(nc.vector.tensor_scalar(out=GAINC[:], in0=GAINC[:], scalar1=1.0 - LAM_INIT, scalar2=None, op0=ALU.mult))
            barrier()

        def rms_rstd(st, src_of_tile, nt, gdram_row, tag):
            G = sb("G" + tag, [128, D], F32, st)
            JUNK = sb("JUNK" + tag, [128, D], BF16, st)
            SS = sb("SS" + tag, [128, nt], F32, st)
            RSTD = sb("RSTD" + tag, [128, nt], F32, st)
            dsg = newds("ds_g" + tag)
            dma(sp, dsg, G[:], gdram_row.partition_broadcast(128))
            m0 = dve.mark(nc.vector.memset(SS[:], 0.0))
            act.wait(m0)
            m = None
            for i in range(nt):
                m = act.mark(nc.scalar.activation(out=JUNK[:], in_=src_of_tile(i), func=AF.Square,
                                                  accum_out=SS[:, i:i + 1]))
            act.wait(m)
            m = act.mark(nc.scalar.activation(out=RSTD[:], in_=SS[:], func=AF.Sqrt, bias=EPST[:, 0:1], scale=1.0 / D))
            dve.wait(m, dsg.all())
            m = dve.mark(nc.vector.reciprocal(out=RSTD[:], in_=RSTD[:]))
            dve.wait(m)
            return G, RSTD

        def norm_AT(gdram_row, tag):
            with ExitStack() as st:
                G, RSTD = rms_rstd(st, lambda i: X[:, i, :], NT, gdram_row, tag)
                HB = sb("HB" + tag, [128, 2, D], BF16, st)
                tp = psum("tp" + tag, [128, 2, D], BF16, st)
                m_tr = [None, None]
                m_ev = [None, None]
                for i in range(NT):
                    s = i % 2
                    dve.wait(m_tr[s])
                    mh = dve.mark(nc.vector.scalar_tensor_tensor(out=HB[:, s, :], in0=X[:, i, :], scalar=RSTD[:, i:i + 1],
                                                                 in1=G[:], op0=ALU.mult, op1=ALU.mult))
                    pe.wait(mh, m_ev[s])
                    for k in range(8):
                        ins = nc.tensor.transpose(tp[:, s, k * 128:(k + 1) * 128], HB[:, s, k * 128:(k + 1) * 128], IDENT[:])
                    m_tr[s] = pe.mark(ins)
                    act.wait(m_tr[s])
                    m_ev[s] = act.mark(nc.scalar.copy(out=AT[:, :, i * 128:(i + 1) * 128],
                                                      in_=tp[:, s, :].rearrange("p (k c) -> p k c", c=128)))
                barrier()

        def inproj(layer):
            wsrc = wb_in[layer]
            dsw = ds_w["in%d" % layer]
            if layer == 0:
                groups = [("kn", 512, 128, 0), ("v", 640, 128, 0), ("k", 1280, 512, 1), ("v", 1792, 512, 2),
                          ("qn", 0, 512, 0), ("q", 768, 512, 4)]
                dv = 64
            else:
                groups = [("k", 1024, 512, 0), ("v", 2048, 512, 0), ("k", 1536, 512, 4), ("v", 2560, 512, 4),
                          ("q", 0, 512, 0), ("q", 512, 512, 4)]
                dv = 128
            ds_kt = newds("ds_kt%d" % layer)
            ds_v = newds("ds_v%d" % layer)
            ds_q = newds("ds_q%d" % layer)
            cc_k = {}
            cc_v = {}
            with ExitStack() as st:
                WG = [sb("WG%d_%d" % (layer, s), [128, 8, 512], BF16, st) for s in range(2)]
                dsW = [newds("ds_WG%d_%d" % (layer, s)) for s in range(2)]
                PR = [sb("PR%d_%d" % (layer, s), [128, 512], F32, st) for s in range(4)]
                TMP = [sb("TMP%d_%d" % (layer, s), [128, 256], F32, st) for s in range(4)]
                SQ = sb("SQ%d" % layer, [128, 512], F32, st)
                SSQ = sb("SSQ%d" % layer, [128, 8], F32, st)
                QK = [sb("QK%d_%d" % (layer, s), [128, 512], BF16, st) for s in range(4)]
                STG = [sb("STG%d_%d" % (layer, s), [128, 8320], BF16, st) for s in range(2)]
                pj = psum("pj%d" % layer, [128, 4, 512], F32, st)
                tq = psum("tq%d" % layer, [128, 4, 1024], BF16, st)

                m_w_free = [None, None]
                m_stg_free = [None, None]
                m_pj_free = [None] * 4
                m_pr_free = [None] * 4
                m_qk_free = [None] * 4
                m_tq_free = [None] * 4
                m_tmp_free = [None, None]
                pending = []
                cnt = 0

                def load_w(gi):
                    kind, c0, wd, _ = groups[gi]
                    s = gi % 2
                    sp.wait(m_w_free[s], dsw.all())
                    dma(sp, dsW[s], WG[s][:, :, 0:wd], wsrc[:, c0:c0 + wd].rearrange("(k p) c -> p k c", p=128))

                load_w(0)
                for gi, (kind, c0, wd, dbase) in enumerate(groups):
                    ws = gi % 2
                    ss = gi % 2
                    if gi + 1 < len(groups):
                        load_w(gi + 1)
                    isv = kind == "v"
                    nh = wd // 64
                    nu = wd // 128
                    stg = STG[ss]
                    if isv:
                        nhv = wd // dv
                        stg_v = stg[:, 0:nhv * 16 * (dv + 1)].rearrange("p (h i d) -> p h i d", h=nhv, i=16)
                        pool.wait(m_stg_free[ss])
                        m_ones = pool.mark(nc.gpsimd.memset(stg[:, 0:nhv * 16 * (dv + 1)], 1.0))
                    else:
                        stg_q = stg[:, 0:nu * T].rearrange("p (u t) -> p u t", u=nu)
                        m_ones = None
                    last_ev = None
                    for i in range(NT):
                        b = cnt % 4
                        cnt += 1
                        pe.wait(dsW[ws].all(), m_pj_free[b])
                        for k in range(8):
                            ins = nc.tensor.matmul(pj[:, b, 0:wd], lhsT=AT[:, k, i * 128:(i + 1) * 128], rhs=WG[ws][:, k, 0:wd],
                                                   start=(k == 0), stop=(k == 7))
                        m_mm = pe.mark(ins)
                        m_w_free[ws] = m_mm
                        while len(pending) > 2:
                            last_ev = pending.pop(0)()
                        if isv:
                            act.wait(m_mm, m_ones, m_stg_free[ss])
                            m_c = act.mark(nc.scalar.copy(out=stg_v[:, :, i, 0:dv],
                                                          in_=pj[:, b, 0:wd].rearrange("p (h d) -> p h d", d=dv)))
                            m_pj_free[b] = m_c
                            last_ev = m_c
                            continue
                        act.wait(m_mm, m_pr_free[b])
                        m_c = act.mark(nc.scalar.copy(out=PR[b][:, 0:wd], in_=pj[:, b, 0:wd]))
                        m_pj_free[b] = m_c
                        pr3 = PR[b][:, 0:wd].rearrange("p (h d) -> p h d", d=64)
                        m_src = m_c
                        if kind in ("qn", "kn"):
                            gsel = 0 if kind == "qn" else 1
                            dve.wait(m_src)
                            m1 = dve.mark(nc.vector.tensor_tensor(out=SQ[:, 0:wd], in0=PR[b][:, 0:wd], in1=PR[b][:, 0:wd], op=ALU.mult))
                            dve.wait(m1)
                            m2 = dve.mark(nc.vector.tensor_reduce(out=SSQ[:, 0:nh], in_=SQ[:, 0:wd].rearrange("p (h d) -> p h d", d=64),
                                                                  axis=AX.X, op=ALU.add))
                            act.wait(m2)
                            m3 = act.mark(nc.scalar.activation(out=SSQ[:, 0:nh], in_=SSQ[:, 0:nh], func=AF.Sqrt,
                                                               bias=EPST[:, 0:1], scale=1.0 / 64))
                            dve.wait(m3)
                            m4 = dve.mark(nc.vector.reciprocal(out=SSQ[:, 0:nh], in_=SSQ[:, 0:nh]))
                            dve.wait(m4)
                            m5 = dve.mark(nc.vector.tensor_tensor(out=pr3, in0=pr3,
                                                                  in1=SSQ[:, 0:nh].unsqueeze(2).to_broadcast([128, nh, 64]), op=ALU.mult))
                            dve.wait(m5)
                            m_src = dve.mark(nc.vector.tensor_tensor(out=pr3, in0=pr3,
                                                                     in1=GQK[:, gsel, :].unsqueeze(1).to_broadcast([128, nh, 64]), op=ALU.mult))
                        ctab, stab = (2, 3) if kind in ("qn", "kn") else (0, 1)
                        cosb = ROPE[:, ctab, i, :].unsqueeze(1).to_broadcast([128, nh, 32])
                        sinb = ROPE[:, stab, i, :].unsqueeze(1).to_broadcast([128, nh, 32])
                        x1 = pr3[:, :, 0:32]
                        x2 = pr3[:, :, 32:64]
                        qk3 = QK[b][:, 0:wd].rearrange("p (h d) -> p h d", d=64)
                        t0 = TMP[0][:, 0:nh * 32].rearrange("p (h d) -> p h d", d=32)
                        t1 = TMP[1][:, 0:nh * 32].rearrange("p (h d) -> p h d", d=32)
                        t2 = TMP[2][:, 0:nh * 32].rearrange("p (h d) -> p h d", d=32)
                        t3 = TMP[3][:, 0:nh * 32].rearrange("p (h d) -> p h d", d=32)
                        dve.wait(m_src, m_qk_free[b], m_tmp_free[0])
                        ma = dve.mark(nc.vector.tensor_tensor(out=t0, in0=x1, in1=cosb, op=ALU.mult))
                        mb = dve.mark(nc.vector.tensor_tensor(out=t1, in0=x2, in1=sinb, op=ALU.mult))
                        dve.wait(ma, mb)
                        mq1 = dve.mark(nc.vector.tensor_tensor(out=qk3[:, :, 0:32], in0=t0, in1=t1, op=ALU.subtract))
                        pool.wait(m_src, m_qk_free[b], m_tmp_free[1])
                        mc = pool.mark(nc.gpsimd.tensor_tensor(out=t2, in0=x1, in1=sinb, op=ALU.mult))
                        md = pool.mark(nc.gpsimd.tensor_tensor(out=t3, in0=x2, in1=cosb, op=ALU.mult))
                        pool.wait(mc, md)
                        mq2 = pool.mark(nc.gpsimd.tensor_tensor(out=qk3[:, :, 32:64], in0=t2, in1=t3, op=ALU.add))
                        m_pr_free[b] = [mq1, mq2]
                        m_tmp_free[0] = mq1
                        m_tmp_free[1] = mq2

                        def do_tr(b=b, i=i, mq1=mq1, mq2=mq2):
                            pe.wait(mq1, mq2, m_tq_free[b])
                            for u in range(nu):
                                ins = nc.tensor.transpose(tq[:, b, u * 128:(u + 1) * 128], QK[b][:, u * 128:(u + 1) * 128], IDENT[:])
                            m_t = pe.mark(ins)
                            m_qk_free[b] = m_t
                            act.wait(m_t, m_stg_free[ss])
                            m_e = act.mark(nc.scalar.copy(out=stg_q[:, :, i * 128:(i + 1) * 128],
                                                          in_=tq[:, b, 0:wd].rearrange("p (u c) -> p u c", c=128)))
                            m_tq_free[b] = m_e
                            return m_e
                        pending.append(do_tr)
                    while pending:
                        last_ev = pending.pop(0)()
                    sp.wait(last_ev)
                    if isv:
                        ng = nhv // VGH[layer]
                        g0 = dbase // VGH[layer]
                        src = stg[:, 0:nhv * 16 * (dv + 1)].rearrange("p (h f) -> p h f", h=nhv)
                        for gg in range(ng):
                            m_st = dma(sp, ds_v, v_loc[layer][g0 + gg].rearrange("(h p) f -> p h f", p=128),
                                       src[:, gg * VGH[layer]:(gg + 1) * VGH[layer], :])
                        m_stg_free[ss] = m_st
                        pool.wait(m_st)
                        for gg in range(ng):
                            d = DS(nc, es, "cc_v%d_%d" % (layer, g0 + gg))
                            ins = nc.gpsimd.collective_compute("AllGather", ALU.bypass, replica_groups=[[0, 1, 2, 3], [4, 5, 6, 7]],
                                                               ins=[v_loc[layer][g0 + gg].opt()], outs=[v_all[layer][g0 + gg].opt()])
                            ins.then_inc(d.sem, 1)
                            d.n = 1
                            cc_v[g0 + gg] = d
                    elif kind in ("q", "qn"):
                        m_stg_free[ss] = dma(sp, ds_q, qt[layer][dbase * 128:(dbase + nu) * 128, :].rearrange("(u p) t -> p u t", p=128), stg_q)
                    else:
                        for uu in range(nu):
                            m_st = dma(sp, ds_kt, kt_loc[layer][dbase + uu], stg_q[:, uu, :])
                        m_stg_free[ss] = m_st
                        pool.wait(m_st)
                        for uu in range(nu):
                            d = DS(nc, es, "cc_k%d_%d" % (layer, dbase + uu))
                            ins = nc.gpsimd.collective_compute("AllGather", ALU.bypass, replica_groups=[[0, 1, 2, 3], [4, 5, 6, 7]],
                                                               ins=[kt_loc[layer][dbase + uu].opt()], outs=[kt_all[layer][dbase + uu].opt()])
                            ins.then_inc(d.sem, 1)
                            d.n = 1
                            cc_k[dbase + uu] = d
                barrier()
            return cc_k, cc_v

        def attention(layer):
            dvp = 65 if layer == 0 else 129
            dv = dvp - 1
            nbank = 2 if layer == 0 else 3
            per_bank = 4 if layer == 0 else 3
            with ExitStack() as st:
                psS = psum("psS%d" % layer, [128, 2, 2, 512], F32, st)
                psA = psum("psA%d" % layer, [128, nbank, 512], F32, st)
                psT = psum("psT%d" % layer, [128, 1024], BF16, st)
                QT = [sb("QT%d_%d" % (layer, s), [128, T], BF16, st) for s in range(2)]
                dsQ = [newds("dsQ%d_%d" % (layer, s)) for s in range(2)]
                PP = sb("PP%d" % layer, [128, 3, 2, 512], BF16, st)
                ACCS = sb("ACCS%d" % layer, [128, 8 * dvp], F32, st)
                RDEN = sb("RDEN%d" % layer, [128, 8], F32, st)
                MT = sb("MT%d" % layer, [128, 512], BF16, st)
                if layer == 1:
                    O1 = sb("O1", [128, 512], F32, st)
                    O2 = sb("O2", [128, 512], F32, st)
                    SS4 = sb("SS4", [128, 4], F32, st)
                    RS4 = sb("RS4", [128, 4], F32, st)
                if layer == 0:
                    KTA = sb("KTA", [128, 4, T], BF16, st)
                    VA = sb("VA", [128, 4, 16, 65], BF16, st)
                    dsKA = newds("dsKA")
                    KTB = [sb("KTB%d" % s, [128, NM * 128], BF16, st) for s in range(2)]
                    VB = [sb("VB%d" % s, [128, 2, NM, 65], BF16, st) for s in range(2)]
                    dsKB = [newds("dsKB%d" % s) for s in range(2)]
                    MASK = sb("MASK", [128, NM, 512], BF16, st)
                    dsM = newds("dsM")
                else:
                    KT = [sb("KT1_%d" % s, [128, 4, T], BF16, st) for s in range(2)]
                    V1 = [sb("V1_%d" % s, [128, 4, 16, 129], BF16, st) for s in range(2)]
                    dsKV = [newds("dsKV%d" % s) for s in range(2)]

                state = dict(m_s_free=[None, None], m_p_free=[None] * 3,
                             m_acc_free=None, m_accs_free=None, m_mt_free=None, m_psT_free=None, kbc=0)
                m_q_free = [None, None]

                def acc_ap(n):
                    bk, o = n // per_bank, (n % per_bank) * dvp
                    return psA[:, bk, o:o + dvp], bk

                def emit_s(sp_, kb):
                    la, lb, _, _, _, _ = sp_["blocks"][kb]
                    s = (sp_["base"] + kb) % 2
                    g_ = sp_["g"]
                    qt_ = sp_["qtile"]
                    pe.wait(sp_["m_load"], state["m_s_free"][s])
                    nc.tensor.matmul(psS[:, s, 0, :], lhsT=la, rhs=qt_[0:64, g_ * 512:(g_ + 1) * 512], start=True, stop=True)
                    ins = nc.tensor.matmul(psS[:, s, 1, :], lhsT=lb, rhs=qt_[64:128, g_ * 512:(g_ + 1) * 512], start=True, stop=True)
                    sp_["m_s"][kb] = pe.mark(ins)

                def run_groups(spec_iter):
                    it = iter(spec_iter)

                    def start(spec, base):
                        spec["base"] = base
                        spec["m_s"] = {}

                    cur = next(it, None)
                    while cur == "SYNC":
                        cur = next(it, None)
                    start(cur, state["kbc"])
                    emit_s(cur, 0)
                    emit_s(cur, 1)
                    while cur is not None:
                        nxt = next(it, None)
                        if nxt == "SYNC":
                            qgroup(cur, None)
                            nxt = next(it, None)
                            if nxt is not None:
                                start(nxt, cur["base"] + len(cur["blocks"]))
                                emit_s(nxt, 0)
                                emit_s(nxt, 1)
                        else:
                            if nxt is not None:
                                start(nxt, cur["base"] + len(cur["blocks"]))
                            qgroup(cur, nxt)
                        cur = nxt

                def qgroup(spec, nxt):
                    blocks = spec["blocks"]
                    g = spec["g"]
                    chunk = spec["chunk"]
                    m_buf_free_cb = spec["cb"]
                    m_s = spec["m_s"]
                    nb = len(blocks)
                    started = set()
                    last_pv = None
                    state["kbc_base"] = spec["base"]
                    for kb in range(nb):
                        _, _, va, vb, mi, jl = blocks[kb]
                        sl = state["kbc_base"] + kb
                        s = sl % 2
                        ps3 = sl % 3
                        act.wait(m_s[kb], state["m_p_free"][ps3])
                        me = act.mark(nc.scalar.activation(out=PP[:, ps3, :, :], in_=psS[:, s, :, :], func=AF.Exp, scale=0.125))
                        state["m_s_free"][s] = me
                        if mi is not None:
                            dve.wait(me)
                            me = dve.mark(nc.vector.tensor_tensor(out=PP[:, ps3, :, :], in0=PP[:, ps3, :, :],
                                                                  in1=MASK[:, mi, :].unsqueeze(1).to_broadcast([128, 2, 512]), op=ALU.mult))
                        if kb + 2 < nb:
                            emit_s(spec, kb + 2)
                        elif nxt is not None:
                            emit_s(nxt, kb + 2 - nb)
                        while state.get("deferred") and state["deferred"][0][0] <= kb:
                            state["deferred"].pop(0)[1]()
                        pe.wait(me, state["m_acc_free"])
                        ins = None
                        for mp in range(2):
                            vv = va if mp == 0 else vb
                            for j in jl:
                                o_ap, bk = acc_ap(mp * 4 + j)
                                first = bk not in started
                                started.add(bk)
                                ins = nc.tensor.matmul(o_ap, lhsT=PP[:, ps3, mp, j * 128:(j + 1) * 128], rhs=vv,
                                                       start=first, stop=(kb == nb - 1), skip_group_check=True)
                        m_pv = pe.mark(ins)
                        state["m_p_free"][ps3] = m_pv
                        last_pv = m_pv
                    state["kbc"] = state["kbc_base"] + nb
                    m_buf_free_cb(last_pv)
                    dve.wait(last_pv, state["m_accs_free"])
                    m = None
                    for bk in range(nbank):
                        ncols = min(per_bank, 8 - bk * per_bank) * dvp
                        m = dve.mark(nc.vector.tensor_copy(out=ACCS[:, bk * per_bank * dvp: bk * per_bank * dvp + ncols], in_=psA[:, bk, 0:ncols]))
                    state["m_acc_free"] = m
                    m_cp = m

                    a3 = ACCS[:].rearrange("p (n d) -> p n d", d=dvp)
                    box = {}

                    def ep_dve1(m=m_cp):
                        dve.wait(m)
                        m = dve.mark(nc.vector.reciprocal(out=RDEN[:], in_=a3[:, :, dv]))
                        dve.wait(m, state["m_mt_free"])
                        if layer == 0:
                            mt3 = MT[:].rearrange("p (j f) -> p j f", f=128)
                            for mp in range(2):
                                m = dve.mark(nc.vector.tensor_tensor(out=mt3[:, :, mp * 64:(mp + 1) * 64], in0=a3[:, mp * 4:(mp + 1) * 4, 0:64],
                                                                     in1=RDEN[:, mp * 4:(mp + 1) * 4].unsqueeze(2).to_broadcast([128, 4, 64]), op=ALU.mult))
                            state["m_accs_free"] = m
                            box["m_mt"] = m
                        else:
                            o13 = O1[:].rearrange("p (j f) -> p j f", f=128)
                            o23 = O2[:].rearrange("p (j f) -> p j f", f=128)
                            m = dve.mark(nc.vector.tensor_scalar(out=RDEN[:, 4:8], in0=RDEN[:, 4:8], scalar1=NEGLAM[:, 0:1], scalar2=None, op0=ALU.mult))
                            dve.wait(m)
                            nc.vector.tensor_tensor(out=o13, in0=a3[:, 0:4, 0:128], in1=RDEN[:, 0:4].unsqueeze(2).to_broadcast([128, 4, 128]), op=ALU.mult)
                            m = dve.mark(nc.vector.tensor_tensor(out=o23, in0=a3[:, 4:8, 0:128], in1=RDEN[:, 4:8].unsqueeze(2).to_broadcast([128, 4, 128]), op=ALU.mult))
                            state["m_accs_free"] = m
                            dve.wait(m)
                            m = dve.mark(nc.vector.tensor_tensor(out=O1[:], in0=O1[:], in1=O2[:], op=ALU.add))
                            dve.wait(m)
                            m = dve.mark(nc.vector.tensor_tensor(out=O2[:], in0=O1[:], in1=O1[:], op=ALU.mult))
                            dve.wait(m)
                            box["m_ss"] = dve.mark(nc.vector.tensor_reduce(out=SS4[:], in_=o23, axis=AX.X, op=ALU.add))

                    def ep_act():
                        act.wait(box["m_ss"])
                        m = act.mark(nc.scalar.activation(out=RS4[:], in_=SS4[:], func=AF.Ln, bias=EPST[:, 0:1], scale=1.0 / 128))
                        act.wait(m)
                        box["m_rs"] = act.mark(nc.scalar.activation(out=RS4[:], in_=RS4[:], func=AF.Exp, scale=-0.5))

                    def ep_dve2():
                        o13 = O1[:].rearrange("p (j f) -> p j f", f=128)
                        dve.wait(box["m_rs"])
                        m = dve.mark(nc.vector.tensor_tensor(out=o13, in0=o13, in1=RS4[:].unsqueeze(2).to_broadcast([128, 4, 128]), op=ALU.mult))
                        dve.wait(m)
                        box["m_mt"] = dve.mark(nc.vector.tensor_tensor(out=MT[:].rearrange("p (j f) -> p j f", f=128), in0=o13,
                                                                       in1=GAINC[:].unsqueeze(1).to_broadcast([128, 4, 128]), op=ALU.mult))

                    def ep_tr(chunk=chunk, g=g):
                        pe.wait(box["m_mt"], state["m_psT_free"])
                        for j in range(4):
                            ins = nc.tensor.transpose(psT[:, j * 128:(j + 1) * 128], MT[:, j * 128:(j + 1) * 128], IDENT[:])
                        m_t = pe.mark(ins)
                        state["m_mt_free"] = m_t
                        dve.wait(m_t)
                        state["m_psT_free"] = dve.mark(nc.vector.tensor_copy(out=AT[:, chunk, g * 512:(g + 1) * 512], in_=psT[:, 0:512]))

                    if layer == 0:
                        state["deferred"] = [(4, ep_dve1), (10, ep_tr)]
                    else:
                        state["deferred"] = [(4, ep_dve1), (16, ep_act), (18, ep_dve2), (24, ep_tr)]

                def load_q(u, slot):
                    sp.wait(m_q_free[slot])
                    return dma(sp, dsQ[slot], QT[slot][:], qt[layer][u * 128:(u + 1) * 128, :])

                if layer == 0:
                    cc_k, cc_v = cc[0]
                    kA = kt_all[0][0].rearrange("(r p) t -> p r t", p=128)
                    vA = v_all[0][0].rearrange("(r h p) f -> h p r f", r=4, h=2)
                    ds_rl = newds("ds_rl")
                    ds_win = newds("ds_win")
                    pool.wait(ds_pad.all())
                    for u in range(4):
                        pool.wait(cc_k[1 + u].all())
                        dma(pool, ds_rl, ktB_pad[u * 128:(u + 1) * 128, 1024:1024 + 4 * T].rearrange("p (r t) -> p r t", r=4),
                            kt_all[0][1 + u].rearrange("(r p) t -> p r t", p=128))
                    for gi in range(4):
                        pool.wait(cc_v[1 + gi].all())
                        srcv = v_all[0][1 + gi].rearrange("(r h p) f -> h p r f", r=4, h=2)
                        for hh in range(2):
                            hB = 2 * gi + hh
                            dma(pool, ds_rl, vB_pad[hB * 128:(hB + 1) * 128, 8 * 65:72 * 65].rearrange("p (r f) -> p r f", r=4), srcv[hh])
                    pid = nc.gpsimd.partition_id()
                    rank = pid % 4
                    pool.wait(ds_rl.all())
                    dma(pool, ds_win, ktB_win, ktB_pad[:, bass.ds(rank * T, 4096)])
                    dma(pool, ds_win, vB_win, vB_pad[:, bass.ds(rank * (16 * 65), 32 * 65)])
                    unit_last = {}

                    def specs_AB():
                        mq = {0: load_q(0, 0)}
                        m_ka_free = None
                        for u in range(4):
                            kvh = u // 2
                            if u % 2 == 0:
                                if u == 2:
                                    yield "SYNC"
                                sp.wait(unit_last.get(u - 1), cc_k[0].all(), cc_v[0].all())
                                for half in range(2):
                                    dma(sp, dsKA, KTA[half * 64:(half + 1) * 64, :, :], kA[kvh * 64:(kvh + 1) * 64, :, :])
                                dma(sp, dsKA, VA[:].rearrange("p r i d -> p r (i d)"), vA[kvh])
                            lh = []
                            for g in range(4):
                                if g == 1:
                                    if u == 1:
                                        pool.wait(unit_last[0])
                                        casts_layer0_rest()
                                        dma(sp, dsM, MASK[:], masks_in.rearrange("m p q -> p m q"))
                                    sp.wait(unit_last.get(u - 1))
                                    mq[u + 1] = dma(sp, dsQ[(u + 1) % 2], QT[(u + 1) % 2][:], qt[0][(u + 1) * 128:(u + 2) * 128, :])
                                blocks = []
                                for r in range(4):
                                    for i in range(16):
                                        blocks.append((KTA[0:64, r, i * 128:(i + 1) * 128], KTA[64:128, r, i * 128:(i + 1) * 128],
                                                       VA[:, r, i, :], VA[:, r, i, :], None, [0, 1, 2, 3]))

                                def cbA(m, u=u, lh=lh):
                                    lh.append(m)
                                    unit_last[u] = m
                                yield dict(qtile=QT[u % 2], g=g, chunk=u, blocks=blocks, m_load=[mq[u], dsKA.all()], cb=cbA)
                        m_kb_free = [None, None]
                        widx = 0
                        for ub in range(4):
                            u = 4 + ub
                            for g in range(4):
                                ws = widx % 2
                                widx += 1
                                sp.wait(m_kb_free[ws], ds_win.all())
                                dma(sp, dsKB[ws], KTB[ws][:], ktB_win[ub * 128:(ub + 1) * 128, g * 512:g * 512 + NM * 128])
                                for hh in range(2):
                                    h = 2 * ub + hh
                                    dma(sp, dsKB[ws], VB[ws][:, hh, :, :].rearrange("p m d -> p (m d)"),
                                        vB_win[h * 128:(h + 1) * 128, g * 4 * 65:(g * 4 + NM) * 65])
                                if g == 1 and u < 7:
                                    sp.wait(unit_last.get(u - 1))
                                    mq[u + 1] = dma(sp, dsQ[(u + 1) % 2], QT[(u + 1) % 2][:], qt[0][(u + 1) * 128:(u + 2) * 128, :])
                                blocks = []
                                for mi in range(NM):
                                    jl = [j for j in range(4) if j <= mi <= j + 16]
                                    blocks.append((KTB[ws][0:64, mi * 128:(mi + 1) * 128], KTB[ws][64:128, mi * 128:(mi + 1) * 128],
                                                   VB[ws][:, 0, mi, :], VB[ws][:, 1, mi, :], mi, jl))

                                def cbB(m, ws=ws, u=u):
                                    m_kb_free[ws] = m
                                    unit_last[u] = m
                                yield dict(qtile=QT[u % 2], g=g, chunk=u, blocks=blocks, m_load=[mq[u], dsKB[ws].all(), dsM.all()], cb=cbB)
                    run_groups(specs_AB())
                else:
                    cc_k, cc_v = cc[1]
                    m_kv_free = [None, None]

                    def load_kv(h, slot):
                        sp.wait(m_kv_free[slot], cc_k[h].all(), cc_v[h].all())
                        dma(sp, dsKV[slot], KT[slot][:], kt_all[1][h].rearrange("(r p) t -> p r t", p=128))
                        dma(sp, dsKV[slot], V1[slot][:].rearrange("p r i d -> p r (i d)"), v_all[1][h].rearrange("(r p) f -> p r f", p=128))

                    head_last = {}

                    def specs_C():
                        mq = {0: load_q(0, 0)}
                        load_kv(0, 0)
                        for h in range(8):
                            slot = h % 2
                            for g in range(4):
                                if g == 1 and h < 7:
                                    if h == 1:
                                        pool.wait(head_last[0])
                                        casts_layer1_rest()
                                    m_kv_free[1 - slot] = head_last.get(h - 1)
                                    sp.wait(head_last.get(h - 1))
                                    mq[h + 1] = dma(sp, dsQ[1 - slot], QT[1 - slot][:], qt[1][(h + 1) * 128:(h + 2) * 128, :])
                                    load_kv(h + 1, 1 - slot)
                                blocks = []
                                for r in range(4):
                                    for i in range(16):
                                        blocks.append((KT[slot][0:64, r, i * 128:(i + 1) * 128], KT[slot][64:128, r, i * 128:(i + 1) * 128],
                                                       V1[slot][:, r, i, :], V1[slot][:, r, i, :], None, [0, 1, 2, 3]))

                                def cbC(m, h=h):
                                    head_last[h] = m
                                yield dict(qtile=QT[slot], g=g, chunk=h, blocks=blocks, m_load=[mq[h], dsKV[slot].all()], cb=cbC)
                    run_groups(specs_C())
                while state.get("deferred"):
                    state["deferred"].pop(0)[1]()
                barrier()

        def outproj(layer):
            with ExitStack() as st:
                WO = sb("WO%d" % layer, [128, 8, D], BF16, st)
                dsO = newds("dsO%d" % layer)
                psO = psum("psO%d" % layer, [128, 2, 512], F32, st)
                sp.wait(ds_w["out%d" % layer].all())
                dma(sp, dsO, WO[:], wb_out[layer].rearrange("(k p) c -> p k c", p=128))
                m_free = [None, None]
                cnt = 0
                for i in range(NT):
                    for c in range(2):
                        b = cnt % 2
                        cnt += 1
                        pe.wait(dsO.all(), m_free[b])
                        for k in range(8):
                            ins = nc.tensor.matmul(psO[:, b, :], lhsT=AT[:, k, i * 128:(i + 1) * 128], rhs=WO[:, k, c * 512:(c + 1) * 512],
                                                   start=(k == 0), stop=(k == 7))
                        mm = pe.mark(ins)
                        dve.wait(mm)
                        m_free[b] = dve.mark(nc.vector.tensor_tensor(out=X[:, i, c * 512:(c + 1) * 512], in0=X[:, i, c * 512:(c + 1) * 512],
                                                                     in1=psO[:, b, :], op=ALU.add))
                barrier()

        def ffn(layer):
            with ExitStack() as st:
                WD = sb("WD%d" % layer, [128, NF, D], BF16, st)
                dsD = newds("dsD%d" % layer)
                WGU = [sb("WGU%d_%d" % (layer, s), [128, 2, 8, 256], BF16, st) for s in range(2)]
                dsGU = [newds("dsGU%d_%d" % (layer, s)) for s in range(2)]
                AFT = sb("AFT%d" % layer, [128, NF, 512], BF16, st)
                SG = [sb("SG%d_%d" % (layer, s), [128, 512], F32, st) for s in range(2)]
                psG = psum("psG%d" % layer, [128, 2, 512], F32, st)
                psU = psum("psU%d" % layer, [128, 2, 512], F32, st)
                psD = psum("psD%d" % layer, [128, 2, 512], F32, st)
                sp.wait(ds_w["d%d" % layer].all(), ds_w["g%d" % layer].all(), ds_w["u%d" % layer].all())
                for half in range(2):
                    dma(sp, dsD, WD[:, half * 11:(half + 1) * 11, :],
                        wb_d[layer][half * 11 * 128:(half + 1) * 11 * 128, :].rearrange("(f p) c -> p f c", p=128))
                m_gu_free = [None, None]
                m_g_free = [None, None]
                m_u_free = [None, None]
                m_sg_free = [None, None]
                m_d_free = [None, None]
                m_down_last = None
                lc = 0
                fc_cnt = 0
                dcnt = 0

                def load_gu(fg, slot):
                    sp.wait(m_gu_free[slot])
                    dma(sp, dsGU[slot], WGU[slot][:, 0, :, :], wb_g[layer][:, fg * 256:(fg + 1) * 256].rearrange("(k p) c -> p k c", p=128))
                    dma(sp, dsGU[slot], WGU[slot][:, 1, :, :], wb_u[layer][:, fg * 256:(fg + 1) * 256].rearrange("(k p) c -> p k c", p=128))

                seq = [(tg, fg) for tg in range(4) for fg in range(11)]
                load_gu(seq[0][1], 0)
                for si, (tg, fg) in enumerate(seq):
                    slot = si % 2
                    if si + 1 < len(seq):
                        load_gu(seq[si + 1][1], 1 - slot)
                    for fc in range(2):
                        f = fg * 2 + fc
                        b = fc_cnt % 2
                        fc_cnt += 1
                        pe.wait(dsGU[slot].all(), m_g_free[b], m_u_free[b])
                        for k in range(8):
                            nc.tensor.matmul(psG[:, b, :], lhsT=WGU[slot][:, 0, k, fc * 128:(fc + 1) * 128], rhs=AT[:, k, tg * 512:(tg + 1) * 512],
                                             start=(k == 0), stop=(k == 7))
                        for k in range(8):
                            ins = nc.tensor.matmul(psU[:, b, :], lhsT=WGU[slot][:, 1, k, fc * 128:(fc + 1) * 128], rhs=AT[:, k, tg * 512:(tg + 1) * 512],
                                                   start=(k == 0), stop=(k == 7))
                        mm = pe.mark(ins)
                        m_gu_free[slot] = mm
                        act.wait(mm, m_sg_free[b])
                        ms = act.mark(nc.scalar.activation(out=SG[b][:], in_=psG[:, b, :], func=AF.Silu))
                        m_g_free[b] = ms
                        dve.wait(ms, m_down_last)
                        ma = dve.mark(nc.vector.tensor_tensor(out=AFT[:, f, :], in0=SG[b][:], in1=psU[:, b, :], op=ALU.mult))
                        m_u_free[b] = ma
                        m_sg_free[b] = ma
                        last_a = ma
                    if fg == 10:
                        for j in range(4):
                            for c in range(2):
                                b = dcnt % 2
                                dcnt += 1
                                pe.wait(last_a, dsD.all(), m_d_free[b])
                                for f in range(NF):
                                    ins = nc.tensor.matmul(psD[:, b, :], lhsT=AFT[:, f, j * 128:(j + 1) * 128], rhs=WD[:, f, c * 512:(c + 1) * 512],
                                                           start=(f == 0), stop=(f == NF - 1))
                                mm = pe.mark(ins)
                                m_down_last = mm
                                i = tg * 4 + j
                                dve.wait(mm)
                                m_d_free[b] = dve.mark(nc.vector.tensor_tensor(out=X[:, i, c * 512:(c + 1) * 512], in0=X[:, i, c * 512:(c + 1) * 512],
                                                                               in1=psD[:, b, :], op=ALU.add))
                barrier()

        def final():
            with ExitStack() as st:
                G, RSTD = rms_rstd(st, lambda i: X[:, i, :], NT, final_norm[0:1, :], "fin")
                OB = [sb("OB%d" % s, [128, D], F32, st) for s in range(2)]
                ds_o = newds("ds_out")
                m_free = [None, None]
                for i in range(NT):
                    s = i % 2
                    dve.wait(m_free[s])
                    m = dve.mark(nc.vector.scalar_tensor_tensor(out=OB[s][:], in0=X[:, i, :], scalar=RSTD[:, i:i + 1], in1=G[:],
                                                                op0=ALU.mult, op1=ALU.mult))
                    sp.wait(m)
                    m_free[s] = dma(sp, ds_o, out[i * 128:(i + 1) * 128, :], OB[s][:])
                sp.wait(ds_o.all())
                pool.wait(ds_o.all())

        cc = {}
        dve.wait(ds_x.all(), ds_c.all())
        act.wait(ds_x.all(), ds_c.all())
        pe.wait(ds_c.all())
        pool.wait(ds_c.all())

        def dump(layer):
            barrier()
            dsd = newds("ds_dbg")
            for i in range(NT):
                dma(sp, dsd, dbg["X"][i * 128:(i + 1) * 128, :], X[:, i, :])
            dma(sp, dsd, dbg["AT"], AT[:].rearrange("p k t -> p (k t)"))
            dma(sp, dsd, dbg["qt"], qt[layer])
            for u in range(NKU[layer]):
                dma(sp, dsd, dbg["kt"][u * 512:(u + 1) * 512, :], kt_all[layer][u])
            for g in range(NVG[layer]):
                nr = 4 * VGH[layer] * 128
                dma(sp, dsd, dbg["v"][g * nr:(g + 1) * nr, 0:VF[layer]], v_all[layer][g])
            sp.wait(dsd.all())
            pool.wait(dsd.all())

        steps = [
            ("norm_a0", lambda: norm_AT(attn_norm[0:1, :], "a0")),
            ("inproj0", lambda: cc.__setitem__(0, inproj(0))),
            ("attn0", lambda: attention(0)),
            ("outproj0", lambda: outproj(0)),
            ("norm_f0", lambda: norm_AT(ffn_norm[0:1, :], "f0")),
            ("ffn0", lambda: ffn(0)),
            ("norm_a1", lambda: norm_AT(attn_norm[1:2, :], "a1")),
            ("inproj1", lambda: cc.__setitem__(1, inproj(1))),
            ("attn1", lambda: attention(1)),
            ("outproj1", lambda: outproj(1)),
            ("norm_f1", lambda: norm_AT(ffn_norm[1:2, :], "f1")),
            ("ffn1", lambda: ffn(1)),
            ("final", final),
        ]
        for name, fn in steps:
            fn()
            if debug and debug == name:
                dump(0 if name in ("norm_a0", "inproj0", "attn0", "outproj0", "norm_f0", "ffn0", "norm_a1") else 1)
                break
    return nc


def _const_tables():
    theta = 10000.0
    pos = np.arange(S, dtype=np.float32)
    inv64 = (theta ** (-np.arange(0, 64, 2, dtype=np.float32) / 64)).astype(np.float32)
    inv32 = (theta ** (-np.arange(0, 32, 2, dtype=np.float32) / 32)).astype(np.float32)
    ang1 = (pos[:, None] * inv64[None, :]).astype(np.float32)
    rows = (np.arange(S) // 64).astype(np.float32)
    cols = (np.arange(S) % 64).astype(np.float32)
    ang2 = np.concatenate([rows[:, None] * inv32[None, :], cols[:, None] * inv32[None, :]], axis=-1).astype(np.float32)
    rope = np.stack([np.cos(ang1.astype(np.float64)), np.sin(ang1.astype(np.float64)),
                     np.cos(ang2.astype(np.float64)), np.sin(ang2.astype(np.float64))], 0).astype(np.float32)
    k = np.arange(128)[:, None]
    q = np.arange(512)[None, :]
    masks = np.zeros((NM, 128, 512), np.float32)
    for m in range(NM):
        d = 128 * m + k - q - 1024
        ad = np.abs(d)
        masks[m] = (ad <= 64).astype(np.float32) + ((d % 4 == 0) & (ad <= 256)) + ((d % 16 == 0) & (ad <= 1024))
    ident = np.eye(128, dtype=np.float32)
    return rope, masks.astype(ml_dtypes.bfloat16), ident.astype(ml_dtypes.bfloat16)


_NC_CACHE = {}


def kernel(x, attn_norm, ffn_norm, final_norm, w_in_even, a_q_norm, a_k_norm, w_out_even,
           w_in_odd, lambda_q1, lambda_k1, lambda_q2, lambda_k2, c_sub_norm, w_out_odd,
           w_gate, w_up, w_down, _debug=False):
    f = lambda a: np.ascontiguousarray(np.asarray(a, dtype=np.float32))
    x = f(x)
    rope, masks, ident = _const_tables()
    shared = {
        "attn_norm": f(attn_norm), "ffn_norm": f(ffn_norm), "final_norm": f(final_norm).reshape(1, D),
        "w_in_even": f(w_in_even)[0], "a_q_norm": f(a_q_norm), "a_k_norm": f(a_k_norm),
        "w_out_even": f(w_out_even)[0], "w_in_odd": f(w_in_odd)[0],
        "lambda_q1": f(lambda_q1), "lambda_k1": f(lambda_k1), "lambda_q2": f(lambda_q2), "lambda_k2": f(lambda_k2),
        "c_sub_norm": f(c_sub_norm), "w_out_odd": f(w_out_odd)[0],
        "w_gate": f(w_gate), "w_up": f(w_up), "w_down": f(w_down),
        "masks": masks, "ident": ident,
    }
    in_maps = []
    for c in range(NCORE):
        b, r = c // 4, c % 4
        m = dict(shared)
        m["x"] = np.ascontiguousarray(x[b, r * T:(r + 1) * T, :])
        m["rope"] = np.ascontiguousarray(rope[:, r * T:(r + 1) * T, :])
        in_maps.append(m)
    key = _debug
    if key not in _NC_CACHE:
        _NC_CACHE[key] = build(debug=key)
    nc = _NC_CACHE[key]
    res = run_bass_kernel_spmd(nc, in_maps, core_ids=list(range(NCORE)))
    outp = np.zeros((2, S, D), np.float32)
    for c in range(NCORE):
        b, r = c // 4, c % 4
        outp[b, r * T:(r + 1) * T, :] = np.asarray(res.results[c]["out"])
    if _debug:
        return outp, res
    return outp
```

```python
import math
from contextlib import ExitStack

import numpy as np
import ml_dtypes

import concourse.bass as bass
import concourse.mybir as mybir
from concourse.bass_utils import run_bass_kernel_spmd

F32 = mybir.dt.float32
BF16 = mybir.dt.bfloat16
ALU = mybir.AluOpType
AF = mybir.ActivationFunctionType
AX = mybir.AxisListType

NCORE = 8
S = 8192
D = 1024
T = 2048
NT = 16
DFF = 2816
NF = 22
EPS = 1e-6
LAM_INIT = 0.8 - 0.6 * math.exp(-0.3 * 1)
KPAD = 1024 + S + 1536
VBLK = 8 + 64 + 12
NM = 20


class TL:
    def __init__(self, nc, es, eng, name):
        self.e = eng
        self.sem = es.enter_context(nc.semaphore(name))
        self.n = 0
        self.seen = {}

    def wait(self, *ms):
        for m in ms:
            if m is None:
                continue
            if isinstance(m, list):
                self.wait(*m)
                continue
            sem, v = m
            if sem is self.sem and False:
                continue
            k = id(sem)
            if self.seen.get(k, 0) >= v:
                continue
            self.e.wait_ge(sem, v)
            self.seen[k] = v

    def mark(self, ins):
        self.n += 1
        ins.then_inc(self.sem, 1)
        return (self.sem, self.n)


class DS:
    def __init__(self, nc, es, name):
        self.sem = es.enter_context(nc.semaphore(name))
        self.n = 0

    def add(self, ins):
        self.n += 16
        ins.then_inc(self.sem, 16)
        return (self.sem, self.n)

    def all(self):
        return (self.sem, self.n) if self.n else None


def build(debug=False):
    nc = bass.Bass("TRN2", target_bir_lowering=False)

    def din(name, shape, dt=F32):
        return nc.dram_tensor(name, list(shape), dt, kind="ExternalInput").ap()

    def dint(name, shape, dt=BF16):
        return nc.dram_tensor(name, list(shape), dt, kind="Internal").ap()

    x_in = din("x", [T, D])
    attn_norm = din("attn_norm", [2, D])
    ffn_norm = din("ffn_norm", [2, D])
    final_norm = din("final_norm", [1, D])
    w_in_even = din("w_in_even", [D, 2304])
    a_q_norm = din("a_q_norm", [1, 64])
    a_k_norm = din("a_k_norm", [1, 64])
    w_out_even = din("w_out_even", [D, D])
    w_in_odd = din("w_in_odd", [D, 3072])
    lam_in = [din(n, [1, 64]) for n in ("lambda_q1", "lambda_k1", "lambda_q2", "lambda_k2")]
    c_sub_norm = din("c_sub_norm", [1, 128])
    w_out_odd = din("w_out_odd", [D, D])
    w_gate = din("w_gate", [2, D, DFF])
    w_up = din("w_up", [2, D, DFF])
    w_down = din("w_down", [2, DFF, D])
    rope_in = din("rope", [4, T, 32])
    masks_in = din("masks", [NM, 128, 512], BF16)
    ident_in = din("ident", [128, 128], BF16)
    out = nc.dram_tensor("out", [T, D], F32, kind="ExternalOutput").ap()

    wb_in = [dint("wb_in0", [D, 2304]), dint("wb_in1", [D, 3072])]
    wb_out = [dint("wb_out0", [D, D]), dint("wb_out1", [D, D])]
    wb_g = [dint(f"wb_g{l}", [D, DFF]) for l in range(2)]
    wb_u = [dint(f"wb_u{l}", [D, DFF]) for l in range(2)]
    wb_d = [dint(f"wb_d{l}", [DFF, D]) for l in range(2)]

    qt = [dint("qt0", [8 * 128, T]), dint("qt1", [8 * 128, T])]
    NKU = [5, 8]
    VGH = [2, 1]
    NVG = [5, 8]
    VF = [16 * 65, 16 * 129]
    kt_loc = [[dint("kt%d_loc%d" % (l, u), [128, T]) for u in range(NKU[l])] for l in range(2)]
    kt_all = [[dint("kt%d_all%d" % (l, u), [4 * 128, T]) for u in range(NKU[l])] for l in range(2)]
    v_loc = [[dint("v%d_loc%d" % (l, g), [VGH[l] * 128, VF[l]]) for g in range(NVG[l])] for l in range(2)]
    v_all = [[dint("v%d_all%d" % (l, g), [4 * VGH[l] * 128, VF[l]]) for g in range(NVG[l])] for l in range(2)]
    ktB_pad = dint("ktB_pad", [4 * 128, KPAD])
    vB_pad = dint("vB_pad", [8 * 128, VBLK * 65])
    ktB_win = dint("ktB_win", [4 * 128, 4096])
    vB_win = dint("vB_win", [8 * 128, 32 * 65])

    dbg = {}
    if debug:
        dbg["X"] = nc.dram_tensor("dbg_X", [T, D], F32, kind="ExternalOutput").ap()
        dbg["AT"] = nc.dram_tensor("dbg_AT", [128, 8 * T], BF16, kind="ExternalOutput").ap()
        dbg["qt"] = nc.dram_tensor("dbg_qt", [8 * 128, T], BF16, kind="ExternalOutput").ap()
        dbg["kt"] = nc.dram_tensor("dbg_kt", [8 * 512, T], BF16, kind="ExternalOutput").ap()
        dbg["v"] = nc.dram_tensor("dbg_v", [8 * 1024, 16 * 129], BF16, kind="ExternalOutput").ap()

    es = ExitStack()
    with es:
        pe = TL(nc, es, nc.tensor, "t_pe")
        act = TL(nc, es, nc.scalar, "t_act")
        dve = TL(nc, es, nc.vector, "t_dve")
        pool = TL(nc, es, nc.gpsimd, "t_pool")
        sp = TL(nc, es, nc.sync, "t_sp")
        engines = [pe, act, dve, pool, sp]

        def sb(name, shape, dt, stack=None):
            return (stack or es).enter_context(nc.sbuf_tensor(name, list(shape), dt))

        def psum(name, shape, dt, stack):
            return stack.enter_context(nc.psum_tensor(name, list(shape), dt))

        def dma(q, ds, out_ap, in_ap):
            return ds.add(q.e.dma_start(out=out_ap, in_=in_ap))

        X = sb("X", [128, NT, D], F32)
        AT = sb("AT", [128, 8, T], BF16)
        ROPE = sb("ROPE", [128, 4, NT, 32], F32)
        IDENT = sb("IDENT", [128, 128], BF16)
        EPST = sb("EPST", [128, 1], F32)
        FENCE = sb("FENCE", [128, 4], F32)
        NEGLAM = sb("NEGLAM", [128, 1], F32)
        GAINC = sb("GAINC", [128, 128], F32)
        GQK = sb("GQK", [128, 2, 64], F32)
        ZERO = sb("ZERO", [128, 2048], BF16)

        fence_ps_stack = ExitStack()

        def barrier():
            ms = []
            ms.append(dve.mark(nc.vector.memset(FENCE[:, 0:1], 0.0)))
            ms.append(pool.mark(nc.gpsimd.memset(FENCE[:, 1:2], 0.0)))
            ms.append(act.mark(nc.scalar.activation(out=FENCE[:, 2:3], in_=EPST[:, 0:1], func=AF.Copy)))
            ms.append((pe.sem, pe.n) if pe.n else None)
            ms += [d.all() for d in all_ds]
            for e in engines:
                e.wait(*ms)

        all_ds = []

        def newds(name, in_barrier=True):
            d = DS(nc, es, name)
            if in_barrier:
                all_ds.append(d)
            return d

        ds_w = {}
        ds_c = newds("ds_c")
        ds_xg = [newds("ds_x%d" % j, in_barrier=False) for j in range(4)]
        nc.vector.memset(EPST[:], EPS)
        m_zero = pool.mark(nc.gpsimd.memset(ZERO[:], 0.0))
        dma(sp, ds_c, IDENT[:], ident_in)
        dma(sp, ds_c, ROPE[:], rope_in.rearrange("c (i p) d -> p c i d", p=128))
        dma(sp, ds_c, GQK[:, 0, :], a_q_norm.partition_broadcast(128))
        dma(sp, ds_c, GQK[:, 1, :], a_k_norm.partition_broadcast(128))
        dma(sp, ds_c, GAINC[:], c_sub_norm.partition_broadcast(128))
        for i in range(NT):
            dma(sp, ds_xg[i // 4], X[:, i, :], x_in[i * 128:(i + 1) * 128, :])

        def cast_w(key, dst, src, rows):
            d = newds("ds_w_" + key, in_barrier=False)
            ds_w[key] = d
            for r0 in range(0, rows, 128):
                dma(pool, d, dst[r0:r0 + 128, :], src[r0:r0 + 128, :])

        L0_GROUPS = [("kn", 512, 128, 0), ("v", 640, 128, 0), ("k", 1280, 512, 1), ("v", 1792, 512, 2),
                     ("qn", 0, 512, 0), ("q", 768, 512, 4)]
        for gi_, (_, c0_, wd_, _) in enumerate(L0_GROUPS):
            d_ = newds("ds_w_in0_%d" % gi_, in_barrier=False)
            ds_w["in0_%d" % gi_] = d_
            for r0 in range(0, D, 128):
                dma(pool, d_, wb_in[0][r0:r0 + 128, c0_:c0_ + wd_], w_in_even[r0:r0 + 128, c0_:c0_ + wd_])
        ds_pad = newds("ds_pad")
        sp.wait(m_zero)
        for u in range(4):
            dma(sp, ds_pad, ktB_pad[u * 128:(u + 1) * 128, 0:1024], ZERO[:, 0:1024])
            dma(sp, ds_pad, ktB_pad[u * 128:(u + 1) * 128, 1024 + S:KPAD], ZERO[:, 0:1536])
        for h in range(8):
            dma(sp, ds_pad, vB_pad[h * 128:(h + 1) * 128, 0:8 * 65], ZERO[:, 0:8 * 65])
            dma(sp, ds_pad, vB_pad[h * 128:(h + 1) * 128, 72 * 65:VBLK * 65], ZERO[:, 0:12 * 65])
        def casts_layer0_rest():
            cast_w("out0", wb_out[0], w_out_even, D)
            cast_w("g0", wb_g[0], w_gate[0], D)
            cast_w("u0", wb_u[0], w_up[0], D)
            cast_w("d0", wb_d[0], w_down[0], DFF)

        def casts_layer1_first():
            cast_w("in1", wb_in[1], w_in_odd, D)
            cast_w("out1", wb_out[1], w_out_odd, D)

        def casts_layer1_rest():
            cast_w("g1", wb_g[1], w_gate[1], D)
            cast_w("u1", wb_u[1], w_up[1], D)
            cast_w("d1", wb_d[1], w_down[1], DFF)

        with ExitStack() as st:
            LQ = sb("LQ", [128, 4, 64], F32, st)
            LS = sb("LS", [128, 4], F32, st)
            LJ = sb("LJ", [128, 64], F32, st)
            ds_l = newds("ds_l")
            for j in range(4):
                dma(sp, ds_l, LQ[:, j, :], lam_in[j].partition_broadcast(128))
            dve.wait(ds_l.all(), ds_c.all())
            nc.vector.memset(LS[:], 0.0)
            m = None
            for j in range(2):
                m = dve.mark(nc.vector.tensor_tensor(out=LJ[:], in0=LQ[:, 2 * j, :], in1=LQ[:, 2 * j + 1, :], op=ALU.mult))
                dve.wait(m)
                m = dve.mark(nc.vector.tensor_reduce(out=LS[:, j:j + 1], in_=LJ[:], axis=AX.X, op=ALU.add))
                dve.wait(m)
            act.wait(m)
            m = act.mark(nc.scalar.activation(out=LS[:, 2:4], in_=LS[:, 0:2], func=AF.Exp))
            dve.wait(m)
            m = dve.mark(nc.vector.tensor_tensor(out=NEGLAM[:], in0=LS[:, 3:4], in1=LS[:, 2:3], op=ALU.subtract))
            dve.wait(m)
            m = dve.mark(nc.vector.tensor_scalar(out=NEGLAM[:], in0=NEGLAM[:], scalar1=-LAM_INIT, scalar2=None, op0=ALU.add))
            dve.wait(m)
            dve.mark(nc.vector.tensor_scalar(out=GAINC[:], in0=GAINC[:], scalar1=1.0 - LAM_INIT, scalar2=None, op0=ALU.mult))
            barrier()

        def rms_rstd(st, src_of_tile, nt, gdram_row, tag, tile_ready=None):
            G = sb("G" + tag, [128, D], F32, st)
            JUNK = sb("JUNK" + tag, [128, D], BF16, st)
            SS = sb("SS" + tag, [128, nt], F32, st)
            RSTD = sb("RSTD" + tag, [128, nt], F32, st)
            dsg = newds("ds_g" + tag)
            dma(sp, dsg, G[:], gdram_row.partition_broadcast(128))
            m0 = dve.mark(nc.vector.memset(SS[:], 0.0))
            act.wait(m0)
            m = None
            for i in range(nt):
                if tile_ready is not None:
                    act.wait(tile_ready(i))
                m = act.mark(nc.scalar.activation(out=JUNK[:], in_=src_of_tile(i), func=AF.Square,
                                                  accum_out=SS[:, i:i + 1]))
            act.wait(m)
            m = act.mark(nc.scalar.activation(out=RSTD[:], in_=SS[:], func=AF.Sqrt, bias=EPST[:, 0:1], scale=1.0 / D))
            dve.wait(m, dsg.all())
            m = dve.mark(nc.vector.reciprocal(out=RSTD[:], in_=RSTD[:]))
            dve.wait(m)
            return G, RSTD

        def norm_AT(gdram_row, tag, tile_ready=None):
            with ExitStack() as st:
                G, RSTD = rms_rstd(st, lambda i: X[:, i, :], NT, gdram_row, tag, tile_ready)
                HB = sb("HB" + tag, [128, 2, D], BF16, st)
                tp = psum("tp" + tag, [128, 2, D], BF16, st)
                m_tr = [None, None]
                m_ev = [None, None]
                for i in range(NT):
                    s = i % 2
                    dve.wait(m_tr[s])
                    mh = dve.mark(nc.vector.scalar_tensor_tensor(out=HB[:, s, :], in0=X[:, i, :], scalar=RSTD[:, i:i + 1],
                                                                 in1=G[:], op0=ALU.mult, op1=ALU.mult))
                    pe.wait(mh, m_ev[s])
                    for k in range(8):
                        ins = nc.tensor.transpose(tp[:, s, k * 128:(k + 1) * 128], HB[:, s, k * 128:(k + 1) * 128], IDENT[:])
                    m_tr[s] = pe.mark(ins)
                    act.wait(m_tr[s])
                    m_ev[s] = act.mark(nc.scalar.copy(out=AT[:, :, i * 128:(i + 1) * 128],
                                                      in_=tp[:, s, :].rearrange("p (k c) -> p k c", c=128)))
                barrier()

        def inproj(layer):
            wsrc = wb_in[layer]
            dsw_of = (lambda gi: ds_w["in0_%d" % gi]) if layer == 0 else (lambda gi: ds_w["in1"])
            if layer == 0:
                groups = L0_GROUPS
                dv = 64
            else:
                groups = [("k", 1024, 512, 0), ("v", 2048, 512, 0), ("k", 1536, 512, 4), ("v", 2560, 512, 4),
                          ("q", 0, 512, 0), ("q", 512, 512, 4)]
                dv = 128
            ds_kt = newds("ds_kt%d" % layer)
            ds_v = newds("ds_v%d" % layer)
            ds_q = newds("ds_q%d" % layer)
            cc_k = {}
            cc_v = {}
            with ExitStack() as st:
                WG = [sb("WG%d_%d" % (layer, s), [128, 8, 512], BF16, st) for s in range(2)]
                dsW = [newds("ds_WG%d_%d" % (layer, s)) for s in range(2)]
                PR = [sb("PR%d_%d" % (layer, s), [128, 512], F32, st) for s in range(4)]
                TMP = [sb("TMP%d_%d" % (layer, s), [128, 256], F32, st) for s in range(4)]
                SQ = sb("SQ%d" % layer, [128, 512], F32, st)
                SSQ = sb("SSQ%d" % layer, [128, 8], F32, st)
                QK = [sb("QK%d_%d" % (layer, s), [128, 512], BF16, st) for s in range(4)]
                STG = [sb("STG%d_%d" % (layer, s), [128, 8320], BF16, st) for s in range(2)]
                pj = psum("pj%d" % layer, [128, 4, 512], F32, st)
                tq = psum("tq%d" % layer, [128, 4, 1024], BF16, st)

                m_w_free = [None, None]
                m_stg_free = [None, None]
                m_pj_free = [None] * 4
                m_pr_free = [None] * 4
                m_qk_free = [None] * 4
                m_tq_free = [None] * 4
                m_tmp_free = [None, None]
                pending = []
                cnt = 0

                def load_w(gi):
                    kind, c0, wd, _ = groups[gi]
                    s = gi % 2
                    sp.wait(m_w_free[s], dsw_of(gi).all())
                    dma(sp, dsW[s], WG[s][:, :, 0:wd], wsrc[:, c0:c0 + wd].rearrange("(k p) c -> p k c", p=128))

                load_w(0)
                for gi, (kind, c0, wd, dbase) in enumerate(groups):
                    ws = gi % 2
                    ss = gi % 2
                    if gi + 1 < len(groups):
                        load_w(gi + 1)
                    isv = kind == "v"
                    nh = wd // 64
                    nu = wd // 128
                    stg = STG[ss]
                    if isv:
                        nhv = wd // dv
                        stg_v = stg[:, 0:nhv * 16 * (dv + 1)].rearrange("p (h i d) -> p h i d", h=nhv, i=16)
                        pool.wait(m_stg_free[ss])
                        m_ones = pool.mark(nc.gpsimd.memset(stg[:, 0:nhv * 16 * (dv + 1)], 1.0))
                    else:
                        stg_q = stg[:, 0:nu * T].rearrange("p (u t) -> p u t", u=nu)
                        m_ones = None
                    last_ev = None
                    for i in range(NT):
                        b = cnt % 4
                        cnt += 1
                        pe.wait(dsW[ws].all(), m_pj_free[b])
                        for k in range(8):
                            ins = nc.tensor.matmul(pj[:, b, 0:wd], lhsT=AT[:, k, i * 128:(i + 1) * 128], rhs=WG[ws][:, k, 0:wd],
                                                   start=(k == 0), stop=(k == 7))
                        m_mm = pe.mark(ins)
                        m_w_free[ws] = m_mm
                        while len(pending) > 2:
                            last_ev = pending.pop(0)()
                        if isv:
                            act.wait(m_mm, m_ones, m_stg_free[ss])
                            m_c = act.mark(nc.scalar.copy(out=stg_v[:, :, i, 0:dv],
                                                          in_=pj[:, b, 0:wd].rearrange("p (h d) -> p h d", d=dv)))
                            m_pj_free[b] = m_c
                            last_ev = m_c
                            continue
                        act.wait(m_mm, m_pr_free[b])
                        m_c = act.mark(nc.scalar.copy(out=PR[b][:, 0:wd], in_=pj[:, b, 0:wd]))
                        m_pj_free[b] = m_c
                        pr3 = PR[b][:, 0:wd].rearrange("p (h d) -> p h d", d=64)
                        m_src = m_c
                        if kind in ("qn", "kn"):
                            gsel = 0 if kind == "qn" else 1
                            dve.wait(m_src)
                            m1 = dve.mark(nc.vector.tensor_tensor(out=SQ[:, 0:wd], in0=PR[b][:, 0:wd], in1=PR[b][:, 0:wd], op=ALU.mult))
                            dve.wait(m1)
                            m2 = dve.mark(nc.vector.tensor_reduce(out=SSQ[:, 0:nh], in_=SQ[:, 0:wd].rearrange("p (h d) -> p h d", d=64),
                                                                  axis=AX.X, op=ALU.add))
                            act.wait(m2)
                            m3 = act.mark(nc.scalar.activation(out=SSQ[:, 0:nh], in_=SSQ[:, 0:nh], func=AF.Sqrt,
                                                               bias=EPST[:, 0:1], scale=1.0 / 64))
                            dve.wait(m3)
                            m4 = dve.mark(nc.vector.reciprocal(out=SSQ[:, 0:nh], in_=SSQ[:, 0:nh]))
                            dve.wait(m4)
                            m5 = dve.mark(nc.vector.tensor_tensor(out=pr3, in0=pr3,
                                                                  in1=SSQ[:, 0:nh].unsqueeze(2).to_broadcast([128, nh, 64]), op=ALU.mult))
                            dve.wait(m5)
                            m_src = dve.mark(nc.vector.tensor_tensor(out=pr3, in0=pr3,
                                                                     in1=GQK[:, gsel, :].unsqueeze(1).to_broadcast([128, nh, 64]), op=ALU.mult))
                        ctab, stab = (2, 3) if kind in ("qn", "kn") else (0, 1)
                        cosb = ROPE[:, ctab, i, :].unsqueeze(1).to_broadcast([128, nh, 32])
                        sinb = ROPE[:, stab, i, :].unsqueeze(1).to_broadcast([128, nh, 32])
                        x1 = pr3[:, :, 0:32]
                        x2 = pr3[:, :, 32:64]
                        qk3 = QK[b][:, 0:wd].rearrange("p (h d) -> p h d", d=64)
                        t0 = TMP[0][:, 0:nh * 32].rearrange("p (h d) -> p h d", d=32)
                        t1 = TMP[1][:, 0:nh * 32].rearrange("p (h d) -> p h d", d=32)
                        t2 = TMP[2][:, 0:nh * 32].rearrange("p (h d) -> p h d", d=32)
                        t3 = TMP[3][:, 0:nh * 32].rearrange("p (h d) -> p h d", d=32)
                        dve.wait(m_src, m_qk_free[b], m_tmp_free[0])
                        ma = dve.mark(nc.vector.tensor_tensor(out=t0, in0=x1, in1=cosb, op=ALU.mult))
                        mb = dve.mark(nc.vector.tensor_tensor(out=t1, in0=x2, in1=sinb, op=ALU.mult))
                        dve.wait(ma, mb)
                        mq1 = dve.mark(nc.vector.tensor_tensor(out=qk3[:, :, 0:32], in0=t0, in1=t1, op=ALU.subtract))
                        pool.wait(m_src, m_qk_free[b], m_tmp_free[1])
                        mc = pool.mark(nc.gpsimd.tensor_tensor(out=t2, in0=x1, in1=sinb, op=ALU.mult))
                        md = pool.mark(nc.gpsimd.tensor_tensor(out=t3, in0=x2, in1=cosb, op=ALU.mult))
                        pool.wait(mc, md)
                        mq2 = pool.mark(nc.gpsimd.tensor_tensor(out=qk3[:, :, 32:64], in0=t2, in1=t3, op=ALU.add))
                        m_pr_free[b] = [mq1, mq2]
                        m_tmp_free[0] = mq1
                        m_tmp_free[1] = mq2

                        def do_tr(b=b, i=i, mq1=mq1, mq2=mq2):
                            pe.wait(mq1, mq2, m_tq_free[b])
                            for u in range(nu):
                                ins = nc.tensor.transpose(tq[:, b, u * 128:(u + 1) * 128], QK[b][:, u * 128:(u + 1) * 128], IDENT[:])
                            m_t = pe.mark(ins)
                            m_qk_free[b] = m_t
                            act.wait(m_t, m_stg_free[ss])
                            m_e = act.mark(nc.scalar.copy(out=stg_q[:, :, i * 128:(i + 1) * 128],
                                                          in_=tq[:, b, 0:wd].rearrange("p (u c) -> p u c", c=128)))
                            m_tq_free[b] = m_e
                            return m_e
                        pending.append(do_tr)
                    while pending:
                        last_ev = pending.pop(0)()
                    sp.wait(last_ev)
                    if isv:
                        ng = nhv // VGH[layer]
                        g0 = dbase // VGH[layer]
                        src = stg[:, 0:nhv * 16 * (dv + 1)].rearrange("p (h f) -> p h f", h=nhv)
                        for gg in range(ng):
                            m_st = dma(sp, ds_v, v_loc[layer][g0 + gg].rearrange("(h p) f -> p h f", p=128),
                                       src[:, gg * VGH[layer]:(gg + 1) * VGH[layer], :])
                        m_stg_free[ss] = m_st
                        pool.wait(m_st)
                        for gg in range(ng):
                            d = DS(nc, es, "cc_v%d_%d" % (layer, g0 + gg))
                            ins = nc.gpsimd.collective_compute("AllGather", ALU.bypass, replica_groups=[[0, 1, 2, 3], [4, 5, 6, 7]],
                                                               ins=[v_loc[layer][g0 + gg].opt()], outs=[v_all[layer][g0 + gg].opt()])
                            ins.then_inc(d.sem, 1)
                            d.n = 1
                            cc_v[g0 + gg] = d
                    elif kind in ("q", "qn"):
                        m_stg_free[ss] = dma(sp, ds_q, qt[layer][dbase * 128:(dbase + nu) * 128, :].rearrange("(u p) t -> p u t", p=128), stg_q)
                    else:
                        for uu in range(nu):
                            m_st = dma(sp, ds_kt, kt_loc[layer][dbase + uu], stg_q[:, uu, :])
                        m_stg_free[ss] = m_st
                        pool.wait(m_st)
                        for uu in range(nu):
                            d = DS(nc, es, "cc_k%d_%d" % (layer, dbase + uu))
                            ins = nc.gpsimd.collective_compute("AllGather", ALU.bypass, replica_groups=[[0, 1, 2, 3], [4, 5, 6, 7]],
                                                               ins=[kt_loc[layer][dbase + uu].opt()], outs=[kt_all[layer][dbase + uu].opt()])
                            ins.then_inc(d.sem, 1)
                            d.n = 1
                            cc_k[dbase + uu] = d
                barrier()
            return cc_k, cc_v

        def attention(layer):
            dvp = 65 if layer == 0 else 129
            dv = dvp - 1
            nbank = 2 if layer == 0 else 3
            per_bank = 4 if layer == 0 else 3
            with ExitStack() as st:
                psS = psum("psS%d" % layer, [128, 2, 2, 512], F32, st)
                psA = psum("psA%d" % layer, [128, nbank, 512], F32, st)
                psT = psum("psT%d" % layer, [128, 1024], BF16, st)
                QT = [sb("QT%d_%d" % (layer, s), [128, T], BF16, st) for s in range(2)]
                dsQ = [newds("dsQ%d_%d" % (layer, s)) for s in range(2)]
                PP = sb("PP%d" % layer, [128, 3, 2, 512], BF16, st)
                ACCS = sb("ACCS%d" % layer, [128, 8 * dvp], F32, st)
                RDEN = sb("RDEN%d" % layer, [128, 8], F32, st)
                MT = sb("MT%d" % layer, [128, 512], BF16, st)
                if layer == 1:
                    O1 = sb("O1", [128, 512], F32, st)
                    O2 = sb("O2", [128, 512], F32, st)
                    SS4 = sb("SS4", [128, 4], F32, st)
                    RS4 = sb("RS4", [128, 4], F32, st)
                if layer == 0:
                    KTA = sb("KTA", [128, 4, T], BF16, st)
                    VA = sb("VA", [128, 4, 16, 65], BF16, st)
                    dsKA = newds("dsKA")
                    KTB = [sb("KTB%d" % s, [128, NM * 128], BF16, st) for s in range(2)]
                    VB = [sb("VB%d" % s, [128, 2, NM, 65], BF16, st) for s in range(2)]
                    dsKB = [newds("dsKB%d" % s) for s in range(2)]
                    MASK = sb("MASK", [128, NM, 512], BF16, st)
                    dsM = newds("dsM")
                else:
                    KT = [sb("KT1_%d" % s, [128, 4, T], BF16, st) for s in range(2)]
                    V1 = [sb("V1_%d" % s, [128, 4, 16, 129], BF16, st) for s in range(2)]
                    dsKV = [newds("dsKV%d" % s) for s in range(2)]

                state = dict(m_s_free=[None, None], m_p_free=[None] * 3,
                             m_acc_free=None, m_accs_free=None, m_mt_free=None, m_psT_free=None, kbc=0)
                m_q_free = [None, None]

                def acc_ap(n):
                    bk, o = n // per_bank, (n % per_bank) * dvp
                    return psA[:, bk, o:o + dvp], bk

                def emit_s(sp_, kb):
                    la, lb, _, _, _, _ = sp_["blocks"][kb]
                    s = (sp_["base"] + kb) % 2
                    g_ = sp_["g"]
                    qt_ = sp_["qtile"]
                    pe.wait(sp_["m_load"], state["m_s_free"][s])
                    nc.tensor.matmul(psS[:, s, 0, :], lhsT=la, rhs=qt_[0:64, g_ * 512:(g_ + 1) * 512], start=True, stop=True)
                    ins = nc.tensor.matmul(psS[:, s, 1, :], lhsT=lb, rhs=qt_[64:128, g_ * 512:(g_ + 1) * 512], start=True, stop=True)
                    sp_["m_s"][kb] = pe.mark(ins)

                def run_groups(spec_iter):
                    it = iter(spec_iter)

                    def start(spec, base):
                        spec["base"] = base
                        spec["m_s"] = {}

                    cur = next(it, None)
                    while cur == "SYNC":
                        cur = next(it, None)
                    start(cur, state["kbc"])
                    emit_s(cur, 0)
                    emit_s(cur, 1)
                    while cur is not None:
                        nxt = next(it, None)
                        if nxt == "SYNC":
                            qgroup(cur, None)
                            nxt = next(it, None)
                            if nxt is not None:
                                start(nxt, cur["base"] + len(cur["blocks"]))
                                emit_s(nxt, 0)
                                emit_s(nxt, 1)
                        else:
                            if nxt is not None:
                                start(nxt, cur["base"] + len(cur["blocks"]))
                            qgroup(cur, nxt)
                        cur = nxt

                def qgroup(spec, nxt):
                    blocks = spec["blocks"]
                    g = spec["g"]
                    chunk = spec["chunk"]
                    m_buf_free_cb = spec["cb"]
                    m_s = spec["m_s"]
                    nb = len(blocks)
                    started = set()
                    last_pv = None
                    state["kbc_base"] = spec["base"]
                    for kb in range(nb):
                        _, _, va, vb, mi, jl = blocks[kb]
                        sl = state["kbc_base"] + kb
                        s = sl % 2
                        ps3 = sl % 3
                        act.wait(m_s[kb], state["m_p_free"][ps3])
                        me = act.mark(nc.scalar.activation(out=PP[:, ps3, :, :], in_=psS[:, s, :, :], func=AF.Exp, scale=0.125))
                        state["m_s_free"][s] = me
                        if mi is not None:
                            dve.wait(me)
                            me = dve.mark(nc.vector.tensor_tensor(out=PP[:, ps3, :, :], in0=PP[:, ps3, :, :],
                                                                  in1=MASK[:, mi, :].unsqueeze(1).to_broadcast([128, 2, 512]), op=ALU.mult))
                        if kb + 2 < nb:
                            emit_s(spec, kb + 2)
                        elif nxt is not None:
                            emit_s(nxt, kb + 2 - nb)
                        while state.get("deferred") and state["deferred"][0][0] <= kb:
                            state["deferred"].pop(0)[1]()
                        pe.wait(me, state["m_acc_free"])
                        ins = None
                        for mp in range(2):
                            vv = va if mp == 0 else vb
                            for j in jl:
                                o_ap, bk = acc_ap(mp * 4 + j)
                                first = bk not in started
                                started.add(bk)
                                ins = nc.tensor.matmul(o_ap, lhsT=PP[:, ps3, mp, j * 128:(j + 1) * 128], rhs=vv,
                                                       start=first, stop=(kb == nb - 1), skip_group_check=True)
                        m_pv = pe.mark(ins)
                        state["m_p_free"][ps3] = m_pv
                        last_pv = m_pv
                    state["kbc"] = state["kbc_base"] + nb
                    m_buf_free_cb(last_pv)
                    dve.wait(last_pv, state["m_accs_free"])
                    m = None
                    for bk in range(nbank):
                        ncols = min(per_bank, 8 - bk * per_bank) * dvp
                        m = dve.mark(nc.vector.tensor_copy(out=ACCS[:, bk * per_bank * dvp: bk * per_bank * dvp + ncols], in_=psA[:, bk, 0:ncols]))
                    state["m_acc_free"] = m
                    m_cp = m

                    a3 = ACCS[:].rearrange("p (n d) -> p n d", d=dvp)
                    box = {}

                    def ep_dve1(m=m_cp):
                        dve.wait(m)
                        m = dve.mark(nc.vector.reciprocal(out=RDEN[:], in_=a3[:, :, dv]))
                        dve.wait(m, state["m_mt_free"])
                        if layer == 0:
                            mt3 = MT[:].rearrange("p (j f) -> p j f", f=128)
                            for mp in range(2):
                                m = dve.mark(nc.vector.tensor_tensor(out=mt3[:, :, mp * 64:(mp + 1) * 64], in0=a3[:, mp * 4:(mp + 1) * 4, 0:64],
                                                                     in1=RDEN[:, mp * 4:(mp + 1) * 4].unsqueeze(2).to_broadcast([128, 4, 64]), op=ALU.mult))
                            state["m_accs_free"] = m
                            box["m_mt"] = m
                        else:
                            o13 = O1[:].rearrange("p (j f) -> p j f", f=128)
                            o23 = O2[:].rearrange("p (j f) -> p j f", f=128)
                            m = dve.mark(nc.vector.tensor_scalar(out=RDEN[:, 4:8], in0=RDEN[:, 4:8], scalar1=NEGLAM[:, 0:1], scalar2=None, op0=ALU.mult))
                            dve.wait(m)
                            nc.vector.tensor_tensor(out=o13, in0=a3[:, 0:4, 0:128], in1=RDEN[:, 0:4].unsqueeze(2).to_broadcast([128, 4, 128]), op=ALU.mult)
                            m = dve.mark(nc.vector.tensor_tensor(out=o23, in0=a3[:, 4:8, 0:128], in1=RDEN[:, 4:8].unsqueeze(2).to_broadcast([128, 4, 128]), op=ALU.mult))
                            state["m_accs_free"] = m
                            dve.wait(m)
                            m = dve.mark(nc.vector.tensor_tensor(out=O1[:], in0=O1[:], in1=O2[:], op=ALU.add))
                            dve.wait(m)
                            m = dve.mark(nc.vector.tensor_tensor(out=O2[:], in0=O1[:], in1=O1[:], op=ALU.mult))
                            dve.wait(m)
                            box["m_ss"] = dve.mark(nc.vector.tensor_reduce(out=SS4[:], in_=o23, axis=AX.X, op=ALU.add))

                    def ep_act():
                        act.wait(box["m_ss"])
                        m = act.mark(nc.scalar.activation(out=RS4[:], in_=SS4[:], func=AF.Ln, bias=EPST[:, 0:1], scale=1.0 / 128))
                        act.wait(m)
                        box["m_rs"] = act.mark(nc.scalar.activation(out=RS4[:], in_=RS4[:], func=AF.Exp, scale=-0.5))

                    def ep_dve2():
                        o13 = O1[:].rearrange("p (j f) -> p j f", f=128)
                        dve.wait(box["m_rs"])
                        m = dve.mark(nc.vector.tensor_tensor(out=o13, in0=o13, in1=RS4[:].unsqueeze(2).to_broadcast([128, 4, 128]), op=ALU.mult))
                        dve.wait(m)
                        box["m_mt"] = dve.mark(nc.vector.tensor_tensor(out=MT[:].rearrange("p (j f) -> p j f", f=128), in0=o13,
                                                                       in1=GAINC[:].unsqueeze(1).to_broadcast([128, 4, 128]), op=ALU.mult))

                    def ep_tr(chunk=chunk, g=g):
                        pe.wait(box["m_mt"], state["m_psT_free"])
                        for j in range(4):
                            ins = nc.tensor.transpose(psT[:, j * 128:(j + 1) * 128], MT[:, j * 128:(j + 1) * 128], IDENT[:])
                        m_t = pe.mark(ins)
                        state["m_mt_free"] = m_t
                        dve.wait(m_t)
                        state["m_psT_free"] = dve.mark(nc.vector.tensor_copy(out=AT[:, chunk, g * 512:(g + 1) * 512], in_=psT[:, 0:512]))

                    if layer == 0:
                        state["deferred"] = [(4, ep_dve1), (10, ep_tr)]
                    else:
                        state["deferred"] = [(4, ep_dve1), (16, ep_act), (18, ep_dve2), (24, ep_tr)]

                def load_q(u, slot):
                    sp.wait(m_q_free[slot])
                    return dma(sp, dsQ[slot], QT[slot][:], qt[layer][u * 128:(u + 1) * 128, :])

                if layer == 0:
                    cc_k, cc_v = cc[0]
                    kA = kt_all[0][0].rearrange("(r p) t -> p r t", p=128)
                    vA = v_all[0][0].rearrange("(r h p) f -> h p r f", r=4, h=2)
                    ds_rl = newds("ds_rl")
                    ds_win = newds("ds_win")

                    def emit_relayout():
                        pool.wait(ds_pad.all())
                        for u in range(4):
                            pool.wait(cc_k[1 + u].all())
                            dma(pool, ds_rl, ktB_pad[u * 128:(u + 1) * 128, 1024:1024 + 4 * T].rearrange("p (r t) -> p r t", r=4),
                                kt_all[0][1 + u].rearrange("(r p) t -> p r t", p=128))
                        for gi in range(4):
                            pool.wait(cc_v[1 + gi].all())
                            srcv = v_all[0][1 + gi].rearrange("(r h p) f -> h p r f", r=4, h=2)
                            for hh in range(2):
                                hB = 2 * gi + hh
                                dma(pool, ds_rl, vB_pad[hB * 128:(hB + 1) * 128, 8 * 65:72 * 65].rearrange("p (r f) -> p r f", r=4), srcv[hh])
                        pid = nc.gpsimd.partition_id()
                        rank = pid % 4
                        pool.wait(ds_rl.all())
                        dma(pool, ds_win, ktB_win, ktB_pad[:, bass.ds(rank * T, 4096)])
                        dma(pool, ds_win, vB_win, vB_pad[:, bass.ds(rank * (16 * 65), 32 * 65)])

                    unit_last = {}

                    def specs_AB():
                        mq = {0: load_q(0, 0)}
                        m_ka_free = None
                        for u in range(4):
                            kvh = u // 2
                            if u % 2 == 0:
                                if u == 2:
                                    yield "SYNC"
                                sp.wait(unit_last.get(u - 1), cc_k[0].all(), cc_v[0].all())
                                for half in range(2):
                                    dma(sp, dsKA, KTA[half * 64:(half + 1) * 64, :, :], kA[kvh * 64:(kvh + 1) * 64, :, :])
                                dma(sp, dsKA, VA[:].rearrange("p r i d -> p r (i d)"), vA[kvh])
                            lh = []
                            for g in range(4):
                                if g == 1:
                                    if u == 0:
                                        pool.wait(mq[0], dsKA.all())
                                        emit_relayout()
                                    if u == 1:
                                        pool.wait(unit_last[0])
                                        casts_layer0_rest()
                                        dma(sp, dsM, MASK[:], masks_in.rearrange("m p q -> p m q"))
                                    sp.wait(unit_last.get(u - 1))
                                    mq[u + 1] = dma(sp, dsQ[(u + 1) % 2], QT[(u + 1) % 2][:], qt[0][(u + 1) * 128:(u + 2) * 128, :])
                                blocks = []
                                for r in range(4):
                                    for i in range(16):
                                        blocks.append((KTA[0:64, r, i * 128:(i + 1) * 128], KTA[64:128, r, i * 128:(i + 1) * 128],
                                                       VA[:, r, i, :], VA[:, r, i, :], None, [0, 1, 2, 3]))

                                def cbA(m, u=u, lh=lh):
                                    lh.append(m)
                                    unit_last[u] = m
                                yield dict(qtile=QT[u % 2], g=g, chunk=u, blocks=blocks, m_load=[mq[u], dsKA.all()], cb=cbA)
                        m_kb_free = [None, None]
                        widx = 0
                        for ub in range(4):
                            u = 4 + ub
                            for g in range(4):
                                ws = widx % 2
                                widx += 1
                                sp.wait(m_kb_free[ws], ds_win.all())
                                dma(sp, dsKB[ws], KTB[ws][:], ktB_win[ub * 128:(ub + 1) * 128, g * 512:g * 512 + NM * 128])
                                for hh in range(2):
                                    h = 2 * ub + hh
                                    dma(sp, dsKB[ws], VB[ws][:, hh, :, :].rearrange("p m d -> p (m d)"),
                                        vB_win[h * 128:(h + 1) * 128, g * 4 * 65:(g * 4 + NM) * 65])
                                if g == 1 and ub == 0:
                                    pool.wait(unit_last[3])
                                    casts_layer1_first()
                                if g == 1 and u < 7:
                                    sp.wait(unit_last.get(u - 1))
                                    mq[u + 1] = dma(sp, dsQ[(u + 1) % 2], QT[(u + 1) % 2][:], qt[0][(u + 1) * 128:(u + 2) * 128, :])
                                blocks = []
                                for mi in range(NM):
                                    jl = [j for j in range(4) if j <= mi <= j + 16]
                                    blocks.append((KTB[ws][0:64, mi * 128:(mi + 1) * 128], KTB[ws][64:128, mi * 128:(mi + 1) * 128],
                                                   VB[ws][:, 0, mi, :], VB[ws][:, 1, mi, :], mi, jl))

                                def cbB(m, ws=ws, u=u):
                                    m_kb_free[ws] = m
                                    unit_last[u] = m
                                yield dict(qtile=QT[u % 2], g=g, chunk=u, blocks=blocks, m_load=[mq[u], dsKB[ws].all(), dsM.all()], cb=cbB)
                    run_groups(specs_AB())
                else:
                    cc_k, cc_v = cc[1]
                    m_kv_free = [None, None]

                    def load_kv(h, slot):
                        sp.wait(m_kv_free[slot], cc_k[h].all(), cc_v[h].all())
                        dma(sp, dsKV[slot], KT[slot][:], kt_all[1][h].rearrange("(r p) t -> p r t", p=128))
                        dma(sp, dsKV[slot], V1[slot][:].rearrange("p r i d -> p r (i d)"), v_all[1][h].rearrange("(r p) f -> p r f", p=128))

                    head_last = {}

                    def specs_C():
                        mq = {0: load_q(0, 0)}
                        load_kv(0, 0)
                        for h in range(8):
                            slot = h % 2
                            for g in range(4):
                                if g == 1 and h < 7:
                                    if h == 3:
                                        pool.wait(head_last[2])
                                        casts_layer1_rest()
                                    m_kv_free[1 - slot] = head_last.get(h - 1)
                                    sp.wait(head_last.get(h - 1))
                                    mq[h + 1] = dma(sp, dsQ[1 - slot], QT[1 - slot][:], qt[1][(h + 1) * 128:(h + 2) * 128, :])
                                    load_kv(h + 1, 1 - slot)
                                blocks = []
                                for r in range(4):
                                    for i in range(16):
                                        blocks.append((KT[slot][0:64, r, i * 128:(i + 1) * 128], KT[slot][64:128, r, i * 128:(i + 1) * 128],
                                                       V1[slot][:, r, i, :], V1[slot][:, r, i, :], None, [0, 1, 2, 3]))

                                def cbC(m, h=h):
                                    head_last[h] = m
                                yield dict(qtile=QT[slot], g=g, chunk=h, blocks=blocks, m_load=[mq[h], dsKV[slot].all()], cb=cbC)
                    run_groups(specs_C())
                while state.get("deferred"):
                    state["deferred"].pop(0)[1]()
                barrier()

        def outproj(layer):
            with ExitStack() as st:
                WO = sb("WO%d" % layer, [128, 8, D], BF16, st)
                dsO = newds("dsO%d" % layer)
                psO = psum("psO%d" % layer, [128, 2, 512], F32, st)
                sp.wait(ds_w["out%d" % layer].all())
                dma(sp, dsO, WO[:], wb_out[layer].rearrange("(k p) c -> p k c", p=128))
                m_free = [None, None]
                cnt = 0
                for i in range(NT):
                    for c in range(2):
                        b = cnt % 2
                        cnt += 1
                        pe.wait(dsO.all(), m_free[b])
                        for k in range(8):
                            ins = nc.tensor.matmul(psO[:, b, :], lhsT=AT[:, k, i * 128:(i + 1) * 128], rhs=WO[:, k, c * 512:(c + 1) * 512],
                                                   start=(k == 0), stop=(k == 7))
                        mm = pe.mark(ins)
                        dve.wait(mm)
                        m_free[b] = dve.mark(nc.vector.tensor_tensor(out=X[:, i, c * 512:(c + 1) * 512], in0=X[:, i, c * 512:(c + 1) * 512],
                                                                     in1=psO[:, b, :], op=ALU.add))
                barrier()

        def ffn(layer):
            with ExitStack() as st:
                WD = sb("WD%d" % layer, [128, NF, D], BF16, st)
                dsD = newds("dsD%d" % layer)
                WGU = [sb("WGU%d_%d" % (layer, s), [128, 2, 8, 256], BF16, st) for s in range(2)]
                dsGU = [newds("dsGU%d_%d" % (layer, s)) for s in range(2)]
                AFT = sb("AFT%d" % layer, [128, NF, 512], BF16, st)
                SG = [sb("SG%d_%d" % (layer, s), [128, 512], F32, st) for s in range(2)]
                psG = psum("psG%d" % layer, [128, 2, 512], F32, st)
                psU = psum("psU%d" % layer, [128, 2, 512], F32, st)
                psD = psum("psD%d" % layer, [128, 2, 512], F32, st)
                sp.wait(ds_w["d%d" % layer].all(), ds_w["g%d" % layer].all(), ds_w["u%d" % layer].all())
                for half in range(2):
                    dma(sp, dsD, WD[:, half * 11:(half + 1) * 11, :],
                        wb_d[layer][half * 11 * 128:(half + 1) * 11 * 128, :].rearrange("(f p) c -> p f c", p=128))
                m_gu_free = [None, None]
                m_g_free = [None, None]
                m_u_free = [None, None]
                m_sg_free = [None, None]
                m_d_free = [None, None]
                m_down_last = None
                lc = 0
                fc_cnt = 0
                dcnt = 0

                def load_gu(fg, slot):
                    sp.wait(m_gu_free[slot])
                    dma(sp, dsGU[slot], WGU[slot][:, 0, :, :], wb_g[layer][:, fg * 256:(fg + 1) * 256].rearrange("(k p) c -> p k c", p=128))
                    dma(sp, dsGU[slot], WGU[slot][:, 1, :, :], wb_u[layer][:, fg * 256:(fg + 1) * 256].rearrange("(k p) c -> p k c", p=128))

                seq = [(tg, fg) for tg in range(4) for fg in range(11)]
                load_gu(seq[0][1], 0)
                for si, (tg, fg) in enumerate(seq):
                    slot = si % 2
                    if si + 1 < len(seq):
                        load_gu(seq[si + 1][1], 1 - slot)
                    for fc in range(2):
                        f = fg * 2 + fc
                        b = fc_cnt % 2
                        fc_cnt += 1
                        pe.wait(dsGU[slot].all(), m_g_free[b], m_u_free[b])
                        for k in range(8):
                            nc.tensor.matmul(psG[:, b, :], lhsT=WGU[slot][:, 0, k, fc * 128:(fc + 1) * 128], rhs=AT[:, k, tg * 512:(tg + 1) * 512],
                                             start=(k == 0), stop=(k == 7))
                        for k in range(8):
                            ins = nc.tensor.matmul(psU[:, b, :], lhsT=WGU[slot][:, 1, k, fc * 128:(fc + 1) * 128], rhs=AT[:, k, tg * 512:(tg + 1) * 512],
                                                   start=(k == 0), stop=(k == 7))
                        mm = pe.mark(ins)
                        m_gu_free[slot] = mm
                        act.wait(mm, m_sg_free[b])
                        ms = act.mark(nc.scalar.activation(out=SG[b][:], in_=psG[:, b, :], func=AF.Silu))
                        m_g_free[b] = ms
                        dve.wait(ms, m_down_last)
                        ma = dve.mark(nc.vector.tensor_tensor(out=AFT[:, f, :], in0=SG[b][:], in1=psU[:, b, :], op=ALU.mult))
                        m_u_free[b] = ma
                        m_sg_free[b] = ma
                        last_a = ma
                    if fg == 10:
                        for j in range(4):
                            for c in range(2):
                                b = dcnt % 2
                                dcnt += 1
                                pe.wait(last_a, dsD.all(), m_d_free[b])
                                for f in range(NF):
                                    ins = nc.tensor.matmul(psD[:, b, :], lhsT=AFT[:, f, j * 128:(j + 1) * 128], rhs=WD[:, f, c * 512:(c + 1) * 512],
                                                           start=(f == 0), stop=(f == NF - 1))
                                mm = pe.mark(ins)
                                m_down_last = mm
                                i = tg * 4 + j
                                dve.wait(mm)
                                m_d_free[b] = dve.mark(nc.vector.tensor_tensor(out=X[:, i, c * 512:(c + 1) * 512], in0=X[:, i, c * 512:(c + 1) * 512],
                                                                               in1=psD[:, b, :], op=ALU.add))
                barrier()

        def final():
            with ExitStack() as st:
                G, RSTD = rms_rstd(st, lambda i: X[:, i, :], NT, final_norm[0:1, :], "fin")
                OB = [sb("OB%d" % s, [128, D], F32, st) for s in range(2)]
                ds_o = newds("ds_out")
                m_free = [None, None]
                for i in range(NT):
                    s = i % 2
                    dve.wait(m_free[s])
                    m = dve.mark(nc.vector.scalar_tensor_tensor(out=OB[s][:], in0=X[:, i, :], scalar=RSTD[:, i:i + 1], in1=G[:],
                                                                op0=ALU.mult, op1=ALU.mult))
                    sp.wait(m)
                    m_free[s] = dma(sp, ds_o, out[i * 128:(i + 1) * 128, :], OB[s][:])
                sp.wait(ds_o.all())
                pool.wait(ds_o.all())

        cc = {}
        dve.wait(ds_c.all())
        act.wait(ds_c.all())
        pe.wait(ds_c.all())
        pool.wait(ds_c.all())

        def dump(layer):
            barrier()
            dsd = newds("ds_dbg")
            for i in range(NT):
                dma(sp, dsd, dbg["X"][i * 128:(i + 1) * 128, :], X[:, i, :])
            dma(sp, dsd, dbg["AT"], AT[:].rearrange("p k t -> p (k t)"))
            dma(sp, dsd, dbg["qt"], qt[layer])
            for u in range(NKU[layer]):
                dma(sp, dsd, dbg["kt"][u * 512:(u + 1) * 512, :], kt_all[layer][u])
            for g in range(NVG[layer]):
                nr = 4 * VGH[layer] * 128
                dma(sp, dsd, dbg["v"][g * nr:(g + 1) * nr, 0:VF[layer]], v_all[layer][g])
            sp.wait(dsd.all())
            pool.wait(dsd.all())

        steps = [
            ("norm_a0", lambda: norm_AT(attn_norm[0:1, :], "a0", tile_ready=lambda i: ds_xg[i // 4].all())),
            ("inproj0", lambda: cc.__setitem__(0, inproj(0))),
            ("attn0", lambda: attention(0)),
            ("outproj0", lambda: outproj(0)),
            ("norm_f0", lambda: norm_AT(ffn_norm[0:1, :], "f0")),
            ("ffn0", lambda: ffn(0)),
            ("norm_a1", lambda: norm_AT(attn_norm[1:2, :], "a1")),
            ("inproj1", lambda: cc.__setitem__(1, inproj(1))),
            ("attn1", lambda: attention(1)),
            ("outproj1", lambda: outproj(1)),
            ("norm_f1", lambda: norm_AT(ffn_norm[1:2, :], "f1")),
            ("ffn1", lambda: ffn(1)),
            ("final", final),
        ]
        for name, fn in steps:
            fn()
            if debug and debug == name:
                dump(0 if name in ("norm_a0", "inproj0", "attn0", "outproj0", "norm_f0", "ffn0", "norm_a1") else 1)
                break
    return nc


def _const_tables():
    theta = 10000.0
    pos = np.arange(S, dtype=np.float32)
    inv64 = (theta ** (-np.arange(0, 64, 2, dtype=np.float32) / 64)).astype(np.float32)
    inv32 = (theta ** (-np.arange(0, 32, 2, dtype=np.float32) / 32)).astype(np.float32)
    ang1 = (pos[:, None] * inv64[None, :]).astype(np.float32)
    rows = (np.arange(S) // 64).astype(np.float32)
    cols = (np.arange(S) % 64).astype(np.float32)
    ang2 = np.concatenate([rows[:, None] * inv32[None, :], cols[:, None] * inv32[None, :]], axis=-1).astype(np.float32)
    rope = np.stack([np.cos(ang1.astype(np.float64)), np.sin(ang1.astype(np.float64)),
                     np.cos(ang2.astype(np.float64)), np.sin(ang2.astype(np.float64))], 0).astype(np.float32)
    k = np.arange(128)[:, None]
    q = np.arange(512)[None, :]
    masks = np.zeros((NM, 128, 512), np.float32)
    for m in range(NM):
        d = 128 * m + k - q - 1024
        ad = np.abs(d)
        masks[m] = (ad <= 64).astype(np.float32) + ((d % 4 == 0) & (ad <= 256)) + ((d % 16 == 0) & (ad <= 1024))
    ident = np.eye(128, dtype=np.float32)
    return rope, masks.astype(ml_dtypes.bfloat16), ident.astype(ml_dtypes.bfloat16)


_NC_CACHE = {}


def kernel(x, attn_norm, ffn_norm, final_norm, w_in_even, a_q_norm, a_k_norm, w_out_even,
           w_in_odd, lambda_q1, lambda_k1, lambda_q2, lambda_k2, c_sub_norm, w_out_odd,
           w_gate, w_up, w_down, _debug=False):
    f = lambda a: np.ascontiguousarray(np.asarray(a, dtype=np.float32))
    x = f(x)
    rope, masks, ident = _const_tables()
    shared = {
        "attn_norm": f(attn_norm), "ffn_norm": f(ffn_norm), "final_norm": f(final_norm).reshape(1, D),
        "w_in_even": f(w_in_even)[0], "a_q_norm": f(a_q_norm), "a_k_norm": f(a_k_norm),
        "w_out_even": f(w_out_even)[0], "w_in_odd": f(w_in_odd)[0],
        "lambda_q1": f(lambda_q1), "lambda_k1": f(lambda_k1), "lambda_q2": f(lambda_q2), "lambda_k2": f(lambda_k2),
        "c_sub_norm": f(c_sub_norm), "w_out_odd": f(w_out_odd)[0],
        "w_gate": f(w_gate), "w_up": f(w_up), "w_down": f(w_down),
        "masks": masks, "ident": ident,
    }
    in_maps = []
    for c in range(NCORE):
        b, r = c // 4, c % 4
        m = dict(shared)
        m["x"] = np.ascontiguousarray(x[b, r * T:(r + 1) * T, :])
        m["rope"] = np.ascontiguousarray(rope[:, r * T:(r + 1) * T, :])
        in_maps.append(m)
    key = _debug
    if key not in _NC_CACHE:
        _NC_CACHE[key] = build(debug=key)
    nc = _NC_CACHE[key]
    res = run_bass_kernel_spmd(nc, in_maps, core_ids=list(range(NCORE)))
    outp = np.zeros((2, S, D), np.float32)
    for c in range(NCORE):
        b, r = c // 4, c % 4
        outp[b, r * T:(r + 1) * T, :] = np.asarray(res.results[c]["out"])
    if _debug:
        return outp, res
    return outp
```

```python
import math
from contextlib import ExitStack

import numpy as np
import ml_dtypes

import concourse.bass as bass
import concourse.mybir as mybir
from concourse.bass_utils import run_bass_kernel_spmd

F32 = mybir.dt.float32
BF16 = mybir.dt.bfloat16
ALU = mybir.AluOpType
AF = mybir.ActivationFunctionType
AX = mybir.AxisListType

NCORE = 8
S = 8192
D = 1024
T = 2048
NT = 16
DFF = 2816
NF = 22
EPS = 1e-6
LAM_INIT = 0.8 - 0.6 * math.exp(-0.3 * 1)
KPAD = 1024 + S + 1536
VBLK = 8 + 64 + 12
NM = 20


class TL:
    def __init__(self, nc, es, eng, name):
        self.e = eng
        self.sem = es.enter_context(nc.semaphore(name))
        self.n = 0
        self.seen = {}

    def wait(self, *ms):
        for m in ms:
            if m is None:
                continue
            if isinstance(m, list):
                self.wait(*m)
                continue
            sem, v = m
            if sem is self.sem and False:
                continue
            k = id(sem)
            if self.seen.get(k, 0) >= v:
                continue
            self.e.wait_ge(sem, v)
            self.seen[k] = v

    def mark(self, ins):
        self.n += 1
        ins.then_inc(self.sem, 1)
        return (self.sem, self.n)


class DS:
    def __init__(self, nc, es, name):
        self.sem = es.enter_context(nc.semaphore(name))
        self.n = 0

    def add(self, ins):
        self.n += 16
        ins.then_inc(self.sem, 16)
        return (self.sem, self.n)

    def all(self):
        return (self.sem, self.n) if self.n else None


def build(debug=False):
    nc = bass.Bass("TRN2", target_bir_lowering=False)

    def din(name, shape, dt=F32):
        return nc.dram_tensor(name, list(shape), dt, kind="ExternalInput").ap()

    def dint(name, shape, dt=BF16):
        return nc.dram_tensor(name, list(shape), dt, kind="Internal").ap()

    x_in = din("x", [T, D])
    attn_norm = din("attn_norm", [2, D])
    ffn_norm = din("ffn_norm", [2, D])
    final_norm = din("final_norm", [1, D])
    w_in_even = din("w_in_even", [D, 2304])
    a_q_norm = din("a_q_norm", [1, 64])
    a_k_norm = din("a_k_norm", [1, 64])
    w_out_even = din("w_out_even", [D, D])
    w_in_odd = din("w_in_odd", [D, 3072])
    lam_in = [din(n, [1, 64]) for n in ("lambda_q1", "lambda_k1", "lambda_q2", "lambda_k2")]
    c_sub_norm = din("c_sub_norm", [1, 128])
    w_out_odd = din("w_out_odd", [D, D])
    w_gate = din("w_gate", [2, D, DFF])
    w_up = din("w_up", [2, D, DFF])
    w_down = din("w_down", [2, DFF, D])
    rope_in = din("rope", [4, T, 32])
    masks_in = din("masks", [NM, 128, 512], BF16)
    ident_in = din("ident", [128, 128], BF16)
    out = nc.dram_tensor("out", [T, D], F32, kind="ExternalOutput").ap()

    wb_in = [dint("wb_in0", [D, 2304]), dint("wb_in1", [D, 3072])]
    wb_out = [dint("wb_out0", [D, D]), dint("wb_out1", [D, D])]
    wb_g = [dint(f"wb_g{l}", [D, DFF]) for l in range(2)]
    wb_u = [dint(f"wb_u{l}", [D, DFF]) for l in range(2)]
    wb_d = [dint(f"wb_d{l}", [DFF, D]) for l in range(2)]

    qt = [dint("qt0", [8 * 128, T]), dint("qt1", [8 * 128, T])]
    NKU = [5, 8]
    VGH = [2, 1]
    NVG = [5, 8]
    VF = [16 * 65, 16 * 129]
    kt_loc = [[dint("kt%d_loc%d" % (l, u), [128, T]) for u in range(NKU[l])] for l in range(2)]
    kt_all = [[dint("kt%d_all%d" % (l, u), [4 * 128, T]) for u in range(NKU[l])] for l in range(2)]
    v_loc = [[dint("v%d_loc%d" % (l, g), [VGH[l] * 128, VF[l]]) for g in range(NVG[l])] for l in range(2)]
    v_all = [[dint("v%d_all%d" % (l, g), [4 * VGH[l] * 128, VF[l]]) for g in range(NVG[l])] for l in range(2)]
    ktB_pad = dint("ktB_pad", [4 * 128, KPAD])
    vB_pad = dint("vB_pad", [8 * 128, VBLK * 65])
    ktB_win = dint("ktB_win", [4 * 128, 4096])
    vB_win = dint("vB_win", [8 * 128, 32 * 65])

    dbg = {}
    if debug:
        dbg["X"] = nc.dram_tensor("dbg_X", [T, D], F32, kind="ExternalOutput").ap()
        dbg["AT"] = nc.dram_tensor("dbg_AT", [128, 8 * T], BF16, kind="ExternalOutput").ap()
        dbg["qt"] = nc.dram_tensor("dbg_qt", [8 * 128, T], BF16, kind="ExternalOutput").ap()
        dbg["kt"] = nc.dram_tensor("dbg_kt", [8 * 512, T], BF16, kind="ExternalOutput").ap()
        dbg["v"] = nc.dram_tensor("dbg_v", [8 * 1024, 16 * 129], BF16, kind="ExternalOutput").ap()

    es = ExitStack()
    with es:
        pe = TL(nc, es, nc.tensor, "t_pe")
        act = TL(nc, es, nc.scalar, "t_act")
        dve = TL(nc, es, nc.vector, "t_dve")
        pool = TL(nc, es, nc.gpsimd, "t_pool")
        sp = TL(nc, es, nc.sync, "t_sp")
        engines = [pe, act, dve, pool, sp]

        def sb(name, shape, dt, stack=None):
            return (stack or es).enter_context(nc.sbuf_tensor(name, list(shape), dt))

        def psum(name, shape, dt, stack):
            return stack.enter_context(nc.psum_tensor(name, list(shape), dt))

        def dma(q, ds, out_ap, in_ap):
            return ds.add(q.e.dma_start(out=out_ap, in_=in_ap))

        X = sb("X", [128, NT, D], F32)
        AT = sb("AT", [128, 8, T], BF16)
        ROPE = sb("ROPE", [128, 4, NT, 32], F32)
        IDENT = sb("IDENT", [128, 128], BF16)
        EPST = sb("EPST", [128, 1], F32)
        FENCE = sb("FENCE", [128, 4], F32)
        NEGLAM = sb("NEGLAM", [128, 1], F32)
        GAINC = sb("GAINC", [128, 128], F32)
        GQK = sb("GQK", [128, 2, 64], F32)
        ZERO = sb("ZERO", [128, 2048], BF16)

        fence_ps_stack = ExitStack()

        def barrier():
            ms = []
            ms.append(dve.mark(nc.vector.memset(FENCE[:, 0:1], 0.0)))
            ms.append(pool.mark(nc.gpsimd.memset(FENCE[:, 1:2], 0.0)))
            ms.append(act.mark(nc.scalar.activation(out=FENCE[:, 2:3], in_=EPST[:, 0:1], func=AF.Copy)))
            ms.append((pe.sem, pe.n) if pe.n else None)
            ms += [d.all() for d in all_ds]
            for e in engines:
                e.wait(*ms)

        all_ds = []

        def newds(name, in_barrier=True):
            d = DS(nc, es, name)
            if in_barrier:
                all_ds.append(d)
            return d

        ds_w = {}
        ds_c = newds("ds_c")
        ds_xg = [newds("ds_x%d" % j, in_barrier=False) for j in range(4)]
        nc.vector.memset(EPST[:], EPS)
        m_zero = pool.mark(nc.gpsimd.memset(ZERO[:], 0.0))
        dma(sp, ds_c, IDENT[:], ident_in)
        dma(sp, ds_c, ROPE[:], rope_in.rearrange("c (i p) d -> p c i d", p=128))
        dma(sp, ds_c, GQK[:, 0, :], a_q_norm.partition_broadcast(128))
        dma(sp, ds_c, GQK[:, 1, :], a_k_norm.partition_broadcast(128))
        dma(sp, ds_c, GAINC[:], c_sub_norm.partition_broadcast(128))
        for i in range(NT):
            dma(sp, ds_xg[i // 4], X[:, i, :], x_in[i * 128:(i + 1) * 128, :])

        def cast_w(key, dst, src, rows):
            d = newds("ds_w_" + key, in_barrier=False)
            ds_w[key] = d
            for r0 in range(0, rows, 128):
                dma(pool, d, dst[r0:r0 + 128, :], src[r0:r0 + 128, :])

        L0_GROUPS = [("kn", 512, 128, 0), ("v", 640, 128, 0), ("k", 1280, 512, 1), ("v", 1792, 512, 2),
                     ("qn", 0, 512, 0), ("q", 768, 512, 4)]
        for gi_, (_, c0_, wd_, _) in enumerate(L0_GROUPS):
            d_ = newds("ds_w_in0_%d" % gi_, in_barrier=False)
            ds_w["in0_%d" % gi_] = d_
            for r0 in range(0, D, 128):
                dma(pool, d_, wb_in[0][r0:r0 + 128, c0_:c0_ + wd_], w_in_even[r0:r0 + 128, c0_:c0_ + wd_])
        ds_pad = newds("ds_pad")
        sp.wait(m_zero)
        for u in range(4):
            dma(sp, ds_pad, ktB_pad[u * 128:(u + 1) * 128, 0:1024], ZERO[:, 0:1024])
            dma(sp, ds_pad, ktB_pad[u * 128:(u + 1) * 128, 1024 + S:KPAD], ZERO[:, 0:1536])
        for h in range(8):
            dma(sp, ds_pad, vB_pad[h * 128:(h + 1) * 128, 0:8 * 65], ZERO[:, 0:8 * 65])
            dma(sp, ds_pad, vB_pad[h * 128:(h + 1) * 128, 72 * 65:VBLK * 65], ZERO[:, 0:12 * 65])
        def casts_layer0_rest():
            cast_w("out0", wb_out[0], w_out_even, D)
            cast_w("g0", wb_g[0], w_gate[0], D)
            cast_w("u0", wb_u[0], w_up[0], D)
            cast_w("d0", wb_d[0], w_down[0], DFF)

        def casts_layer1_first():
            cast_w("in1", wb_in[1], w_in_odd, D)
            cast_w("out1", wb_out[1], w_out_odd, D)

        def casts_layer1_rest():
            cast_w("g1", wb_g[1], w_gate[1], D)
            cast_w("u1", wb_u[1], w_up[1], D)
            cast_w("d1", wb_d[1], w_down[1], DFF)

        with ExitStack() as st:
            LQ = sb("LQ", [128, 4, 64], F32, st)
            LS = sb("LS", [128, 4], F32, st)
            LJ = sb("LJ", [128, 64], F32, st)
            ds_l = newds("ds_l")
            for j in range(4):
                dma(sp, ds_l, LQ[:, j, :], lam_in[j].partition_broadcast(128))
            dve.wait(ds_l.all(), ds_c.all())
            nc.vector.memset(LS[:], 0.0)
            m = None
            for j in range(2):
                m = dve.mark(nc.vector.tensor_tensor(out=LJ[:], in0=LQ[:, 2 * j, :], in1=LQ[:, 2 * j + 1, :], op=ALU.mult))
                dve.wait(m)
                m = dve.mark(nc.vector.tensor_reduce(out=LS[:, j:j + 1], in_=LJ[:], axis=AX.X, op=ALU.add))
                dve.wait(m)
            act.wait(m)
            m = act.mark(nc.scalar.activation(out=LS[:, 2:4], in_=LS[:, 0:2], func=AF.Exp))
            dve.wait(m)
            m = dve.mark(nc.vector.tensor_tensor(out=NEGLAM[:], in0=LS[:, 3:4], in1=LS[:, 2:3], op=ALU.subtract))
            dve.wait(m)
            m = dve.mark(nc.vector.tensor_scalar(out=NEGLAM[:], in0=NEGLAM[:], scalar1=-LAM_INIT, scalar2=None, op0=ALU.add))
            dve.wait(m)
            dve.mark(nc.vector.tensor_scalar(out=GAINC[:], in0=GAINC[:], scalar1=1.0 - LAM_INIT, scalar2=None, op0=ALU.mult))
            barrier()

        def rms_rstd(st, src_of_tile, nt, gdram_row, tag, tile_ready=None):
            G = sb("G" + tag, [128, D], F32, st)
            JUNK = sb("JUNK" + tag, [128, D], BF16, st)
            SS = sb("SS" + tag, [128, nt], F32, st)
            RSTD = sb("RSTD" + tag, [128, nt], F32, st)
            dsg = newds("ds_g" + tag)
            dma(sp, dsg, G[:], gdram_row.partition_broadcast(128))
            m0 = dve.mark(nc.vector.memset(SS[:], 0.0))
            act.wait(m0)
            m = None
            for i in range(nt):
                if tile_ready is not None:
                    act.wait(tile_ready(i))
                m = act.mark(nc.scalar.activation(out=JUNK[:], in_=src_of_tile(i), func=AF.Square,
                                                  accum_out=SS[:, i:i + 1]))
            act.wait(m)
            m = act.mark(nc.scalar.activation(out=RSTD[:], in_=SS[:], func=AF.Sqrt, bias=EPST[:, 0:1], scale=1.0 / D))
            dve.wait(m, dsg.all())
            m = dve.mark(nc.vector.reciprocal(out=RSTD[:], in_=RSTD[:]))
            dve.wait(m)
            return G, RSTD

        def norm_AT(gdram_row, tag, tile_ready=None):
            with ExitStack() as st:
                G, RSTD = rms_rstd(st, lambda i: X[:, i, :], NT, gdram_row, tag, tile_ready)
                HB = sb("HB" + tag, [128, 2, D], BF16, st)
                tp = psum("tp" + tag, [128, 2, D], BF16, st)
                m_tr = [None, None]
                m_ev = [None, None]
                for i in range(NT):
                    s = i % 2
                    dve.wait(m_tr[s])
                    mh = dve.mark(nc.vector.scalar_tensor_tensor(out=HB[:, s, :], in0=X[:, i, :], scalar=RSTD[:, i:i + 1],
                                                                 in1=G[:], op0=ALU.mult, op1=ALU.mult))
                    pe.wait(mh, m_ev[s])
                    for k in range(8):
                        ins = nc.tensor.transpose(tp[:, s, k * 128:(k + 1) * 128], HB[:, s, k * 128:(k + 1) * 128], IDENT[:])
                    m_tr[s] = pe.mark(ins)
                    act.wait(m_tr[s])
                    m_ev[s] = act.mark(nc.scalar.copy(out=AT[:, :, i * 128:(i + 1) * 128],
                                                      in_=tp[:, s, :].rearrange("p (k c) -> p k c", c=128)))
                barrier()

        def inproj(layer):
            wsrc = wb_in[layer]
            dsw_of = (lambda gi: ds_w["in0_%d" % gi]) if layer == 0 else (lambda gi: ds_w["in1"])
            if layer == 0:
                groups = L0_GROUPS
                dv = 64
            else:
                groups = [("k", 1024, 512, 0), ("v", 2048, 512, 0), ("k", 1536, 512, 4), ("v", 2560, 512, 4),
                          ("q", 0, 512, 0), ("q", 512, 512, 4)]
                dv = 128
            ds_kt = newds("ds_kt%d" % layer)
            ds_v = newds("ds_v%d" % layer)
            ds_q = newds("ds_q%d" % layer)
            cc_k = {}
            cc_v = {}
            with ExitStack() as st:
                WG = [sb("WG%d_%d" % (layer, s), [128, 8, 512], BF16, st) for s in range(3)]
                dsW = [newds("ds_WG%d_%d" % (layer, s)) for s in range(3)]
                PR = [sb("PR%d_%d" % (layer, s), [128, 512], F32, st) for s in range(4)]
                TMP = [sb("TMP%d_%d" % (layer, s), [128, 256], F32, st) for s in range(4)]
                SQ = sb("SQ%d" % layer, [128, 512], F32, st)
                SSQ = sb("SSQ%d" % layer, [128, 8], F32, st)
                QK = [sb("QK%d_%d" % (layer, s), [128, 512], BF16, st) for s in range(4)]
                STG = [sb("STG%d_%d" % (layer, s), [128, 8320], BF16, st) for s in range(2)]
                pj = psum("pj%d" % layer, [128, 4, 512], F32, st)
                tq = psum("tq%d" % layer, [128, 4, 1024], BF16, st)

                m_w_free = [None, None, None]
                m_stg_free = [None, None]
                m_pj_free = [None] * 4
                m_pr_free = [None] * 4
                m_qk_free = [None] * 4
                m_tq_free = [None] * 4
                m_tmp_free = [None, None]
                m_sq_free = [None]
                pending = []
                cnt = 0

                def load_w(gi):
                    kind, c0, wd, _ = groups[gi]
                    s = gi % 3
                    sp.wait(m_w_free[s], dsw_of(gi).all())
                    dma(sp, dsW[s], WG[s][:, :, 0:wd], wsrc[:, c0:c0 + wd].rearrange("(k p) c -> p k c", p=128))

                load_w(0)
                load_w(1)
                cc_pending = []
                for gi, (kind, c0, wd, dbase) in enumerate(groups):
                    ws = gi % 3
                    ss = gi % 2
                    if gi + 2 < len(groups):
                        load_w(gi + 2)
                    isv = kind == "v"
                    nh = wd // 64
                    nu = wd // 128
                    stg = STG[ss]
                    if isv:
                        nhv = wd // dv
                        stg_v = stg[:, 0:nhv * 16 * (dv + 1)].rearrange("p (h i d) -> p h i d", h=nhv, i=16)
                        pool.wait(m_stg_free[ss])
                        m_ones = pool.mark(nc.gpsimd.memset(stg_v[:, :, :, dv:dv + 1], 1.0))
                    else:
                        stg_q = stg[:, 0:nu * T].rearrange("p (u t) -> p u t", u=nu)
                        m_ones = None
                    last_ev = None
                    for i in range(NT):
                        b = cnt % 4
                        cnt += 1
                        pe.wait(dsW[ws].all(), m_pj_free[b])
                        for k in range(8):
                            ins = nc.tensor.matmul(pj[:, b, 0:wd], lhsT=AT[:, k, i * 128:(i + 1) * 128], rhs=WG[ws][:, k, 0:wd],
                                                   start=(k == 0), stop=(k == 7))
                        m_mm = pe.mark(ins)
                        m_w_free[ws] = m_mm
                        while len(pending) > 2:
                            last_ev = pending.pop(0)()
                        if isv:
                            act.wait(m_mm, m_ones, m_stg_free[ss])
                            m_c = act.mark(nc.scalar.copy(out=stg_v[:, :, i, 0:dv],
                                                          in_=pj[:, b, 0:wd].rearrange("p (h d) -> p h d", d=dv)))
                            m_pj_free[b] = m_c
                            last_ev = m_c
                            continue
                        act.wait(m_mm, m_pr_free[b])
                        m_c = act.mark(nc.scalar.copy(out=PR[b][:, 0:wd], in_=pj[:, b, 0:wd]))
                        m_pj_free[b] = m_c
                        pr3 = PR[b][:, 0:wd].rearrange("p (h d) -> p h d", d=64)
                        m_src = m_c
                        if kind in ("qn", "kn"):
                            gsel = 0 if kind == "qn" else 1
                            pool.wait(m_src, m_sq_free[0])
                            m1 = pool.mark(nc.gpsimd.tensor_tensor(out=SQ[:, 0:wd], in0=PR[b][:, 0:wd], in1=PR[b][:, 0:wd], op=ALU.mult))
                            dve.wait(m1)
                            m2 = dve.mark(nc.vector.tensor_reduce(out=SSQ[:, 0:nh], in_=SQ[:, 0:wd].rearrange("p (h d) -> p h d", d=64),
                                                                  axis=AX.X, op=ALU.add))
                            m_sq_free[0] = m2
                            act.wait(m2)
                            m3 = act.mark(nc.scalar.activation(out=SSQ[:, 0:nh], in_=SSQ[:, 0:nh], func=AF.Sqrt,
                                                               bias=EPST[:, 0:1], scale=1.0 / 64))
                            dve.wait(m3)
                            m4 = dve.mark(nc.vector.reciprocal(out=SSQ[:, 0:nh], in_=SSQ[:, 0:nh]))
                            dve.wait(m4)
                            m5 = dve.mark(nc.vector.tensor_tensor(out=pr3, in0=pr3,
                                                                  in1=SSQ[:, 0:nh].unsqueeze(2).to_broadcast([128, nh, 64]), op=ALU.mult))
                            dve.wait(m5)
                            m_src = dve.mark(nc.vector.tensor_tensor(out=pr3, in0=pr3,
                                                                     in1=GQK[:, gsel, :].unsqueeze(1).to_broadcast([128, nh, 64]), op=ALU.mult))
                        ctab, stab = (2, 3) if kind in ("qn", "kn") else (0, 1)
                        cosb = ROPE[:, ctab, i, :].unsqueeze(1).to_broadcast([128, nh, 32])
                        sinb = ROPE[:, stab, i, :].unsqueeze(1).to_broadcast([128, nh, 32])
                        x1 = pr3[:, :, 0:32]
                        x2 = pr3[:, :, 32:64]
                        qk3 = QK[b][:, 0:wd].rearrange("p (h d) -> p h d", d=64)
                        t0 = TMP[0][:, 0:nh * 32].rearrange("p (h d) -> p h d", d=32)
                        t1 = TMP[1][:, 0:nh * 32].rearrange("p (h d) -> p h d", d=32)
                        t2 = TMP[2][:, 0:nh * 32].rearrange("p (h d) -> p h d", d=32)
                        t3 = TMP[3][:, 0:nh * 32].rearrange("p (h d) -> p h d", d=32)
                        if kind in ("qn", "kn"):
                            dve.wait(m_src, m_qk_free[b], m_tmp_free[0])
                            ma = dve.mark(nc.vector.tensor_tensor(out=t0, in0=x1, in1=cosb, op=ALU.mult))
                            mb = dve.mark(nc.vector.tensor_tensor(out=t1, in0=x2, in1=sinb, op=ALU.mult))
                            dve.wait(ma, mb)
                            mq1 = dve.mark(nc.vector.tensor_tensor(out=qk3[:, :, 0:32], in0=t0, in1=t1, op=ALU.subtract))
                            pool.wait(m_src, m_qk_free[b], m_tmp_free[1])
                            mc = pool.mark(nc.gpsimd.tensor_tensor(out=t2, in0=x1, in1=sinb, op=ALU.mult))
                            md = pool.mark(nc.gpsimd.tensor_tensor(out=t3, in0=x2, in1=cosb, op=ALU.mult))
                            pool.wait(mc, md)
                            mq2 = pool.mark(nc.gpsimd.tensor_tensor(out=qk3[:, :, 32:64], in0=t2, in1=t3, op=ALU.add))
                        else:
                            dve.wait(m_src, m_qk_free[b], m_tmp_free[0], m_tmp_free[1])
                            ma = dve.mark(nc.vector.tensor_tensor(out=t0, in0=x1, in1=cosb, op=ALU.mult))
                            mb = dve.mark(nc.vector.tensor_tensor(out=t1, in0=x2, in1=sinb, op=ALU.mult))
                            md = dve.mark(nc.vector.tensor_tensor(out=t3, in0=x2, in1=cosb, op=ALU.mult))
                            dve.wait(ma, mb)
                            mq1 = dve.mark(nc.vector.tensor_tensor(out=qk3[:, :, 0:32], in0=t0, in1=t1, op=ALU.subtract))
                            pool.wait(m_src, m_qk_free[b], m_tmp_free[1])
                            mc = pool.mark(nc.gpsimd.tensor_tensor(out=t2, in0=x1, in1=sinb, op=ALU.mult))
                            pool.wait(mc, md)
                            mq2 = pool.mark(nc.gpsimd.tensor_tensor(out=qk3[:, :, 32:64], in0=t2, in1=t3, op=ALU.add))
                        m_pr_free[b] = [mq1, mq2]
                        m_tmp_free[0] = mq1
                        m_tmp_free[1] = mq2

                        def do_tr(b=b, i=i, mq1=mq1, mq2=mq2):
                            pe.wait(mq1, mq2, m_tq_free[b])
                            for u in range(nu):
                                ins = nc.tensor.transpose(tq[:, b, u * 128:(u + 1) * 128], QK[b][:, u * 128:(u + 1) * 128], IDENT[:])
                            m_t = pe.mark(ins)
                            m_qk_free[b] = m_t
                            act.wait(m_t, m_stg_free[ss])
                            m_e = act.mark(nc.scalar.copy(out=stg_q[:, :, i * 128:(i + 1) * 128],
                                                          in_=tq[:, b, 0:wd].rearrange("p (u c) -> p u c", c=128)))
                            m_tq_free[b] = m_e
                            return m_e
                        pending.append(do_tr)
                    while pending:
                        last_ev = pending.pop(0)()
                    sp.wait(last_ev)
                    while cc_pending:
                        cc_pending.pop(0)()
                    if isv:
                        ng = nhv // VGH[layer]
                        g0 = dbase // VGH[layer]
                        src = stg[:, 0:nhv * 16 * (dv + 1)].rearrange("p (h f) -> p h f", h=nhv)
                        for gg in range(ng):
                            m_st = dma(sp, ds_v, v_loc[layer][g0 + gg].rearrange("(h p) f -> p h f", p=128),
                                       src[:, gg * VGH[layer]:(gg + 1) * VGH[layer], :])
                        m_stg_free[ss] = m_st

                        def launch_v(m_st=m_st, ng=ng, g0=g0):
                            pool.wait(m_st)
                            for gg in range(ng):
                                d = DS(nc, es, "cc_v%d_%d" % (layer, g0 + gg))
                                ins = nc.gpsimd.collective_compute("AllGather", ALU.bypass, replica_groups=[[0, 1, 2, 3], [4, 5, 6, 7]],
                                                                   ins=[v_loc[layer][g0 + gg].opt()], outs=[v_all[layer][g0 + gg].opt()])
                                ins.then_inc(d.sem, 1)
                                d.n = 1
                                cc_v[g0 + gg] = d
                        cc_pending.append(launch_v)
                    elif kind in ("q", "qn"):
                        m_stg_free[ss] = dma(sp, ds_q, qt[layer][dbase * 128:(dbase + nu) * 128, :].rearrange("(u p) t -> p u t", p=128), stg_q)
                    else:
                        for uu in range(nu):
                            m_st = dma(sp, ds_kt, kt_loc[layer][dbase + uu], stg_q[:, uu, :])
                        m_stg_free[ss] = m_st

                        def launch_k(m_st=m_st, nu=nu, dbase=dbase):
                            pool.wait(m_st)
                            for uu in range(nu):
                                d = DS(nc, es, "cc_k%d_%d" % (layer, dbase + uu))
                                ins = nc.gpsimd.collective_compute("AllGather", ALU.bypass, replica_groups=[[0, 1, 2, 3], [4, 5, 6, 7]],
                                                                   ins=[kt_loc[layer][dbase + uu].opt()], outs=[kt_all[layer][dbase + uu].opt()])
                                ins.then_inc(d.sem, 1)
                                d.n = 1
                                cc_k[dbase + uu] = d
                        cc_pending.append(launch_k)
                while cc_pending:
                    cc_pending.pop(0)()
                barrier()
            return cc_k, cc_v

        def attention(layer):
            dvp = 65 if layer == 0 else 129
            dv = dvp - 1
            nbank = 2 if layer == 0 else 3
            per_bank = 4 if layer == 0 else 3
            with ExitStack() as st:
                psS = psum("psS%d" % layer, [128, 2, 2, 512], F32, st)
                psA = psum("psA%d" % layer, [128, nbank, 512], F32, st)
                psT = psum("psT%d" % layer, [128, 1024], BF16, st)
                QT = [sb("QT%d_%d" % (layer, s), [128, T], BF16, st) for s in range(2)]
                dsQ = [newds("dsQ%d_%d" % (layer, s)) for s in range(2)]
                PP = sb("PP%d" % layer, [128, 3, 2, 512], BF16, st)
                ACCS = sb("ACCS%d" % layer, [128, 8 * dvp], F32, st)
                RDEN = sb("RDEN%d" % layer, [128, 8], F32, st)
                MT = sb("MT%d" % layer, [128, 512], BF16, st)
                if layer == 1:
                    O1 = sb("O1", [128, 512], F32, st)
                    O2 = sb("O2", [128, 512], F32, st)
                    SS4 = sb("SS4", [128, 4], F32, st)
                    RS4 = sb("RS4", [128, 4], F32, st)
                if layer == 0:
                    KTA = sb("KTA", [128, 4, T], BF16, st)
                    VA = sb("VA", [128, 4, 16, 65], BF16, st)
                    dsKA = newds("dsKA")
                    KTB = [sb("KTB%d" % s, [128, NM * 128], BF16, st) for s in range(2)]
                    VB = [sb("VB%d" % s, [128, 2, NM, 65], BF16, st) for s in range(2)]
                    dsKB = [newds("dsKB%d" % s) for s in range(2)]
                    MASK = sb("MASK", [128, NM, 512], BF16, st)
                    dsM = newds("dsM")
                else:
                    KT = [sb("KT1_%d" % s, [128, 4, T], BF16, st) for s in range(2)]
                    V1 = [sb("V1_%d" % s, [128, 4, 16, 129], BF16, st) for s in range(2)]
                    dsKV = [newds("dsKV%d" % s) for s in range(2)]

                state = dict(m_s_free=[None, None], m_p_free=[None] * 3,
                             m_acc_free=None, m_accs_free=None, m_mt_free=None, m_psT_free=None, kbc=0)
                m_q_free = [None, None]

                def acc_ap(n):
                    bk, o = n // per_bank, (n % per_bank) * dvp
                    return psA[:, bk, o:o + dvp], bk

                def emit_s(sp_, kb):
                    la, lb, _, _, _, _ = sp_["blocks"][kb]
                    s = (sp_["base"] + kb) % 2
                    g_ = sp_["g"]
                    qt_ = sp_["qtile"]
                    pe.wait(sp_["m_load"], state["m_s_free"][s])
                    nc.tensor.matmul(psS[:, s, 0, :], lhsT=la, rhs=qt_[0:64, g_ * 512:(g_ + 1) * 512], start=True, stop=True)
                    ins = nc.tensor.matmul(psS[:, s, 1, :], lhsT=lb, rhs=qt_[64:128, g_ * 512:(g_ + 1) * 512], start=True, stop=True)
                    sp_["m_s"][kb] = pe.mark(ins)

                def run_groups(spec_iter):
                    it = iter(spec_iter)

                    def start(spec, base):
                        spec["base"] = base
                        spec["m_s"] = {}

                    cur = next(it, None)
                    while cur == "SYNC":
                        cur = next(it, None)
                    start(cur, state["kbc"])
                    emit_s(cur, 0)
                    emit_s(cur, 1)
                    while cur is not None:
                        nxt = next(it, None)
                        if nxt == "SYNC":
                            qgroup(cur, None)
                            nxt = next(it, None)
                            if nxt is not None:
                                start(nxt, cur["base"] + len(cur["blocks"]))
                                emit_s(nxt, 0)
                                emit_s(nxt, 1)
                        else:
                            if nxt is not None:
                                start(nxt, cur["base"] + len(cur["blocks"]))
                            qgroup(cur, nxt)
                        cur = nxt

                def qgroup(spec, nxt):
                    blocks = spec["blocks"]
                    g = spec["g"]
                    chunk = spec["chunk"]
                    m_buf_free_cb = spec["cb"]
                    m_s = spec["m_s"]
                    nb = len(blocks)
                    started = set()
                    last_pv = None
                    state["kbc_base"] = spec["base"]
                    for kb in range(nb):
                        _, _, va, vb, mi, jl = blocks[kb]
                        sl = state["kbc_base"] + kb
                        s = sl % 2
                        ps3 = sl % 3
                        act.wait(m_s[kb], state["m_p_free"][ps3])
                        me = act.mark(nc.scalar.activation(out=PP[:, ps3, :, :], in_=psS[:, s, :, :], func=AF.Exp, scale=0.125))
                        state["m_s_free"][s] = me
                        if mi is not None:
                            dve.wait(me)
                            me = dve.mark(nc.vector.tensor_tensor(out=PP[:, ps3, :, :], in0=PP[:, ps3, :, :],
                                                                  in1=MASK[:, mi, :].unsqueeze(1).to_broadcast([128, 2, 512]), op=ALU.mult))
                        if kb + 2 < nb:
                            emit_s(spec, kb + 2)
                        elif nxt is not None:
                            emit_s(nxt, kb + 2 - nb)
                        while state.get("deferred") and state["deferred"][0][0] <= kb:
                            state["deferred"].pop(0)[1]()
                        pe.wait(me, state["m_acc_free"])
                        ins = None
                        for mp in range(2):
                            vv = va if mp == 0 else vb
                            for j in jl:
                                o_ap, bk = acc_ap(mp * 4 + j)
                                first = bk not in started
                                started.add(bk)
                                ins = nc.tensor.matmul(o_ap, lhsT=PP[:, ps3, mp, j * 128:(j + 1) * 128], rhs=vv,
                                                       start=first, stop=(kb == nb - 1), skip_group_check=True)
                        m_pv = pe.mark(ins)
                        state["m_p_free"][ps3] = m_pv
                        last_pv = m_pv
                    state["kbc"] = state["kbc_base"] + nb
                    m_buf_free_cb(last_pv)
                    dve.wait(last_pv, state["m_accs_free"])
                    m = None
                    for bk in range(nbank):
                        ncols = min(per_bank, 8 - bk * per_bank) * dvp
                        m = dve.mark(nc.vector.tensor_copy(out=ACCS[:, bk * per_bank * dvp: bk * per_bank * dvp + ncols], in_=psA[:, bk, 0:ncols]))
                    state["m_acc_free"] = m
                    m_cp = m

                    a3 = ACCS[:].rearrange("p (n d) -> p n d", d=dvp)
                    box = {}

                    def ep_dve1(m=m_cp):
                        dve.wait(m)
                        m = dve.mark(nc.vector.reciprocal(out=RDEN[:], in_=a3[:, :, dv]))
                        dve.wait(m, state["m_mt_free"])
                        if layer == 0:
                            mt3 = MT[:].rearrange("p (j f) -> p j f", f=128)
                            for mp in range(2):
                                m = dve.mark(nc.vector.tensor_tensor(out=mt3[:, :, mp * 64:(mp + 1) * 64], in0=a3[:, mp * 4:(mp + 1) * 4, 0:64],
                                                                     in1=RDEN[:, mp * 4:(mp + 1) * 4].unsqueeze(2).to_broadcast([128, 4, 64]), op=ALU.mult))
                            state["m_accs_free"] = m
                            box["m_mt"] = m
                        else:
                            o13 = O1[:].rearrange("p (j f) -> p j f", f=128)
                            o23 = O2[:].rearrange("p (j f) -> p j f", f=128)
                            m = dve.mark(nc.vector.tensor_scalar(out=RDEN[:, 4:8], in0=RDEN[:, 4:8], scalar1=NEGLAM[:, 0:1], scalar2=None, op0=ALU.mult))
                            dve.wait(m)
                            nc.vector.tensor_tensor(out=o13, in0=a3[:, 0:4, 0:128], in1=RDEN[:, 0:4].unsqueeze(2).to_broadcast([128, 4, 128]), op=ALU.mult)
                            m = dve.mark(nc.vector.tensor_tensor(out=o23, in0=a3[:, 4:8, 0:128], in1=RDEN[:, 4:8].unsqueeze(2).to_broadcast([128, 4, 128]), op=ALU.mult))
                            state["m_accs_free"] = m
                            dve.wait(m)
                            m = dve.mark(nc.vector.tensor_tensor(out=O1[:], in0=O1[:], in1=O2[:], op=ALU.add))
                            dve.wait(m)
                            m = dve.mark(nc.vector.tensor_tensor(out=O2[:], in0=O1[:], in1=O1[:], op=ALU.mult))
                            dve.wait(m)
                            box["m_ss"] = dve.mark(nc.vector.tensor_reduce(out=SS4[:], in_=o23, axis=AX.X, op=ALU.add))

                    def ep_act():
                        act.wait(box["m_ss"])
                        m = act.mark(nc.scalar.activation(out=RS4[:], in_=SS4[:], func=AF.Ln, bias=EPST[:, 0:1], scale=1.0 / 128))
                        act.wait(m)
                        box["m_rs"] = act.mark(nc.scalar.activation(out=RS4[:], in_=RS4[:], func=AF.Exp, scale=-0.5))

                    def ep_dve2():
                        o13 = O1[:].rearrange("p (j f) -> p j f", f=128)
                        dve.wait(box["m_rs"])
                        m = dve.mark(nc.vector.tensor_tensor(out=o13, in0=o13, in1=RS4[:].unsqueeze(2).to_broadcast([128, 4, 128]), op=ALU.mult))
                        dve.wait(m)
                        box["m_mt"] = dve.mark(nc.vector.tensor_tensor(out=MT[:].rearrange("p (j f) -> p j f", f=128), in0=o13,
                                                                       in1=GAINC[:].unsqueeze(1).to_broadcast([128, 4, 128]), op=ALU.mult))

                    def ep_tr(chunk=chunk, g=g):
                        pe.wait(box["m_mt"], state["m_psT_free"])
                        for j in range(4):
                            ins = nc.tensor.transpose(psT[:, j * 128:(j + 1) * 128], MT[:, j * 128:(j + 1) * 128], IDENT[:])
                        m_t = pe.mark(ins)
                        state["m_mt_free"] = m_t
                        dve.wait(m_t)
                        state["m_psT_free"] = dve.mark(nc.vector.tensor_copy(out=AT[:, chunk, g * 512:(g + 1) * 512], in_=psT[:, 0:512]))

                    if layer == 0:
                        state["deferred"] = [(4, ep_dve1), (10, ep_tr)]
                    else:
                        state["deferred"] = [(4, ep_dve1), (16, ep_act), (18, ep_dve2), (24, ep_tr)]

                def load_q(u, slot):
                    sp.wait(m_q_free[slot])
                    return dma(sp, dsQ[slot], QT[slot][:], qt[layer][u * 128:(u + 1) * 128, :])

                if layer == 0:
                    cc_k, cc_v = cc[0]
                    kA = kt_all[0][0].rearrange("(r p) t -> p r t", p=128)
                    vA = v_all[0][0].rearrange("(r h p) f -> h p r f", r=4, h=2)
                    ds_rl = newds("ds_rl")
                    ds_win = newds("ds_win")

                    def emit_relayout():
                        pool.wait(ds_pad.all())
                        for u in range(4):
                            pool.wait(cc_k[1 + u].all())
                            dma(pool, ds_rl, ktB_pad[u * 128:(u + 1) * 128, 1024:1024 + 4 * T].rearrange("p (r t) -> p r t", r=4),
                                kt_all[0][1 + u].rearrange("(r p) t -> p r t", p=128))
                        for gi in range(4):
                            pool.wait(cc_v[1 + gi].all())
                            srcv = v_all[0][1 + gi].rearrange("(r h p) f -> h p r f", r=4, h=2)
                            for hh in range(2):
                                hB = 2 * gi + hh
                                dma(pool, ds_rl, vB_pad[hB * 128:(hB + 1) * 128, 8 * 65:72 * 65].rearrange("p (r f) -> p r f", r=4), srcv[hh])
                        pid = nc.gpsimd.partition_id()
                        rank = pid % 4
                        pool.wait(ds_rl.all())
                        dma(pool, ds_win, ktB_win, ktB_pad[:, bass.ds(rank * T, 4096)])
                        dma(pool, ds_win, vB_win, vB_pad[:, bass.ds(rank * (16 * 65), 32 * 65)])

                    unit_last = {}

                    def specs_AB():
                        mq = {0: load_q(0, 0)}
                        m_ka_free = None
                        for u in range(4):
                            kvh = u // 2
                            if u % 2 == 0:
                                if u == 2:
                                    yield "SYNC"
                                sp.wait(unit_last.get(u - 1), cc_k[0].all(), cc_v[0].all())
                                for half in range(2):
                                    dma(sp, dsKA, KTA[half * 64:(half + 1) * 64, :, :], kA[kvh * 64:(kvh + 1) * 64, :, :])
                                dma(sp, dsKA, VA[:].rearrange("p r i d -> p r (i d)"), vA[kvh])
                            lh = []
                            for g in range(4):
                                if g == 1:
                                    if u == 0:
                                        pool.wait(mq[0], dsKA.all())
                                        emit_relayout()
                                    if u == 1:
                                        pool.wait(unit_last[0])
                                        casts_layer0_rest()
                                        dma(sp, dsM, MASK[:], masks_in.rearrange("m p q -> p m q"))
                                    sp.wait(unit_last.get(u - 1))
                                    mq[u + 1] = dma(sp, dsQ[(u + 1) % 2], QT[(u + 1) % 2][:], qt[0][(u + 1) * 128:(u + 2) * 128, :])
                                blocks = []
                                for r in range(4):
                                    for i in range(16):
                                        blocks.append((KTA[0:64, r, i * 128:(i + 1) * 128], KTA[64:128, r, i * 128:(i + 1) * 128],
                                                       VA[:, r, i, :], VA[:, r, i, :], None, [0, 1, 2, 3]))

                                def cbA(m, u=u, lh=lh):
                                    lh.append(m)
                                    unit_last[u] = m
                                yield dict(qtile=QT[u % 2], g=g, chunk=u, blocks=blocks, m_load=[mq[u], dsKA.all()], cb=cbA)
                        m_kb_free = [None, None]
                        widx = 0
                        for ub in range(4):
                            u = 4 + ub
                            for g in range(4):
                                ws = widx % 2
                                widx += 1
                                sp.wait(m_kb_free[ws], ds_win.all())
                                dma(sp, dsKB[ws], KTB[ws][:], ktB_win[ub * 128:(ub + 1) * 128, g * 512:g * 512 + NM * 128])
                                for hh in range(2):
                                    h = 2 * ub + hh
                                    dma(sp, dsKB[ws], VB[ws][:, hh, :, :].rearrange("p m d -> p (m d)"),
                                        vB_win[h * 128:(h + 1) * 128, g * 4 * 65:(g * 4 + NM) * 65])
                                if g == 1 and ub == 0:
                                    pool.wait(unit_last[3])
                                    casts_layer1_first()
                                if g == 1 and u < 7:
                                    sp.wait(unit_last.get(u - 1))
                                    mq[u + 1] = dma(sp, dsQ[(u + 1) % 2], QT[(u + 1) % 2][:], qt[0][(u + 1) * 128:(u + 2) * 128, :])
                                blocks = []
                                for mi in range(NM):
                                    jl = [j for j in range(4) if j <= mi <= j + 16]
                                    blocks.append((KTB[ws][0:64, mi * 128:(mi + 1) * 128], KTB[ws][64:128, mi * 128:(mi + 1) * 128],
                                                   VB[ws][:, 0, mi, :], VB[ws][:, 1, mi, :], mi, jl))

                                def cbB(m, ws=ws, u=u):
                                    m_kb_free[ws] = m
                                    unit_last[u] = m
                                yield dict(qtile=QT[u % 2], g=g, chunk=u, blocks=blocks, m_load=[mq[u], dsKB[ws].all(), dsM.all()], cb=cbB)
                    run_groups(specs_AB())
                else:
                    cc_k, cc_v = cc[1]
                    m_kv_free = [None, None]

                    def load_kv(h, slot):
                        sp.wait(m_kv_free[slot], cc_k[h].all(), cc_v[h].all())
                        dma(sp, dsKV[slot], KT[slot][:], kt_all[1][h].rearrange("(r p) t -> p r t", p=128))
                        dma(sp, dsKV[slot], V1[slot][:].rearrange("p r i d -> p r (i d)"), v_all[1][h].rearrange("(r p) f -> p r f", p=128))

                    head_last = {}

                    def specs_C():
                        mq = {0: load_q(0, 0)}
                        load_kv(0, 0)
                        for h in range(8):
                            slot = h % 2
                            for g in range(4):
                                if g == 1 and h < 7:
                                    if h == 3:
                                        pool.wait(head_last[2])
                                        casts_layer1_rest()
                                    m_kv_free[1 - slot] = head_last.get(h - 1)
                                    sp.wait(head_last.get(h - 1))
                                    mq[h + 1] = dma(sp, dsQ[1 - slot], QT[1 - slot][:], qt[1][(h + 1) * 128:(h + 2) * 128, :])
                                    load_kv(h + 1, 1 - slot)
                                blocks = []
                                for r in range(4):
                                    for i in range(16):
                                        blocks.append((KT[slot][0:64, r, i * 128:(i + 1) * 128], KT[slot][64:128, r, i * 128:(i + 1) * 128],
                                                       V1[slot][:, r, i, :], V1[slot][:, r, i, :], None, [0, 1, 2, 3]))

                                def cbC(m, h=h):
                                    head_last[h] = m
                                yield dict(qtile=QT[slot], g=g, chunk=h, blocks=blocks, m_load=[mq[h], dsKV[slot].all()], cb=cbC)
                    run_groups(specs_C())
                while state.get("deferred"):
                    state["deferred"].pop(0)[1]()
                barrier()

        def outproj(layer):
            with ExitStack() as st:
                WO = sb("WO%d" % layer, [128, 8, D], BF16, st)
                dsO = newds("dsO%d" % layer)
                psO = psum("psO%d" % layer, [128, 2, 512], F32, st)
                sp.wait(ds_w["out%d" % layer].all())
                dma(sp, dsO, WO[:], wb_out[layer].rearrange("(k p) c -> p k c", p=128))
                m_free = [None, None]
                cnt = 0
                for i in range(NT):
                    for c in range(2):
                        b = cnt % 2
                        cnt += 1
                        pe.wait(dsO.all(), m_free[b])
                        for k in range(8):
                            ins = nc.tensor.matmul(psO[:, b, :], lhsT=AT[:, k, i * 128:(i + 1) * 128], rhs=WO[:, k, c * 512:(c + 1) * 512],
                                                   start=(k == 0), stop=(k == 7))
                        mm = pe.mark(ins)
                        dve.wait(mm)
                        m_free[b] = dve.mark(nc.vector.tensor_tensor(out=X[:, i, c * 512:(c + 1) * 512], in0=X[:, i, c * 512:(c + 1) * 512],
                                                                     in1=psO[:, b, :], op=ALU.add))
                barrier()

        def ffn(layer):
            with ExitStack() as st:
                WD = sb("WD%d" % layer, [128, NF, D], BF16, st)
                dsD = newds("dsD%d" % layer)
                WGU = [sb("WGU%d_%d" % (layer, s), [128, 2, 8, 256], BF16, st) for s in range(2)]
                dsGU = [newds("dsGU%d_%d" % (layer, s)) for s in range(2)]
                AFT = sb("AFT%d" % layer, [128, NF, 512], BF16, st)
                SG = [sb("SG%d_%d" % (layer, s), [128, 512], F32, st) for s in range(2)]
                psG = psum("psG%d" % layer, [128, 2, 512], F32, st)
                psU = psum("psU%d" % layer, [128, 2, 512], F32, st)
                psD = psum("psD%d" % layer, [128, 2, 512], F32, st)
                sp.wait(ds_w["d%d" % layer].all(), ds_w["g%d" % layer].all(), ds_w["u%d" % layer].all())
                for half in range(2):
                    dma(sp, dsD, WD[:, half * 11:(half + 1) * 11, :],
                        wb_d[layer][half * 11 * 128:(half + 1) * 11 * 128, :].rearrange("(f p) c -> p f c", p=128))
                m_gu_free = [None, None]
                m_g_free = [None, None]
                m_u_free = [None, None]
                m_sg_free = [None, None]
                m_d_free = [None, None]
                m_down_last = None
                lc = 0
                fc_cnt = 0
                dcnt = 0

                def load_gu(fg, slot):
                    sp.wait(m_gu_free[slot])
                    dma(sp, dsGU[slot], WGU[slot][:, 0, :, :], wb_g[layer][:, fg * 256:(fg + 1) * 256].rearrange("(k p) c -> p k c", p=128))
                    dma(sp, dsGU[slot], WGU[slot][:, 1, :, :], wb_u[layer][:, fg * 256:(fg + 1) * 256].rearrange("(k p) c -> p k c", p=128))

                seq = [(tg, fg) for tg in range(4) for fg in range(11)]
                load_gu(seq[0][1], 0)
                for si, (tg, fg) in enumerate(seq):
                    slot = si % 2
                    if si + 1 < len(seq):
                        load_gu(seq[si + 1][1], 1 - slot)
                    for fc in range(2):
                        f = fg * 2 + fc
                        b = fc_cnt % 2
                        fc_cnt += 1
                        pe.wait(dsGU[slot].all(), m_g_free[b], m_u_free[b])
                        for k in range(8):
                            nc.tensor.matmul(psG[:, b, :], lhsT=WGU[slot][:, 0, k, fc * 128:(fc + 1) * 128], rhs=AT[:, k, tg * 512:(tg + 1) * 512],
                                             start=(k == 0), stop=(k == 7))
                        for k in range(8):
                            ins = nc.tensor.matmul(psU[:, b, :], lhsT=WGU[slot][:, 1, k, fc * 128:(fc + 1) * 128], rhs=AT[:, k, tg * 512:(tg + 1) * 512],
                                                   start=(k == 0), stop=(k == 7))
                        mm = pe.mark(ins)
                        m_gu_free[slot] = mm
                        act.wait(mm, m_sg_free[b])
                        ms = act.mark(nc.scalar.activation(out=SG[b][:], in_=psG[:, b, :], func=AF.Silu))
                        m_g_free[b] = ms
                        dve.wait(ms, m_down_last)
                        ma = dve.mark(nc.vector.tensor_tensor(out=AFT[:, f, :], in0=SG[b][:], in1=psU[:, b, :], op=ALU.mult))
                        m_u_free[b] = ma
                        m_sg_free[b] = ma
                        last_a = ma
                    if fg == 10:
                        for j in range(4):
                            for c in range(2):
                                b = dcnt % 2
                                dcnt += 1
                                pe.wait(last_a, dsD.all(), m_d_free[b])
                                for f in range(NF):
                                    ins = nc.tensor.matmul(psD[:, b, :], lhsT=AFT[:, f, j * 128:(j + 1) * 128], rhs=WD[:, f, c * 512:(c + 1) * 512],
                                                           start=(f == 0), stop=(f == NF - 1))
                                mm = pe.mark(ins)
                                m_down_last = mm
                                i = tg * 4 + j
                                dve.wait(mm)
                                m_d_free[b] = dve.mark(nc.vector.tensor_tensor(out=X[:, i, c * 512:(c + 1) * 512], in0=X[:, i, c * 512:(c + 1) * 512],
                                                                               in1=psD[:, b, :], op=ALU.add))
                barrier()

        def final():
            with ExitStack() as st:
                G, RSTD = rms_rstd(st, lambda i: X[:, i, :], NT, final_norm[0:1, :], "fin")
                OB = [sb("OB%d" % s, [128, D], F32, st) for s in range(2)]
                ds_o = newds("ds_out")
                m_free = [None, None]
                for i in range(NT):
                    s = i % 2
                    dve.wait(m_free[s])
                    m = dve.mark(nc.vector.scalar_tensor_tensor(out=OB[s][:], in0=X[:, i, :], scalar=RSTD[:, i:i + 1], in1=G[:],
                                                                op0=ALU.mult, op1=ALU.mult))
                    sp.wait(m)
                    m_free[s] = dma(sp, ds_o, out[i * 128:(i + 1) * 128, :], OB[s][:])
                sp.wait(ds_o.all())
                pool.wait(ds_o.all())

        cc = {}
        dve.wait(ds_c.all())
        act.wait(ds_c.all())
        pe.wait(ds_c.all())
        pool.wait(ds_c.all())

        def dump(layer):
            barrier()
            dsd = newds("ds_dbg")
            for i in range(NT):
                dma(sp, dsd, dbg["X"][i * 128:(i + 1) * 128, :], X[:, i, :])
            dma(sp, dsd, dbg["AT"], AT[:].rearrange("p k t -> p (k t)"))
            dma(sp, dsd, dbg["qt"], qt[layer])
            for u in range(NKU[layer]):
                dma(sp, dsd, dbg["kt"][u * 512:(u + 1) * 512, :], kt_all[layer][u])
            for g in range(NVG[layer]):
                nr = 4 * VGH[layer] * 128
                dma(sp, dsd, dbg["v"][g * nr:(g + 1) * nr, 0:VF[layer]], v_all[layer][g])
            sp.wait(dsd.all())
            pool.wait(dsd.all())

        steps = [
            ("norm_a0", lambda: norm_AT(attn_norm[0:1, :], "a0", tile_ready=lambda i: ds_xg[i // 4].all())),
            ("inproj0", lambda: cc.__setitem__(0, inproj(0))),
            ("attn0", lambda: attention(0)),
            ("outproj0", lambda: outproj(0)),
            ("norm_f0", lambda: norm_AT(ffn_norm[0:1, :], "f0")),
            ("ffn0", lambda: ffn(0)),
            ("norm_a1", lambda: norm_AT(attn_norm[1:2, :], "a1")),
            ("inproj1", lambda: cc.__setitem__(1, inproj(1))),
            ("attn1", lambda: attention(1)),
            ("outproj1", lambda: outproj(1)),
            ("norm_f1", lambda: norm_AT(ffn_norm[1:2, :], "f1")),
            ("ffn1", lambda: ffn(1)),
            ("final", final),
        ]
        for name, fn in steps:
            fn()
            if debug and debug == name:
                dump(0 if name in ("norm_a0", "inproj0", "attn0", "outproj0", "norm_f0", "ffn0", "norm_a1") else 1)
                break
    return nc


def _const_tables():
    theta = 10000.0
    pos = np.arange(S, dtype=np.float32)
    inv64 = (theta ** (-np.arange(0, 64, 2, dtype=np.float32) / 64)).astype(np.float32)
    inv32 = (theta ** (-np.arange(0, 32, 2, dtype=np.float32) / 32)).astype(np.float32)
    ang1 = (pos[:, None] * inv64[None, :]).astype(np.float32)
    rows = (np.arange(S) // 64).astype(np.float32)
    cols = (np.arange(S) % 64).astype(np.float32)
    ang2 = np.concatenate([rows[:, None] * inv32[None, :], cols[:, None] * inv32[None, :]], axis=-1).astype(np.float32)
    rope = np.stack([np.cos(ang1.astype(np.float64)), np.sin(ang1.astype(np.float64)),
                     np.cos(ang2.astype(np.float64)), np.sin(ang2.astype(np.float64))], 0).astype(np.float32)
    k = np.arange(128)[:, None]
    q = np.arange(512)[None, :]
    masks = np.zeros((NM, 128, 512), np.float32)
    for m in range(NM):
        d = 128 * m + k - q - 1024
        ad = np.abs(d)
        masks[m] = (ad <= 64).astype(np.float32) + ((d % 4 == 0) & (ad <= 256)) + ((d % 16 == 0) & (ad <= 1024))
    ident = np.eye(128, dtype=np.float32)
    return rope, masks.astype(ml_dtypes.bfloat16), ident.astype(ml_dtypes.bfloat16)


_NC_CACHE = {}


def kernel(x, attn_norm, ffn_norm, final_norm, w_in_even, a_q_norm, a_k_norm, w_out_even,
           w_in_odd, lambda_q1, lambda_k1, lambda_q2, lambda_k2, c_sub_norm, w_out_odd,
           w_gate, w_up, w_down, _debug=False):
    f = lambda a: np.ascontiguousarray(np.asarray(a, dtype=np.float32))
    x = f(x)
    rope, masks, ident = _const_tables()
    shared = {
        "attn_norm": f(attn_norm), "ffn_norm": f(ffn_norm), "final_norm": f(final_norm).reshape(1, D),
        "w_in_even": f(w_in_even)[0], "a_q_norm": f(a_q_norm), "a_k_norm": f(a_k_norm),
        "w_out_even": f(w_out_even)[0], "w_in_odd": f(w_in_odd)[0],
        "lambda_q1": f(lambda_q1), "lambda_k1": f(lambda_k1), "lambda_q2": f(lambda_q2), "lambda_k2": f(lambda_k2),
        "c_sub_norm": f(c_sub_norm), "w_out_odd": f(w_out_odd)[0],
        "w_gate": f(w_gate), "w_up": f(w_up), "w_down": f(w_down),
        "masks": masks, "ident": ident,
    }
    in_maps = []
    for c in range(NCORE):
        b, r = c // 4, c % 4
        m = dict(shared)
        m["x"] = np.ascontiguousarray(x[b, r * T:(r + 1) * T, :])
        m["rope"] = np.ascontiguousarray(rope[:, r * T:(r + 1) * T, :])
        in_maps.append(m)
    key = _debug
    if key not in _NC_CACHE:
        _NC_CACHE[key] = build(debug=key)
    nc = _NC_CACHE[key]
    res = run_bass_kernel_spmd(nc, in_maps, core_ids=list(range(NCORE)))
    outp = np.zeros((2, S, D), np.float32)
    for c in range(NCORE):
        b, r = c // 4, c % 4
        outp[b, r * T:(r + 1) * T, :] = np.asarray(res.results[c]["out"])
    if _debug:
        return outp, res
    return outp
```

```python
import math
from contextlib import ExitStack

import numpy as np
import ml_dtypes

import concourse.bass as bass
import concourse.mybir as mybir
from concourse.bass_utils import run_bass_kernel_spmd

F32 = mybir.dt.float32
BF16 = mybir.dt.bfloat16
ALU = mybir.AluOpType
AF = mybir.ActivationFunctionType
AX = mybir.AxisListType

NCORE = 8
S = 8192
D = 1024
T = 2048
NT = 16
DFF = 2816
NF = 22
EPS = 1e-6
LAM_INIT = 0.8 - 0.6 * math.exp(-0.3 * 1)
KPAD = 1024 + S + 1536
VBLK = 8 + 64 + 12
NM = 20


class TL:
    def __init__(self, nc, es, eng, name):
        self.e = eng
        self.sem = es.enter_context(nc.semaphore(name))
        self.n = 0
        self.seen = {}

    def wait(self, *ms):
        for m in ms:
            if m is None:
                continue
            if isinstance(m, list):
                self.wait(*m)
                continue
            sem, v = m
            if sem is self.sem and False:
                continue
            k = id(sem)
            if self.seen.get(k, 0) >= v:
                continue
            self.e.wait_ge(sem, v)
            self.seen[k] = v

    def mark(self, ins):
        self.n += 1
        ins.then_inc(self.sem, 1)
        return (self.sem, self.n)


class DS:
    def __init__(self, nc, es, name):
        self.sem = es.enter_context(nc.semaphore(name))
        self.n = 0

    def add(self, ins):
        self.n += 16
        ins.then_inc(self.sem, 16)
        return (self.sem, self.n)

    def all(self):
        return (self.sem, self.n) if self.n else None


def build(debug=False):
    nc = bass.Bass("TRN2", target_bir_lowering=False)

    def din(name, shape, dt=F32):
        return nc.dram_tensor(name, list(shape), dt, kind="ExternalInput").ap()

    def dint(name, shape, dt=BF16):
        return nc.dram_tensor(name, list(shape), dt, kind="Internal").ap()

    x_in = din("x", [T, D])
    attn_norm = din("attn_norm", [2, D])
    ffn_norm = din("ffn_norm", [2, D])
    final_norm = din("final_norm", [1, D])
    w_in_even = din("w_in_even", [D, 2304])
    a_q_norm = din("a_q_norm", [1, 64])
    a_k_norm = din("a_k_norm", [1, 64])
    w_out_even = din("w_out_even", [D, D])
    w_in_odd = din("w_in_odd", [D, 3072])
    lam_in = [din(n, [1, 64]) for n in ("lambda_q1", "lambda_k1", "lambda_q2", "lambda_k2")]
    c_sub_norm = din("c_sub_norm", [1, 128])
    w_out_odd = din("w_out_odd", [D, D])
    w_gate = din("w_gate", [2, D, DFF])
    w_up = din("w_up", [2, D, DFF])
    w_down = din("w_down", [2, DFF, D])
    rope_in = din("rope", [4, T, 32])
    masks_in = din("masks", [NM, 128, 512], BF16)
    ident_in = din("ident", [128, 128], BF16)
    out = nc.dram_tensor("out", [T, D], F32, kind="ExternalOutput").ap()

    wb_in = [dint("wb_in0", [D, 2304]), dint("wb_in1", [D, 3072])]
    wb_out = [dint("wb_out0", [D, D]), dint("wb_out1", [D, D])]
    wb_g = [dint(f"wb_g{l}", [D, DFF]) for l in range(2)]
    wb_u = [dint(f"wb_u{l}", [D, DFF]) for l in range(2)]
    wb_d = [dint(f"wb_d{l}", [DFF, D]) for l in range(2)]

    qt = [dint("qt0", [8 * 128, T]), dint("qt1", [8 * 128, T])]
    NKU = [5, 8]
    VGH = [2, 1]
    NVG = [5, 8]
    VF = [16 * 65, 16 * 129]
    kt_loc = [[dint("kt%d_loc%d" % (l, u), [128, T]) for u in range(NKU[l])] for l in range(2)]
    kt_all = [[dint("kt%d_all%d" % (l, u), [4 * 128, T]) for u in range(NKU[l])] for l in range(2)]
    v_loc = [[dint("v%d_loc%d" % (l, g), [VGH[l] * 128, VF[l]]) for g in range(NVG[l])] for l in range(2)]
    v_all = [[dint("v%d_all%d" % (l, g), [4 * VGH[l] * 128, VF[l]]) for g in range(NVG[l])] for l in range(2)]
    ktB_pad = dint("ktB_pad", [4 * 128, KPAD])
    vB_pad = dint("vB_pad", [8 * 128, VBLK * 65])
    ktB_win = dint("ktB_win", [4 * 128, 4096])
    vB_win = dint("vB_win", [8 * 128, 32 * 65])

    dbg = {}
    if debug:
        dbg["X"] = nc.dram_tensor("dbg_X", [T, D], F32, kind="ExternalOutput").ap()
        dbg["AT"] = nc.dram_tensor("dbg_AT", [128, 8 * T], BF16, kind="ExternalOutput").ap()
        dbg["qt"] = nc.dram_tensor("dbg_qt", [8 * 128, T], BF16, kind="ExternalOutput").ap()
        dbg["kt"] = nc.dram_tensor("dbg_kt", [8 * 512, T], BF16, kind="ExternalOutput").ap()
        dbg["v"] = nc.dram_tensor("dbg_v", [8 * 1024, 16 * 129], BF16, kind="ExternalOutput").ap()

    es = ExitStack()
    with es:
        pe = TL(nc, es, nc.tensor, "t_pe")
        act = TL(nc, es, nc.scalar, "t_act")
        dve = TL(nc, es, nc.vector, "t_dve")
        pool = TL(nc, es, nc.gpsimd, "t_pool")
        sp = TL(nc, es, nc.sync, "t_sp")
        engines = [pe, act, dve, pool, sp]

        def sb(name, shape, dt, stack=None):
            return (stack or es).enter_context(nc.sbuf_tensor(name, list(shape), dt))

        def psum(name, shape, dt, stack):
            return stack.enter_context(nc.psum_tensor(name, list(shape), dt))

        def dma(q, ds, out_ap, in_ap):
            return ds.add(q.e.dma_start(out=out_ap, in_=in_ap))

        X = sb("X", [128, NT, D], F32)
        AT = sb("AT", [128, 8, T], BF16)
        ROPE = sb("ROPE", [128, 4, NT, 32], F32)
        IDENT = sb("IDENT", [128, 128], BF16)
        EPST = sb("EPST", [128, 1], F32)
        FENCE = sb("FENCE", [128, 4], F32)
        NEGLAM = sb("NEGLAM", [128, 1], F32)
        GAINC = sb("GAINC", [128, 128], F32)
        GQK = sb("GQK", [128, 2, 64], F32)
        ZERO = sb("ZERO", [128, 2048], BF16)

        fence_ps_stack = ExitStack()

        def barrier():
            ms = []
            ms.append(dve.mark(nc.vector.memset(FENCE[:, 0:1], 0.0)))
            ms.append(pool.mark(nc.gpsimd.memset(FENCE[:, 1:2], 0.0)))
            ms.append(act.mark(nc.scalar.activation(out=FENCE[:, 2:3], in_=EPST[:, 0:1], func=AF.Copy)))
            ms.append((pe.sem, pe.n) if pe.n else None)
            ms += [d.all() for d in all_ds]
            for e in engines:
                e.wait(*ms)

        all_ds = []

        def newds(name, in_barrier=True):
            d = DS(nc, es, name)
            if in_barrier:
                all_ds.append(d)
            return d

        ds_w = {}
        ds_c = newds("ds_c")
        ds_xg = [newds("ds_x%d" % j, in_barrier=False) for j in range(4)]
        nc.vector.memset(EPST[:], EPS)
        m_zero = pool.mark(nc.gpsimd.memset(ZERO[:], 0.0))
        dma(sp, ds_c, IDENT[:], ident_in)
        dma(sp, ds_c, ROPE[:], rope_in.rearrange("c (i p) d -> p c i d", p=128))
        dma(sp, ds_c, GQK[:, 0, :], a_q_norm.partition_broadcast(128))
        dma(sp, ds_c, GQK[:, 1, :], a_k_norm.partition_broadcast(128))
        dma(sp, ds_c, GAINC[:], c_sub_norm.partition_broadcast(128))
        for i in range(NT):
            dma(sp, ds_xg[i // 4], X[:, i, :], x_in[i * 128:(i + 1) * 128, :])

        def cast_w(key, dst, src, rows):
            d = newds("ds_w_" + key, in_barrier=False)
            ds_w[key] = d
            for r0 in range(0, rows, 128):
                dma(pool, d, dst[r0:r0 + 128, :], src[r0:r0 + 128, :])

        L0_GROUPS = [("kn", 512, 128, 0), ("v", 640, 128, 0), ("k", 1280, 512, 1), ("v", 1792, 512, 2),
                     ("qn", 0, 512, 0), ("q", 768, 512, 4)]
        for gi_, (_, c0_, wd_, _) in enumerate(L0_GROUPS):
            d_ = newds("ds_w_in0_%d" % gi_, in_barrier=False)
            ds_w["in0_%d" % gi_] = d_
            for r0 in range(0, D, 128):
                dma(pool, d_, wb_in[0][r0:r0 + 128, c0_:c0_ + wd_], w_in_even[r0:r0 + 128, c0_:c0_ + wd_])
        ds_pad = newds("ds_pad")
        sp.wait(m_zero)
        for u in range(4):
            dma(sp, ds_pad, ktB_pad[u * 128:(u + 1) * 128, 0:1024], ZERO[:, 0:1024])
            dma(sp, ds_pad, ktB_pad[u * 128:(u + 1) * 128, 1024 + S:KPAD], ZERO[:, 0:1536])
        for h in range(8):
            dma(sp, ds_pad, vB_pad[h * 128:(h + 1) * 128, 0:8 * 65], ZERO[:, 0:8 * 65])
            dma(sp, ds_pad, vB_pad[h * 128:(h + 1) * 128, 72 * 65:VBLK * 65], ZERO[:, 0:12 * 65])
        def casts_layer0_rest():
            cast_w("out0", wb_out[0], w_out_even, D)
            cast_w("g0", wb_g[0], w_gate[0], D)
            cast_w("u0", wb_u[0], w_up[0], D)
            cast_w("d0", wb_d[0], w_down[0], DFF)

        def casts_layer1_first():
            cast_w("in1", wb_in[1], w_in_odd, D)
            cast_w("out1", wb_out[1], w_out_odd, D)

        def casts_layer1_rest():
            cast_w("g1", wb_g[1], w_gate[1], D)
            cast_w("u1", wb_u[1], w_up[1], D)
            cast_w("d1", wb_d[1], w_down[1], DFF)

        with ExitStack() as st:
            LQ = sb("LQ", [128, 4, 64], F32, st)
            LS = sb("LS", [128, 4], F32, st)
            LJ = sb("LJ", [128, 64], F32, st)
            ds_l = newds("ds_l")
            for j in range(4):
                dma(sp, ds_l, LQ[:, j, :], lam_in[j].partition_broadcast(128))
            dve.wait(ds_l.all(), ds_c.all())
            nc.vector.memset(LS[:], 0.0)
            m = None
            for j in range(2):
                m = dve.mark(nc.vector.tensor_tensor(out=LJ[:], in0=LQ[:, 2 * j, :], in1=LQ[:, 2 * j + 1, :], op=ALU.mult))
                dve.wait(m)
                m = dve.mark(nc.vector.tensor_reduce(out=LS[:, j:j + 1], in_=LJ[:], axis=AX.X, op=ALU.add))
                dve.wait(m)
            act.wait(m)
            m = act.mark(nc.scalar.activation(out=LS[:, 2:4], in_=LS[:, 0:2], func=AF.Exp))
            dve.wait(m)
            m = dve.mark(nc.vector.tensor_tensor(out=NEGLAM[:], in0=LS[:, 3:4], in1=LS[:, 2:3], op=ALU.subtract))
            dve.wait(m)
            m = dve.mark(nc.vector.tensor_scalar(out=NEGLAM[:], in0=NEGLAM[:], scalar1=-LAM_INIT, scalar2=None, op0=ALU.add))
            dve.wait(m)
            dve.mark(nc.vector.tensor_scalar(out=GAINC[:], in0=GAINC[:], scalar1=1.0 - LAM_INIT, scalar2=None, op0=ALU.mult))
            barrier()

        def rms_rstd(st, src_of_tile, nt, gdram_row, tag, tile_ready=None):
            G = sb("G" + tag, [128, D], F32, st)
            JUNK = sb("JUNK" + tag, [128, D], BF16, st)
            SS = sb("SS" + tag, [128, nt], F32, st)
            RSTD = sb("RSTD" + tag, [128, nt], F32, st)
            dsg = newds("ds_g" + tag)
            dma(sp, dsg, G[:], gdram_row.partition_broadcast(128))
            m0 = dve.mark(nc.vector.memset(SS[:], 0.0))
            act.wait(m0)
            m = None
            for i in range(nt):
                if tile_ready is not None:
                    act.wait(tile_ready(i))
                m = act.mark(nc.scalar.activation(out=JUNK[:], in_=src_of_tile(i), func=AF.Square,
                                                  accum_out=SS[:, i:i + 1]))
            act.wait(m)
            m = act.mark(nc.scalar.activation(out=RSTD[:], in_=SS[:], func=AF.Sqrt, bias=EPST[:, 0:1], scale=1.0 / D))
            dve.wait(m, dsg.all())
            m = dve.mark(nc.vector.reciprocal(out=RSTD[:], in_=RSTD[:]))
            dve.wait(m)
            return G, RSTD

        def norm_AT(gdram_row, tag, tile_ready=None):
            with ExitStack() as st:
                G, RSTD = rms_rstd(st, lambda i: X[:, i, :], NT, gdram_row, tag, tile_ready)
                HB = sb("HB" + tag, [128, 2, D], BF16, st)
                tp = psum("tp" + tag, [128, 2, D], BF16, st)
                m_tr = [None, None]
                m_ev = [None, None]
                for i in range(NT):
                    s = i % 2
                    dve.wait(m_tr[s])
                    mh = dve.mark(nc.vector.scalar_tensor_tensor(out=HB[:, s, :], in0=X[:, i, :], scalar=RSTD[:, i:i + 1],
                                                                 in1=G[:], op0=ALU.mult, op1=ALU.mult))
                    pe.wait(mh, m_ev[s])
                    for k in range(8):
                        ins = nc.tensor.transpose(tp[:, s, k * 128:(k + 1) * 128], HB[:, s, k * 128:(k + 1) * 128], IDENT[:])
                    m_tr[s] = pe.mark(ins)
                    act.wait(m_tr[s])
                    m_ev[s] = act.mark(nc.scalar.copy(out=AT[:, :, i * 128:(i + 1) * 128],
                                                      in_=tp[:, s, :].rearrange("p (k c) -> p k c", c=128)))
                barrier()

        def inproj(layer):
            wsrc = wb_in[layer]
            dsw_of = (lambda gi: ds_w["in0_%d" % gi]) if layer == 0 else (lambda gi: ds_w["in1"])
            if layer == 0:
                groups = L0_GROUPS
                dv = 64
            else:
                groups = [("k", 1024, 512, 0), ("v", 2048, 512, 0), ("k", 1536, 512, 4), ("v", 2560, 512, 4),
                          ("q", 0, 512, 0), ("q", 512, 512, 4)]
                dv = 128
            ds_kt = newds("ds_kt%d" % layer)
            ds_v = newds("ds_v%d" % layer)
            ds_q = newds("ds_q%d" % layer)
            cc_k = {}
            cc_v = {}
            with ExitStack() as st:
                WG = [sb("WG%d_%d" % (layer, s), [128, 8, 512], BF16, st) for s in range(3)]
                dsW = [newds("ds_WG%d_%d" % (layer, s)) for s in range(3)]
                PR = [sb("PR%d_%d" % (layer, s), [128, 512], F32, st) for s in range(4)]
                TMP = [sb("TMP%d_%d" % (layer, s), [128, 256], F32, st) for s in range(4)]
                SQ = sb("SQ%d" % layer, [128, 512], F32, st)
                SSQ = sb("SSQ%d" % layer, [128, 8], F32, st)
                QK = [sb("QK%d_%d" % (layer, s), [128, 512], BF16, st) for s in range(4)]
                STG = [sb("STG%d_%d" % (layer, s), [128, 8320], BF16, st) for s in range(2)]
                pj = psum("pj%d" % layer, [128, 4, 512], F32, st)
                tq = psum("tq%d" % layer, [128, 4, 1024], BF16, st)

                m_w_free = [None, None, None]
                m_stg_free = [None, None]
                m_pj_free = [None] * 4
                m_pr_free = [None] * 4
                m_qk_free = [None] * 4
                m_tq_free = [None] * 4
                m_tmp_free = [None, None]
                m_sq_free = [None]
                pending = []
                cnt = 0

                def load_w(gi):
                    kind, c0, wd, _ = groups[gi]
                    s = gi % 3
                    sp.wait(m_w_free[s], dsw_of(gi).all())
                    dma(sp, dsW[s], WG[s][:, :, 0:wd], wsrc[:, c0:c0 + wd].rearrange("(k p) c -> p k c", p=128))

                load_w(0)
                load_w(1)
                cc_pending = []
                for gi, (kind, c0, wd, dbase) in enumerate(groups):
                    ws = gi % 3
                    ss = gi % 2
                    if gi + 2 < len(groups):
                        load_w(gi + 2)
                    isv = kind == "v"
                    nh = wd // 64
                    nu = wd // 128
                    stg = STG[ss]
                    if isv:
                        nhv = wd // dv
                        stg_v = stg[:, 0:nhv * 16 * (dv + 1)].rearrange("p (h i d) -> p h i d", h=nhv, i=16)
                        pool.wait(m_stg_free[ss])
                        m_ones = pool.mark(nc.gpsimd.memset(stg_v[:, :, :, dv:dv + 1], 1.0))
                    else:
                        stg_q = stg[:, 0:nu * T].rearrange("p (u t) -> p u t", u=nu)
                        m_ones = None
                    last_ev = None
                    for i in range(NT):
                        b = cnt % 4
                        cnt += 1
                        pe.wait(dsW[ws].all(), m_pj_free[b])
                        for k in range(8):
                            ins = nc.tensor.matmul(pj[:, b, 0:wd], lhsT=AT[:, k, i * 128:(i + 1) * 128], rhs=WG[ws][:, k, 0:wd],
                                                   start=(k == 0), stop=(k == 7))
                        m_mm = pe.mark(ins)
                        m_w_free[ws] = m_mm
                        while len(pending) > 2:
                            last_ev = pending.pop(0)()
                        if isv:
                            act.wait(m_mm, m_ones, m_stg_free[ss])
                            m_c = act.mark(nc.scalar.copy(out=stg_v[:, :, i, 0:dv],
                                                          in_=pj[:, b, 0:wd].rearrange("p (h d) -> p h d", d=dv)))
                            m_pj_free[b] = m_c
                            last_ev = m_c
                            continue
                        act.wait(m_mm, m_pr_free[b])
                        m_c = act.mark(nc.scalar.copy(out=PR[b][:, 0:wd], in_=pj[:, b, 0:wd]))
                        m_pj_free[b] = m_c
                        pr3 = PR[b][:, 0:wd].rearrange("p (h d) -> p h d", d=64)
                        m_src = m_c
                        if kind in ("qn", "kn"):
                            gsel = 0 if kind == "qn" else 1
                            pool.wait(m_src, m_sq_free[0])
                            m1 = pool.mark(nc.gpsimd.tensor_tensor(out=SQ[:, 0:wd], in0=PR[b][:, 0:wd], in1=PR[b][:, 0:wd], op=ALU.mult))
                            dve.wait(m1)
                            m2 = dve.mark(nc.vector.tensor_reduce(out=SSQ[:, 0:nh], in_=SQ[:, 0:wd].rearrange("p (h d) -> p h d", d=64),
                                                                  axis=AX.X, op=ALU.add))
                            m_sq_free[0] = m2
                            act.wait(m2)
                            m3 = act.mark(nc.scalar.activation(out=SSQ[:, 0:nh], in_=SSQ[:, 0:nh], func=AF.Sqrt,
                                                               bias=EPST[:, 0:1], scale=1.0 / 64))
                            dve.wait(m3)
                            m4 = dve.mark(nc.vector.reciprocal(out=SSQ[:, 0:nh], in_=SSQ[:, 0:nh]))
                            dve.wait(m4)
                            m5 = dve.mark(nc.vector.tensor_tensor(out=pr3, in0=pr3,
                                                                  in1=SSQ[:, 0:nh].unsqueeze(2).to_broadcast([128, nh, 64]), op=ALU.mult))
                            dve.wait(m5)
                            m_src = dve.mark(nc.vector.tensor_tensor(out=pr3, in0=pr3,
                                                                     in1=GQK[:, gsel, :].unsqueeze(1).to_broadcast([128, nh, 64]), op=ALU.mult))
                        ctab, stab = (2, 3) if kind in ("qn", "kn") else (0, 1)
                        cosb = ROPE[:, ctab, i, :].unsqueeze(1).to_broadcast([128, nh, 32])
                        sinb = ROPE[:, stab, i, :].unsqueeze(1).to_broadcast([128, nh, 32])
                        x1 = pr3[:, :, 0:32]
                        x2 = pr3[:, :, 32:64]
                        qk3 = QK[b][:, 0:wd].rearrange("p (h d) -> p h d", d=64)
                        t0 = TMP[0][:, 0:nh * 32].rearrange("p (h d) -> p h d", d=32)
                        t1 = TMP[1][:, 0:nh * 32].rearrange("p (h d) -> p h d", d=32)
                        t2 = TMP[2][:, 0:nh * 32].rearrange("p (h d) -> p h d", d=32)
                        t3 = TMP[3][:, 0:nh * 32].rearrange("p (h d) -> p h d", d=32)
                        if kind in ("qn", "kn"):
                            dve.wait(m_src, m_qk_free[b], m_tmp_free[0])
                            ma = dve.mark(nc.vector.tensor_tensor(out=t0, in0=x1, in1=cosb, op=ALU.mult))
                            mb = dve.mark(nc.vector.tensor_tensor(out=t1, in0=x2, in1=sinb, op=ALU.mult))
                            dve.wait(ma, mb)
                            mq1 = dve.mark(nc.vector.tensor_tensor(out=qk3[:, :, 0:32], in0=t0, in1=t1, op=ALU.subtract))
                            pool.wait(m_src, m_qk_free[b], m_tmp_free[1])
                            mc = pool.mark(nc.gpsimd.tensor_tensor(out=t2, in0=x1, in1=sinb, op=ALU.mult))
                            md = pool.mark(nc.gpsimd.tensor_tensor(out=t3, in0=x2, in1=cosb, op=ALU.mult))
                            pool.wait(mc, md)
                            mq2 = pool.mark(nc.gpsimd.tensor_tensor(out=qk3[:, :, 32:64], in0=t2, in1=t3, op=ALU.add))
                        else:
                            dve.wait(m_src, m_qk_free[b], m_tmp_free[0], m_tmp_free[1])
                            ma = dve.mark(nc.vector.tensor_tensor(out=t0, in0=x1, in1=cosb, op=ALU.mult))
                            mb = dve.mark(nc.vector.tensor_tensor(out=t1, in0=x2, in1=sinb, op=ALU.mult))
                            md = dve.mark(nc.vector.tensor_tensor(out=t3, in0=x2, in1=cosb, op=ALU.mult))
                            dve.wait(ma, mb)
                            mq1 = dve.mark(nc.vector.tensor_tensor(out=qk3[:, :, 0:32], in0=t0, in1=t1, op=ALU.subtract))
                            pool.wait(m_src, m_qk_free[b], m_tmp_free[1])
                            mc = pool.mark(nc.gpsimd.tensor_tensor(out=t2, in0=x1, in1=sinb, op=ALU.mult))
                            pool.wait(mc, md)
                            mq2 = pool.mark(nc.gpsimd.tensor_tensor(out=qk3[:, :, 32:64], in0=t2, in1=t3, op=ALU.add))
                        m_pr_free[b] = [mq1, mq2]
                        m_tmp_free[0] = mq1
                        m_tmp_free[1] = mq2

                        def do_tr(b=b, i=i, mq1=mq1, mq2=mq2):
                            pe.wait(mq1, mq2, m_tq_free[b])
                            for u in range(nu):
                                ins = nc.tensor.transpose(tq[:, b, u * 128:(u + 1) * 128], QK[b][:, u * 128:(u + 1) * 128], IDENT[:])
                            m_t = pe.mark(ins)
                            m_qk_free[b] = m_t
                            act.wait(m_t, m_stg_free[ss])
                            m_e = act.mark(nc.scalar.copy(out=stg_q[:, :, i * 128:(i + 1) * 128],
                                                          in_=tq[:, b, 0:wd].rearrange("p (u c) -> p u c", c=128)))
                            m_tq_free[b] = m_e
                            return m_e
                        pending.append(do_tr)
                    while pending:
                        last_ev = pending.pop(0)()
                    sp.wait(last_ev)
                    while cc_pending:
                        cc_pending.pop(0)()
                    if isv:
                        ng = nhv // VGH[layer]
                        g0 = dbase // VGH[layer]
                        src = stg[:, 0:nhv * 16 * (dv + 1)].rearrange("p (h f) -> p h f", h=nhv)
                        for gg in range(ng):
                            m_st = dma(sp, ds_v, v_loc[layer][g0 + gg].rearrange("(h p) f -> p h f", p=128),
                                       src[:, gg * VGH[layer]:(gg + 1) * VGH[layer], :])
                        m_stg_free[ss] = m_st

                        def launch_v(m_st=m_st, ng=ng, g0=g0):
                            pool.wait(m_st)
                            for gg in range(ng):
                                d = DS(nc, es, "cc_v%d_%d" % (layer, g0 + gg))
                                ins = nc.gpsimd.collective_compute("AllGather", ALU.bypass, replica_groups=[[0, 1, 2, 3], [4, 5, 6, 7]],
                                                                   ins=[v_loc[layer][g0 + gg].opt()], outs=[v_all[layer][g0 + gg].opt()])
                                ins.then_inc(d.sem, 1)
                                d.n = 1
                                cc_v[g0 + gg] = d
                        cc_pending.append(launch_v)
                    elif kind in ("q", "qn"):
                        m_stg_free[ss] = dma(sp, ds_q, qt[layer][dbase * 128:(dbase + nu) * 128, :].rearrange("(u p) t -> p u t", p=128), stg_q)
                    else:
                        for uu in range(nu):
                            m_st = dma(sp, ds_kt, kt_loc[layer][dbase + uu], stg_q[:, uu, :])
                        m_stg_free[ss] = m_st

                        def launch_k(m_st=m_st, nu=nu, dbase=dbase):
                            pool.wait(m_st)
                            for uu in range(nu):
                                d = DS(nc, es, "cc_k%d_%d" % (layer, dbase + uu))
                                ins = nc.gpsimd.collective_compute("AllGather", ALU.bypass, replica_groups=[[0, 1, 2, 3], [4, 5, 6, 7]],
                                                                   ins=[kt_loc[layer][dbase + uu].opt()], outs=[kt_all[layer][dbase + uu].opt()])
                                ins.then_inc(d.sem, 1)
                                d.n = 1
                                cc_k[dbase + uu] = d
                        cc_pending.append(launch_k)
                while cc_pending:
                    cc_pending.pop(0)()
                barrier()
            return cc_k, cc_v

        def attention(layer):
            dvp = 65 if layer == 0 else 129
            dv = dvp - 1
            nbank = 2 if layer == 0 else 3
            per_bank = 4 if layer == 0 else 3
            with ExitStack() as st:
                psS = psum("psS%d" % layer, [128, 2, 2, 512], F32, st)
                psA = psum("psA%d" % layer, [128, nbank, 512], F32, st)
                psT = psum("psT%d" % layer, [128, 1024], BF16, st)
                QT = [sb("QT%d_%d" % (layer, s), [128, T], BF16, st) for s in range(2)]
                dsQ = [newds("dsQ%d_%d" % (layer, s)) for s in range(2)]
                PP = sb("PP%d" % layer, [128, 3, 2, 512], BF16, st)
                ACCS = sb("ACCS%d" % layer, [128, 8 * dvp], F32, st)
                RDEN = sb("RDEN%d" % layer, [128, 8], F32, st)
                MT = sb("MT%d" % layer, [128, 512], BF16, st)
                if layer == 1:
                    O1 = sb("O1", [128, 512], F32, st)
                    O2 = sb("O2", [128, 512], F32, st)
                    SS4 = sb("SS4", [128, 4], F32, st)
                    RS4 = sb("RS4", [128, 4], F32, st)
                if layer == 0:
                    KTA = sb("KTA", [128, 4, T], BF16, st)
                    VA = sb("VA", [128, 4, 16, 65], BF16, st)
                    dsKA = newds("dsKA")
                    KTB = [sb("KTB%d" % s, [128, NM * 128], BF16, st) for s in range(2)]
                    VB = [sb("VB%d" % s, [128, 2, NM, 65], BF16, st) for s in range(2)]
                    dsKB = [newds("dsKB%d" % s) for s in range(2)]
                    MASK = sb("MASK", [128, NM, 512], BF16, st)
                    dsM = newds("dsM")
                else:
                    KT = [sb("KT1_%d" % s, [128, 4, T], BF16, st) for s in range(2)]
                    V1 = [sb("V1_%d" % s, [128, 4, 16, 129], BF16, st) for s in range(2)]
                    dsKV = [newds("dsKV%d" % s) for s in range(2)]

                state = dict(m_s_free=[None, None], m_p_free=[None] * 3,
                             m_acc_free=None, m_accs_free=None, m_mt_free=None, m_psT_free=None, kbc=0)
                m_q_free = [None, None]

                def acc_ap(n):
                    bk, o = n // per_bank, (n % per_bank) * dvp
                    return psA[:, bk, o:o + dvp], bk

                def emit_s(sp_, kb):
                    la, lb, _, _, _, jl_ = sp_["blocks"][kb]
                    s = (sp_["base"] + kb) % 2
                    g_ = sp_["g"]
                    qt_ = sp_["qtile"]
                    c0, c1 = jl_[0] * 128, (jl_[-1] + 1) * 128
                    pe.wait(sp_["m_load"], state["m_s_free"][s])
                    nc.tensor.matmul(psS[:, s, 0, c0:c1], lhsT=la, rhs=qt_[0:64, g_ * 512 + c0:g_ * 512 + c1], start=True, stop=True)
                    ins = nc.tensor.matmul(psS[:, s, 1, c0:c1], lhsT=lb, rhs=qt_[64:128, g_ * 512 + c0:g_ * 512 + c1], start=True, stop=True)
                    sp_["m_s"][kb] = pe.mark(ins)

                def run_groups(spec_iter):
                    it = iter(spec_iter)

                    def start(spec, base):
                        spec["base"] = base
                        spec["m_s"] = {}

                    cur = next(it, None)
                    while cur == "SYNC":
                        cur = next(it, None)
                    start(cur, state["kbc"])
                    emit_s(cur, 0)
                    emit_s(cur, 1)
                    while cur is not None:
                        nxt = next(it, None)
                        if nxt == "SYNC":
                            qgroup(cur, None)
                            nxt = next(it, None)
                            if nxt is not None:
                                start(nxt, cur["base"] + len(cur["blocks"]))
                                emit_s(nxt, 0)
                                emit_s(nxt, 1)
                        else:
                            if nxt is not None:
                                start(nxt, cur["base"] + len(cur["blocks"]))
                            qgroup(cur, nxt)
                        cur = nxt

                def qgroup(spec, nxt):
                    blocks = spec["blocks"]
                    g = spec["g"]
                    chunk = spec["chunk"]
                    m_buf_free_cb = spec["cb"]
                    m_s = spec["m_s"]
                    nb = len(blocks)
                    started = set()
                    last_pv = None
                    state["kbc_base"] = spec["base"]
                    for kb in range(nb):
                        _, _, va, vb, mi, jl = blocks[kb]
                        sl = state["kbc_base"] + kb
                        s = sl % 2
                        ps3 = sl % 3
                        act.wait(m_s[kb], state["m_p_free"][ps3])
                        c0, c1 = jl[0] * 128, (jl[-1] + 1) * 128
                        me = act.mark(nc.scalar.activation(out=PP[:, ps3, :, c0:c1], in_=psS[:, s, :, c0:c1], func=AF.Exp, scale=0.125))
                        state["m_s_free"][s] = me
                        if mi is not None:
                            dve.wait(me)
                            me = dve.mark(nc.vector.tensor_tensor(out=PP[:, ps3, :, c0:c1], in0=PP[:, ps3, :, c0:c1],
                                                                  in1=MASK[:, mi, c0:c1].unsqueeze(1).to_broadcast([128, 2, c1 - c0]), op=ALU.mult))
                        if kb + 2 < nb:
                            emit_s(spec, kb + 2)
                        elif nxt is not None:
                            emit_s(nxt, kb + 2 - nb)
                        while state.get("deferred") and state["deferred"][0][0] <= kb:
                            state["deferred"].pop(0)[1]()
                        pe.wait(me, state["m_acc_free"])
                        ins = None
                        for mp in range(2):
                            vv = va if mp == 0 else vb
                            for j in jl:
                                o_ap, bk = acc_ap(mp * 4 + j)
                                first = bk not in started
                                started.add(bk)
                                ins = nc.tensor.matmul(o_ap, lhsT=PP[:, ps3, mp, j * 128:(j + 1) * 128], rhs=vv,
                                                       start=first, stop=(kb == nb - 1), skip_group_check=True)
                        m_pv = pe.mark(ins)
                        state["m_p_free"][ps3] = m_pv
                        last_pv = m_pv
                    state["kbc"] = state["kbc_base"] + nb
                    m_buf_free_cb(last_pv)
                    dve.wait(last_pv, state["m_accs_free"])
                    m = None
                    for bk in range(nbank):
                        ncols = min(per_bank, 8 - bk * per_bank) * dvp
                        m = dve.mark(nc.vector.tensor_copy(out=ACCS[:, bk * per_bank * dvp: bk * per_bank * dvp + ncols], in_=psA[:, bk, 0:ncols]))
                    state["m_acc_free"] = m
                    m_cp = m

                    a3 = ACCS[:].rearrange("p (n d) -> p n d", d=dvp)
                    box = {}

                    def ep_dve1(m=m_cp):
                        dve.wait(m)
                        m = dve.mark(nc.vector.reciprocal(out=RDEN[:], in_=a3[:, :, dv]))
                        dve.wait(m, state["m_mt_free"])
                        if layer == 0:
                            mt3 = MT[:].rearrange("p (j f) -> p j f", f=128)
                            for mp in range(2):
                                m = dve.mark(nc.vector.tensor_tensor(out=mt3[:, :, mp * 64:(mp + 1) * 64], in0=a3[:, mp * 4:(mp + 1) * 4, 0:64],
                                                                     in1=RDEN[:, mp * 4:(mp + 1) * 4].unsqueeze(2).to_broadcast([128, 4, 64]), op=ALU.mult))
                            state["m_accs_free"] = m
                            box["m_mt"] = m
                        else:
                            o13 = O1[:].rearrange("p (j f) -> p j f", f=128)
                            o23 = O2[:].rearrange("p (j f) -> p j f", f=128)
                            m = dve.mark(nc.vector.tensor_scalar(out=RDEN[:, 4:8], in0=RDEN[:, 4:8], scalar1=NEGLAM[:, 0:1], scalar2=None, op0=ALU.mult))
                            dve.wait(m)
                            nc.vector.tensor_tensor(out=o13, in0=a3[:, 0:4, 0:128], in1=RDEN[:, 0:4].unsqueeze(2).to_broadcast([128, 4, 128]), op=ALU.mult)
                            m = dve.mark(nc.vector.tensor_tensor(out=o23, in0=a3[:, 4:8, 0:128], in1=RDEN[:, 4:8].unsqueeze(2).to_broadcast([128, 4, 128]), op=ALU.mult))
                            state["m_accs_free"] = m
                            dve.wait(m)
                            m = dve.mark(nc.vector.tensor_tensor(out=O1[:], in0=O1[:], in1=O2[:], op=ALU.add))
                            dve.wait(m)
                            m = dve.mark(nc.vector.tensor_tensor(out=O2[:], in0=O1[:], in1=O1[:], op=ALU.mult))
                            dve.wait(m)
                            box["m_ss"] = dve.mark(nc.vector.tensor_reduce(out=SS4[:], in_=o23, axis=AX.X, op=ALU.add))

                    def ep_act():
                        act.wait(box["m_ss"])
                        m = act.mark(nc.scalar.activation(out=RS4[:], in_=SS4[:], func=AF.Ln, bias=EPST[:, 0:1], scale=1.0 / 128))
                        act.wait(m)
                        box["m_rs"] = act.mark(nc.scalar.activation(out=RS4[:], in_=RS4[:], func=AF.Exp, scale=-0.5))

                    def ep_dve2():
                        o13 = O1[:].rearrange("p (j f) -> p j f", f=128)
                        dve.wait(box["m_rs"])
                        m = dve.mark(nc.vector.tensor_tensor(out=o13, in0=o13, in1=RS4[:].unsqueeze(2).to_broadcast([128, 4, 128]), op=ALU.mult))
                        dve.wait(m)
                        box["m_mt"] = dve.mark(nc.vector.tensor_tensor(out=MT[:].rearrange("p (j f) -> p j f", f=128), in0=o13,
                                                                       in1=GAINC[:].unsqueeze(1).to_broadcast([128, 4, 128]), op=ALU.mult))

                    def ep_tr(chunk=chunk, g=g):
                        pe.wait(box["m_mt"], state["m_psT_free"])
                        for j in range(4):
                            ins = nc.tensor.transpose(psT[:, j * 128:(j + 1) * 128], MT[:, j * 128:(j + 1) * 128], IDENT[:])
                        m_t = pe.mark(ins)
                        state["m_mt_free"] = m_t
                        dve.wait(m_t)
                        state["m_psT_free"] = dve.mark(nc.vector.tensor_copy(out=AT[:, chunk, g * 512:(g + 1) * 512], in_=psT[:, 0:512]))

                    if layer == 0:
                        state["deferred"] = [(4, ep_dve1), (10, ep_tr)]
                    else:
                        state["deferred"] = [(4, ep_dve1), (16, ep_act), (18, ep_dve2), (24, ep_tr)]

                def load_q(u, slot):
                    sp.wait(m_q_free[slot])
                    return dma(sp, dsQ[slot], QT[slot][:], qt[layer][u * 128:(u + 1) * 128, :])

                if layer == 0:
                    cc_k, cc_v = cc[0]
                    kA = kt_all[0][0].rearrange("(r p) t -> p r t", p=128)
                    vA = v_all[0][0].rearrange("(r h p) f -> h p r f", r=4, h=2)
                    ds_rl = newds("ds_rl")
                    ds_win = newds("ds_win")

                    def emit_relayout():
                        pool.wait(ds_pad.all())
                        for u in range(4):
                            pool.wait(cc_k[1 + u].all())
                            dma(pool, ds_rl, ktB_pad[u * 128:(u + 1) * 128, 1024:1024 + 4 * T].rearrange("p (r t) -> p r t", r=4),
                                kt_all[0][1 + u].rearrange("(r p) t -> p r t", p=128))
                        for gi in range(4):
                            pool.wait(cc_v[1 + gi].all())
                            srcv = v_all[0][1 + gi].rearrange("(r h p) f -> h p r f", r=4, h=2)
                            for hh in range(2):
                                hB = 2 * gi + hh
                                dma(pool, ds_rl, vB_pad[hB * 128:(hB + 1) * 128, 8 * 65:72 * 65].rearrange("p (r f) -> p r f", r=4), srcv[hh])
                        pid = nc.gpsimd.partition_id()
                        rank = pid % 4
                        pool.wait(ds_rl.all())
                        dma(pool, ds_win, ktB_win, ktB_pad[:, bass.ds(rank * T, 4096)])
                        dma(pool, ds_win, vB_win, vB_pad[:, bass.ds(rank * (16 * 65), 32 * 65)])

                    unit_last = {}

                    def specs_AB():
                        mq = {0: load_q(0, 0)}
                        m_ka_free = None
                        for u in range(4):
                            kvh = u // 2
                            if u % 2 == 0:
                                if u == 2:
                                    yield "SYNC"
                                sp.wait(unit_last.get(u - 1), cc_k[0].all(), cc_v[0].all())
                                for half in range(2):
                                    dma(sp, dsKA, KTA[half * 64:(half + 1) * 64, :, :], kA[kvh * 64:(kvh + 1) * 64, :, :])
                                dma(sp, dsKA, VA[:].rearrange("p r i d -> p r (i d)"), vA[kvh])
                            lh = []
                            for g in range(4):
                                if g == 1:
                                    if u == 0:
                                        pool.wait(mq[0], dsKA.all())
                                        emit_relayout()
                                    if u == 1:
                                        pool.wait(unit_last[0])
                                        casts_layer0_rest()
                                        dma(sp, dsM, MASK[:], masks_in.rearrange("m p q -> p m q"))
                                    sp.wait(unit_last.get(u - 1))
                                    mq[u + 1] = dma(sp, dsQ[(u + 1) % 2], QT[(u + 1) % 2][:], qt[0][(u + 1) * 128:(u + 2) * 128, :])
                                blocks = []
                                for r in range(4):
                                    for i in range(16):
                                        blocks.append((KTA[0:64, r, i * 128:(i + 1) * 128], KTA[64:128, r, i * 128:(i + 1) * 128],
                                                       VA[:, r, i, :], VA[:, r, i, :], None, [0, 1, 2, 3]))

                                def cbA(m, u=u, lh=lh):
                                    lh.append(m)
                                    unit_last[u] = m
                                yield dict(qtile=QT[u % 2], g=g, chunk=u, blocks=blocks, m_load=[mq[u], dsKA.all()], cb=cbA)
                        m_kb_free = [None, None]
                        widx = 0
                        for ub in range(4):
                            u = 4 + ub
                            for g in range(4):
                                ws = widx % 2
                                widx += 1
                                sp.wait(m_kb_free[ws], ds_win.all())
                                dma(sp, dsKB[ws], KTB[ws][:], ktB_win[ub * 128:(ub + 1) * 128, g * 512:g * 512 + NM * 128])
                                for hh in range(2):
                                    h = 2 * ub + hh
                                    dma(sp, dsKB[ws], VB[ws][:, hh, :, :].rearrange("p m d -> p (m d)"),
                                        vB_win[h * 128:(h + 1) * 128, g * 4 * 65:(g * 4 + NM) * 65])
                                if g == 1 and ub == 0:
                                    pool.wait(unit_last[3])
                                    casts_layer1_first()
                                if g == 1 and u < 7:
                                    sp.wait(unit_last.get(u - 1))
                                    mq[u + 1] = dma(sp, dsQ[(u + 1) % 2], QT[(u + 1) % 2][:], qt[0][(u + 1) * 128:(u + 2) * 128, :])
                                blocks = []
                                for mi in range(NM):
                                    jl = [j for j in range(4) if j <= mi <= j + 16]
                                    blocks.append((KTB[ws][0:64, mi * 128:(mi + 1) * 128], KTB[ws][64:128, mi * 128:(mi + 1) * 128],
                                                   VB[ws][:, 0, mi, :], VB[ws][:, 1, mi, :], mi, jl))

                                def cbB(m, ws=ws, u=u):
                                    m_kb_free[ws] = m
                                    unit_last[u] = m
                                yield dict(qtile=QT[u % 2], g=g, chunk=u, blocks=blocks, m_load=[mq[u], dsKB[ws].all(), dsM.all()], cb=cbB)
                    run_groups(specs_AB())
                else:
                    cc_k, cc_v = cc[1]
                    m_kv_free = [None, None]

                    def load_kv(h, slot):
                        sp.wait(m_kv_free[slot], cc_k[h].all(), cc_v[h].all())
                        dma(sp, dsKV[slot], KT[slot][:], kt_all[1][h].rearrange("(r p) t -> p r t", p=128))
                        dma(sp, dsKV[slot], V1[slot][:].rearrange("p r i d -> p r (i d)"), v_all[1][h].rearrange("(r p) f -> p r f", p=128))

                    head_last = {}

                    def specs_C():
                        mq = {0: load_q(0, 0)}
                        load_kv(0, 0)
                        for h in range(8):
                            slot = h % 2
                            for g in range(4):
                                if g == 1 and h < 7:
                                    if h == 3:
                                        pool.wait(head_last[2])
                                        casts_layer1_rest()
                                    m_kv_free[1 - slot] = head_last.get(h - 1)
                                    sp.wait(head_last.get(h - 1))
                                    mq[h + 1] = dma(sp, dsQ[1 - slot], QT[1 - slot][:], qt[1][(h + 1) * 128:(h + 2) * 128, :])
                                    load_kv(h + 1, 1 - slot)
                                blocks = []
                                for r in range(4):
                                    for i in range(16):
                                        blocks.append((KT[slot][0:64, r, i * 128:(i + 1) * 128], KT[slot][64:128, r, i * 128:(i + 1) * 128],
                                                       V1[slot][:, r, i, :], V1[slot][:, r, i, :], None, [0, 1, 2, 3]))

                                def cbC(m, h=h):
                                    head_last[h] = m
                                yield dict(qtile=QT[slot], g=g, chunk=h, blocks=blocks, m_load=[mq[h], dsKV[slot].all()], cb=cbC)
                    run_groups(specs_C())
                while state.get("deferred"):
                    state["deferred"].pop(0)[1]()
                barrier()

        def outproj(layer):
            with ExitStack() as st:
                WO = sb("WO%d" % layer, [128, 8, D], BF16, st)
                dsO = newds("dsO%d" % layer)
                psO = psum("psO%d" % layer, [128, 2, 512], F32, st)
                sp.wait(ds_w["out%d" % layer].all())
                dma(sp, dsO, WO[:], wb_out[layer].rearrange("(k p) c -> p k c", p=128))
                m_free = [None, None]
                cnt = 0
                for i in range(NT):
                    for c in range(2):
                        b = cnt % 2
                        cnt += 1
                        pe.wait(dsO.all(), m_free[b])
                        for k in range(8):
                            ins = nc.tensor.matmul(psO[:, b, :], lhsT=AT[:, k, i * 128:(i + 1) * 128], rhs=WO[:, k, c * 512:(c + 1) * 512],
                                                   start=(k == 0), stop=(k == 7))
                        mm = pe.mark(ins)
                        dve.wait(mm)
                        m_free[b] = dve.mark(nc.vector.tensor_tensor(out=X[:, i, c * 512:(c + 1) * 512], in0=X[:, i, c * 512:(c + 1) * 512],
                                                                     in1=psO[:, b, :], op=ALU.add))
                barrier()

        def ffn(layer):
            with ExitStack() as st:
                WD = sb("WD%d" % layer, [128, NF, D], BF16, st)
                dsD = newds("dsD%d" % layer)
                WGU = [sb("WGU%d_%d" % (layer, s), [128, 2, 8, 256], BF16, st) for s in range(2)]
                dsGU = [newds("dsGU%d_%d" % (layer, s)) for s in range(2)]
                AFT = sb("AFT%d" % layer, [128, NF, 512], BF16, st)
                SG = [sb("SG%d_%d" % (layer, s), [128, 512], F32, st) for s in range(2)]
                psG = psum("psG%d" % layer, [128, 2, 512], F32, st)
                psU = psum("psU%d" % layer, [128, 2, 512], F32, st)
                psD = psum("psD%d" % layer, [128, 2, 512], F32, st)
                sp.wait(ds_w["d%d" % layer].all(), ds_w["g%d" % layer].all(), ds_w["u%d" % layer].all())
                for half in range(2):
                    dma(sp, dsD, WD[:, half * 11:(half + 1) * 11, :],
                        wb_d[layer][half * 11 * 128:(half + 1) * 11 * 128, :].rearrange("(f p) c -> p f c", p=128))
                m_gu_free = [None, None]
                m_g_free = [None, None]
                m_u_free = [None, None]
                m_sg_free = [None, None]
                m_d_free = [None, None]
                m_down_last = None
                lc = 0
                fc_cnt = 0
                dcnt = 0

                def load_gu(fg, slot):
                    sp.wait(m_gu_free[slot])
                    dma(sp, dsGU[slot], WGU[slot][:, 0, :, :], wb_g[layer][:, fg * 256:(fg + 1) * 256].rearrange("(k p) c -> p k c", p=128))
                    dma(sp, dsGU[slot], WGU[slot][:, 1, :, :], wb_u[layer][:, fg * 256:(fg + 1) * 256].rearrange("(k p) c -> p k c", p=128))

                seq = [(tg, fg) for tg in range(4) for fg in range(11)]
                load_gu(seq[0][1], 0)
                for si, (tg, fg) in enumerate(seq):
                    slot = si % 2
                    if si + 1 < len(seq):
                        load_gu(seq[si + 1][1], 1 - slot)
                    for fc in range(2):
                        f = fg * 2 + fc
                        b = fc_cnt % 2
                        fc_cnt += 1
                        pe.wait(dsGU[slot].all(), m_g_free[b], m_u_free[b])
                        for k in range(8):
                            nc.tensor.matmul(psG[:, b, :], lhsT=WGU[slot][:, 0, k, fc * 128:(fc + 1) * 128], rhs=AT[:, k, tg * 512:(tg + 1) * 512],
                                             start=(k == 0), stop=(k == 7))
                        for k in range(8):
                            ins = nc.tensor.matmul(psU[:, b, :], lhsT=WGU[slot][:, 1, k, fc * 128:(fc + 1) * 128], rhs=AT[:, k, tg * 512:(tg + 1) * 512],
                                                   start=(k == 0), stop=(k == 7))
                        mm = pe.mark(ins)
                        m_gu_free[slot] = mm
                        act.wait(mm, m_sg_free[b])
                        ms = act.mark(nc.scalar.activation(out=SG[b][:], in_=psG[:, b, :], func=AF.Silu))
                        m_g_free[b] = ms
                        dve.wait(ms, m_down_last)
                        ma = dve.mark(nc.vector.tensor_tensor(out=AFT[:, f, :], in0=SG[b][:], in1=psU[:, b, :], op=ALU.mult))
                        m_u_free[b] = ma
                        m_sg_free[b] = ma
                        last_a = ma
                    if fg == 10:
                        for j in range(4):
                            for c in range(2):
                                b = dcnt % 2
                                dcnt += 1
                                pe.wait(last_a, dsD.all(), m_d_free[b])
                                for f in range(NF):
                                    ins = nc.tensor.matmul(psD[:, b, :], lhsT=AFT[:, f, j * 128:(j + 1) * 128], rhs=WD[:, f, c * 512:(c + 1) * 512],
                                                           start=(f == 0), stop=(f == NF - 1))
                                mm = pe.mark(ins)
                                m_down_last = mm
                                i = tg * 4 + j
                                dve.wait(mm)
                                m_d_free[b] = dve.mark(nc.vector.tensor_tensor(out=X[:, i, c * 512:(c + 1) * 512], in0=X[:, i, c * 512:(c + 1) * 512],
                                                                               in1=psD[:, b, :], op=ALU.add))
                barrier()

        def final():
            with ExitStack() as st:
                G, RSTD = rms_rstd(st, lambda i: X[:, i, :], NT, final_norm[0:1, :], "fin")
                OB = [sb("OB%d" % s, [128, D], F32, st) for s in range(2)]
                ds_o = newds("ds_out")
                m_free = [None, None]
                for i in range(NT):
                    s = i % 2
                    dve.wait(m_free[s])
                    m = dve.mark(nc.vector.scalar_tensor_tensor(out=OB[s][:], in0=X[:, i, :], scalar=RSTD[:, i:i + 1], in1=G[:],
                                                                op0=ALU.mult, op1=ALU.mult))
                    sp.wait(m)
                    m_free[s] = dma(sp, ds_o, out[i * 128:(i + 1) * 128, :], OB[s][:])
                sp.wait(ds_o.all())
                pool.wait(ds_o.all())

        cc = {}
        dve.wait(ds_c.all())
        act.wait(ds_c.all())
        pe.wait(ds_c.all())
        pool.wait(ds_c.all())

        def dump(layer):
            barrier()
            dsd = newds("ds_dbg")
            for i in range(NT):
                dma(sp, dsd, dbg["X"][i * 128:(i + 1) * 128, :], X[:, i, :])
            dma(sp, dsd, dbg["AT"], AT[:].rearrange("p k t -> p (k t)"))
            dma(sp, dsd, dbg["qt"], qt[layer])
            for u in range(NKU[layer]):
                dma(sp, dsd, dbg["kt"][u * 512:(u + 1) * 512, :], kt_all[layer][u])
            for g in range(NVG[layer]):
                nr = 4 * VGH[layer] * 128
                dma(sp, dsd, dbg["v"][g * nr:(g + 1) * nr, 0:VF[layer]], v_all[layer][g])
            sp.wait(dsd.all())
            pool.wait(dsd.all())

        steps = [
            ("norm_a0", lambda: norm_AT(attn_norm[0:1, :], "a0", tile_ready=lambda i: ds_xg[i // 4].all())),
            ("inproj0", lambda: cc.__setitem__(0, inproj(0))),
            ("attn0", lambda: attention(0)),
            ("outproj0", lambda: outproj(0)),
            ("norm_f0", lambda: norm_AT(ffn_norm[0:1, :], "f0")),
            ("ffn0", lambda: ffn(0)),
            ("norm_a1", lambda: norm_AT(attn_norm[1:2, :], "a1")),
            ("inproj1", lambda: cc.__setitem__(1, inproj(1))),
            ("attn1", lambda: attention(1)),
            ("outproj1", lambda: outproj(1)),
            ("norm_f1", lambda: norm_AT(ffn_norm[1:2, :], "f1")),
            ("ffn1", lambda: ffn(1)),
            ("final", final),
        ]
        for name, fn in steps:
            fn()
            if debug and debug == name:
                dump(0 if name in ("norm_a0", "inproj0", "attn0", "outproj0", "norm_f0", "ffn0", "norm_a1") else 1)
                break
    return nc


def _const_tables():
    theta = 10000.0
    pos = np.arange(S, dtype=np.float32)
    inv64 = (theta ** (-np.arange(0, 64, 2, dtype=np.float32) / 64)).astype(np.float32)
    inv32 = (theta ** (-np.arange(0, 32, 2, dtype=np.float32) / 32)).astype(np.float32)
    ang1 = (pos[:, None] * inv64[None, :]).astype(np.float32)
    rows = (np.arange(S) // 64).astype(np.float32)
    cols = (np.arange(S) % 64).astype(np.float32)
    ang2 = np.concatenate([rows[:, None] * inv32[None, :], cols[:, None] * inv32[None, :]], axis=-1).astype(np.float32)
    rope = np.stack([np.cos(ang1.astype(np.float64)), np.sin(ang1.astype(np.float64)),
                     np.cos(ang2.astype(np.float64)), np.sin(ang2.astype(np.float64))], 0).astype(np.float32)
    k = np.arange(128)[:, None]
    q = np.arange(512)[None, :]
    masks = np.zeros((NM, 128, 512), np.float32)
    for m in range(NM):
        d = 128 * m + k - q - 1024
        ad = np.abs(d)
        masks[m] = (ad <= 64).astype(np.float32) + ((d % 4 == 0) & (ad <= 256)) + ((d % 16 == 0) & (ad <= 1024))
    ident = np.eye(128, dtype=np.float32)
    return rope, masks.astype(ml_dtypes.bfloat16), ident.astype(ml_dtypes.bfloat16)


_NC_CACHE = {}


def kernel(x, attn_norm, ffn_norm, final_norm, w_in_even, a_q_norm, a_k_norm, w_out_even,
           w_in_odd, lambda_q1, lambda_k1, lambda_q2, lambda_k2, c_sub_norm, w_out_odd,
           w_gate, w_up, w_down, _debug=False):
    f = lambda a: np.ascontiguousarray(np.asarray(a, dtype=np.float32))
    x = f(x)
    rope, masks, ident = _const_tables()
    shared = {
        "attn_norm": f(attn_norm), "ffn_norm": f(ffn_norm), "final_norm": f(final_norm).reshape(1, D),
        "w_in_even": f(w_in_even)[0], "a_q_norm": f(a_q_norm), "a_k_norm": f(a_k_norm),
        "w_out_even": f(w_out_even)[0], "w_in_odd": f(w_in_odd)[0],
        "lambda_q1": f(lambda_q1), "lambda_k1": f(lambda_k1), "lambda_q2": f(lambda_q2), "lambda_k2": f(lambda_k2),
        "c_sub_norm": f(c_sub_norm), "w_out_odd": f(w_out_odd)[0],
        "w_gate": f(w_gate), "w_up": f(w_up), "w_down": f(w_down),
        "masks": masks, "ident": ident,
    }
    in_maps = []
    for c in range(NCORE):
        b, r = c // 4, c % 4
        m = dict(shared)
        m["x"] = np.ascontiguousarray(x[b, r * T:(r + 1) * T, :])
        m["rope"] = np.ascontiguousarray(rope[:, r * T:(r + 1) * T, :])
        in_maps.append(m)
    key = _debug
    if key not in _NC_CACHE:
        _NC_CACHE[key] = build(debug=key)
    nc = _NC_CACHE[key]
    res = run_bass_kernel_spmd(nc, in_maps, core_ids=list(range(NCORE)))
    outp = np.zeros((2, S, D), np.float32)
    for c in range(NCORE):
        b, r = c // 4, c % 4
        outp[b, r * T:(r + 1) * T, :] = np.asarray(res.results[c]["out"])
    if _debug:
        return outp, res
    return outp
```
